# Optimizing a Trainium2 kernel written in Bass

```python
import jax, jax.numpy as jnp
from jax import lax
import numpy as np

D_MODEL = 2048
BATCH = 4
SEQ = 2048
DEPTH = 2
DEC_BATCH = 128
DEC_SEQ = 4
PAST_LEN = 16384
PAGE_SIZE = 128

MIX_WIDTH = D_MODEL
BRANCH = MIX_WIDTH // 4
CHUNK = 128
EPS = 1e-6
ML_HEADS = 4
ML_DK = BRANCH // ML_HEADS
ML_DV = BRANCH // ML_HEADS
NEG_BIG = -1e30
SSD_HEADDIM = 64
SSD_HEADS = BRANCH // SSD_HEADDIM
SSD_GROUPS = 2
SSD_STATE = 128
SSD_CONV = 4
SSD_CONV_DIM = BRANCH + 2 * SSD_GROUPS * SSD_STATE
SC_WIDTH = 3
RG_BLOCKS = 8
RG_BLOCK_DIM = BRANCH // RG_BLOCKS
RG_CONV = 4
RG_C = 8.0

SPLIT_SIZES = (BRANCH, BRANCH, BRANCH, BRANCH, BRANCH, ML_HEADS, ML_HEADS,
               BRANCH, SSD_CONV_DIM, SSD_HEADS,
               BRANCH, BRANCH, BRANCH, BRANCH,
               BRANCH, BRANCH)
PROJ_OUT = sum(SPLIT_SIZES)

kernel_name = "hybrid_mlstm_ssd_shortconv_rglru_step"


def rms_norm(x, w):
    x32 = x.astype(jnp.float32)
    y = x32 * lax.rsqrt(jnp.mean(x32 * x32, axis=-1, keepdims=True) + EPS)
    return (y * w.astype(jnp.float32)).astype(x.dtype)


def split_columns(proj):
    idx, acc = [], 0
    for s in SPLIT_SIZES[:-1]:
        acc += s
        idx.append(acc)
    return jnp.split(proj, idx, axis=-1)


def pick_chunk(t):
    return CHUNK if t % CHUNK == 0 else t


def to_chunks(a, chunk):
    bsz, t = a.shape[:2]
    a = a.reshape((bsz, t // chunk, chunk) + a.shape[2:])
    return jnp.moveaxis(jnp.moveaxis(a, 1, 0), 3, 2)


def from_chunks(a):
    nc, bsz, h, l, d = a.shape
    return jnp.transpose(a, (1, 0, 3, 2, 4)).reshape(bsz, nc * l, h, d)


def causal_conv(u, buf, w, b=None):
    width = w.shape[0]
    t = u.shape[1]
    up = jnp.concatenate([buf.astype(u.dtype), u], axis=1)
    y = up[:, 0:t] * w[0]
    for j in range(1, width):
        y = y + up[:, j:j + t] * w[j]
    if b is not None:
        y = y + b
    return y, up[:, t:]


def mlstm_chunkwise(q, k, v, i_pre, f_pre, c0, n0, m0, chunk):
    lf = jax.nn.log_sigmoid(f_pre)
    causal = jnp.tril(jnp.ones((chunk, chunk), dtype=bool))
    xs = (to_chunks(q, chunk), to_chunks(k, chunk), to_chunks(v, chunk),
          to_chunks(i_pre, chunk), to_chunks(lf, chunk))

    def step(carry, inp):
        c, n, m = carry
        qc, kc, vc, ic, lfc = inp
        b = jnp.cumsum(lfc, axis=-1)
        dmat = b[..., :, None] - b[..., None, :] + ic[..., None, :]
        dmat = jnp.where(causal, dmat, -jnp.inf)
        inter = b + m[..., None]
        m_t = jnp.maximum(inter, jnp.max(dmat, axis=-1))
        w_intra = jnp.exp(dmat - m_t[..., None])
        w_inter = jnp.exp(inter - m_t)
        s = jnp.einsum('bhld,bhsd->bhls', qc, kc) * w_intra
        num = (jnp.einsum('bhls,bhsv->bhlv', s, vc)
               + w_inter[..., None] * jnp.einsum('bhld,bhdv->bhlv', qc, c))
        den = jnp.sum(s, axis=-1) + w_inter * jnp.einsum('bhld,bhd->bhl', qc, n)
        h = num / jnp.maximum(jnp.abs(den), jnp.exp(-m_t))[..., None]
        b_last = b[..., -1]
        g = b_last[..., None] - b + ic
        inter_end = b_last + m
        m_new = jnp.maximum(inter_end, jnp.max(g, axis=-1))
        wg = jnp.exp(g - m_new[..., None])
        we = jnp.exp(inter_end - m_new)
        c_new = we[..., None, None] * c + jnp.einsum('bhl,bhld,bhlv->bhdv', wg, kc, vc)
        n_new = we[..., None] * n + jnp.einsum('bhl,bhld->bhd', wg, kc)
        return (c_new, n_new, m_new), h

    (c1, n1, m1), h = lax.scan(step, (c0, n0, m0), xs)
    return from_chunks(h), c1, n1, m1


def ssd_chunkwise(x, dt, a_coef, bmat, cmat, s0, chunk):
    da = dt * a_coef
    causal = jnp.tril(jnp.ones((chunk, chunk), dtype=bool))
    xs = (to_chunks(x, chunk), to_chunks(dt, chunk), to_chunks(da, chunk),
          to_chunks(bmat, chunk), to_chunks(cmat, chunk))

    def step(s, inp):
        xc, dtc, dac, bc, cc = inp
        acs = jnp.cumsum(dac, axis=-1)
        seg = jnp.where(causal, acs[..., :, None] - acs[..., None, :], -jnp.inf)
        scores = jnp.einsum('bhln,bhsn->bhls', cc, bc) * jnp.exp(seg) * dtc[..., None, :]
        y = (jnp.einsum('bhls,bhsp->bhlp', scores, xc)
             + jnp.exp(acs)[..., None] * jnp.einsum('bhln,bhpn->bhlp', cc, s))
        a_last = acs[..., -1]
        w = jnp.exp(a_last[..., None] - acs) * dtc
        s_new = jnp.exp(a_last)[..., None, None] * s + jnp.einsum('bhl,bhlp,bhln->bhpn', w, xc, bc)
        return s_new, y

    s1, y = lax.scan(step, s0, xs)
    return from_chunks(y), s1


def rglru_scan(xr, r, ig, lam, h0):
    log_a = -RG_C * r * jax.nn.softplus(-lam)
    a = jnp.exp(log_a)
    bterm = jnp.sqrt(-jnp.expm1(2.0 * log_a)) * (ig * xr)
    bterm = bterm.at[:, 0].add(a[:, 0] * h0)

    def combine(e1, e2):
        a1, b1 = e1
        a2, b2 = e2
        return a1 * a2, a2 * b1 + b2

    _, h = lax.associative_scan(combine, (a, bterm), axis=1)
    return h, h[:, -1]


def mixer_layer(x, state, p, chunk):
    c0, n0, m0, s0, ssd_buf, sc_buf, h0, rg_buf = state
    (norm_w, w_in, ml_i_bias, ml_f_bias, ml_norm_w, ssd_conv_w, ssd_conv_b, ssd_dt_bias,
     ssd_A_log, ssd_D, ssd_norm_w, sc_conv_w, rg_conv_w, rg_conv_b, rg_wa, rg_ba, rg_wx,
     rg_bx, rg_lambda, w_out) = p
    f32 = jnp.float32
    bsz, t, _ = x.shape
    xn = rms_norm(x, norm_w)
    proj = xn @ w_in
    (ml_q, ml_k, ml_v, ml_o, ml_z, ml_i, ml_f, ssd_z, ssd_xbc, ssd_dt,
     sc_b, sc_c, sc_h, sc_z, rg_x, rg_z) = split_columns(proj)

    q = ml_q.reshape(bsz, t, ML_HEADS, ML_DK).astype(f32)
    k = ml_k.reshape(bsz, t, ML_HEADS, ML_DK).astype(f32) * (ML_DK ** -0.5)
    v = ml_v.reshape(bsz, t, ML_HEADS, ML_DV).astype(f32)
    i_pre = (ml_i + ml_i_bias).astype(f32)
    f_pre = (ml_f + ml_f_bias).astype(f32)
    h_ml, c1, n1, m1 = mlstm_chunkwise(q, k, v, i_pre, f_pre, c0.astype(f32), n0.astype(f32),
                                       m0.astype(f32), chunk)
    mu = jnp.mean(h_ml, axis=-1, keepdims=True)
    var = jnp.mean(jnp.square(h_ml - mu), axis=-1, keepdims=True)
    h_ml = (h_ml - mu) * lax.rsqrt(var + EPS) * ml_norm_w.astype(f32).reshape(ML_HEADS, ML_DV)
    h_ml = h_ml.reshape(bsz, t, BRANCH)
    y_ml = (jax.nn.sigmoid(ml_o.astype(f32)) * h_ml * jax.nn.silu(ml_z.astype(f32))).astype(x.dtype)

    xbc, ssd_buf1 = causal_conv(ssd_xbc, ssd_buf, ssd_conv_w, ssd_conv_b)
    xbc = jax.nn.silu(xbc.astype(f32))
    xs_, bm, cm = jnp.split(xbc, [BRANCH, BRANCH + SSD_GROUPS * SSD_STATE], axis=-1)
    xs_ = xs_.reshape(bsz, t, SSD_HEADS, SSD_HEADDIM)
    rep = SSD_HEADS // SSD_GROUPS
    bm = jnp.repeat(bm.reshape(bsz, t, SSD_GROUPS, SSD_STATE), rep, axis=2)
    cm = jnp.repeat(cm.reshape(bsz, t, SSD_GROUPS, SSD_STATE), rep, axis=2)
    dt = jax.nn.softplus((ssd_dt + ssd_dt_bias).astype(f32))
    a_coef = -jnp.exp(ssd_A_log.astype(f32))
    y_ssd, s1 = ssd_chunkwise(xs_, dt, a_coef, bm, cm, s0.astype(f32), chunk)
    y_ssd = y_ssd + ssd_D.astype(f32)[:, None] * xs_
    y_ssd = y_ssd.reshape(bsz, t, BRANCH) * jax.nn.silu(ssd_z.astype(f32))
    yg = y_ssd.reshape(bsz, t, SSD_GROUPS, BRANCH // SSD_GROUPS)
    yg = yg * lax.rsqrt(jnp.mean(yg * yg, axis=-1, keepdims=True) + EPS)
    y_ssd = (yg.reshape(bsz, t, BRANCH) * ssd_norm_w.astype(f32)).astype(x.dtype)

    u = sc_c * sc_h
    cu, sc_buf1 = causal_conv(u, sc_buf, sc_conv_w)
    y_sc = (sc_b * cu * jax.nn.silu(sc_z)).astype(x.dtype)

    xr, rg_buf1 = causal_conv(rg_x, rg_buf, rg_conv_w, rg_conv_b)
    xr32 = xr.astype(f32)
    xb = xr32.reshape(bsz, t, RG_BLOCKS, RG_BLOCK_DIM)
    r = jax.nn.sigmoid(jnp.einsum('btnd,nde->btne', xb, rg_wa.astype(f32)).reshape(bsz, t, BRANCH)
                       + rg_ba.astype(f32))
    ig = jax.nn.sigmoid(jnp.einsum('btnd,nde->btne', xb, rg_wx.astype(f32)).reshape(bsz, t, BRANCH)
                        + rg_bx.astype(f32))
    h_rg, h1 = rglru_scan(xr32, r, ig, rg_lambda.astype(f32), h0.astype(f32))
    y_rg = (h_rg * jax.nn.silu(rg_z.astype(f32))).astype(x.dtype)

    mix = jnp.concatenate([y_ml, y_ssd, y_sc, y_rg], axis=-1)
    y = x + mix @ w_out
    dtp = x.dtype
    new_state = (c1.astype(dtp), n1.astype(dtp), m1.astype(dtp), s1.astype(dtp),
                 ssd_buf1.astype(dtp), sc_buf1.astype(dtp), h1.astype(dtp), rg_buf1.astype(dtp))
    return y, new_state


def setup_inputs(seed: int = 0) -> dict:
    key = jax.random.key(seed)
    ks = jax.random.split(key, 40)
    f32 = jnp.float32
    nrm = lambda k, shape, s: jax.random.normal(k, shape, f32) * s
    L = DEPTH
    u_dt = jax.random.uniform(ks[20], (L, SSD_HEADS), f32, np.log(1e-3), np.log(1e-1))
    dt0 = jnp.exp(u_dt)
    u_a = jax.random.uniform(ks[25], (L, BRANCH), f32, 0.9, 0.999)
    return {
        "x_prompt": nrm(ks[0], (BATCH, SEQ, D_MODEL), 1.0),
        "x_sample": nrm(ks[1], (DEC_BATCH, DEC_SEQ, D_MODEL), 1.0),
        "state_mlstm_C": nrm(ks[2], (L, DEC_BATCH, ML_HEADS, ML_DK, ML_DV), 0.1),
        "state_mlstm_n": nrm(ks[3], (L, DEC_BATCH, ML_HEADS, ML_DK), 0.1),
        "state_mlstm_m": nrm(ks[4], (L, DEC_BATCH, ML_HEADS), 0.5),
        "state_ssd": nrm(ks[5], (L, DEC_BATCH, SSD_HEADS, SSD_HEADDIM, SSD_STATE), 0.1),
        "state_ssd_conv": nrm(ks[6], (L, DEC_BATCH, SSD_CONV - 1, SSD_CONV_DIM), 1.0),
        "state_sconv_conv": nrm(ks[7], (L, DEC_BATCH, SC_WIDTH - 1, BRANCH), 1.0),
        "state_rglru_h": nrm(ks[8], (L, DEC_BATCH, BRANCH), 0.5),
        "state_rglru_conv": nrm(ks[9], (L, DEC_BATCH, RG_CONV - 1, BRANCH), 1.0),
        "norm_w": 1.0 + nrm(ks[10], (L, D_MODEL), 0.02),
        "w_in": nrm(ks[11], (L, D_MODEL, PROJ_OUT), D_MODEL ** -0.5),
        "ml_i_bias": nrm(ks[12], (L, ML_HEADS), 0.1),
        "ml_f_bias": 3.0 + nrm(ks[13], (L, ML_HEADS), 0.5),
        "ml_norm_w": 1.0 + nrm(ks[14], (L, BRANCH), 0.02),
        "ssd_conv_w": nrm(ks[15], (L, SSD_CONV, SSD_CONV_DIM), SSD_CONV ** -0.5),
        "ssd_conv_b": nrm(ks[16], (L, SSD_CONV_DIM), 0.02),
        "ssd_dt_bias": dt0 + jnp.log(-jnp.expm1(-dt0)),
        "ssd_A_log": jnp.log(jax.random.uniform(ks[17], (L, SSD_HEADS), f32, 1.0, 16.0)),
        "ssd_D": 1.0 + nrm(ks[18], (L, SSD_HEADS), 0.1),
        "ssd_norm_w": 1.0 + nrm(ks[19], (L, BRANCH), 0.02),
        "sc_conv_w": nrm(ks[21], (L, SC_WIDTH, BRANCH), SC_WIDTH ** -0.5),
        "rg_conv_w": nrm(ks[22], (L, RG_CONV, BRANCH), RG_CONV ** -0.5),
        "rg_conv_b": nrm(ks[23], (L, BRANCH), 0.02),
        "rg_wa": nrm(ks[24], (L, RG_BLOCKS, RG_BLOCK_DIM, RG_BLOCK_DIM), RG_BLOCK_DIM ** -0.5),
        "rg_ba": nrm(ks[26], (L, BRANCH), 0.02),
        "rg_wx": nrm(ks[27], (L, RG_BLOCKS, RG_BLOCK_DIM, RG_BLOCK_DIM), RG_BLOCK_DIM ** -0.5),
        "rg_bx": nrm(ks[28], (L, BRANCH), 0.02),
        "rg_lambda": jnp.log(u_a / (1.0 - u_a)),
        "w_out": nrm(ks[29], (L, MIX_WIDTH, D_MODEL), MIX_WIDTH ** -0.5),
        "final_norm_w": 1.0 + nrm(ks[30], (D_MODEL,), 0.02),
    }


def stack_state(states, i):
    return jnp.stack([st[i] for st in states], axis=0)


def reference(x_prompt, x_sample, state_mlstm_C, state_mlstm_n, state_mlstm_m, state_ssd,
              state_ssd_conv, state_sconv_conv, state_rglru_h, state_rglru_conv,
              norm_w, w_in, ml_i_bias, ml_f_bias, ml_norm_w, ssd_conv_w, ssd_conv_b,
              ssd_dt_bias, ssd_A_log, ssd_D, ssd_norm_w, sc_conv_w, rg_conv_w, rg_conv_b,
              rg_wa, rg_ba, rg_wx, rg_bx, rg_lambda, w_out, final_norm_w):
    f32 = jnp.float32
    bp = x_prompt.shape[0]
    dtp = x_prompt.dtype
    init_p = (jnp.zeros((bp, ML_HEADS, ML_DK, ML_DV), f32),
              jnp.zeros((bp, ML_HEADS, ML_DK), f32),
              jnp.full((bp, ML_HEADS), NEG_BIG, f32),
              jnp.zeros((bp, SSD_HEADS, SSD_HEADDIM, SSD_STATE), f32),
              jnp.zeros((bp, SSD_CONV - 1, SSD_CONV_DIM), dtp),
              jnp.zeros((bp, SC_WIDTH - 1, BRANCH), dtp),
              jnp.zeros((bp, BRANCH), f32),
              jnp.zeros((bp, RG_CONV - 1, BRANCH), dtp))
    chunk_p = pick_chunk(x_prompt.shape[1])
    chunk_s = pick_chunk(x_sample.shape[1])
    yp, ys = x_prompt, x_sample
    new_p, new_s = [], []
    for l in range(DEPTH):
        p = (norm_w[l], w_in[l], ml_i_bias[l], ml_f_bias[l], ml_norm_w[l], ssd_conv_w[l],
             ssd_conv_b[l], ssd_dt_bias[l], ssd_A_log[l], ssd_D[l], ssd_norm_w[l], sc_conv_w[l],
             rg_conv_w[l], rg_conv_b[l], rg_wa[l], rg_ba[l], rg_wx[l], rg_bx[l], rg_lambda[l],
             w_out[l])
        state_s = (state_mlstm_C[l], state_mlstm_n[l], state_mlstm_m[l], state_ssd[l],
                   state_ssd_conv[l], state_sconv_conv[l], state_rglru_h[l], state_rglru_conv[l])
        yp, st_p = mixer_layer(yp, init_p, p, chunk_p)
        ys, st_s = mixer_layer(ys, state_s, p, chunk_s)
        new_p.append(st_p)
        new_s.append(st_s)
    y_prompt = rms_norm(yp, final_norm_w)
    y_sample = rms_norm(ys, final_norm_w)
    p_C = stack_state(new_p, 0)
    p_n = stack_state(new_p, 1)
    p_m = stack_state(new_p, 2)
    p_ssd = stack_state(new_p, 3)
    p_ssd_conv = stack_state(new_p, 4)
    p_sc_conv = stack_state(new_p, 5)
    p_rg_h = stack_state(new_p, 6)
    p_rg_conv = stack_state(new_p, 7)
    s_C = stack_state(new_s, 0)
    s_n = stack_state(new_s, 1)
    s_m = stack_state(new_s, 2)
    s_ssd = stack_state(new_s, 3)
    s_ssd_conv = stack_state(new_s, 4)
    s_sc_conv = stack_state(new_s, 5)
    s_rg_h = stack_state(new_s, 6)
    s_rg_conv = stack_state(new_s, 7)
    return (y_prompt, y_sample,
            p_C, p_n, p_m, p_ssd, p_ssd_conv, p_sc_conv, p_rg_h, p_rg_conv,
            s_C, s_n, s_m, s_ssd, s_ssd_conv, s_sc_conv, s_rg_h, s_rg_conv)
```

```python
from contextlib import ExitStack

import numpy as np

import concourse.bass as bass
import concourse.mybir as mybir
from concourse.bass_utils import run_bass_kernel_spmd

F32 = mybir.dt.float32
BF16 = mybir.dt.bfloat16
ALU = mybir.AluOpType
AF = mybir.ActivationFunctionType

D = 2048
BR = 512
PO = 7184
DEPTH = 2
EPS = 1e-6
NEG = -30000.0
OQ, OKK, OV, OO, OZ, OI, OF = 0, 512, 1024, 1536, 2048, 2560, 2564
OSZ, OXBC, ODT = 2568, 3080, 4104
OSB, OSC, OSH, OSCZ = 4112, 4624, 5136, 5648
ORX, ORZ = 6160, 6672

ENGS = ("pe", "act", "dve", "pool", "sp")
NDMASEM = 8


class Op:
    __slots__ = ("eng", "fn", "reads", "writes", "dma", "deps", "sig", "tick", "dsem", "dcount", "idx", "prev")

    def __init__(self, eng, fn, reads, writes, dma):
        self.eng, self.fn, self.reads, self.writes, self.dma = eng, fn, reads, writes, dma
        self.deps = []
        self.sig = False
        self.tick = 0
        self.dsem = -1
        self.dcount = 0
        self.prev = 0


class _Stop(Exception):
    pass


class Rec:
    def __init__(self):
        self.ops = []
        self.last_w = {}
        self.readers = {}
        import os as _os
        self.maxops = int(_os.environ.get("KOPS", "100000000"))
        self.fence_ops = []
        self.pending = set()
        self.fence_from = 0

    PERSIST = frozenset(("c128", "c8", "identb", "maskPb", "maskSb", "smkb", "xnT", "st4", "tm4", "Dbc", "cvec", "g8",
                         "g8n", "nsp8", "nba", "wb0", "wb1", "wsmb", "wabd", "Caug", "Cbf", "mcar", "STs", "STb", "cvcar",
                         "hcar", "ubp", "ubs", "gsm", "RM", "RMv", "scal", "bcw", "ncol", "cvtmp", "h0s", "small_o"))

    def touches_arena(self, op):
        for b in op.reads + op.writes:
            if b in self.PERSIST or b.startswith("ps") or b.startswith("xres"):
                continue
            return True
        return False

    def fence(self):
        last = {}
        dmas = []
        for op in self.ops[self.fence_from:]:
            if not self.touches_arena(op):
                continue
            if op.dma:
                dmas.append(op.idx)
            else:
                last[op.eng] = op.idx
        self.fence_ops = sorted(set(self.fence_ops) | set(last.values()) | set(dmas)) if self.pending else \
            sorted(set(last.values()) | set(dmas))
        self.pending = set(ENGS)
        self.fence_from = len(self.ops)

    def add(self, eng, fn, reads=(), writes=(), dma=False):
        if len(self.ops) >= self.maxops:
            return None
        writes = tuple(writes) + tuple(b for b in reads if isinstance(b, str) and b.startswith("ps") and b not in writes)
        op = Op(eng, fn, tuple(reads), tuple(writes), dma)
        op.idx = len(self.ops)
        deps = set()
        if eng in self.pending and self.touches_arena(op):
            deps |= set(self.fence_ops)
            self.pending.discard(eng)
        for b in op.reads:
            w = self.last_w.get(b)
            if w is not None:
                deps.add(w)
        for b in op.writes:
            w = self.last_w.get(b)
            if w is not None:
                deps.add(w)
            for r in self.readers.get(b, ()):
                deps.add(r)
        deps.discard(op.idx)
        op.deps = sorted(deps)
        for b in op.reads:
            self.readers.setdefault(b, []).append(op.idx)
        for b in op.writes:
            self.last_w[b] = op.idx
            self.readers[b] = []
        self.ops.append(op)
        return op

    def emit(self, nc, block, stack):
        ops = self.ops
        for op in ops:
            for d in op.deps:
                p = ops[d]
                if p.dma:
                    continue
                if p.eng == "pe" and op.eng == "pe" and not op.dma:
                    continue
                p.sig = True
        sem = {e: stack.enter_context(nc.semaphore("s_" + e)) for e in ("pe", "act", "dve", "pool")}
        dsem = {q: [stack.enter_context(nc.semaphore("d_%s%d" % (q, i))) for i in range(NDMASEM)]
                for q in ("sp", "pool")}
        cnt = {e: 0 for e in sem}
        dn = {q: 0 for q in dsem}
        dc = {q: [0] * NDMASEM for q in dsem}
        for op in ops:
            if op.dma:
                q = op.eng
                i = dn[q] % NDMASEM
                dn[q] += 1
                op.dsem = i
                op.prev = dc[q][i]
                dc[q][i] += 1
                op.dcount = dc[q][i]
            elif op.sig:
                cnt[op.eng] += 1
                op.tick = cnt[op.eng]
        per = {e: [o for o in ops if o.eng == e] for e in ENGS}

        def run(eng_name, e):
            waited = {}

            def need(key, s, v):
                if waited.get(key, 0) >= v:
                    return
                waited[key] = v
                e.wait_ge(s, v)

            for op in per[eng_name]:
                for d in op.deps:
                    p = ops[d]
                    if p.dma:
                        need(("d", p.eng, p.dsem), dsem[p.eng][p.dsem], 16 * p.dcount)
                    else:
                        if p.eng == "pe" and eng_name == "pe" and not op.dma:
                            continue
                        need(("c", p.eng), sem[p.eng], p.tick)
                if op.dma:
                    if op.prev > 0:
                        need(("d", op.eng, op.dsem), dsem[op.eng][op.dsem], 16 * op.prev)
                    op.fn(e).then_inc(dsem[op.eng][op.dsem], 16)
                else:
                    ins = op.fn(e)
                    if op.sig:
                        ins.then_inc(sem[op.eng], 1)
            if eng_name in dsem:
                for i in range(NDMASEM):
                    if dc[eng_name][i] > 0:
                        need(("d", eng_name, i), dsem[eng_name][i], 16 * dc[eng_name][i])

        @block.tensor
        def _(e):
            run("pe", e)

        @block.scalar
        def _(e):
            run("act", e)

        @block.vector
        def _(e):
            run("dve", e)

        @block.gpsimd
        def _(e):
            run("pool", e)

        @block.sync
        def _(e):
            run("sp", e)


def host_consts(NCH, NSQ):
    LS = 4 * NSQ
    NP = 128 * NCH
    c128 = {}
    c128["ident"] = np.eye(128, dtype=np.float32)
    s = np.arange(128)[:, None]
    l = np.arange(128)[None, :]
    mp = np.where(s <= l, 0.0, NEG).astype(np.float32)
    c128["maskP"] = np.tile(mp[:, None, :], (1, 4, 1)).reshape(128, 512)
    ms = np.full((128, 128), NEG, np.float32)
    sl = np.arange(LS)
    okm = (sl[:, None] // 4 == sl[None, :] // 4) & (sl[:, None] <= sl[None, :])
    ms[:LS, :LS] = np.where(okm, 0.0, NEG)
    c128["maskS"] = np.tile(ms[:, None, :LS], (1, 4, 1)).reshape(128, 4 * LS)
    smk = (np.arange(NSQ)[:, None] == (sl[None, :] // 4)).astype(np.float32)
    c128["smk"] = np.tile(smk.reshape(1, NSQ * LS), (128, 1))
    smT = np.zeros((128, NSQ), np.float32)
    smT[:LS] = smk.T
    c128["smT"] = smT
    C128 = np.concatenate([c128[k] for k in ("ident", "maskP", "maskS", "smk", "smT")], axis=1)
    offs128 = {}
    o = 0
    for k in ("ident", "maskP", "maskS", "smk", "smT"):
        offs128[k] = (o, c128[k].shape[1])
        o += c128[k].shape[1]
    c8 = {}
    keepP = np.ones((8, NP), np.float32)
    keepP[:, ::128] = 0
    c8["keepP"] = keepP
    keepS = np.ones((8, LS), np.float32)
    keepS[:, ::4] = 0
    c8["keepS"] = keepS
    negS = np.zeros((8, LS), np.float32)
    negS[:, ::4] = -1e30
    c8["negS"] = negS
    pm = np.array([1, 1, 1, 1, 0, 0, 0, 0], np.float32)[:, None]
    c8["pm"] = pm
    c8["opm"] = 1 - pm
    mf = np.zeros((8, 4, 128), np.float32)
    for k in range(8):
        mf[k, k % 4, :] = 1
    c8["maskfull"] = mf.reshape(8, 512)
    m8 = np.zeros((8, 4), np.float32)
    for k in range(4):
        m8[k, k] = 1
    c8["mask8h"] = m8
    lv = np.zeros((8, 128), np.float32)
    lv[4:] = 1
    c8["LV"] = lv
    c8["ones8"] = np.ones((8, 128), np.float32)
    names8 = ("keepP", "keepS", "negS", "pm", "opm", "maskfull", "mask8h", "LV", "ones8")
    C8 = np.concatenate([c8[k] for k in names8], axis=1)
    offs8 = {}
    o = 0
    for k in names8:
        offs8[k] = (o, c8[k].shape[1])
        o += c8[k].shape[1]
    return C128, offs128, C8, offs8


def build(TP, NCH, NSQ):
    LS = 4 * NSQ
    NPASS = TP // (128 * NCH)
    assert NPASS * 128 * NCH == TP
    NTM = NCH + 1
    NMAX = 128 * NCH + LS
    NP = 128 * NCH
    L2 = DEPTH
    C128h, o128, C8h, o8 = host_consts(NCH, NSQ)
    nc = bass.Bass("TRN2", target_bir_lowering=False)

    def din(name, shape):
        return nc.dram_tensor(name, list(shape), F32, kind="ExternalInput").ap()

    def dout(name, shape):
        return nc.dram_tensor(name, list(shape), F32, kind="ExternalOutput").ap()

    xp_d = din("xp", [TP, D])
    xs_d = din("xs", [LS, D])
    sC_d = din("sC", [L2, NSQ, 4, 128, 128])
    sn_d = din("sn", [L2, NSQ, 512])
    sm_d = din("sm", [L2, 8, NSQ])
    sS_d = din("sS", [L2, NSQ, 8, 64, 128])
    sscv_d = din("sscv", [L2, NSQ * 3, 1024])
    sccv_d = din("sccv", [L2, NSQ * 2, 512])
    srh_d = din("srh", [L2, NSQ, 512])
    srcv_d = din("srcv", [L2, NSQ * 3, 512])
    wpk_d = din("wpk", [L2, 28, 128, 4096])
    wsm_d = din("wsm", [L2, 128, 512])
    wopk_d = din("wopk", [L2, 4, 8, 128, 1024])
    wabd_d = din("wabd", [L2, 128, 1024])
    cvec_d = din("cvec", [L2, 128, 84])
    g8_d = din("g8", [L2, 8, 8])
    normw_d = din("norm_w", [L2, D])
    mlnw_d = din("ml_norm_w", [L2, BR])
    ssdnw_d = din("ssd_norm_w", [L2, BR])
    ssdD_d = din("ssd_D", [L2, 8])
    fnw_d = din("final_norm_w", [D])
    c128_d = din("c128", list(C128h.shape))
    c8_d = din("c8", list(C8h.shape))

    yp_o = dout("yp", [TP, D])
    ys_o = dout("ys", [LS, D])
    pC_o = dout("pC", [L2, 4, 128, 128])
    pn_o = dout("pn", [L2, 4, 128])
    pm_o = dout("pm", [L2, 4])
    pS_o = dout("pS", [L2, 512, 128])
    pscv_o = dout("pscv", [L2, 3, 1024])
    pccv_o = dout("pccv", [L2, 2, 512])
    prh_o = dout("prh", [L2, 4, 128])
    prcv_o = dout("prcv", [L2, 3, 512])
    oC_o = dout("oC", [L2, NSQ, 4, 128, 128])
    on_o = dout("on", [L2, NSQ, 512])
    om_o = dout("om", [L2, 4, NSQ])
    oS_o = dout("oS", [L2, NSQ, 8, 64, 128])
    oscv_o = dout("oscv", [L2, NSQ * 3, 1024])
    occv_o = dout("occv", [L2, NSQ * 2, 512])
    orh_o = dout("orh", [L2, NSQ, 512])
    orcv_o = dout("orcv", [L2, NSQ * 3, 512])

    R = Rec()
    st = ExitStack()
    st.enter_context(nc.allow_non_contiguous_dma(reason="small strided state/constant transfers"))

    def sb(name, shape, dt=F32):
        return st.enter_context(nc.sbuf_tensor("sb_" + name, list(shape), dt))

    def psum(name):
        return st.enter_context(nc.psum_tensor(name, [128, 512], F32))

    def TT(eng, out, in0, in1, op, r, w):
        R.add(eng, lambda e: e.tensor_tensor(out=out, in0=in0, in1=in1, op=op), r, w)

    def TS(eng, out, in0, s1, s2, op0, op1, r, w):
        if op1 is None:
            R.add(eng, lambda e: e.tensor_scalar(out=out, in0=in0, scalar1=s1, scalar2=None, op0=op0), r, w)
        else:
            R.add(eng, lambda e: e.tensor_scalar(out=out, in0=in0, scalar1=s1, scalar2=s2, op0=op0, op1=op1), r, w)

    def STT(eng, out, in0, scalar, in1, op0, op1, r, w):
        R.add(eng, lambda e: e.scalar_tensor_tensor(out=out, in0=in0, scalar=scalar, in1=in1, op0=op0, op1=op1), r, w)

    def CP(eng, out, in_, r, w):
        if eng == "act":
            R.add(eng, lambda e: e.activation(out=out, in_=in_, func=AF.Copy), r, w)
        else:
            R.add(eng, lambda e: e.tensor_copy(out=out, in_=in_), r, w)

    def ACT(out, in_, func, r, w, bias=None, scale=1.0, accum=None):
        kw = {}
        if bias is not None:
            kw["bias"] = bias
        if accum is not None:
            kw["accum_out"] = accum
        R.add("act", lambda e: e.activation(out=out, in_=in_, func=func, scale=scale, **kw), r, w)

    def MM(out, lhsT, rhs, start, stop, r, w):
        R.add("pe", lambda e: e.matmul(out, lhsT=lhsT, rhs=rhs, start=start, stop=stop), r, w)

    def TR(out, in_, ident, r, w):
        R.add("pe", lambda e: e.transpose(out=out, in_=in_, identity=ident), r, w)

    def DMA(out, in_, r, w, q="sp"):
        R.add(q, lambda e: e.dma_start(out=out, in_=in_), r, w, dma=True)

    def SCAN(out, d0, d1, init, op0, op1, r, w):
        R.add("dve", lambda e: e.tensor_tensor_scan(out=out, data0=d0, data1=d1, initial=init, op0=op0, op1=op1), r, w)

    def MEMSET(eng, ap, val, w):
        R.add(eng, lambda e: e.memset(ap, val), (), w)

    def RSUM(out, in_, r, w):
        R.add("dve", lambda e: e.reduce_sum(out=out, in_=in_, axis=mybir.AxisListType.X), r, w)

    def RECIP(out, in_, r, w):
        R.add("dve", lambda e: e.reciprocal(out=out, in_=in_), r, w)

    def sigmoid_from_exp(t, r, w):
        ACT(t, t, AF.Ln, r, w, bias=1.0)
        ACT(t, t, AF.Exp, r, w, scale=-1.0)

    c128 = sb("c128", [128, 128 + NSQ])
    c8 = sb("c8", C8h.shape)
    identf = c128[:, 0:128]
    smT = c128[:, 128:128 + NSQ]

    def k8(name):
        o, n = o8[name]
        return c8[:, o:o + n]

    identb = sb("identb", [128, 128], BF16)
    maskPb = sb("maskPb", [128, 512], BF16)
    maskSb = sb("maskSb", [128, 4 * LS], BF16)
    smkb = sb("smkb", [128, NSQ * LS], BF16)
    pm8 = k8("pm")
    opm8 = k8("opm")
    xres = sb("xres", [128, NTM, D])
    xnT = sb("xnT", [128, 16, NMAX], BF16)
    st4 = sb("st4", [128, 8])
    tm4 = sb("tm4", [128, 16])
    Dbc = sb("Dbc", [128, 8])
    cvec = sb("cvec", [128, 84])
    g8 = sb("g8", [8, 8])
    g8n = sb("g8n", [8, 8])
    nsp8 = sb("nsp8", [128, 4])
    nba = sb("nba", [128, 8])
    NWB = 2
    wb = [sb("wb%d" % i, [128, 16, 256], BF16) for i in range(NWB)]
    wsmb = sb("wsmb", [128, 16, 32], BF16)
    wabd = sb("wabd", [128, 8, 128], BF16)
    PS = {k: psum(k) for k in ("psA", "psB", "psS", "psE", "psV", "psN0", "psN1", "psU")}
    Caug = sb("Caug", [128, L2, 4, 129])
    Cbf = sb("Cbf", [128, 4, 129], BF16)
    mcar = sb("mcar", [8, L2])
    STs = sb("STs", [128, L2, 512])
    STb = sb("STb", [128, 512], BF16)
    cvcar = sb("cvcar", [128, L2, 16, 3])
    hcar = sb("hcar", [128, L2, 4])
    ubp = sb("ubp", [128, 3 + NP])
    ubs = sb("ubs", [128, NSQ, 7])
    gsm = sb("gsm", [8, 64])
    RM = sb("RM", [8, 512])
    RMv = sb("RMv", [8, 512])
    scal = sb("scal", [128, NTM, 2, 8])
    bcw = sb("bcw", [128, 2, 4, 32])
    ncol = sb("ncol", [128, NSQ])
    cvtmp = sb("cvtmp", [128, NSQ * 3])
    h0s = sb("h0s", [128, NSQ])
    small_o = sb("small_o", [8, 1024])
    AFW = 15168
    ABW = 18728
    arenaF = sb("arenaF", [128, AFW])
    arenaB = sb("arenaB", [128, ABW], BF16)
    aoff = [0, 0]
    amax = [0, 0]
    import os as _os
    print('SBUF remaining after persistent', nc.sbuf_bytes_remaining, flush=True)

    def stage_reset():
        R.fence()
        aoff[0] = 0
        aoff[1] = 0

    def carve(arena, k, cap, shape, pat):
        n = 1
        for d_ in shape[1:]:
            n *= d_
        o = aoff[k]
        aoff[k] += n
        amax[k] = max(amax[k], aoff[k])
        if _os.environ.get("KDRY"):
            o = 0
        else:
            assert aoff[k] <= cap, ("arena overflow", k, aoff[k], cap)
        v = arena[:shape[0], o:o + n]
        if len(shape) == 3:
            v = v.rearrange("p (a b) -> p a b", a=shape[1])
        elif len(shape) == 4:
            v = v.rearrange("p (a b c) -> p a b c", a=shape[1], b=shape[2])
        return v

    def af(shape):
        return carve(arenaF, 0, AFW, shape, None)

    def ab(shape):
        return carve(arenaB, 1, ABW, shape, None)

    DMA(c128[:, 0:128], c128_d[:, 0:128], (), ["c128"])
    o_, n_ = o128["smT"]
    DMA(c128[:, 128:128 + NSQ], c128_d[:, o_:o_ + n_], (), ["c128"])
    DMA(c8[:], c8_d, (), ["c8"])
    DMA(identb[:], c128_d[:, 0:128], (), ["identb"], q="pool")
    o_, n_ = o128["maskP"]
    DMA(maskPb[:], c128_d[:, o_:o_ + n_], (), ["maskPb"], q="pool")
    o_, n_ = o128["maskS"]
    DMA(maskSb[:], c128_d[:, o_:o_ + n_], (), ["maskSb"], q="pool")
    o_, n_ = o128["smk"]
    DMA(smkb[:], c128_d[:, o_:o_ + n_], (), ["smkb"], q="pool")
    MEMSET("pool", xnT[:], 0.0, ["xnT"])
    MEMSET("pool", Caug[:], 0.0, ["Caug"])
    MEMSET("pool", mcar[:], -1e30, ["mcar"])
    MEMSET("pool", STs[:], 0.0, ["STs"])
    MEMSET("pool", cvcar[:], 0.0, ["cvcar"])
    MEMSET("pool", hcar[:], 0.0, ["hcar"])
    MEMSET("pool", ubs[:], 0.0, ["ubs"])
    MEMSET("pool", ubp[:], 0.0, ["ubp"])
    MEMSET("pool", arenaF[:], 0.0, ["arenaF"])
    MEMSET("pool", arenaB[:], 0.0, ["arenaB"])
    stage_reset()

    wctr = [0]

    def load_w(src2d, nct=16):
        i = wctr[0] % NWB
        wctr[0] += 1
        DMA(wb[i][:, 0:nct, :].rearrange("p a b -> p (a b)"), src2d, (), ["wb%d" % i], q="pool")
        return wb[i], "wb%d" % i

    pctr = [0]

    def next_ps():
        k = ("psA", "psB", "psN1")[pctr[0] % 3]
        pctr[0] += 1
        return PS[k], k

    def next_ps2():
        k = ("psA", "psB")[pctr[0] % 2]
        pctr[0] += 1
        return PS[k], k

    import os as _os
    KSTOP = int(_os.environ.get("KSTOP", "99"))
    kst = [0]

    def checkpoint():
        kst[0] += 1
        if kst[0] > KSTOP:
            raise _Stop()

    try:
      for pi in range(NPASS):
          tiles = [dict(kind="p", L=128, col=128 * i, ti=i, tok=pi * NP + 128 * i) for i in range(NCH)]
          last_pass = pi == NPASS - 1
          if last_pass:
              tiles.append(dict(kind="s", L=LS, col=NP, ti=NCH, tok=0))
          NT = len(tiles)
          N = (NP + LS) if last_pass else NP
          groups = []
          c0 = 0
          while c0 < N:
              groups.append((c0, min(512, N - c0)))
              c0 += 512
          has_s = last_pass

          for t in tiles:
              src = xp_d[t["tok"]:t["tok"] + 128, :] if t["kind"] == "p" else xs_d
              DMA(xres[:t["L"], t["ti"], :], src, (), ["xres%d" % t["ti"]])

          for ly in range(L2):
              DMA(Dbc[:], ssdD_d[ly].partition_broadcast(128), (), ["Dbc"])
              DMA(cvec[:], cvec_d[ly], (), ["cvec"])
              DMA(g8[:], g8_d[ly], (), ["g8"])
              DMA(wsmb[:, :, :].rearrange("p a b -> p (a b)"), wsm_d[ly], (), ["wsmb"], q="pool")
              DMA(wabd[:, :, :].rearrange("p a b -> p (a b)"), wabd_d[ly], (), ["wabd"], q="pool")
              TS("dve", g8n[:, 0:2], g8[:, 0:2], -1.0, None, ALU.mult, None, ["g8"], ["g8n"])
              ACT(g8n[:, 4:6], g8[:, 4:6], AF.Exp, ["g8"], ["g8n"])
              TS("dve", g8n[:, 4:6], g8n[:, 4:6], -1.0, None, ALU.mult, None, ["g8n"], ["g8n"])
              ACT(nsp8[:], cvec[:, 80:84], AF.Exp, ["cvec"], ["nsp8"], scale=-1.0)
              ACT(nsp8[:], nsp8[:], AF.Ln, ["nsp8"], ["nsp8"], bias=1.0)
              TS("dve", nsp8[:], nsp8[:], -8.0, None, ALU.mult, None, ["nsp8"], ["nsp8"])
              TS("dve", nba[:], cvec[:, 72:80], -1.0, None, ALU.mult, None, ["cvec"], ["nba"])

              def rmsnorm_tile(t, wsrc_d, xn32, junk, nwbc):
                  L, ti = t["L"], t["ti"]
                  xr_id = "xres%d" % ti
                  ACT(junk[:L], xres[:L, ti, :], AF.Square, [xr_id], ["junk", "st4"], accum=st4[:L, 0:1])
                  ACT(st4[:L, 1:2], st4[:L, 0:1], AF.Ln, ["st4"], ["st4"], bias=EPS, scale=1.0 / D)
                  ACT(st4[:L, 2:3], st4[:L, 1:2], AF.Exp, ["st4"], ["st4"], scale=-0.5)
                  STT("dve", xn32[:L], xres[:L, ti, :], st4[:L, 2:3], nwbc[:L], ALU.mult, ALU.mult,
                      [xr_id, "st4", "nwbc"], ["xn32"])

              stage_reset()
              xn32 = af([128, D])
              nwbc = af([128, D])
              junk = ab([128, D])
              DMA(nwbc, normw_d[ly].partition_broadcast(128), (), ["nwbc"])
              for t in tiles:
                  L, ti, col = t["L"], t["ti"], t["col"]
                  rmsnorm_tile(t, None, xn32, junk, nwbc)
                  for q4 in range(4):
                      for j in range(4):
                          dt_ = q4 * 4 + j
                          TR(PS["psU"][:, j * 128:j * 128 + L], xn32[:L, dt_ * 128:(dt_ + 1) * 128], identf[:L, :L],
                             ["xn32", "c128"], ["psU"])
                      CP("act" if q4 % 2 else "dve", xnT[:, q4 * 4:q4 * 4 + 4, col:col + L],
                         PS["psU"][:, :].rearrange("p (j c) -> p j c", j=4)[:, :, :L], ["psU"], ["xnT"])

              checkpoint()
              def proj_fm(wt, wid, wo, M, evac):
                  for (g0, n) in groups:
                      ps, pid = next_ps()
                      for dt_ in range(16):
                          MM(ps[:M, :n], wt[:, dt_, wo:wo + M], xnT[:, dt_, g0:g0 + n], dt_ == 0, dt_ == 15,
                             [wid, "xnT"], [pid])
                      evac(ps, pid, g0, n)

              def proj_tm(wt, wid, wo, n, t, evac):
                  ps, pid = next_ps()
                  L, col = t["L"], t["col"]
                  for dt_ in range(16):
                      MM(ps[:L, :n], xnT[:, dt_, col:col + L], wt[:, dt_, wo:wo + n], dt_ == 0, dt_ == 15,
                         [wid, "xnT"], [pid])
                  evac(ps, pid)

              def wout_part(mixs, ct0):
                  for dg in range(8):
                      wt1, w1 = load_w(wopk_d[ly, ct0 // 4, dg], nct=4)
                      for t in tiles:
                          L, ti, col = t["L"], t["ti"], t["col"]
                          ps, pid = next_ps()
                          for ct in range(4):
                              MM(ps[:L, :256], mixs[:, ct, col:col + L], wt1[:, ct, :], ct == 0, ct == 3, ["mixs", w1], [pid])
                          TT("dve", xres[:L, ti, dg * 256:(dg + 1) * 256], xres[:L, ti, dg * 256:(dg + 1) * 256],
                             ps[:L, :256], ALU.add, ["xres%d" % ti, pid], ["xres%d" % ti])

              def conv_load_hist(stream):
                  CP("dve", ubp[:, 0:3], cvcar[:, ly, stream, :], ["cvcar"], ["ubp"])

              def conv_save_hist(stream):
                  CP("dve", cvcar[:, ly, stream, :], ubp[:, NP:NP + 3], ["ubp"], ["cvcar"])

              def conv_apply(out_fm, oid, W, wcol, bcol):
                  s0 = 3 - (W - 1)
                  views = [(out_fm[:, 0:NP], lambda j: ubp[:, s0 + j:s0 + j + NP])]
                  if has_s:
                      views.append((out_fm[:, NP:NP + LS].rearrange("p (b t) -> p b t", t=4),
                                    lambda j: ubs[:, :, s0 + j:s0 + j + 4]))
                  for (o, src) in views:
                      if bcol is None:
                          TS("dve", o, src(0), cvec[:, wcol:wcol + 1], None, ALU.mult, None,
                             ["ubp", "ubs", "cvec"], [oid])
                      else:
                          TS("dve", o, src(0), cvec[:, wcol:wcol + 1], cvec[:, bcol:bcol + 1], ALU.mult, ALU.add,
                             ["ubp", "ubs", "cvec"], [oid])
                      for j in range(1, W):
                          STT("dve", o, src(j), cvec[:, wcol + j:wcol + j + 1], o, ALU.mult, ALU.add,
                              ["ubp", "ubs", "cvec", oid], [oid])

              def sample_hist_tile(cvin, W, ctile, src_d):
                  rows = NSQ * (W - 1)
                  DMA(cvin[:rows, :], src_d[:, ctile * 128:(ctile + 1) * 128], (), ["cvin"])
                  TR(PS["psU"][:, :rows], cvin[:rows, :], identf[:rows, :rows],
                     ["cvin", "c128"], ["psU"])
                  CP("dve", ubs[:, :, 3 - (W - 1):3], PS["psU"][:, :rows].rearrange("p (b r) -> p b r", r=W - 1),
                     ["psU"], ["ubs"])

              def sample_hist_out_tile(cvout, W, ctile, dst_d):
                  rows = NSQ * (W - 1)
                  CP("dve", cvtmp[:, :rows].rearrange("p (b r) -> p b r", r=W - 1), ubs[:, :, 7 - (W - 1):7],
                     ["ubs"], ["cvtmp"])
                  TR(PS["psU"][:rows, :128], cvtmp[:, :rows], identf, ["cvtmp", "c128"], ["psU"])
                  CP("dve", cvout[:rows, :], PS["psU"][:rows, :128], ["psU"], ["cvout"])
                  DMA(dst_d[:, ctile * 128:(ctile + 1) * 128], cvout[:rows, :], ["cvout"], ())

              def prompt_hist_out(W, streams, dst):
                  for i, s_ in enumerate(streams):
                      CP("dve", cvtmp[:, 0:W - 1], cvcar[:, ly, s_, 3 - (W - 1):3], ["cvcar"], ["cvtmp"])
                      TR(PS["psU"][:W - 1, :128], cvtmp[:, 0:W - 1], identf, ["cvtmp", "c128"], ["psU"])
                      CP("dve", small_o[:W - 1, i * 128:(i + 1) * 128], PS["psU"][:W - 1, :128], ["psU"], ["small_o"])
                  DMA(dst, small_o[:W - 1, :128 * len(streams)], ["small_o"], ())

              def fill_ub(ps, pid, g0, n, eng="act"):
                  pe_ = min(g0 + n, NP)
                  if pe_ > g0:
                      CP(eng, ubp[:, 3 + g0:3 + pe_], ps[:, 0:pe_ - g0], [pid], ["ubp"])
                  if has_s and g0 + n > NP:
                      a0 = max(g0, NP)
                      CP(eng, ubs[:, :, 3:7], ps[:, a0 - g0:a0 - g0 + LS].rearrange("p (b t) -> p b t", t=4),
                         [pid], ["ubs"])

              stage_reset()
              fa = [af([128, NMAX]) for _ in range(6)]
              cvin = af([NSQ * 3, 128])
              cvout = af([NSQ * 3, 128])
              htm = af([NSQ, 512])
              htmo = af([NSQ, 512])
              mixs = ab([128, 4, NMAX])
              xrb = ab([128, NMAX])
              for j in range(4):
                  wt1, w1 = load_w(wpk_d[ly, 2 * j])
                  wt2, w2 = load_w(wpk_d[ly, 2 * j + 1])
                  conv_load_hist(j)
                  if has_s:
                      sample_hist_tile(cvin, 3, j, sccv_d[ly])
                  proj_fm(wt1, w1, 0, 128, lambda ps, pid, g0, n: CP("act", fa[0][:, g0:g0 + n], ps[:, :n], [pid], ["fa0"]))
                  proj_fm(wt1, w1, 128, 128, lambda ps, pid, g0, n: CP("act", fa[1][:, g0:g0 + n], ps[:, :n], [pid], ["fa1"]))

                  def ev_h(ps, pid, g0, n):
                      TT("dve", fa[1][:, g0:g0 + n], fa[1][:, g0:g0 + n], ps[:, :n], ALU.mult, ["fa1", pid], ["fa1"])
                  proj_fm(wt2, w2, 0, 128, ev_h)

                  def ev_z(ps, pid, g0, n):
                      CP("act", fa[2][:, g0:g0 + n], ps[:, :n], [pid], ["fa2"])
                      ACT(fa[3][:, g0:g0 + n], fa[2][:, g0:g0 + n], AF.Exp, ["fa2"], ["fa3"], scale=-1.0)
                  proj_fm(wt2, w2, 128, 128, ev_z)
                  CP("dve", ubp[:, 3:3 + NP], fa[1][:, 0:NP], ["fa1"], ["ubp"])
                  if has_s:
                      CP("dve", ubs[:, :, 3:7], fa[1][:, NP:NP + LS].rearrange("p (b t) -> p b t", t=4), ["fa1"], ["ubs"])
                  conv_apply(fa[4], "fa4", 3, 40 + 3 * j, None)
                  conv_save_hist(j)
                  if has_s:
                      sample_hist_out_tile(cvout, 3, j, occv_o[ly])
                  sigmoid_from_exp(fa[3][:, :N], ["fa3"], ["fa3"])
                  TT("dve", fa[2][:, :N], fa[2][:, :N], fa[3][:, :N], ALU.mult, ["fa2", "fa3"], ["fa2"])
                  TT("dve", fa[2][:, :N], fa[2][:, :N], fa[0][:, :N], ALU.mult, ["fa2", "fa0"], ["fa2"])
                  TT("dve", mixs[:, j, :N], fa[2][:, :N], fa[4][:, :N], ALU.mult, ["fa2", "fa4"], ["mixs"])
              if last_pass:
                  prompt_hist_out(3, [0, 1, 2, 3], pccv_o[ly])
              checkpoint()
              wout_part(mixs, 8)
              checkpoint()

              if has_s:
                  DMA(htm[:], srh_d[ly], (), ["htm"])
              for j in range(4):
                  wt1, w1 = load_w(wpk_d[ly, 8 + j])
                  conv_load_hist(4 + j)
                  if has_s:
                      sample_hist_tile(cvin, 4, j, srcv_d[ly])
                      TR(PS["psU"][:, :NSQ], htm[:NSQ, j * 128:(j + 1) * 128], identf[:NSQ, :NSQ], ["htm", "c128"], ["psU"])
                      CP("dve", h0s[:], PS["psU"][:, :NSQ], ["psU"], ["h0s"])
                  proj_fm(wt1, w1, 0, 128, lambda ps, pid, g0, n: fill_ub(ps, pid, g0, n))

                  def ev_rz(ps, pid, g0, n):
                      CP("act", fa[2][:, g0:g0 + n], ps[:, :n], [pid], ["fa2"])
                      ACT(fa[3][:, g0:g0 + n], fa[2][:, g0:g0 + n], AF.Exp, ["fa2"], ["fa3"], scale=-1.0)
                  proj_fm(wt1, w1, 128, 128, ev_rz)
                  conv_apply(fa[0], "fa0", 4, 52 + 4 * j, 68 + j)
                  conv_save_hist(4 + j)
                  if has_s:
                      sample_hist_out_tile(cvout, 4, j, orcv_o[ly])
                  CP("act", xrb[:, :N], fa[0][:, :N], ["fa0"], ["xrb"])
                  for (g0, n) in groups:
                      ps, pid = next_ps()
                      MM(ps[:, :n], wabd[:, j, :], xrb[:, g0:g0 + n], True, True, ["wabd", "xrb"], [pid])
                      ACT(fa[1][:, g0:g0 + n], ps[:, :n], AF.Exp, [pid, "nba"], ["fa1"], bias=nba[:, j:j + 1], scale=-1.0)
                      ps, pid = next_ps()
                      MM(ps[:, :n], wabd[:, 4 + j, :], xrb[:, g0:g0 + n], True, True, ["wabd", "xrb"], [pid])
                      ACT(fa[4][:, g0:g0 + n], ps[:, :n], AF.Exp, [pid, "nba"], ["fa4"], bias=nba[:, 4 + j:5 + j], scale=-1.0)
                  sigmoid_from_exp(fa[1][:, :N], ["fa1"], ["fa1"])
                  sigmoid_from_exp(fa[4][:, :N], ["fa4"], ["fa4"])
                  ACT(fa[1][:, :N], fa[1][:, :N], AF.Exp, ["fa1", "nsp8"], ["fa1"], scale=nsp8[:, j:j + 1])
                  TT("dve", fa[5][:, :N], fa[1][:, :N], fa[1][:, :N], ALU.mult, ["fa1"], ["fa5"])
                  TS("dve", fa[5][:, :N], fa[5][:, :N], -1.0, 1.0, ALU.mult, ALU.add, ["fa5"], ["fa5"])
                  TS("dve", fa[5][:, :N], fa[5][:, :N], 1e-30, None, ALU.max, None, ["fa5"], ["fa5"])
                  ACT(fa[5][:, :N], fa[5][:, :N], AF.Ln, ["fa5"], ["fa5"])
                  ACT(fa[5][:, :N], fa[5][:, :N], AF.Exp, ["fa5"], ["fa5"], scale=0.5)
                  TT("dve", fa[4][:, :N], fa[4][:, :N], fa[0][:, :N], ALU.mult, ["fa4", "fa0"], ["fa4"])
                  TT("dve", fa[4][:, :N], fa[4][:, :N], fa[5][:, :N], ALU.mult, ["fa4", "fa5"], ["fa4"])
                  STT("dve", fa[4][:, 0:1], fa[1][:, 0:1], hcar[:, ly, j:j + 1], fa[4][:, 0:1], ALU.mult, ALU.add,
                      ["fa1", "fa4", "hcar"], ["fa4"])
                  MEMSET("dve", fa[1][:, 0:1], 0.0, ["fa1"])
                  SCAN(fa[0][:, 0:NP], fa[1][:, 0:NP], fa[4][:, 0:NP], 0.0, ALU.mult, ALU.add, ["fa1", "fa4"], ["fa0"])
                  CP("dve", hcar[:, ly, j:j + 1], fa[0][:, NP - 1:NP], ["fa0"], ["hcar"])
                  if has_s:
                      a0v = fa[1][:, NP:NP + LS].rearrange("p (b t) -> p b t", t=4)[:, :, 0]
                      b0v = fa[4][:, NP:NP + LS].rearrange("p (b t) -> p b t", t=4)[:, :, 0]
                      TT("dve", tm4[:, :NSQ], a0v, h0s[:], ALU.mult, ["fa1", "h0s"], ["tm4"])
                      TT("dve", b0v, b0v, tm4[:, :NSQ], ALU.add, ["fa4", "tm4"], ["fa4"])
                      MEMSET("dve", a0v, 0.0, ["fa1"])
                      SCAN(fa[0][:, NP:NP + LS], fa[1][:, NP:NP + LS], fa[4][:, NP:NP + LS], 0.0, ALU.mult, ALU.add,
                           ["fa1", "fa4"], ["fa0"])
                      CP("dve", ncol[:, :NSQ], fa[0][:, NP:NP + LS].rearrange("p (b t) -> p b t", t=4)[:, :, 3], ["fa0"], ["ncol"])
                      TR(PS["psU"][:NSQ, :128], ncol[:, :NSQ], identf, ["ncol", "c128"], ["psU"])
                      CP("dve", htmo[:NSQ, j * 128:(j + 1) * 128], PS["psU"][:NSQ, :128], ["psU"], ["htmo"])
                  sigmoid_from_exp(fa[3][:, :N], ["fa3"], ["fa3"])
                  TT("dve", fa[2][:, :N], fa[2][:, :N], fa[3][:, :N], ALU.mult, ["fa2", "fa3"], ["fa2"])
                  TT("dve", mixs[:, j, :N], fa[2][:, :N], fa[0][:, :N], ALU.mult, ["fa2", "fa0"], ["mixs"])
              if has_s:
                  DMA(orh_o[ly], htmo[:], ["htmo"], ())
              if last_pass:
                  prompt_hist_out(4, [4, 5, 6, 7], prcv_o[ly])
                  for j in range(4):
                      TR(PS["psU"][:1, j * 128:(j + 1) * 128], hcar[:, ly, j:j + 1], identf, ["hcar", "c128"], ["psU"])
                  CP("dve", small_o[:1, 0:512], PS["psU"][:1, 0:512], ["psU"], ["small_o"])
                  DMA(prh_o[ly].rearrange("(o t) c -> o (t c)", o=1), small_o[:1, 0:512], ["small_o"], ())
              checkpoint()
              wout_part(mixs, 12)
              checkpoint()

              def chunk_ends(row, t_kind):
                  if t_kind == "p":
                      return row[:, 0:NP].rearrange("p (c t) -> p c t", t=128)[:, :, 127]
                  return row[:, NP:NP + LS].rearrange("p (b t) -> p b t", t=4)[:, :, 3]

              def expo(a_row, aid, c_row, cid, cv_row, cvid, t, Wt, Wv):
                  L, col = t["L"], t["col"]
                  mf = k8("maskfull").rearrange("p (h l) -> p h l", h=4)[:, :, :L]
                  RMl = RM[:, :4 * L].rearrange("p (h l) -> p h l", h=4)
                  RMvl = RMv[:, :4 * L].rearrange("p (h l) -> p h l", h=4)
                  TT("dve", RMl, c_row[:, col:col + L].unsqueeze(1).broadcast_to([8, 4, L]), mf, ALU.mult,
                     [cid, "c8"], ["RM"])
                  TT("dve", RMvl, cv_row[:, col:col + L].unsqueeze(1).broadcast_to([8, 4, L]), mf, ALU.mult,
                     [cvid, "c8"], ["RMv"])
                  msk = (maskPb if t["kind"] == "p" else maskSb)
                  MM(PS["psE"][:L, :4 * L], a_row[:, col:col + L], RM[:, :4 * L], True, False, [aid, "RM"], ["psE"])
                  MM(PS["psE"][:L, :4 * L], identb[:L, :L], msk[:L, :4 * L], False, True, ["identb", "maskPb", "maskSb"], ["psE"])
                  MM(PS["psV"][:, :4 * L], k8("LV"), RMv[:, :4 * L], True, True, ["c8", "RMv"], ["psV"])
                  ACT(Wt[:L, :4 * L], PS["psE"][:L, :4 * L], AF.Exp, ["psE"], ["Wt"])
                  ACT(Wv[:, :4 * L], PS["psV"][:, :4 * L], AF.Exp, ["psV"], ["Wv"])

              def bcast_rows(src_row, sid, ncols, slot):
                  w8 = gsm[:, :4 * ncols].rearrange("p (h c) -> p h c", h=4)
                  TT("dve", w8, src_row.unsqueeze(1).broadcast_to([8, 4, ncols]),
                     k8("mask8h").unsqueeze(2).broadcast_to([8, 4, ncols]), ALU.mult, [sid, "c8"], ["gsm"])
                  MM(PS["psU"][:, :4 * ncols], k8("ones8"), gsm[:, :4 * ncols], True, True, ["c8", "gsm"], ["psU"])
                  CP("dve", bcw[:, slot, :, :ncols], PS["psU"][:, :4 * ncols].rearrange("p (h c) -> p h c", h=4),
                     ["psU"], ["bcw"])

              def to_tok_major(rowA, aid, rowB, bid, t, slot, scr, scrid):
                  L, col, ti = t["L"], t["col"], t["ti"]
                  TS("dve", scr[:, :L], rowA[:, col:col + L], pm8, None, ALU.mult, None, [aid, "c8"], [scrid])
                  STT("dve", scr[:, :L], rowB[:, col:col + L], opm8, scr[:, :L], ALU.mult, ALU.add,
                      [bid, "c8", scrid], [scrid])
                  TR(PS["psU"][:L, :8], scr[:, :L], identf[:8, :8], [scrid, "c128"], ["psU"])
                  CP("dve", scal[:L, ti, slot, :], PS["psU"][:L, :8], ["psU"], ["scal"])

              stage_reset()
              Gml = af([128, NTM, 512])
              gr = [af([8, NMAX]) for _ in range(10)]
              mlnwbc = af([128, BR])
              DMA(mlnwbc, mlnw_d[ly].partition_broadcast(128), (), ["mlnwbc"])
              Wt = af([128, 512])
              Wv = af([128, 512])
              hn = af([128, 512])
              ytm = af([128, 512])
              ytm2 = [ytm, af([128, 512])]
              Cs = af([128, NSQ, 129])
              ntm = af([NSQ, 512])
              ntmo = af([NSQ, 512])
              qT = ab([128, 4, NMAX])
              kT = ab([128, 4, NMAX])
              ktm = ab([128, NTM, 512])
              vaug = ab([128, NTM, 4, 129])
              SwT = ab([128, 512])
              qv = ab([128, 512])
              kw = ab([128, 512])
              SwT2 = [SwT, ab([128, 512])]
              qv2 = [qv, ab([128, 512])]
              kw2 = [kw, ab([128, 512])]
              qvm = ab([128, NSQ, LS])
              Csb = ab([128, NSQ, 129])
              vm = ab([128, 3, 129])
              mixs = ab([128, 4, NMAX])
              junk = ab([128, 128])
              hsq = af([128, 512])
              MEMSET("pool", vaug, 1.0, ["vaug"])
              CP("act", Cbf[:], Caug[:, ly], ["Caug"], ["Cbf"])
              for cg in range(4):
                  wt1, w1 = load_w(wpk_d[ly, 12 + cg])
                  for t in tiles:
                      L, ti = t["L"], t["ti"]

                      def ev_g(ps, pid, L=L, ti=ti, cg=cg):
                          CP("act", ytm[:L, 0:256], ps[:L, 0:256], [pid], ["ytm"])
                          ACT(hn[:L, 0:256], ytm[:L, 0:256], AF.Exp, ["ytm"], ["hn"], scale=-1.0)
                          ACT(hn[:L, 0:256], hn[:L, 0:256], AF.Ln, ["hn"], ["hn"], bias=1.0)
                          TT("dve", hn[:L, 0:128], hn[:L, 0:128], hn[:L, 128:256], ALU.add, ["hn"], ["hn"])
                          ACT(hn[:L, 0:128], hn[:L, 0:128], AF.Exp, ["hn"], ["hn"], scale=-1.0)
                          TT("dve", hn[:L, 0:128], hn[:L, 0:128], ytm[:L, 128:256], ALU.mult, ["hn", "ytm"], ["hn"])
                          TT("dve", Gml[:L, ti, cg * 128:(cg + 1) * 128], hn[:L, 0:128], mlnwbc[:L, cg * 128:(cg + 1) * 128],
                             ALU.mult, ["hn", "mlnwbc"], ["Gml"])
                      proj_tm(wt1, w1, 0, 256, t, ev_g)
              for hp in range(2):
                  wt1, w1 = load_w(wpk_d[ly, 16 + hp])
                  for hh in range(2):
                      h = hp * 2 + hh
                      proj_fm(wt1, w1, hh * 128, 128,
                              lambda ps, pid, g0, n, h=h: CP("act", qT[:, h, g0:g0 + n], ps[:, :n], [pid], ["qT"]))
              for hp in range(2):
                  wt1, w1 = load_w(wpk_d[ly, 18 + hp])
                  for hh in range(2):
                      h = hp * 2 + hh
                      proj_fm(wt1, w1, hh * 128, 128,
                              lambda ps, pid, g0, n, h=h: ACT(kT[:, h, g0:g0 + n], ps[:, :n], AF.Copy, [pid], ["kT"],
                                                             scale=128.0 ** -0.5))
                  for t in tiles:
                      L, ti = t["L"], t["ti"]
                      proj_tm(wt1, w1, 0, 256, t,
                              lambda ps, pid, L=L, ti=ti, hp=hp: ACT(ktm[:L, ti, hp * 256:(hp + 1) * 256], ps[:L, :256],
                                                                    AF.Copy, [pid], ["ktm"], scale=128.0 ** -0.5))
              for hp in range(2):
                  wt1, w1 = load_w(wpk_d[ly, 20 + hp])
                  for t in tiles:
                      L, ti = t["L"], t["ti"]
                      proj_tm(wt1, w1, 0, 256, t,
                              lambda ps, pid, L=L, ti=ti, hp=hp: CP("dve", vaug[:L, ti, 2 * hp:2 * hp + 2, 0:128],
                                                                   ps[:L, :256].rearrange("p (h v) -> p h v", h=2),
                                                                   [pid], ["vaug"]))
              gi, gf, gb, gm, ga, gc, gcv, gwg, genm, gmp = gr
              glf = gf
              proj_fm(wsmb, "wsmb", 0, 8,
                      lambda ps, pid, g0, n: TS("dve", gi[:, g0:g0 + n], ps[:8, :n], g8[:, 0:1], None, ALU.add, None,
                                                [pid, "g8"], ["gr0"]))
              proj_fm(wsmb, "wsmb", 8, 8,
                      lambda ps, pid, g0, n: ACT(gf[:, g0:g0 + n], ps[:8, :n], AF.Exp, [pid, "g8n"], ["gr1"],
                                                 bias=g8n[:, 1:2], scale=-1.0))
              ACT(glf[:, :N], gf[:, :N], AF.Ln, ["gr1", "gr2"], ["gr1", "gr2"], bias=1.0)
              TS("dve", glf[:, :N], glf[:, :N], -1.0, None, ALU.mult, None, ["gr2"], ["gr2"])
              SCAN(gb[:, 0:NP], k8("keepP"), glf[:, 0:NP], 0.0, ALU.mult, ALU.add, ["c8", "gr2"], ["gr3"])
              SCAN(gm[:, 0:NP], glf[:, 0:NP], gi[:, 0:NP], mcar[:, ly:ly + 1], ALU.add, ALU.max, ["gr2", "gr0", "mcar"], ["gr4"])
              CP("dve", gmp[:, 1:NP], gm[:, 0:NP - 1], ["gr4"], ["gr10"])
              CP("dve", gmp[:, 0:1], mcar[:, ly:ly + 1], ["mcar"], ["gr10"])
              CP("dve", mcar[:, ly:ly + 1], gm[:, NP - 1:NP], ["gr4"], ["mcar"])
              m0 = small_o[:8, 0:NSQ]
              if has_s:
                  sl_ = slice(NP, NP + LS)
                  DMA(m0, sm_d[ly], (), ["small_o"])
                  SCAN(gb[:, sl_], k8("keepS"), glf[:, sl_], 0.0, ALU.mult, ALU.add, ["c8", "gr2"], ["gr3"])
                  TT("dve", gmp[:, sl_], glf[:, sl_], k8("keepS"), ALU.mult, ["gr2", "c8"], ["gr10"])
                  TT("dve", gmp[:, sl_], gmp[:, sl_], k8("negS"), ALU.add, ["gr10", "c8"], ["gr10"])
                  CP("dve", gwg[:, sl_], gi[:, sl_], ["gr0"], ["gr8"])
                  lf0 = glf[:, sl_].rearrange("p (b t) -> p b t", t=4)[:, :, 0]
                  i0 = gwg[:, sl_].rearrange("p (b t) -> p b t", t=4)[:, :, 0]
                  TT("dve", small_o[:8, 64:64 + NSQ], lf0, m0, ALU.add, ["gr2", "small_o"], ["small_o"])
                  TT("dve", i0, i0, small_o[:8, 64:64 + NSQ], ALU.max, ["gr8", "small_o"], ["gr8"])
                  SCAN(gm[:, sl_], gmp[:, sl_], gwg[:, sl_], 0.0, ALU.add, ALU.max, ["gr10", "gr8"], ["gr4"])
              TT("dve", gc[:, :N], gb[:, :N], gm[:, :N], ALU.subtract, ["gr3", "gr4"], ["gr6"])
              TT("dve", ga[:, :N], gi[:, :N], gb[:, :N], ALU.subtract, ["gr0", "gr3"], ["gr5"])
              mprev_p = gmp[:, 0:NP].rearrange("p (c t) -> p c t", t=128)[:, :, 0]
              TT("dve", gcv[:, 0:NP].rearrange("p (c t) -> p c t", t=128), gc[:, 0:NP].rearrange("p (c t) -> p c t", t=128),
                 mprev_p.unsqueeze(2).broadcast_to([8, NCH, 128]), ALU.add, ["gr6", "gr10"], ["gr7"])
              if has_s:
                  TT("dve", gcv[:, sl_].rearrange("p (b t) -> p b t", t=4), gc[:, sl_].rearrange("p (b t) -> p b t", t=4),
                     m0.unsqueeze(2).broadcast_to([8, NSQ, 4]), ALU.add, ["gr6", "small_o"], ["gr7"])
              for kind in (("p", "s") if has_s else ("p",)):
                  ncn = NCH if kind == "p" else NSQ
                  T_ = 128 if kind == "p" else 4
                  base = 0 if kind == "p" else NP
                  gcst = gsm[:, 0:ncn]
                  TT("dve", gcst, chunk_ends(gb, kind), chunk_ends(gm, kind), ALU.subtract, ["gr3", "gr4"], ["gsm"])
                  seg = slice(base, base + ncn * T_)
                  TT("dve", gwg[:, seg].rearrange("p (c t) -> p c t", t=T_), ga[:, seg].rearrange("p (c t) -> p c t", t=T_),
                     gcst.unsqueeze(2).broadcast_to([8, ncn, T_]), ALU.add, ["gr5", "gsm"], ["gr8"])
                  mpv = mprev_p if kind == "p" else m0
                  TT("dve", small_o[:8, 128:128 + ncn], gcst, mpv, ALU.add, ["gsm", "gr10", "small_o"], ["small_o"])
                  ACT(small_o[:8, 128:128 + ncn], small_o[:8, 128:128 + ncn], AF.Exp, ["small_o"], ["small_o"])
                  bcast_rows(small_o[:8, 128:128 + ncn], "small_o", ncn, 0 if kind == "p" else 1)
              ACT(gwg[:, :N], gwg[:, :N], AF.Exp, ["gr8"], ["gr8"])
              ACT(genm[:, :N], gm[:, :N], AF.Exp, ["gr4"], ["gr9"], scale=-1.0)
              TS("dve", ga[:, :N], ga[:, :N], pm8, opm8, ALU.mult, ALU.add, ["gr5", "c8"], ["gr5"])
              TS("dve", gc[:, :N], gc[:, :N], opm8, pm8, ALU.mult, ALU.add, ["gr6", "c8"], ["gr6"])
              TS("dve", gcv[:, :N], gcv[:, :N], opm8, pm8, ALU.mult, ALU.add, ["gr7", "c8"], ["gr7"])
              if has_s:
                  DMA(ntm[:], sn_d[ly], (), ["ntm"])
              if last_pass:
                  DMA(pm_o[ly].rearrange("(h o) -> h o", o=1), mcar[0:4, ly:ly + 1], ["mcar"], ())
              if has_s:
                  CP("dve", small_o[:8, 512:512 + NSQ], chunk_ends(gm, "s"), ["gr4"], ["small_o"])
                  DMA(om_o[ly], small_o[0:4, 512:512 + NSQ], ["small_o"], ())

              def ml_front(t):
                  L, ti, col, kind = t["L"], t["ti"], t["col"], t["kind"]
                  pb = t["ti"] % 2
                  SwT, qv, kw = SwT2[pb], qv2[pb], kw2[pb]
                  idS, idQ, idK = "SwT%d" % pb, "qv%d" % pb, "kw%d" % pb
                  to_tok_major(gwg, "gr8", genm, "gr9", t, 0, gmp, "gr10")
                  for h in range(4):
                      MM(PS["psS"][:L, h * L:(h + 1) * L], kT[:, h, col:col + L], qT[:, h, col:col + L], True, True,
                         ["kT", "qT"], ["psS"])
                  expo(ga, "gr5", gc, "gr6", gcv, "gr7", t, Wt, Wv)
                  TT("dve", SwT[:L, :4 * L], PS["psS"][:L, :4 * L], Wt[:L, :4 * L], ALU.mult, ["psS", "Wt"], [idS])
                  TT("pool", qv[:, :4 * L].rearrange("p (h l) -> p h l", h=4), qT[:, :, col:col + L],
                     Wv[:, :4 * L].rearrange("p (h l) -> p h l", h=4), ALU.mult, ["qT", "Wv"], [idQ])
                  TT("pool", kw[:L, :].rearrange("p (h d) -> p h d", h=4), ktm[:L, ti, :].rearrange("p (h d) -> p h d", h=4),
                     scal[:L, ti, 0, 0:4].unsqueeze(2).broadcast_to([L, 4, 128]), ALU.mult, ["ktm", "scal"], [idK])

              def ml_mm(t):
                  L, ti, col, kind = t["L"], t["ti"], t["col"], t["kind"]
                  pb = t["ti"] % 2
                  SwT, qv, kw = SwT2[pb], qv2[pb], kw2[pb]
                  idS, idQ, idK = "SwT%d" % pb, "qv%d" % pb, "kw%d" % pb
                  for h in range(4):
                      pn = PS["psN0"] if h < 2 else PS["psN1"]
                      pnid = "psN0" if h < 2 else "psN1"
                      o_ = (h % 2) * 129
                      if kind == "s":
                          for b4 in range(0, NSQ, 4):
                              DMA(Cs[:, b4:b4 + 4, 0:128], sC_d[ly][b4:b4 + 4, h].rearrange("b d v -> d b v"), (), ["Cs"])
                          TR(PS["psU"][:, :NSQ], ntm[:NSQ, h * 128:(h + 1) * 128], identf[:NSQ, :NSQ], ["ntm", "c128"], ["psU"])
                          CP("dve", Cs[:, :, 128], PS["psU"][:, :NSQ], ["psU"], ["Cs"])
                          CP("act", Csb, Cs, ["Cs"], ["Csb"])
                          TT("pool", qvm, qv[:, h * L:(h + 1) * L].unsqueeze(1).broadcast_to([128, NSQ, L]),
                             smkb[:, :].rearrange("p (b l) -> p b l", b=NSQ), ALU.mult, [idQ, "smkb"], ["qvm"])
                      MM(pn[:L, o_:o_ + 129], SwT[:L, h * L:(h + 1) * L], vaug[:L, ti, h, :], True, False,
                         [idS, "vaug"], [pnid])
                      if kind == "p":
                          MM(pn[:L, o_:o_ + 129], qv[:, h * L:(h + 1) * L], Cbf[:, h, :], False, True, [idQ, "Cbf"], [pnid])
                      else:
                          for b in range(NSQ):
                              MM(pn[:L, o_:o_ + 129], qvm[:, b, :], Csb[:, b, :], False, b == NSQ - 1,
                                 ["qvm", "Csb"], [pnid])
                      kwh = kw[:L, h * 128:(h + 1) * 128]
                      if kind == "p":
                          psu, psuid = next_ps2()
                          MM(psu[:, 0:129], kwh, vaug[:L, ti, h, :], True, True, [idK, "vaug"], [psuid])
                          STT("dve", Caug[:, ly, h, :], Caug[:, ly, h, :], bcw[:, 0, h, ti:ti + 1], psu[:, 0:129],
                              ALU.mult, ALU.add, ["Caug", "bcw", psuid], ["Caug"])
                      else:
                          b0 = 0
                          while b0 < NSQ:
                              nb = min(3, NSQ - b0)
                              TT("pool", vm[:L, 0:nb, :], vaug[:L, ti, h, :].unsqueeze(1).broadcast_to([L, nb, 129]),
                                 smT[:L, b0:b0 + nb].unsqueeze(2).broadcast_to([L, nb, 129]), ALU.mult, ["vaug", "c128"], ["vm"])
                              MM(PS["psU"][:, 0:nb * 129], kwh, vm[:L, 0:nb, :], True, True, [idK, "vm"], ["psU"])
                              TT("dve", Cs[:, b0:b0 + nb, :], Cs[:, b0:b0 + nb, :],
                                 bcw[:, 1, h, b0:b0 + nb].unsqueeze(2).broadcast_to([128, nb, 129]), ALU.mult,
                                 ["Cs", "bcw"], ["Cs"])
                              TT("dve", Cs[:, b0:b0 + nb, :], Cs[:, b0:b0 + nb, :],
                                 PS["psU"][:, 0:nb * 129].rearrange("p (b v) -> p b v", b=nb), ALU.add, ["Cs", "psU"], ["Cs"])
                              b0 += nb
                          for b4 in range(0, NSQ, 4):
                              DMA(oC_o[ly][b4:b4 + 4, h].rearrange("b d v -> d b v"), Cs[:, b4:b4 + 4, 0:128], ["Cs"], ())
                          CP("dve", ncol[:, :NSQ], Cs[:, :, 128], ["Cs"], ["ncol"])
                          TR(PS["psU"][:NSQ, :128], ncol[:, :NSQ], identf, ["ncol", "c128"], ["psU"])
                          CP("dve", ntmo[:NSQ, h * 128:(h + 1) * 128], PS["psU"][:NSQ, :128], ["psU"], ["ntmo"])
                  if kind == "p":
                      CP("act", Cbf[:], Caug[:, ly], ["Caug"], ["Cbf"])

              def ml_stats(t):
                  L, ti, col, kind = t["L"], t["ti"], t["col"], t["kind"]
                  pb = t["ti"] % 2
                  SwT, qv, kw = SwT2[pb], qv2[pb], kw2[pb]
                  idS, idQ, idK = "SwT%d" % pb, "qv%d" % pb, "kw%d" % pb
                  ytm = ytm2[pb]
                  idY = "ytm" if pb == 0 else "ytm1"
                  for half in range(2):
                      pn = PS["psN0"] if half == 0 else PS["psN1"]
                      pnid = "psN0" if half == 0 else "psN1"
                      den = pn[:L, 0:258].rearrange("p (h v) -> p h v", h=2)[:, :, 128]
                      ACT(tm4[:L, 2 * half:2 * half + 2], den, AF.Abs, [pnid], ["tm4"])
                  TT("dve", tm4[:L, 0:4], tm4[:L, 0:4], scal[:L, ti, 0, 4:8], ALU.max, ["tm4", "scal"], ["tm4"])
                  RECIP(tm4[:L, 0:4], tm4[:L, 0:4], ["tm4"], ["tm4"])
                  for half in range(2):
                      pn = PS["psN0"] if half == 0 else PS["psN1"]
                      pnid = "psN0" if half == 0 else "psN1"
                      CP("act", hn[:L, half * 256:(half + 1) * 256].rearrange("p (h v) -> p h v", h=2),
                         pn[:L, 0:258].rearrange("p (h v) -> p h v", h=2)[:, :, 0:128], [pnid], ["hn"])
                  TT("pool", hsq[:L, :], hn[:L, :], hn[:L, :], ALU.mult, ["hn"], ["hsq"])
                  RSUM(tm4[:L, 4:8], hn[:L, :].rearrange("p (h v) -> p h v", h=4), ["hn"], ["tm4"])
                  RSUM(tm4[:L, 8:12], hsq[:L, :].rearrange("p (h v) -> p h v", h=4), ["hsq"], ["tm4"])
                  TS("dve", tm4[:L, 4:8], tm4[:L, 4:8], 1.0 / 128, None, ALU.mult, None, ["tm4"], ["tm4"])
                  TS("dve", tm4[:L, 8:12], tm4[:L, 8:12], 1.0 / 128, None, ALU.mult, None, ["tm4"], ["tm4"])
                  TT("dve", tm4[:L, 12:16], tm4[:L, 4:8], tm4[:L, 4:8], ALU.mult, ["tm4"], ["tm4"])
                  TT("dve", tm4[:L, 8:12], tm4[:L, 8:12], tm4[:L, 12:16], ALU.subtract, ["tm4"], ["tm4"])
                  TT("dve", tm4[:L, 12:16], tm4[:L, 0:4], tm4[:L, 0:4], ALU.mult, ["tm4"], ["tm4"])
                  TT("dve", tm4[:L, 8:12], tm4[:L, 8:12], tm4[:L, 12:16], ALU.mult, ["tm4"], ["tm4"])
                  TS("dve", tm4[:L, 8:12], tm4[:L, 8:12], 0.0, None, ALU.max, None, ["tm4"], ["tm4"])
                  ACT(tm4[:L, 8:12], tm4[:L, 8:12], AF.Ln, ["tm4"], ["tm4"], bias=EPS)
                  ACT(tm4[:L, 8:12], tm4[:L, 8:12], AF.Exp, ["tm4"], ["tm4"], scale=-0.5)
                  TT("dve", tm4[:L, 8:12], tm4[:L, 8:12], tm4[:L, 0:4], ALU.mult, ["tm4"], ["tm4"])
                  hn3 = hn[:L, :].rearrange("p (h v) -> p h v", h=4)
                  TT("dve", hn3, hn3, tm4[:L, 4:8].unsqueeze(2).broadcast_to([L, 4, 128]), ALU.subtract, ["hn", "tm4"], ["hn"])
                  TT("dve", hn3, hn3, tm4[:L, 8:12].unsqueeze(2).broadcast_to([L, 4, 128]), ALU.mult, ["hn", "tm4"], ["hn"])
                  TT("pool", ytm[:L, :], hn[:L, :], Gml[:L, ti, :], ALU.mult, ["hn", "Gml"], [idY])

              def ml_tail(t):
                  L, ti, col, kind = t["L"], t["ti"], t["col"], t["kind"]
                  pb = t["ti"] % 2
                  SwT, qv, kw = SwT2[pb], qv2[pb], kw2[pb]
                  idS, idQ, idK = "SwT%d" % pb, "qv%d" % pb, "kw%d" % pb
                  ytm = ytm2[pb]
                  idY = "ytm" if pb == 0 else "ytm1"
                  for h in range(4):
                      TR(PS["psU"][:, h * 128:h * 128 + L], ytm[:L, h * 128:(h + 1) * 128], identf[:L, :L], [idY, "c128"], ["psU"])
                  CP("act", mixs[:, 0:4, col:col + L], PS["psU"][:, :].rearrange("p (j c) -> p j c", j=4)[:, :, :L],
                     ["psU"], ["mixs"])

              ml_front(tiles[0])
              prev_ = None
              for i_, t in enumerate(tiles):
                  if i_ + 1 < len(tiles):
                      ml_front(tiles[i_ + 1])
                  ml_mm(t)
                  if prev_ is not None:
                      ml_tail(prev_)
                  ml_stats(t)
                  prev_ = t
              ml_tail(prev_)
              if has_s:
                  DMA(on_o[ly], ntmo[:], ["ntmo"], ())
              if last_pass:
                  DMA(pC_o[ly].rearrange("h d v -> d h v"), Caug[:, ly, :, 0:128], ["Caug"], ())
                  CP("dve", ncol[:, 0:4], Caug[:, ly, :, 128], ["Caug"], ["ncol"])
                  TR(PS["psU"][:4, :128], ncol[:, 0:4], identf, ["ncol", "c128"], ["psU"])
                  CP("dve", small_o[:4, 600:728], PS["psU"][:4, :128], ["psU"], ["small_o"])
                  DMA(pn_o[ly], small_o[:4, 600:728], ["small_o"], ())
              checkpoint()
              wout_part(mixs, 0)
              checkpoint()

              stage_reset()
              fa = [af([128, NMAX]) for _ in range(2)]
              xtm = af([128, NTM, 512])
              gzs = af([128, NTM, 512])
              gr = [af([8, NMAX]) for _ in range(6)]
              Wt = af([128, 512])
              Wv = af([128, 512])
              hn = af([128, 512])
              ytm = af([128, 256])
              ytm2s = [ytm, af([128, 256])]
              Sin = af([64, NSQ, 128])
              cvin = af([NSQ * 3, 128])
              cvout = af([NSQ * 3, 128])
              ssdnwbc = af([128, BR])
              DMA(ssdnwbc, ssdnw_d[ly].partition_broadcast(128), (), ["ssdnwbc"])
              Btm = ab([128, NTM, 2, 128])
              BTb = ab([128, 2, NMAX])
              CTb = ab([128, 2, NMAX])
              Mh = ab([128, 512])
              Ct = ab([128, 512])
              Ctm = ab([128, NSQ, LS])
              xdt = ab([128, 256])
              xw = ab([128, 256])
              Mh2 = [Mh, ab([128, 512])]
              Ct2 = [Ct, ab([128, 512])]
              xdt2 = [xdt, ab([128, 256])]
              xw2 = [xw, ab([128, 256])]
              Ssb = ab([128, NSQ, 64])
              Bm = ab([128, NSQ, 128])
              mixs = ab([128, 4, NMAX])
              junk = ab([128, 256])
              CP("act", STb[:], STs[:, ly], ["STs"], ["STb"])
              fa2 = [fa[0], af([128, NMAX])]
              pend_tail = [None]
              for j in range(8):
                  if j % 2 == 0:
                      wt1, w1 = load_w(wpk_d[ly, 22 + j // 2])
                  conv_load_hist(8 + j)
                  if has_s:
                      sample_hist_tile(cvin, 4, j, sscv_d[ly])
                  proj_fm(wt1, w1, (j % 2) * 128, 128, lambda ps, pid, g0, n: fill_ub(ps, pid, g0, n))
                  fo = fa2[j % 2]
                  fid = "fo%d" % (j % 2)
                  conv_apply(fo, fid, 4, 4 * j, 32 + j)
                  conv_save_hist(8 + j)
                  if has_s:
                      sample_hist_out_tile(cvout, 4, j, oscv_o[ly])
                  ACT(fa[1][:, :N], fo[:, :N], AF.Exp, [fid], ["fa1"], scale=-1.0)
                  sigmoid_from_exp(fa[1][:, :N], ["fa1"], ["fa1"])
                  TT("dve", fo[:, :N], fo[:, :N], fa[1][:, :N], ALU.mult, [fid, "fa1"], [fid])

                  def tail(j=j, fo=fo, fid=fid):
                      if j < 4:
                          for t in tiles:
                              L, ti, col = t["L"], t["ti"], t["col"]
                              TR(PS["psU"][:L, 0:128], fo[:, col:col + L], identf, [fid, "c128"], ["psU"])
                              CP("act", xtm[:L, ti, j * 128:(j + 1) * 128], PS["psU"][:L, 0:128], ["psU"], ["xtm"])
                      elif j < 6:
                          g = j - 4
                          CP("act", BTb[:, g, :N], fo[:, :N], [fid], ["BTb"])
                          for t in tiles:
                              L, ti, col = t["L"], t["ti"], t["col"]
                              TR(PS["psU"][:L, 0:128], fo[:, col:col + L], identf, [fid, "c128"], ["psU"])
                              CP("act", Btm[:L, ti, g, :], PS["psU"][:L, 0:128], ["psU"], ["Btm"])
                      else:
                          g = j - 6
                          CP("act", CTb[:, g, :N], fo[:, :N], [fid], ["CTb"])
                  if pend_tail[0] is not None:
                      pend_tail[0]()
                  pend_tail[0] = tail
              pend_tail[0]()
              if last_pass:
                  prompt_hist_out(4, list(range(8, 16)), pscv_o[ly])
              for hp in range(2):
                  wt1, w1 = load_w(wpk_d[ly, 26 + hp])
                  for t in tiles:
                      L, ti = t["L"], t["ti"]

                      def ev_sz(ps, pid, L=L, ti=ti, hp=hp):
                          CP("act", ytm[:L, 0:256], ps[:L, 0:256], [pid], ["ytm"])
                          ACT(hn[:L, 0:256], ytm[:L, 0:256], AF.Exp, ["ytm"], ["hn"], scale=-1.0)
                          sigmoid_from_exp(hn[:L, 0:256], ["hn"], ["hn"])
                          TT("dve", gzs[:L, ti, hp * 256:(hp + 1) * 256], hn[:L, 0:256], ytm[:L, 0:256], ALU.mult,
                             ["hn", "ytm"], ["gzs"])
                      proj_tm(wt1, w1, 0, 256, t, ev_sz)
              for g in range(2):
                  gdt, gda, gacs, gwr, grr, gscr = gr
                  gna = gda
                  proj_fm(wsmb, "wsmb", 16 + 8 * g, 8,
                          lambda ps, pid, g0, n, g=g: ACT(gdt[:, g0:g0 + n], ps[:8, :n], AF.Exp, [pid, "g8"], ["gr0"],
                                                          bias=g8[:, 2 + g:3 + g]))
                  ACT(gdt[:, :N], gdt[:, :N], AF.Ln, ["gr0"], ["gr0"], bias=1.0)
                  TS("dve", gda[:, :N], gdt[:, :N], g8n[:, 4 + g:5 + g], None, ALU.mult, None, ["gr0", "g8n"], ["gr1", "gr3"])
                  SCAN(gacs[:, 0:NP], k8("keepP"), gda[:, 0:NP], 0.0, ALU.mult, ALU.add, ["c8", "gr1", "gr3"], ["gr2"])
                  if has_s:
                      SCAN(gacs[:, NP:NP + LS], k8("keepS"), gda[:, NP:NP + LS], 0.0, ALU.mult, ALU.add, ["c8", "gr1"], ["gr2"])
                  for kind in (("p", "s") if has_s else ("p",)):
                      ncn = NCH if kind == "p" else NSQ
                      T_ = 128 if kind == "p" else 4
                      base = 0 if kind == "p" else NP
                      seg = slice(base, base + ncn * T_)
                      al = gsm[:, 0:ncn]
                      CP("dve", al, chunk_ends(gacs, kind), ["gr2"], ["gsm"])
                      TT("dve", gwr[:, seg].rearrange("p (c t) -> p c t", t=T_), al.unsqueeze(2).broadcast_to([8, ncn, T_]),
                         gacs[:, seg].rearrange("p (c t) -> p c t", t=T_), ALU.subtract, ["gsm", "gr2"], ["gr4"])
                      ACT(small_o[:8, 128:128 + ncn], al, AF.Exp, ["gsm"], ["small_o"])
                      bcast_rows(small_o[:8, 128:128 + ncn], "small_o", ncn, 0 if kind == "p" else 1)
                  ACT(gwr[:, :N], gwr[:, :N], AF.Exp, ["gr4"], ["gr4"])
                  TT("dve", gwr[:, :N], gwr[:, :N], gdt[:, :N], ALU.mult, ["gr4", "gr0"], ["gr4"])
                  TS("dve", gna[:, :N], gacs[:, :N], -1.0, None, ALU.mult, None, ["gr2", "gr1"], ["gr3", "gr1"])
                  TS("dve", gna[:, :N], gna[:, :N], pm8, opm8, ALU.mult, ALU.add, ["gr3", "c8"], ["gr3"])
                  TS("dve", grr[:, :N], gacs[:, :N], opm8, pm8, ALU.mult, ALU.add, ["gr2", "c8"], ["gr5"])
                  def sd_front(t, g=g):
                      L, ti, col, kind = t["L"], t["ti"], t["col"], t["kind"]
                      pb = t["ti"] % 2
                      Mh, Ct, xdt, xw = Mh2[pb], Ct2[pb], xdt2[pb], xw2[pb]
                      idM, idC, idX, idW = "Mh%d" % pb, "Ct%d" % pb, "xdt%d" % pb, "xw%d" % pb
                      xg = xtm[:L, ti, g * 256:(g + 1) * 256].rearrange("p (h c) -> p h c", h=4)
                      to_tok_major(gdt, "gr0", gwr, "gr4", t, 1, gscr, "gr6")
                      MM(PS["psS"][:L, :L], BTb[:, g, col:col + L], CTb[:, g, col:col + L], True, True, ["BTb", "CTb"], ["psS"])
                      expo(gna, "gr3", grr, "gr5", grr, "gr5", t, Wt, Wv)
                      TT("dve", Mh[:L, :4 * L].rearrange("p (h l) -> p h l", h=4),
                         PS["psS"][:L, :L].unsqueeze(1).broadcast_to([L, 4, L]),
                         Wt[:L, :4 * L].rearrange("p (h l) -> p h l", h=4), ALU.mult, ["psS", "Wt"], [idM])
                      TT("pool", Ct[:, :4 * L].rearrange("p (h l) -> p h l", h=4),
                         CTb[:, g, col:col + L].unsqueeze(1).broadcast_to([128, 4, L]),
                         Wv[:, :4 * L].rearrange("p (h l) -> p h l", h=4), ALU.mult, ["CTb", "Wv"], [idC])
                      xg = xtm[:L, ti, g * 256:(g + 1) * 256].rearrange("p (h c) -> p h c", h=4)
                      TT("dve", xdt[:L, :].rearrange("p (h c) -> p h c", h=4), xg,
                         scal[:L, ti, 1, 0:4].unsqueeze(2).broadcast_to([L, 4, 64]), ALU.mult, ["xtm", "scal"], [idX])
                      TT("dve", xw[:L, :].rearrange("p (h c) -> p h c", h=4), xg,
                         scal[:L, ti, 1, 4:8].unsqueeze(2).broadcast_to([L, 4, 64]), ALU.mult, ["xtm", "scal"], [idW])
                      if kind == "s":
                          TT("pool", Bm[:L, :, :], Btm[:L, ti, g, :].unsqueeze(1).broadcast_to([L, NSQ, 128]),
                             smT[:L, :].unsqueeze(2).broadcast_to([L, NSQ, 128]), ALU.mult, ["Btm", "c128"], ["Bm"])

                  def sd_mm(t, g=g):
                      L, ti, col, kind = t["L"], t["ti"], t["col"], t["kind"]
                      pb = t["ti"] % 2
                      Mh, Ct, xdt, xw = Mh2[pb], Ct2[pb], xdt2[pb], xw2[pb]
                      idM, idC, idX, idW = "Mh%d" % pb, "Ct%d" % pb, "xdt%d" % pb, "xw%d" % pb
                      xg = xtm[:L, ti, g * 256:(g + 1) * 256].rearrange("p (h c) -> p h c", h=4)
                      for h in range(4):
                          hh = g * 4 + h
                          if kind == "s":
                              TT("pool", Ctm, Ct[:, h * L:(h + 1) * L].unsqueeze(1).broadcast_to([128, NSQ, L]),
                                 smkb[:, :].rearrange("p (b l) -> p b l", b=NSQ), ALU.mult, [idC, "smkb"], ["Ctm"])
                              for b4 in range(0, NSQ, 8):
                                  DMA(Sin[:, b4:b4 + 8, :], sS_d[ly][b4:b4 + 8, hh].rearrange("b p n -> p b n"), (), ["Sin"])
                              for bq in range(NSQ // 8):
                                  for b in range(8):
                                      TR(PS["psU"][:, b * 64:(b + 1) * 64], Sin[:, bq * 8 + b, :], identf[:64, :64],
                                         ["Sin", "c128"], ["psU"])
                                  CP("act", Ssb[:, bq * 8:(bq + 1) * 8, :],
                                     PS["psU"][:, :].rearrange("p (b c) -> p b c", b=8), ["psU"], ["Ssb"])
                          MM(PS["psN0"][:L, h * 64:(h + 1) * 64], Mh[:L, h * L:(h + 1) * L], xdt[:L, h * 64:(h + 1) * 64],
                             True, False, [idM, idX], ["psN0"])
                          if kind == "p":
                              MM(PS["psN0"][:L, h * 64:(h + 1) * 64], Ct[:, h * L:(h + 1) * L],
                                 STb[:, hh * 64:(hh + 1) * 64], False, True, [idC, "STb"], ["psN0"])
                          else:
                              for b in range(NSQ):
                                  MM(PS["psN0"][:L, h * 64:(h + 1) * 64], Ctm[:, b, :], Ssb[:, b, :],
                                     False, b == NSQ - 1, ["Ctm", "Ssb"], ["psN0"])
                              for bq in range(NSQ // 4):
                                  MM(PS["psV"][:64, :512], xw[:L, h * 64:(h + 1) * 64],
                                     Bm[:L, bq * 4:(bq + 1) * 4, :], True, True, [idW, "Bm"], ["psV"])
                                  TT("dve", Sin[:, bq * 4:(bq + 1) * 4, :], Sin[:, bq * 4:(bq + 1) * 4, :],
                                     bcw[:64, 1, h, bq * 4:(bq + 1) * 4].unsqueeze(2).broadcast_to([64, 4, 128]), ALU.mult,
                                     ["Sin", "bcw"], ["Sin"])
                                  TT("dve", Sin[:, bq * 4:(bq + 1) * 4, :], Sin[:, bq * 4:(bq + 1) * 4, :],
                                     PS["psV"][:64, :512].rearrange("p (b n) -> p b n", b=4), ALU.add, ["Sin", "psV"], ["Sin"])
                              for b4 in range(0, NSQ, 8):
                                  DMA(oS_o[ly][b4:b4 + 8, hh].rearrange("b p n -> p b n"), Sin[:, b4:b4 + 8, :], ["Sin"], ())
                      if kind == "p":
                          psu, psuid = next_ps2()
                          MM(psu[:, 0:256], Btm[:L, ti, g, :], xw[:L, :], True, True, ["Btm", idW], [psuid])
                          sg = STs[:, ly, g * 256:(g + 1) * 256]
                          TT("pool", sg.rearrange("p (h c) -> p h c", h=4), sg.rearrange("p (h c) -> p h c", h=4),
                             bcw[:, 0, :, ti:ti + 1].broadcast_to([128, 4, 64]), ALU.mult, ["STs", "bcw"], ["STs"])
                          TT("dve", sg, sg, psu[:, 0:256], ALU.add, ["STs", psuid], ["STs"])
                          CP("act", STb[:, g * 256:(g + 1) * 256], sg, ["STs"], ["STb"])

                  def sd_stats(t, g=g):
                      L, ti, col, kind = t["L"], t["ti"], t["col"], t["kind"]
                      pb = t["ti"] % 2
                      Mh, Ct, xdt, xw = Mh2[pb], Ct2[pb], xdt2[pb], xw2[pb]
                      idM, idC, idX, idW = "Mh%d" % pb, "Ct%d" % pb, "xdt%d" % pb, "xw%d" % pb
                      xg = xtm[:L, ti, g * 256:(g + 1) * 256].rearrange("p (h c) -> p h c", h=4)
                      ytm = ytm2s[pb]
                      idY = "ytm" if pb == 0 else "ytm1"
                      TT("dve", hn[:L, 0:256].rearrange("p (h c) -> p h c", h=4), xg,
                         Dbc[:L, g * 4:(g + 1) * 4].unsqueeze(2).broadcast_to([L, 4, 64]), ALU.mult, ["xtm", "Dbc"], ["hn"])
                      TT("dve", hn[:L, 0:256], hn[:L, 0:256], PS["psN0"][:L, 0:256], ALU.add, ["hn", "psN0"], ["hn"])
                      TT("dve", hn[:L, 0:256], hn[:L, 0:256], gzs[:L, ti, g * 256:(g + 1) * 256], ALU.mult, ["hn", "gzs"], ["hn"])
                      ACT(junk[:L, 0:256], hn[:L, 0:256], AF.Square, ["hn"], ["junk", "tm4"], accum=tm4[:L, 0:1])
                      ACT(tm4[:L, 1:2], tm4[:L, 0:1], AF.Ln, ["tm4"], ["tm4"], bias=EPS, scale=1.0 / 256)
                      ACT(tm4[:L, 2:3], tm4[:L, 1:2], AF.Exp, ["tm4"], ["tm4"], scale=-0.5)
                      STT("dve", ytm[:L, 0:256], hn[:L, 0:256], tm4[:L, 2:3], ssdnwbc[:L, g * 256:(g + 1) * 256],
                          ALU.mult, ALU.mult, ["hn", "tm4", "ssdnwbc"], [idY])

                  def sd_tail(t, g=g):
                      L, ti, col, kind = t["L"], t["ti"], t["col"], t["kind"]
                      pb = t["ti"] % 2
                      Mh, Ct, xdt, xw = Mh2[pb], Ct2[pb], xdt2[pb], xw2[pb]
                      idM, idC, idX, idW = "Mh%d" % pb, "Ct%d" % pb, "xdt%d" % pb, "xw%d" % pb
                      xg = xtm[:L, ti, g * 256:(g + 1) * 256].rearrange("p (h c) -> p h c", h=4)
                      ytm = ytm2s[pb]
                      idY = "ytm" if pb == 0 else "ytm1"
                      for h2 in range(2):
                          TR(PS["psU"][:, h2 * 128:h2 * 128 + L], ytm[:L, h2 * 128:(h2 + 1) * 128], identf[:L, :L],
                             [idY, "c128"], ["psU"])
                      CP("act", mixs[:, 2 * g:2 * g + 2, col:col + L],
                         PS["psU"][:, 0:256].rearrange("p (j c) -> p j c", j=2)[:, :, :L], ["psU"], ["mixs"])

                  sd_front(tiles[0])
                  prev_ = None
                  for i_, t in enumerate(tiles):
                      if i_ + 1 < len(tiles):
                          sd_front(tiles[i_ + 1])
                      sd_mm(t)
                      if prev_ is not None:
                          sd_tail(prev_)
                      sd_stats(t)
                      prev_ = t
                  sd_tail(prev_)
              if last_pass:
                  for q4 in range(4):
                      TR(PS["psU"][:, q4 * 128:(q4 + 1) * 128], STs[:, ly, q4 * 128:(q4 + 1) * 128], identf, ["STs", "c128"], ["psU"])
                  CP("dve", hn[:, :], PS["psU"][:, :], ["psU"], ["hn"])
                  DMA(pS_o[ly].rearrange("(q p) n -> p q n", p=128), hn[:, :].rearrange("p (q n) -> p q n", q=4), ["hn"], ())
              checkpoint()
              wout_part(mixs, 4)
              checkpoint()

              if ly == L2 - 1:
                  stage_reset()
                  xn32 = af([128, D])
                  nwbc = af([128, D])
                  junk = ab([128, D])
                  DMA(nwbc, fnw_d.partition_broadcast(128), (), ["nwbc"])
                  for t in tiles:
                      L, ti = t["L"], t["ti"]
                      rmsnorm_tile(t, None, xn32, junk, nwbc)
                      dst = yp_o[t["tok"]:t["tok"] + 128, :] if t["kind"] == "p" else ys_o
                      DMA(dst, xn32[:L], ["xn32"], ())
    except _Stop:
        pass

    print("NOPS", len(R.ops), "arena max words f32/bf16", amax, flush=True)
    block = st.enter_context(nc.Block())
    R.emit(nc, block, st)
    st.close()
    return nc


_CACHE = {}


def _host_layout(inputs, NSQ, c):
    f = np.ascontiguousarray
    b0 = c * NSQ
    L2 = DEPTH
    w_in = inputs["w_in"]
    m = {}
    m["xp"] = f(inputs["x_prompt"][c % inputs["x_prompt"].shape[0]])
    m["xs"] = f(inputs["x_sample"][b0:b0 + NSQ].reshape(NSQ * 4, D))
    m["sC"] = f(inputs["state_mlstm_C"][:, b0:b0 + NSQ])
    m["sn"] = f(inputs["state_mlstm_n"][:, b0:b0 + NSQ].reshape(L2, NSQ, 512))
    smt = np.transpose(inputs["state_mlstm_m"][:, b0:b0 + NSQ], (0, 2, 1))
    m["sm"] = f(np.concatenate([smt, smt], axis=1))
    m["sS"] = f(inputs["state_ssd"][:, b0:b0 + NSQ])
    m["sscv"] = f(inputs["state_ssd_conv"][:, b0:b0 + NSQ].reshape(L2, NSQ * 3, 1024))
    m["sccv"] = f(inputs["state_sconv_conv"][:, b0:b0 + NSQ].reshape(L2, NSQ * 2, 512))
    m["srh"] = f(inputs["state_rglru_h"][:, b0:b0 + NSQ])
    m["srcv"] = f(inputs["state_rglru_conv"][:, b0:b0 + NSQ].reshape(L2, NSQ * 3, 512))
    return m


def _shared_layout(inputs, NCH, NSQ):
    f = np.ascontiguousarray
    L2 = DEPTH
    w_in = inputs["w_in"]
    s = {}
    def cols(o, n):
        return list(range(o, o + n))
    chunks = []
    for j in range(4):
        chunks.append(cols(OSB + j * 128, 128) + cols(OSC + j * 128, 128))
        chunks.append(cols(OSH + j * 128, 128) + cols(OSCZ + j * 128, 128))
    for j in range(4):
        chunks.append(cols(ORX + j * 128, 128) + cols(ORZ + j * 128, 128))
    for cg in range(4):
        chunks.append(cols(OO + cg * 128, 128) + cols(OZ + cg * 128, 128))
    for base in (OQ, OKK, OV):
        for hp in range(2):
            chunks.append(cols(base + hp * 256, 256))
    for jp in range(4):
        chunks.append(cols(OXBC + jp * 256, 256))
    for hp in range(2):
        chunks.append(cols(OSZ + hp * 256, 256))
    assert len(chunks) == 28

    def pack(W):
        K = W.shape[0] // 128
        return np.ascontiguousarray(W.reshape(K, 128, W.shape[1]).transpose(1, 0, 2).reshape(128, K * W.shape[1]))
    wpk = np.empty((L2, 28, 128, 4096), np.float32)
    for ly in range(L2):
        for k, cc in enumerate(chunks):
            wpk[ly, k] = pack(w_in[ly][:, cc])
    s["wpk"] = wpk
    wopk = np.empty((L2, 4, 8, 128, 1024), np.float32)
    for ly in range(L2):
        for g in range(4):
            for dg in range(8):
                wopk[ly, g, dg] = pack(inputs["w_out"][ly][g * 512:(g + 1) * 512, dg * 256:(dg + 1) * 256])
    s["wopk"] = wopk
    wi = w_in[:, :, OI:OI + 4]
    wf = w_in[:, :, OF:OF + 4]
    wd0 = w_in[:, :, ODT:ODT + 4]
    wd1 = w_in[:, :, ODT + 4:ODT + 8]
    wsm = np.concatenate([wi, wi, wf, wf, wd0, wd0, wd1, wd1], axis=2)
    s["wsm"] = np.stack([pack(wsm[ly]) for ly in range(L2)], axis=0)
    wabd = np.zeros((L2, 8, 128, 128), np.float32)
    for ly in range(L2):
        for j in range(4):
            for k in range(2):
                blk = 2 * j + k
                wabd[ly, j, k * 64:(k + 1) * 64, k * 64:(k + 1) * 64] = inputs["rg_wa"][ly, blk]
                wabd[ly, 4 + j, k * 64:(k + 1) * 64, k * 64:(k + 1) * 64] = inputs["rg_wx"][ly, blk]
    s["wabd"] = f(np.transpose(wabd, (0, 2, 1, 3)).reshape(L2, 128, 1024))
    cvec = np.zeros((L2, 128, 84), np.float32)

    def fm(v, nt):
        return v.reshape(nt, 128).T
    for ly in range(L2):
        cw = inputs["ssd_conv_w"][ly]
        cvec[ly, :, 0:32] = np.transpose(cw.reshape(4, 8, 128), (2, 1, 0)).reshape(128, 32)
        cvec[ly, :, 32:40] = fm(inputs["ssd_conv_b"][ly], 8)
        sw = inputs["sc_conv_w"][ly]
        cvec[ly, :, 40:52] = np.transpose(sw.reshape(3, 4, 128), (2, 1, 0)).reshape(128, 12)
        rw = inputs["rg_conv_w"][ly]
        cvec[ly, :, 52:68] = np.transpose(rw.reshape(4, 4, 128), (2, 1, 0)).reshape(128, 16)
        cvec[ly, :, 68:72] = fm(inputs["rg_conv_b"][ly], 4)
        cvec[ly, :, 72:76] = fm(inputs["rg_ba"][ly], 4)
        cvec[ly, :, 76:80] = fm(inputs["rg_bx"][ly], 4)
        cvec[ly, :, 80:84] = fm(inputs["rg_lambda"][ly], 4)
    s["cvec"] = cvec
    g8 = np.zeros((L2, 8, 8), np.float32)
    for ly in range(L2):
        d2 = lambda v: np.concatenate([v, v])
        g8[ly, :, 0] = d2(inputs["ml_i_bias"][ly])
        g8[ly, :, 1] = d2(inputs["ml_f_bias"][ly])
        g8[ly, :, 2] = d2(inputs["ssd_dt_bias"][ly][0:4])
        g8[ly, :, 3] = d2(inputs["ssd_dt_bias"][ly][4:8])
        g8[ly, :, 4] = d2(inputs["ssd_A_log"][ly][0:4])
        g8[ly, :, 5] = d2(inputs["ssd_A_log"][ly][4:8])
    s["g8"] = g8
    for k in ("norm_w", "ml_norm_w", "ssd_norm_w", "ssd_D", "final_norm_w"):
        s[k] = f(inputs[k])
    C128h, _, C8h, _ = host_consts(NCH, NSQ)
    s["c128"] = C128h
    s["c8"] = C8h
    return s


def run(inputs, NCH, n_cores=8):
    inputs = {k: np.asarray(v, dtype=np.float32) for k, v in inputs.items()}
    BP, TP = inputs["x_prompt"].shape[0], inputs["x_prompt"].shape[1]
    BS = inputs["x_sample"].shape[0]
    NSQ = BS // n_cores
    key = (TP, NCH, NSQ)
    if key not in _CACHE:
        _CACHE[key] = build(TP, NCH, NSQ)
    nc = _CACHE[key]
    shared = _shared_layout(inputs, NCH, NSQ)
    in_maps = []
    for c in range(n_cores):
        m = _host_layout(inputs, NSQ, c)
        m.update(shared)
        in_maps.append(m)
    res = run_bass_kernel_spmd(nc, in_maps, core_ids=list(range(n_cores)))
    r = res.results
    L2 = DEPTH
    cat = lambda k, ax: np.concatenate([r[c][k] for c in range(n_cores)], axis=ax)
    stk = lambda k: np.stack([r[c][k] for c in range(BP)], axis=1)
    y_prompt = np.stack([r[c]["yp"] for c in range(BP)], axis=0)
    y_sample = cat("ys", 0).reshape(BS, 4, D)
    p_C = stk("pC")
    p_n = stk("pn")
    p_m = stk("pm")
    p_ssd = stk("pS").reshape(L2, BP, 8, 64, 128)
    p_ssd_conv = stk("pscv")
    p_sc_conv = stk("pccv")
    p_rg_h = stk("prh").reshape(L2, BP, 512)
    p_rg_conv = stk("prcv")
    s_C = cat("oC", 1)
    s_n = cat("on", 1).reshape(L2, BS, 4, 128)
    s_m = np.transpose(cat("om", 2), (0, 2, 1))
    s_ssd = cat("oS", 1)
    s_ssd_conv = cat("oscv", 1).reshape(L2, BS, 3, 1024)
    s_sc_conv = cat("occv", 1).reshape(L2, BS, 2, 512)
    s_rg_h = cat("orh", 1)
    s_rg_conv = cat("orcv", 1).reshape(L2, BS, 3, 512)
    outs = (y_prompt, y_sample, p_C, p_n, p_m, p_ssd, p_ssd_conv, p_sc_conv, p_rg_h, p_rg_conv,
            s_C, s_n, s_m, s_ssd, s_ssd_conv, s_sc_conv, s_rg_h, s_rg_conv)
    return tuple(np.ascontiguousarray(o, dtype=np.float32) for o in outs)


def kernel(**inputs):
    return run(inputs, NCH=4)
```

```python
from contextlib import ExitStack

import numpy as np

import concourse.bass as bass
import concourse.mybir as mybir
from concourse.bass_utils import run_bass_kernel_spmd

F32 = mybir.dt.float32
BF16 = mybir.dt.bfloat16
ALU = mybir.AluOpType
AF = mybir.ActivationFunctionType

D = 2048
BR = 512
PO = 7184
DEPTH = 2
EPS = 1e-6
NEG = -30000.0
OQ, OKK, OV, OO, OZ, OI, OF = 0, 512, 1024, 1536, 2048, 2560, 2564
OSZ, OXBC, ODT = 2568, 3080, 4104
OSB, OSC, OSH, OSCZ = 4112, 4624, 5136, 5648
ORX, ORZ = 6160, 6672

ENGS = ("pe", "act", "dve", "pool", "sp")
NDMASEM = 8


class Op:
    __slots__ = ("eng", "fn", "reads", "writes", "dma", "deps", "sig", "tick", "dsem", "dcount", "idx", "prev")

    def __init__(self, eng, fn, reads, writes, dma):
        self.eng, self.fn, self.reads, self.writes, self.dma = eng, fn, reads, writes, dma
        self.deps = []
        self.sig = False
        self.tick = 0
        self.dsem = -1
        self.dcount = 0
        self.prev = 0


class _Stop(Exception):
    pass


class Rec:
    def __init__(self):
        self.ops = []
        self.last_w = {}
        self.readers = {}
        import os as _os
        self.maxops = int(_os.environ.get("KOPS", "100000000"))
        self.fence_ops = []
        self.pending = set()
        self.fence_from = 0

    PERSIST = frozenset(("c128", "c8", "identb", "maskPb", "maskSb", "smkb", "xnT", "st4", "tm4", "Dbc", "cvec", "g8",
                         "g8n", "nsp8", "nba", "wb0", "wb1", "wb2", "wsmb", "wabd", "Caug", "Cbf", "mcar", "STs", "STb", "cvcar",
                         "hcar", "ubp", "ubs", "gsm", "RM", "RMv", "scal", "bcw", "ncol", "cvtmp", "h0s", "small_o"))

    def touches_arena(self, op):
        for b in op.reads + op.writes:
            if b in self.PERSIST or b.startswith("ps") or b.startswith("xres"):
                continue
            return True
        return False

    def fence(self):
        last = {}
        dmas = []
        for op in self.ops[self.fence_from:]:
            if not self.touches_arena(op):
                continue
            if op.dma:
                dmas.append(op.idx)
            else:
                last[op.eng] = op.idx
        self.fence_ops = sorted(set(self.fence_ops) | set(last.values()) | set(dmas)) if self.pending else \
            sorted(set(last.values()) | set(dmas))
        self.pending = set(ENGS)
        self.fence_from = len(self.ops)

    def add(self, eng, fn, reads=(), writes=(), dma=False):
        if len(self.ops) >= self.maxops:
            return None
        writes = tuple(writes) + tuple(b for b in reads if isinstance(b, str) and b.startswith("ps") and b not in writes)
        op = Op(eng, fn, tuple(reads), tuple(writes), dma)
        op.idx = len(self.ops)
        deps = set()
        if eng in self.pending and self.touches_arena(op):
            deps |= set(self.fence_ops)
            self.pending.discard(eng)
        for b in op.reads:
            w = self.last_w.get(b)
            if w is not None:
                deps.add(w)
        for b in op.writes:
            w = self.last_w.get(b)
            if w is not None:
                deps.add(w)
            for r in self.readers.get(b, ()):
                deps.add(r)
        deps.discard(op.idx)
        op.deps = sorted(deps)
        for b in op.reads:
            self.readers.setdefault(b, []).append(op.idx)
        for b in op.writes:
            self.last_w[b] = op.idx
            self.readers[b] = []
        self.ops.append(op)
        return op

    def emit(self, nc, block, stack):
        ops = self.ops
        for op in ops:
            for d in op.deps:
                p = ops[d]
                if p.dma:
                    continue
                if p.eng == "pe" and op.eng == "pe" and not op.dma:
                    continue
                p.sig = True
        sem = {e: stack.enter_context(nc.semaphore("s_" + e)) for e in ("pe", "act", "dve", "pool")}
        dsem = {q: [stack.enter_context(nc.semaphore("d_%s%d" % (q, i))) for i in range(NDMASEM)]
                for q in ("sp", "pool")}
        cnt = {e: 0 for e in sem}
        dn = {q: 0 for q in dsem}
        dc = {q: [0] * NDMASEM for q in dsem}
        for op in ops:
            if op.dma:
                q = op.eng
                i = dn[q] % NDMASEM
                dn[q] += 1
                op.dsem = i
                op.prev = dc[q][i]
                dc[q][i] += 1
                op.dcount = dc[q][i]
            elif op.sig:
                cnt[op.eng] += 1
                op.tick = cnt[op.eng]
        per = {e: [o for o in ops if o.eng == e] for e in ENGS}

        def run(eng_name, e):
            waited = {}

            def need(key, s, v):
                if waited.get(key, 0) >= v:
                    return
                waited[key] = v
                e.wait_ge(s, v)

            for op in per[eng_name]:
                for d in op.deps:
                    p = ops[d]
                    if p.dma:
                        need(("d", p.eng, p.dsem), dsem[p.eng][p.dsem], 16 * p.dcount)
                    else:
                        if p.eng == "pe" and eng_name == "pe" and not op.dma:
                            continue
                        need(("c", p.eng), sem[p.eng], p.tick)
                if op.dma:
                    if op.prev > 0:
                        need(("d", op.eng, op.dsem), dsem[op.eng][op.dsem], 16 * op.prev)
                    op.fn(e).then_inc(dsem[op.eng][op.dsem], 16)
                else:
                    ins = op.fn(e)
                    if op.sig:
                        ins.then_inc(sem[op.eng], 1)
            if eng_name in dsem:
                for i in range(NDMASEM):
                    if dc[eng_name][i] > 0:
                        need(("d", eng_name, i), dsem[eng_name][i], 16 * dc[eng_name][i])

        @block.tensor
        def _(e):
            run("pe", e)

        @block.scalar
        def _(e):
            run("act", e)

        @block.vector
        def _(e):
            run("dve", e)

        @block.gpsimd
        def _(e):
            run("pool", e)

        @block.sync
        def _(e):
            run("sp", e)


def host_consts(NCH, NSQ):
    LS = 4 * NSQ
    NP = 128 * NCH
    c128 = {}
    c128["ident"] = np.eye(128, dtype=np.float32)
    s = np.arange(128)[:, None]
    l = np.arange(128)[None, :]
    mp = np.where(s <= l, 0.0, NEG).astype(np.float32)
    c128["maskP"] = np.tile(mp[:, None, :], (1, 4, 1)).reshape(128, 512)
    ms = np.full((128, 128), NEG, np.float32)
    sl = np.arange(LS)
    okm = (sl[:, None] // 4 == sl[None, :] // 4) & (sl[:, None] <= sl[None, :])
    ms[:LS, :LS] = np.where(okm, 0.0, NEG)
    c128["maskS"] = np.tile(ms[:, None, :LS], (1, 4, 1)).reshape(128, 4 * LS)
    smk = (np.arange(NSQ)[:, None] == (sl[None, :] // 4)).astype(np.float32)
    c128["smk"] = np.tile(smk.reshape(1, NSQ * LS), (128, 1))
    smT = np.zeros((128, NSQ), np.float32)
    smT[:LS] = smk.T
    c128["smT"] = smT
    C128 = np.concatenate([c128[k] for k in ("ident", "maskP", "maskS", "smk", "smT")], axis=1)
    offs128 = {}
    o = 0
    for k in ("ident", "maskP", "maskS", "smk", "smT"):
        offs128[k] = (o, c128[k].shape[1])
        o += c128[k].shape[1]
    c8 = {}
    keepP = np.ones((8, NP), np.float32)
    keepP[:, ::128] = 0
    c8["keepP"] = keepP
    keepS = np.ones((8, LS), np.float32)
    keepS[:, ::4] = 0
    c8["keepS"] = keepS
    negS = np.zeros((8, LS), np.float32)
    negS[:, ::4] = -1e30
    c8["negS"] = negS
    pm = np.array([1, 1, 1, 1, 0, 0, 0, 0], np.float32)[:, None]
    c8["pm"] = pm
    c8["opm"] = 1 - pm
    mf = np.zeros((8, 4, 128), np.float32)
    for k in range(8):
        mf[k, k % 4, :] = 1
    c8["maskfull"] = mf.reshape(8, 512)
    m8 = np.zeros((8, 4), np.float32)
    for k in range(4):
        m8[k, k] = 1
    c8["mask8h"] = m8
    lv = np.zeros((8, 128), np.float32)
    lv[4:] = 1
    c8["LV"] = lv
    c8["ones8"] = np.ones((8, 128), np.float32)
    names8 = ("keepP", "keepS", "negS", "pm", "opm", "maskfull", "mask8h", "LV", "ones8")
    C8 = np.concatenate([c8[k] for k in names8], axis=1)
    offs8 = {}
    o = 0
    for k in names8:
        offs8[k] = (o, c8[k].shape[1])
        o += c8[k].shape[1]
    return C128, offs128, C8, offs8


def build(TP, NCH, NSQ):
    LS = 4 * NSQ
    NPASS = TP // (128 * NCH)
    assert NPASS * 128 * NCH == TP
    NTM = NCH + 1
    NMAX = 128 * NCH + LS
    NP = 128 * NCH
    L2 = DEPTH
    C128h, o128, C8h, o8 = host_consts(NCH, NSQ)
    nc = bass.Bass("TRN2", target_bir_lowering=False)

    def din(name, shape):
        return nc.dram_tensor(name, list(shape), F32, kind="ExternalInput").ap()

    def dout(name, shape):
        return nc.dram_tensor(name, list(shape), F32, kind="ExternalOutput").ap()

    xp_d = din("xp", [TP, D])
    xs_d = din("xs", [LS, D])
    sC_d = din("sC", [L2, NSQ, 4, 128, 128])
    sn_d = din("sn", [L2, NSQ, 512])
    sm_d = din("sm", [L2, 8, NSQ])
    sS_d = din("sS", [L2, NSQ, 8, 64, 128])
    sscv_d = din("sscv", [L2, NSQ * 3, 1024])
    sccv_d = din("sccv", [L2, NSQ * 2, 512])
    srh_d = din("srh", [L2, NSQ, 512])
    srcv_d = din("srcv", [L2, NSQ * 3, 512])
    wpk_d = din("wpk", [L2, 28, 128, 4096])
    wsm_d = din("wsm", [L2, 128, 512])
    wopk_d = din("wopk", [L2, 4, 8, 128, 1024])
    wabd_d = din("wabd", [L2, 128, 1024])
    cvec_d = din("cvec", [L2, 128, 84])
    g8_d = din("g8", [L2, 8, 8])
    normw_d = din("norm_w", [L2, D])
    mlnw_d = din("ml_norm_w", [L2, BR])
    ssdnw_d = din("ssd_norm_w", [L2, BR])
    ssdD_d = din("ssd_D", [L2, 8])
    fnw_d = din("final_norm_w", [D])
    c128_d = din("c128", list(C128h.shape))
    c8_d = din("c8", list(C8h.shape))

    yp_o = dout("yp", [TP, D])
    ys_o = dout("ys", [LS, D])
    pC_o = dout("pC", [L2, 4, 128, 128])
    pn_o = dout("pn", [L2, 4, 128])
    pm_o = dout("pm", [L2, 4])
    pS_o = dout("pS", [L2, 512, 128])
    pscv_o = dout("pscv", [L2, 3, 1024])
    pccv_o = dout("pccv", [L2, 2, 512])
    prh_o = dout("prh", [L2, 4, 128])
    prcv_o = dout("prcv", [L2, 3, 512])
    oC_o = dout("oC", [L2, NSQ, 4, 128, 128])
    on_o = dout("on", [L2, NSQ, 512])
    om_o = dout("om", [L2, 4, NSQ])
    oS_o = dout("oS", [L2, NSQ, 8, 64, 128])
    oscv_o = dout("oscv", [L2, NSQ * 3, 1024])
    occv_o = dout("occv", [L2, NSQ * 2, 512])
    orh_o = dout("orh", [L2, NSQ, 512])
    orcv_o = dout("orcv", [L2, NSQ * 3, 512])

    R = Rec()
    st = ExitStack()
    st.enter_context(nc.allow_non_contiguous_dma(reason="small strided state/constant transfers"))

    def sb(name, shape, dt=F32):
        return st.enter_context(nc.sbuf_tensor("sb_" + name, list(shape), dt))

    def psum(name):
        return st.enter_context(nc.psum_tensor(name, [128, 512], F32))

    def TT(eng, out, in0, in1, op, r, w):
        R.add(eng, lambda e: e.tensor_tensor(out=out, in0=in0, in1=in1, op=op), r, w)

    def TS(eng, out, in0, s1, s2, op0, op1, r, w):
        if op1 is None:
            R.add(eng, lambda e: e.tensor_scalar(out=out, in0=in0, scalar1=s1, scalar2=None, op0=op0), r, w)
        else:
            R.add(eng, lambda e: e.tensor_scalar(out=out, in0=in0, scalar1=s1, scalar2=s2, op0=op0, op1=op1), r, w)

    def STT(eng, out, in0, scalar, in1, op0, op1, r, w):
        R.add(eng, lambda e: e.scalar_tensor_tensor(out=out, in0=in0, scalar=scalar, in1=in1, op0=op0, op1=op1), r, w)

    def CP(eng, out, in_, r, w):
        if eng == "act":
            R.add(eng, lambda e: e.activation(out=out, in_=in_, func=AF.Copy), r, w)
        else:
            R.add(eng, lambda e: e.tensor_copy(out=out, in_=in_), r, w)

    def ACT(out, in_, func, r, w, bias=None, scale=1.0, accum=None):
        kw = {}
        if bias is not None:
            kw["bias"] = bias
        if accum is not None:
            kw["accum_out"] = accum
        R.add("act", lambda e: e.activation(out=out, in_=in_, func=func, scale=scale, **kw), r, w)

    def MM(out, lhsT, rhs, start, stop, r, w):
        R.add("pe", lambda e: e.matmul(out, lhsT=lhsT, rhs=rhs, start=start, stop=stop), r, w)

    def TR(out, in_, ident, r, w):
        R.add("pe", lambda e: e.transpose(out=out, in_=in_, identity=ident), r, w)

    def DMA(out, in_, r, w, q="sp"):
        R.add(q, lambda e: e.dma_start(out=out, in_=in_), r, w, dma=True)

    def SCAN(out, d0, d1, init, op0, op1, r, w):
        R.add("dve", lambda e: e.tensor_tensor_scan(out=out, data0=d0, data1=d1, initial=init, op0=op0, op1=op1), r, w)

    def MEMSET(eng, ap, val, w):
        R.add(eng, lambda e: e.memset(ap, val), (), w)

    def RSUM(out, in_, r, w):
        R.add("dve", lambda e: e.reduce_sum(out=out, in_=in_, axis=mybir.AxisListType.X), r, w)

    def RECIP(out, in_, r, w):
        R.add("dve", lambda e: e.reciprocal(out=out, in_=in_), r, w)

    def sigmoid_from_exp(t, r, w):
        ACT(t, t, AF.Ln, r, w, bias=1.0)
        ACT(t, t, AF.Exp, r, w, scale=-1.0)

    c128 = sb("c128", [128, 128 + NSQ])
    c8 = sb("c8", C8h.shape)
    identf = c128[:, 0:128]
    smT = c128[:, 128:128 + NSQ]

    def k8(name):
        o, n = o8[name]
        return c8[:, o:o + n]

    identb = sb("identb", [128, 128], BF16)
    maskPb = sb("maskPb", [128, 512], BF16)
    maskSb = sb("maskSb", [128, 4 * LS], BF16)
    smkb = sb("smkb", [128, NSQ * LS], BF16)
    pm8 = k8("pm")
    opm8 = k8("opm")
    xres = sb("xres", [128, NTM, D])
    xnT = sb("xnT", [128, 16, NMAX], BF16)
    st4 = sb("st4", [128, 8])
    tm4 = sb("tm4", [128, 16])
    Dbc = sb("Dbc", [128, 8])
    cvec = sb("cvec", [128, 84])
    g8 = sb("g8", [8, 8])
    g8n = sb("g8n", [8, 8])
    nsp8 = sb("nsp8", [128, 4])
    nba = sb("nba", [128, 8])
    NWB = 3
    wb = [sb("wb%d" % i, [128, 16, 256], BF16) for i in range(NWB)]
    wsmb = sb("wsmb", [128, 16, 32], BF16)
    wabd = sb("wabd", [128, 8, 128], BF16)
    PS = {k: psum(k) for k in ("psA", "psB", "psS", "psE", "psV", "psN0", "psN1", "psU")}
    Caug = sb("Caug", [128, L2, 4, 129])
    Cbf = sb("Cbf", [128, 4, 129], BF16)
    mcar = sb("mcar", [8, L2])
    STs = sb("STs", [128, L2, 512])
    STb = sb("STb", [128, 512], BF16)
    cvcar = sb("cvcar", [128, L2, 16, 3])
    hcar = sb("hcar", [128, L2, 4])
    ubp = sb("ubp", [128, 3 + NP])
    ubs = sb("ubs", [128, NSQ, 7])
    gsm = sb("gsm", [8, 64])
    RM = sb("RM", [8, 512])
    RMv = sb("RMv", [8, 512])
    scal = sb("scal", [128, NTM, 2, 8])
    bcw = sb("bcw", [128, 2, 4, 32])
    ncol = sb("ncol", [128, NSQ])
    cvtmp = sb("cvtmp", [128, NSQ * 3])
    h0s = sb("h0s", [128, NSQ])
    small_o = sb("small_o", [8, 768])
    AFW = 14480
    ABW = 17192
    arenaF = sb("arenaF", [128, AFW])
    arenaB = sb("arenaB", [128, ABW], BF16)
    aoff = [0, 0]
    amax = [0, 0]
    import os as _os
    print('SBUF remaining after persistent', nc.sbuf_bytes_remaining, flush=True)

    def stage_reset():
        R.fence()
        aoff[0] = 0
        aoff[1] = 0

    def carve(arena, k, cap, shape, pat):
        n = 1
        for d_ in shape[1:]:
            n *= d_
        o = aoff[k]
        aoff[k] += n
        amax[k] = max(amax[k], aoff[k])
        if _os.environ.get("KDRY"):
            o = 0
        else:
            assert aoff[k] <= cap, ("arena overflow", k, aoff[k], cap)
        v = arena[:shape[0], o:o + n]
        if len(shape) == 3:
            v = v.rearrange("p (a b) -> p a b", a=shape[1])
        elif len(shape) == 4:
            v = v.rearrange("p (a b c) -> p a b c", a=shape[1], b=shape[2])
        return v

    def af(shape):
        return carve(arenaF, 0, AFW, shape, None)

    def ab(shape):
        return carve(arenaB, 1, ABW, shape, None)

    DMA(c128[:, 0:128], c128_d[:, 0:128], (), ["c128"])
    o_, n_ = o128["smT"]
    DMA(c128[:, 128:128 + NSQ], c128_d[:, o_:o_ + n_], (), ["c128"])
    DMA(c8[:], c8_d, (), ["c8"])
    DMA(identb[:], c128_d[:, 0:128], (), ["identb"], q="pool")
    o_, n_ = o128["maskP"]
    DMA(maskPb[:], c128_d[:, o_:o_ + n_], (), ["maskPb"], q="pool")
    o_, n_ = o128["maskS"]
    DMA(maskSb[:], c128_d[:, o_:o_ + n_], (), ["maskSb"], q="pool")
    o_, n_ = o128["smk"]
    DMA(smkb[:], c128_d[:, o_:o_ + n_], (), ["smkb"], q="pool")
    MEMSET("pool", xnT[:], 0.0, ["xnT"])
    MEMSET("pool", Caug[:], 0.0, ["Caug"])
    MEMSET("pool", mcar[:], -1e30, ["mcar"])
    MEMSET("pool", STs[:], 0.0, ["STs"])
    MEMSET("pool", cvcar[:], 0.0, ["cvcar"])
    MEMSET("pool", hcar[:], 0.0, ["hcar"])
    MEMSET("pool", ubs[:], 0.0, ["ubs"])
    MEMSET("pool", ubp[:], 0.0, ["ubp"])
    MEMSET("pool", arenaF[:], 0.0, ["arenaF"])
    MEMSET("pool", arenaB[:], 0.0, ["arenaB"])
    stage_reset()

    wctr = [0]

    def load_w(src2d, nct=16):
        i = wctr[0] % NWB
        wctr[0] += 1
        DMA(wb[i][:, 0:nct, :].rearrange("p a b -> p (a b)"), src2d, (), ["wb%d" % i], q="pool")
        return wb[i], "wb%d" % i

    pctr = [0]

    def next_ps():
        k = ("psA", "psB", "psN1")[pctr[0] % 3]
        pctr[0] += 1
        return PS[k], k

    def next_ps2():
        k = ("psA", "psB")[pctr[0] % 2]
        pctr[0] += 1
        return PS[k], k

    import os as _os
    KSTOP = int(_os.environ.get("KSTOP", "99"))
    kst = [0]

    def checkpoint():
        kst[0] += 1
        if kst[0] > KSTOP:
            raise _Stop()

    try:
      for pi in range(NPASS):
          tiles = [dict(kind="p", L=128, col=128 * i, ti=i, tok=pi * NP + 128 * i) for i in range(NCH)]
          last_pass = pi == NPASS - 1
          if last_pass:
              tiles.append(dict(kind="s", L=LS, col=NP, ti=NCH, tok=0))
          NT = len(tiles)
          N = (NP + LS) if last_pass else NP
          groups = []
          c0 = 0
          while c0 < N:
              groups.append((c0, min(512, N - c0)))
              c0 += 512
          has_s = last_pass

          for t in tiles:
              src = xp_d[t["tok"]:t["tok"] + 128, :] if t["kind"] == "p" else xs_d
              DMA(xres[:t["L"], t["ti"], :], src, (), ["xres%d" % t["ti"]])

          for ly in range(L2):
              DMA(Dbc[:], ssdD_d[ly].partition_broadcast(128), (), ["Dbc"])
              DMA(cvec[:], cvec_d[ly], (), ["cvec"])
              DMA(g8[:], g8_d[ly], (), ["g8"])
              DMA(wsmb[:, :, :].rearrange("p a b -> p (a b)"), wsm_d[ly], (), ["wsmb"], q="pool")
              DMA(wabd[:, :, :].rearrange("p a b -> p (a b)"), wabd_d[ly], (), ["wabd"], q="pool")
              TS("dve", g8n[:, 0:2], g8[:, 0:2], -1.0, None, ALU.mult, None, ["g8"], ["g8n"])
              ACT(g8n[:, 4:6], g8[:, 4:6], AF.Exp, ["g8"], ["g8n"])
              TS("dve", g8n[:, 4:6], g8n[:, 4:6], -1.0, None, ALU.mult, None, ["g8n"], ["g8n"])
              ACT(nsp8[:], cvec[:, 80:84], AF.Exp, ["cvec"], ["nsp8"], scale=-1.0)
              ACT(nsp8[:], nsp8[:], AF.Ln, ["nsp8"], ["nsp8"], bias=1.0)
              TS("dve", nsp8[:], nsp8[:], -8.0, None, ALU.mult, None, ["nsp8"], ["nsp8"])
              TS("dve", nba[:], cvec[:, 72:80], -1.0, None, ALU.mult, None, ["cvec"], ["nba"])

              def rmsnorm_tile(t, wsrc_d, xn32, junk, nwbc):
                  L, ti = t["L"], t["ti"]
                  xr_id = "xres%d" % ti
                  ACT(junk[:L], xres[:L, ti, :], AF.Square, [xr_id], ["junk", "st4"], accum=st4[:L, 0:1])
                  ACT(st4[:L, 1:2], st4[:L, 0:1], AF.Ln, ["st4"], ["st4"], bias=EPS, scale=1.0 / D)
                  ACT(st4[:L, 2:3], st4[:L, 1:2], AF.Exp, ["st4"], ["st4"], scale=-0.5)
                  STT("dve", xn32[:L], xres[:L, ti, :], st4[:L, 2:3], nwbc[:L], ALU.mult, ALU.mult,
                      [xr_id, "st4", "nwbc"], ["xn32"])

              stage_reset()
              xn32 = af([128, D])
              nwbc = af([128, D])
              junk = ab([128, D])
              DMA(nwbc, normw_d[ly].partition_broadcast(128), (), ["nwbc"])
              for t in tiles:
                  L, ti, col = t["L"], t["ti"], t["col"]
                  rmsnorm_tile(t, None, xn32, junk, nwbc)
                  for q4 in range(4):
                      for j in range(4):
                          dt_ = q4 * 4 + j
                          TR(PS["psU"][:, j * 128:j * 128 + L], xn32[:L, dt_ * 128:(dt_ + 1) * 128], identf[:L, :L],
                             ["xn32", "c128"], ["psU"])
                      CP("act" if q4 % 2 else "dve", xnT[:, q4 * 4:q4 * 4 + 4, col:col + L],
                         PS["psU"][:, :].rearrange("p (j c) -> p j c", j=4)[:, :, :L], ["psU"], ["xnT"])

              checkpoint()
              def proj_fm(wt, wid, wo, M, evac):
                  for (g0, n) in groups:
                      ps, pid = next_ps()
                      for dt_ in range(16):
                          MM(ps[:M, :n], wt[:, dt_, wo:wo + M], xnT[:, dt_, g0:g0 + n], dt_ == 0, dt_ == 15,
                             [wid, "xnT"], [pid])
                      evac(ps, pid, g0, n)

              def proj_tm(wt, wid, wo, n, t, evac):
                  ps, pid = next_ps()
                  L, col = t["L"], t["col"]
                  for dt_ in range(16):
                      MM(ps[:L, :n], xnT[:, dt_, col:col + L], wt[:, dt_, wo:wo + n], dt_ == 0, dt_ == 15,
                         [wid, "xnT"], [pid])
                  evac(ps, pid)

              def wout_part(mixs, ct0):
                  for dg in range(8):
                      wt1, w1 = load_w(wopk_d[ly, ct0 // 4, dg], nct=4)
                      for t in tiles:
                          L, ti, col = t["L"], t["ti"], t["col"]
                          ps, pid = next_ps()
                          for ct in range(4):
                              MM(ps[:L, :256], mixs[:, ct, col:col + L], wt1[:, ct, :], ct == 0, ct == 3, ["mixs", w1], [pid])
                          TT("dve", xres[:L, ti, dg * 256:(dg + 1) * 256], xres[:L, ti, dg * 256:(dg + 1) * 256],
                             ps[:L, :256], ALU.add, ["xres%d" % ti, pid], ["xres%d" % ti])

              def conv_load_hist(stream):
                  CP("dve", ubp[:, 0:3], cvcar[:, ly, stream, :], ["cvcar"], ["ubp"])

              def conv_save_hist(stream):
                  CP("dve", cvcar[:, ly, stream, :], ubp[:, NP:NP + 3], ["ubp"], ["cvcar"])

              def conv_apply(out_fm, oid, W, wcol, bcol):
                  s0 = 3 - (W - 1)
                  views = [(out_fm[:, 0:NP], lambda j: ubp[:, s0 + j:s0 + j + NP])]
                  if has_s:
                      views.append((out_fm[:, NP:NP + LS].rearrange("p (b t) -> p b t", t=4),
                                    lambda j: ubs[:, :, s0 + j:s0 + j + 4]))
                  for (o, src) in views:
                      if bcol is None:
                          TS("dve", o, src(0), cvec[:, wcol:wcol + 1], None, ALU.mult, None,
                             ["ubp", "ubs", "cvec"], [oid])
                      else:
                          TS("dve", o, src(0), cvec[:, wcol:wcol + 1], cvec[:, bcol:bcol + 1], ALU.mult, ALU.add,
                             ["ubp", "ubs", "cvec"], [oid])
                      for j in range(1, W):
                          STT("dve", o, src(j), cvec[:, wcol + j:wcol + j + 1], o, ALU.mult, ALU.add,
                              ["ubp", "ubs", "cvec", oid], [oid])

              def sample_hist_tile(cvin, W, ctile, src_d):
                  rows = NSQ * (W - 1)
                  DMA(cvin[:rows, :], src_d[:, ctile * 128:(ctile + 1) * 128], (), ["cvin"])
                  TR(PS["psU"][:, :rows], cvin[:rows, :], identf[:rows, :rows],
                     ["cvin", "c128"], ["psU"])
                  CP("dve", ubs[:, :, 3 - (W - 1):3], PS["psU"][:, :rows].rearrange("p (b r) -> p b r", r=W - 1),
                     ["psU"], ["ubs"])

              def sample_hist_out_tile(cvout, W, ctile, dst_d):
                  rows = NSQ * (W - 1)
                  CP("dve", cvtmp[:, :rows].rearrange("p (b r) -> p b r", r=W - 1), ubs[:, :, 7 - (W - 1):7],
                     ["ubs"], ["cvtmp"])
                  TR(PS["psU"][:rows, :128], cvtmp[:, :rows], identf, ["cvtmp", "c128"], ["psU"])
                  CP("dve", cvout[:rows, :], PS["psU"][:rows, :128], ["psU"], ["cvout"])
                  DMA(dst_d[:, ctile * 128:(ctile + 1) * 128], cvout[:rows, :], ["cvout"], ())

              def prompt_hist_out(W, streams, dst):
                  for c0 in range(0, len(streams), 4):
                      sub = streams[c0:c0 + 4]
                      for i, s_ in enumerate(sub):
                          CP("dve", cvtmp[:, 0:W - 1], cvcar[:, ly, s_, 3 - (W - 1):3], ["cvcar"], ["cvtmp"])
                          TR(PS["psU"][:W - 1, :128], cvtmp[:, 0:W - 1], identf, ["cvtmp", "c128"], ["psU"])
                          CP("dve", small_o[:W - 1, i * 128:(i + 1) * 128], PS["psU"][:W - 1, :128], ["psU"], ["small_o"])
                      DMA(dst[:, c0 * 128:(c0 + len(sub)) * 128], small_o[:W - 1, :128 * len(sub)], ["small_o"], ())

              def fill_ub(ps, pid, g0, n, eng="act"):
                  pe_ = min(g0 + n, NP)
                  if pe_ > g0:
                      CP(eng, ubp[:, 3 + g0:3 + pe_], ps[:, 0:pe_ - g0], [pid], ["ubp"])
                  if has_s and g0 + n > NP:
                      a0 = max(g0, NP)
                      CP(eng, ubs[:, :, 3:7], ps[:, a0 - g0:a0 - g0 + LS].rearrange("p (b t) -> p b t", t=4),
                         [pid], ["ubs"])

              stage_reset()
              fa = [af([128, NMAX]) for _ in range(6)]
              cvin = af([NSQ * 3, 128])
              cvout = af([NSQ * 3, 128])
              htm = af([NSQ, 512])
              htmo = af([NSQ, 512])
              mixs = ab([128, 4, NMAX])
              xrb = ab([128, NMAX])
              for j in range(4):
                  wt1, w1 = load_w(wpk_d[ly, 2 * j])
                  wt2, w2 = load_w(wpk_d[ly, 2 * j + 1])
                  conv_load_hist(j)
                  if has_s:
                      sample_hist_tile(cvin, 3, j, sccv_d[ly])
                  proj_fm(wt1, w1, 0, 128, lambda ps, pid, g0, n: CP("act", fa[0][:, g0:g0 + n], ps[:, :n], [pid], ["fa0"]))
                  proj_fm(wt1, w1, 128, 128, lambda ps, pid, g0, n: CP("act", fa[1][:, g0:g0 + n], ps[:, :n], [pid], ["fa1"]))

                  def ev_h(ps, pid, g0, n):
                      TT("dve", fa[1][:, g0:g0 + n], fa[1][:, g0:g0 + n], ps[:, :n], ALU.mult, ["fa1", pid], ["fa1"])
                  proj_fm(wt2, w2, 0, 128, ev_h)

                  def ev_z(ps, pid, g0, n):
                      CP("act", fa[2][:, g0:g0 + n], ps[:, :n], [pid], ["fa2"])
                      ACT(fa[3][:, g0:g0 + n], fa[2][:, g0:g0 + n], AF.Exp, ["fa2"], ["fa3"], scale=-1.0)
                  proj_fm(wt2, w2, 128, 128, ev_z)
                  CP("dve", ubp[:, 3:3 + NP], fa[1][:, 0:NP], ["fa1"], ["ubp"])
                  if has_s:
                      CP("dve", ubs[:, :, 3:7], fa[1][:, NP:NP + LS].rearrange("p (b t) -> p b t", t=4), ["fa1"], ["ubs"])
                  conv_apply(fa[4], "fa4", 3, 40 + 3 * j, None)
                  conv_save_hist(j)
                  if has_s:
                      sample_hist_out_tile(cvout, 3, j, occv_o[ly])
                  sigmoid_from_exp(fa[3][:, :N], ["fa3"], ["fa3"])
                  TT("dve", fa[2][:, :N], fa[2][:, :N], fa[3][:, :N], ALU.mult, ["fa2", "fa3"], ["fa2"])
                  TT("dve", fa[2][:, :N], fa[2][:, :N], fa[0][:, :N], ALU.mult, ["fa2", "fa0"], ["fa2"])
                  TT("dve", mixs[:, j, :N], fa[2][:, :N], fa[4][:, :N], ALU.mult, ["fa2", "fa4"], ["mixs"])
              if last_pass:
                  prompt_hist_out(3, [0, 1, 2, 3], pccv_o[ly])
              checkpoint()
              wout_part(mixs, 8)
              checkpoint()

              if has_s:
                  DMA(htm[:], srh_d[ly], (), ["htm"])
              for j in range(4):
                  wt1, w1 = load_w(wpk_d[ly, 8 + j])
                  conv_load_hist(4 + j)
                  if has_s:
                      sample_hist_tile(cvin, 4, j, srcv_d[ly])
                      TR(PS["psU"][:, :NSQ], htm[:NSQ, j * 128:(j + 1) * 128], identf[:NSQ, :NSQ], ["htm", "c128"], ["psU"])
                      CP("dve", h0s[:], PS["psU"][:, :NSQ], ["psU"], ["h0s"])
                  proj_fm(wt1, w1, 0, 128, lambda ps, pid, g0, n: fill_ub(ps, pid, g0, n))

                  def ev_rz(ps, pid, g0, n):
                      CP("act", fa[2][:, g0:g0 + n], ps[:, :n], [pid], ["fa2"])
                      ACT(fa[3][:, g0:g0 + n], fa[2][:, g0:g0 + n], AF.Exp, ["fa2"], ["fa3"], scale=-1.0)
                  proj_fm(wt1, w1, 128, 128, ev_rz)
                  conv_apply(fa[0], "fa0", 4, 52 + 4 * j, 68 + j)
                  conv_save_hist(4 + j)
                  if has_s:
                      sample_hist_out_tile(cvout, 4, j, orcv_o[ly])
                  CP("act", xrb[:, :N], fa[0][:, :N], ["fa0"], ["xrb"])
                  for (g0, n) in groups:
                      ps, pid = next_ps()
                      MM(ps[:, :n], wabd[:, j, :], xrb[:, g0:g0 + n], True, True, ["wabd", "xrb"], [pid])
                      ACT(fa[1][:, g0:g0 + n], ps[:, :n], AF.Exp, [pid, "nba"], ["fa1"], bias=nba[:, j:j + 1], scale=-1.0)
                      ps, pid = next_ps()
                      MM(ps[:, :n], wabd[:, 4 + j, :], xrb[:, g0:g0 + n], True, True, ["wabd", "xrb"], [pid])
                      ACT(fa[4][:, g0:g0 + n], ps[:, :n], AF.Exp, [pid, "nba"], ["fa4"], bias=nba[:, 4 + j:5 + j], scale=-1.0)
                  sigmoid_from_exp(fa[1][:, :N], ["fa1"], ["fa1"])
                  sigmoid_from_exp(fa[4][:, :N], ["fa4"], ["fa4"])
                  ACT(fa[1][:, :N], fa[1][:, :N], AF.Exp, ["fa1", "nsp8"], ["fa1"], scale=nsp8[:, j:j + 1])
                  TT("dve", fa[5][:, :N], fa[1][:, :N], fa[1][:, :N], ALU.mult, ["fa1"], ["fa5"])
                  TS("dve", fa[5][:, :N], fa[5][:, :N], -1.0, 1.0, ALU.mult, ALU.add, ["fa5"], ["fa5"])
                  TS("dve", fa[5][:, :N], fa[5][:, :N], 1e-30, None, ALU.max, None, ["fa5"], ["fa5"])
                  ACT(fa[5][:, :N], fa[5][:, :N], AF.Ln, ["fa5"], ["fa5"])
                  ACT(fa[5][:, :N], fa[5][:, :N], AF.Exp, ["fa5"], ["fa5"], scale=0.5)
                  TT("dve", fa[4][:, :N], fa[4][:, :N], fa[0][:, :N], ALU.mult, ["fa4", "fa0"], ["fa4"])
                  TT("dve", fa[4][:, :N], fa[4][:, :N], fa[5][:, :N], ALU.mult, ["fa4", "fa5"], ["fa4"])
                  STT("dve", fa[4][:, 0:1], fa[1][:, 0:1], hcar[:, ly, j:j + 1], fa[4][:, 0:1], ALU.mult, ALU.add,
                      ["fa1", "fa4", "hcar"], ["fa4"])
                  MEMSET("dve", fa[1][:, 0:1], 0.0, ["fa1"])
                  SCAN(fa[0][:, 0:NP], fa[1][:, 0:NP], fa[4][:, 0:NP], 0.0, ALU.mult, ALU.add, ["fa1", "fa4"], ["fa0"])
                  CP("dve", hcar[:, ly, j:j + 1], fa[0][:, NP - 1:NP], ["fa0"], ["hcar"])
                  if has_s:
                      a0v = fa[1][:, NP:NP + LS].rearrange("p (b t) -> p b t", t=4)[:, :, 0]
                      b0v = fa[4][:, NP:NP + LS].rearrange("p (b t) -> p b t", t=4)[:, :, 0]
                      TT("dve", tm4[:, :NSQ], a0v, h0s[:], ALU.mult, ["fa1", "h0s"], ["tm4"])
                      TT("dve", b0v, b0v, tm4[:, :NSQ], ALU.add, ["fa4", "tm4"], ["fa4"])
                      MEMSET("dve", a0v, 0.0, ["fa1"])
                      SCAN(fa[0][:, NP:NP + LS], fa[1][:, NP:NP + LS], fa[4][:, NP:NP + LS], 0.0, ALU.mult, ALU.add,
                           ["fa1", "fa4"], ["fa0"])
                      CP("dve", ncol[:, :NSQ], fa[0][:, NP:NP + LS].rearrange("p (b t) -> p b t", t=4)[:, :, 3], ["fa0"], ["ncol"])
                      TR(PS["psU"][:NSQ, :128], ncol[:, :NSQ], identf, ["ncol", "c128"], ["psU"])
                      CP("dve", htmo[:NSQ, j * 128:(j + 1) * 128], PS["psU"][:NSQ, :128], ["psU"], ["htmo"])
                  sigmoid_from_exp(fa[3][:, :N], ["fa3"], ["fa3"])
                  TT("dve", fa[2][:, :N], fa[2][:, :N], fa[3][:, :N], ALU.mult, ["fa2", "fa3"], ["fa2"])
                  TT("dve", mixs[:, j, :N], fa[2][:, :N], fa[0][:, :N], ALU.mult, ["fa2", "fa0"], ["mixs"])
              if has_s:
                  DMA(orh_o[ly], htmo[:], ["htmo"], ())
              if last_pass:
                  prompt_hist_out(4, [4, 5, 6, 7], prcv_o[ly])
                  for j in range(4):
                      TR(PS["psU"][:1, j * 128:(j + 1) * 128], hcar[:, ly, j:j + 1], identf, ["hcar", "c128"], ["psU"])
                  CP("dve", small_o[:1, 0:512], PS["psU"][:1, 0:512], ["psU"], ["small_o"])
                  DMA(prh_o[ly].rearrange("(o t) c -> o (t c)", o=1), small_o[:1, 0:512], ["small_o"], ())
              checkpoint()
              wout_part(mixs, 12)
              checkpoint()

              def chunk_ends(row, t_kind):
                  if t_kind == "p":
                      return row[:, 0:NP].rearrange("p (c t) -> p c t", t=128)[:, :, 127]
                  return row[:, NP:NP + LS].rearrange("p (b t) -> p b t", t=4)[:, :, 3]

              def expo(a_row, aid, c_row, cid, cv_row, cvid, t, Wt, Wv):
                  L, col = t["L"], t["col"]
                  mf = k8("maskfull").rearrange("p (h l) -> p h l", h=4)[:, :, :L]
                  RMl = RM[:, :4 * L].rearrange("p (h l) -> p h l", h=4)
                  RMvl = RMv[:, :4 * L].rearrange("p (h l) -> p h l", h=4)
                  TT("dve", RMl, c_row[:, col:col + L].unsqueeze(1).broadcast_to([8, 4, L]), mf, ALU.mult,
                     [cid, "c8"], ["RM"])
                  TT("dve", RMvl, cv_row[:, col:col + L].unsqueeze(1).broadcast_to([8, 4, L]), mf, ALU.mult,
                     [cvid, "c8"], ["RMv"])
                  msk = (maskPb if t["kind"] == "p" else maskSb)
                  MM(PS["psE"][:L, :4 * L], a_row[:, col:col + L], RM[:, :4 * L], True, False, [aid, "RM"], ["psE"])
                  MM(PS["psE"][:L, :4 * L], identb[:L, :L], msk[:L, :4 * L], False, True, ["identb", "maskPb", "maskSb"], ["psE"])
                  MM(PS["psV"][:, :4 * L], k8("LV"), RMv[:, :4 * L], True, True, ["c8", "RMv"], ["psV"])
                  ACT(Wt[:L, :4 * L], PS["psE"][:L, :4 * L], AF.Exp, ["psE"], ["Wt"])
                  ACT(Wv[:, :4 * L], PS["psV"][:, :4 * L], AF.Exp, ["psV"], ["Wv"])

              def bcast_rows(src_row, sid, ncols, slot):
                  w8 = gsm[:, :4 * ncols].rearrange("p (h c) -> p h c", h=4)
                  TT("dve", w8, src_row.unsqueeze(1).broadcast_to([8, 4, ncols]),
                     k8("mask8h").unsqueeze(2).broadcast_to([8, 4, ncols]), ALU.mult, [sid, "c8"], ["gsm"])
                  MM(PS["psU"][:, :4 * ncols], k8("ones8"), gsm[:, :4 * ncols], True, True, ["c8", "gsm"], ["psU"])
                  CP("dve", bcw[:, slot, :, :ncols], PS["psU"][:, :4 * ncols].rearrange("p (h c) -> p h c", h=4),
                     ["psU"], ["bcw"])

              def to_tok_major(rowA, aid, rowB, bid, t, slot, scr, scrid):
                  L, col, ti = t["L"], t["col"], t["ti"]
                  TS("dve", scr[:, :L], rowA[:, col:col + L], pm8, None, ALU.mult, None, [aid, "c8"], [scrid])
                  STT("dve", scr[:, :L], rowB[:, col:col + L], opm8, scr[:, :L], ALU.mult, ALU.add,
                      [bid, "c8", scrid], [scrid])
                  TR(PS["psU"][:L, :8], scr[:, :L], identf[:8, :8], [scrid, "c128"], ["psU"])
                  CP("dve", scal[:L, ti, slot, :], PS["psU"][:L, :8], ["psU"], ["scal"])

              stage_reset()
              Gml = af([128, NTM, 512])
              gr = [af([8, NMAX]) for _ in range(10)]
              mlnwbc = af([128, BR])
              DMA(mlnwbc, mlnw_d[ly].partition_broadcast(128), (), ["mlnwbc"])
              Wt = af([128, 512])
              Wv = af([128, 512])
              hn = af([128, 512])
              ytm = af([128, 512])
              Cs = af([128, NSQ, 129])
              ntm = af([NSQ, 512])
              ntmo = af([NSQ, 512])
              qT = ab([128, 4, NMAX])
              kT = ab([128, 4, NMAX])
              ktm = ab([128, NTM, 512])
              vaug = ab([128, NTM, 4, 129])
              SwT = ab([128, 512])
              qv = ab([128, 512])
              kw = ab([128, 512])
              qvm = ab([128, NSQ, LS])
              Csb = ab([128, NSQ, 129])
              vm = ab([128, 3, 129])
              mixs = ab([128, 4, NMAX])
              junk = ab([128, 128])
              hsq = af([128, 512])
              MEMSET("pool", vaug, 1.0, ["vaug"])
              CP("act", Cbf[:], Caug[:, ly], ["Caug"], ["Cbf"])
              for cg in range(4):
                  wt1, w1 = load_w(wpk_d[ly, 12 + cg])
                  for t in tiles:
                      L, ti = t["L"], t["ti"]

                      def ev_g(ps, pid, L=L, ti=ti, cg=cg):
                          CP("act", ytm[:L, 0:256], ps[:L, 0:256], [pid], ["ytm"])
                          ACT(hn[:L, 0:256], ytm[:L, 0:256], AF.Exp, ["ytm"], ["hn"], scale=-1.0)
                          ACT(hn[:L, 0:256], hn[:L, 0:256], AF.Ln, ["hn"], ["hn"], bias=1.0)
                          TT("dve", hn[:L, 0:128], hn[:L, 0:128], hn[:L, 128:256], ALU.add, ["hn"], ["hn"])
                          ACT(hn[:L, 0:128], hn[:L, 0:128], AF.Exp, ["hn"], ["hn"], scale=-1.0)
                          TT("dve", hn[:L, 0:128], hn[:L, 0:128], ytm[:L, 128:256], ALU.mult, ["hn", "ytm"], ["hn"])
                          TT("dve", Gml[:L, ti, cg * 128:(cg + 1) * 128], hn[:L, 0:128], mlnwbc[:L, cg * 128:(cg + 1) * 128],
                             ALU.mult, ["hn", "mlnwbc"], ["Gml"])
                      proj_tm(wt1, w1, 0, 256, t, ev_g)
              for hp in range(2):
                  wt1, w1 = load_w(wpk_d[ly, 16 + hp])
                  for hh in range(2):
                      h = hp * 2 + hh
                      proj_fm(wt1, w1, hh * 128, 128,
                              lambda ps, pid, g0, n, h=h: CP("act", qT[:, h, g0:g0 + n], ps[:, :n], [pid], ["qT"]))
              for hp in range(2):
                  wt1, w1 = load_w(wpk_d[ly, 18 + hp])
                  for hh in range(2):
                      h = hp * 2 + hh
                      proj_fm(wt1, w1, hh * 128, 128,
                              lambda ps, pid, g0, n, h=h: ACT(kT[:, h, g0:g0 + n], ps[:, :n], AF.Copy, [pid], ["kT"],
                                                             scale=128.0 ** -0.5))
                  for t in tiles:
                      L, ti = t["L"], t["ti"]
                      proj_tm(wt1, w1, 0, 256, t,
                              lambda ps, pid, L=L, ti=ti, hp=hp: ACT(ktm[:L, ti, hp * 256:(hp + 1) * 256], ps[:L, :256],
                                                                    AF.Copy, [pid], ["ktm"], scale=128.0 ** -0.5))
              for hp in range(2):
                  wt1, w1 = load_w(wpk_d[ly, 20 + hp])
                  for t in tiles:
                      L, ti = t["L"], t["ti"]
                      proj_tm(wt1, w1, 0, 256, t,
                              lambda ps, pid, L=L, ti=ti, hp=hp: CP("dve", vaug[:L, ti, 2 * hp:2 * hp + 2, 0:128],
                                                                   ps[:L, :256].rearrange("p (h v) -> p h v", h=2),
                                                                   [pid], ["vaug"]))
              gi, gf, gb, gm, ga, gc, gcv, gwg, genm, gmp = gr
              glf = gf
              proj_fm(wsmb, "wsmb", 0, 8,
                      lambda ps, pid, g0, n: TS("dve", gi[:, g0:g0 + n], ps[:8, :n], g8[:, 0:1], None, ALU.add, None,
                                                [pid, "g8"], ["gr0"]))
              proj_fm(wsmb, "wsmb", 8, 8,
                      lambda ps, pid, g0, n: ACT(gf[:, g0:g0 + n], ps[:8, :n], AF.Exp, [pid, "g8n"], ["gr1"],
                                                 bias=g8n[:, 1:2], scale=-1.0))
              ACT(glf[:, :N], gf[:, :N], AF.Ln, ["gr1", "gr2"], ["gr1", "gr2"], bias=1.0)
              TS("dve", glf[:, :N], glf[:, :N], -1.0, None, ALU.mult, None, ["gr2"], ["gr2"])
              SCAN(gb[:, 0:NP], k8("keepP"), glf[:, 0:NP], 0.0, ALU.mult, ALU.add, ["c8", "gr2"], ["gr3"])
              SCAN(gm[:, 0:NP], glf[:, 0:NP], gi[:, 0:NP], mcar[:, ly:ly + 1], ALU.add, ALU.max, ["gr2", "gr0", "mcar"], ["gr4"])
              CP("dve", gmp[:, 1:NP], gm[:, 0:NP - 1], ["gr4"], ["gr10"])
              CP("dve", gmp[:, 0:1], mcar[:, ly:ly + 1], ["mcar"], ["gr10"])
              CP("dve", mcar[:, ly:ly + 1], gm[:, NP - 1:NP], ["gr4"], ["mcar"])
              m0 = small_o[:8, 0:NSQ]
              if has_s:
                  sl_ = slice(NP, NP + LS)
                  DMA(m0, sm_d[ly], (), ["small_o"])
                  SCAN(gb[:, sl_], k8("keepS"), glf[:, sl_], 0.0, ALU.mult, ALU.add, ["c8", "gr2"], ["gr3"])
                  TT("dve", gmp[:, sl_], glf[:, sl_], k8("keepS"), ALU.mult, ["gr2", "c8"], ["gr10"])
                  TT("dve", gmp[:, sl_], gmp[:, sl_], k8("negS"), ALU.add, ["gr10", "c8"], ["gr10"])
                  CP("dve", gwg[:, sl_], gi[:, sl_], ["gr0"], ["gr8"])
                  lf0 = glf[:, sl_].rearrange("p (b t) -> p b t", t=4)[:, :, 0]
                  i0 = gwg[:, sl_].rearrange("p (b t) -> p b t", t=4)[:, :, 0]
                  TT("dve", small_o[:8, 64:64 + NSQ], lf0, m0, ALU.add, ["gr2", "small_o"], ["small_o"])
                  TT("dve", i0, i0, small_o[:8, 64:64 + NSQ], ALU.max, ["gr8", "small_o"], ["gr8"])
                  SCAN(gm[:, sl_], gmp[:, sl_], gwg[:, sl_], 0.0, ALU.add, ALU.max, ["gr10", "gr8"], ["gr4"])
              TT("dve", gc[:, :N], gb[:, :N], gm[:, :N], ALU.subtract, ["gr3", "gr4"], ["gr6"])
              TT("dve", ga[:, :N], gi[:, :N], gb[:, :N], ALU.subtract, ["gr0", "gr3"], ["gr5"])
              mprev_p = gmp[:, 0:NP].rearrange("p (c t) -> p c t", t=128)[:, :, 0]
              TT("dve", gcv[:, 0:NP].rearrange("p (c t) -> p c t", t=128), gc[:, 0:NP].rearrange("p (c t) -> p c t", t=128),
                 mprev_p.unsqueeze(2).broadcast_to([8, NCH, 128]), ALU.add, ["gr6", "gr10"], ["gr7"])
              if has_s:
                  TT("dve", gcv[:, sl_].rearrange("p (b t) -> p b t", t=4), gc[:, sl_].rearrange("p (b t) -> p b t", t=4),
                     m0.unsqueeze(2).broadcast_to([8, NSQ, 4]), ALU.add, ["gr6", "small_o"], ["gr7"])
              for kind in (("p", "s") if has_s else ("p",)):
                  ncn = NCH if kind == "p" else NSQ
                  T_ = 128 if kind == "p" else 4
                  base = 0 if kind == "p" else NP
                  gcst = gsm[:, 0:ncn]
                  TT("dve", gcst, chunk_ends(gb, kind), chunk_ends(gm, kind), ALU.subtract, ["gr3", "gr4"], ["gsm"])
                  seg = slice(base, base + ncn * T_)
                  TT("dve", gwg[:, seg].rearrange("p (c t) -> p c t", t=T_), ga[:, seg].rearrange("p (c t) -> p c t", t=T_),
                     gcst.unsqueeze(2).broadcast_to([8, ncn, T_]), ALU.add, ["gr5", "gsm"], ["gr8"])
                  mpv = mprev_p if kind == "p" else m0
                  TT("dve", small_o[:8, 128:128 + ncn], gcst, mpv, ALU.add, ["gsm", "gr10", "small_o"], ["small_o"])
                  ACT(small_o[:8, 128:128 + ncn], small_o[:8, 128:128 + ncn], AF.Exp, ["small_o"], ["small_o"])
                  bcast_rows(small_o[:8, 128:128 + ncn], "small_o", ncn, 0 if kind == "p" else 1)
              ACT(gwg[:, :N], gwg[:, :N], AF.Exp, ["gr8"], ["gr8"])
              ACT(genm[:, :N], gm[:, :N], AF.Exp, ["gr4"], ["gr9"], scale=-1.0)
              TS("dve", ga[:, :N], ga[:, :N], pm8, opm8, ALU.mult, ALU.add, ["gr5", "c8"], ["gr5"])
              TS("dve", gc[:, :N], gc[:, :N], opm8, pm8, ALU.mult, ALU.add, ["gr6", "c8"], ["gr6"])
              TS("dve", gcv[:, :N], gcv[:, :N], opm8, pm8, ALU.mult, ALU.add, ["gr7", "c8"], ["gr7"])
              if has_s:
                  DMA(ntm[:], sn_d[ly], (), ["ntm"])
              if last_pass:
                  DMA(pm_o[ly].rearrange("(h o) -> h o", o=1), mcar[0:4, ly:ly + 1], ["mcar"], ())
              if has_s:
                  CP("dve", small_o[:8, 512:512 + NSQ], chunk_ends(gm, "s"), ["gr4"], ["small_o"])
                  DMA(om_o[ly], small_o[0:4, 512:512 + NSQ], ["small_o"], ())

              for t in tiles:
                  L, ti, col, kind = t["L"], t["ti"], t["col"], t["kind"]
                  to_tok_major(gwg, "gr8", genm, "gr9", t, 0, gmp, "gr10")
                  for h in range(4):
                      MM(PS["psS"][:L, h * L:(h + 1) * L], kT[:, h, col:col + L], qT[:, h, col:col + L], True, True,
                         ["kT", "qT"], ["psS"])
                  expo(ga, "gr5", gc, "gr6", gcv, "gr7", t, Wt, Wv)
                  TT("dve", SwT[:L, :4 * L], PS["psS"][:L, :4 * L], Wt[:L, :4 * L], ALU.mult, ["psS", "Wt"], ["SwT"])
                  TT("pool", qv[:, :4 * L].rearrange("p (h l) -> p h l", h=4), qT[:, :, col:col + L],
                     Wv[:, :4 * L].rearrange("p (h l) -> p h l", h=4), ALU.mult, ["qT", "Wv"], ["qv"])
                  TT("pool", kw[:L, :].rearrange("p (h d) -> p h d", h=4), ktm[:L, ti, :].rearrange("p (h d) -> p h d", h=4),
                     scal[:L, ti, 0, 0:4].unsqueeze(2).broadcast_to([L, 4, 128]), ALU.mult, ["ktm", "scal"], ["kw"])
                  for h in range(4):
                      pn = PS["psN0"] if h < 2 else PS["psN1"]
                      pnid = "psN0" if h < 2 else "psN1"
                      o_ = (h % 2) * 129
                      if kind == "s":
                          for b4 in range(0, NSQ, 4):
                              DMA(Cs[:, b4:b4 + 4, 0:128], sC_d[ly][b4:b4 + 4, h].rearrange("b d v -> d b v"), (), ["Cs"])
                          TR(PS["psU"][:, :NSQ], ntm[:NSQ, h * 128:(h + 1) * 128], identf[:NSQ, :NSQ], ["ntm", "c128"], ["psU"])
                          CP("dve", Cs[:, :, 128], PS["psU"][:, :NSQ], ["psU"], ["Cs"])
                          CP("act", Csb, Cs, ["Cs"], ["Csb"])
                          TT("pool", qvm, qv[:, h * L:(h + 1) * L].unsqueeze(1).broadcast_to([128, NSQ, L]),
                             smkb[:, :].rearrange("p (b l) -> p b l", b=NSQ), ALU.mult, ["qv", "smkb"], ["qvm"])
                      MM(pn[:L, o_:o_ + 129], SwT[:L, h * L:(h + 1) * L], vaug[:L, ti, h, :], True, False,
                         ["SwT", "vaug"], [pnid])
                      if kind == "p":
                          MM(pn[:L, o_:o_ + 129], qv[:, h * L:(h + 1) * L], Cbf[:, h, :], False, True, ["qv", "Cbf"], [pnid])
                      else:
                          for b in range(NSQ):
                              MM(pn[:L, o_:o_ + 129], qvm[:, b, :], Csb[:, b, :], False, b == NSQ - 1,
                                 ["qvm", "Csb"], [pnid])
                      kwh = kw[:L, h * 128:(h + 1) * 128]
                      if kind == "p":
                          psu, psuid = next_ps2()
                          MM(psu[:, 0:129], kwh, vaug[:L, ti, h, :], True, True, ["kw", "vaug"], [psuid])
                          STT("dve", Caug[:, ly, h, :], Caug[:, ly, h, :], bcw[:, 0, h, ti:ti + 1], psu[:, 0:129],
                              ALU.mult, ALU.add, ["Caug", "bcw", psuid], ["Caug"])
                      else:
                          b0 = 0
                          while b0 < NSQ:
                              nb = min(3, NSQ - b0)
                              TT("pool", vm[:L, 0:nb, :], vaug[:L, ti, h, :].unsqueeze(1).broadcast_to([L, nb, 129]),
                                 smT[:L, b0:b0 + nb].unsqueeze(2).broadcast_to([L, nb, 129]), ALU.mult, ["vaug", "c128"], ["vm"])
                              MM(PS["psU"][:, 0:nb * 129], kwh, vm[:L, 0:nb, :], True, True, ["kw", "vm"], ["psU"])
                              TT("dve", Cs[:, b0:b0 + nb, :], Cs[:, b0:b0 + nb, :],
                                 bcw[:, 1, h, b0:b0 + nb].unsqueeze(2).broadcast_to([128, nb, 129]), ALU.mult,
                                 ["Cs", "bcw"], ["Cs"])
                              TT("dve", Cs[:, b0:b0 + nb, :], Cs[:, b0:b0 + nb, :],
                                 PS["psU"][:, 0:nb * 129].rearrange("p (b v) -> p b v", b=nb), ALU.add, ["Cs", "psU"], ["Cs"])
                              b0 += nb
                          for b4 in range(0, NSQ, 4):
                              DMA(oC_o[ly][b4:b4 + 4, h].rearrange("b d v -> d b v"), Cs[:, b4:b4 + 4, 0:128], ["Cs"], ())
                          CP("dve", ncol[:, :NSQ], Cs[:, :, 128], ["Cs"], ["ncol"])
                          TR(PS["psU"][:NSQ, :128], ncol[:, :NSQ], identf, ["ncol", "c128"], ["psU"])
                          CP("dve", ntmo[:NSQ, h * 128:(h + 1) * 128], PS["psU"][:NSQ, :128], ["psU"], ["ntmo"])
                  if kind == "p":
                      CP("act", Cbf[:], Caug[:, ly], ["Caug"], ["Cbf"])
                  for half in range(2):
                      pn = PS["psN0"] if half == 0 else PS["psN1"]
                      pnid = "psN0" if half == 0 else "psN1"
                      den = pn[:L, 0:258].rearrange("p (h v) -> p h v", h=2)[:, :, 128]
                      ACT(tm4[:L, 2 * half:2 * half + 2], den, AF.Abs, [pnid], ["tm4"])
                  TT("dve", tm4[:L, 0:4], tm4[:L, 0:4], scal[:L, ti, 0, 4:8], ALU.max, ["tm4", "scal"], ["tm4"])
                  RECIP(tm4[:L, 0:4], tm4[:L, 0:4], ["tm4"], ["tm4"])
                  for half in range(2):
                      pn = PS["psN0"] if half == 0 else PS["psN1"]
                      pnid = "psN0" if half == 0 else "psN1"
                      CP("act", hn[:L, half * 256:(half + 1) * 256].rearrange("p (h v) -> p h v", h=2),
                         pn[:L, 0:258].rearrange("p (h v) -> p h v", h=2)[:, :, 0:128], [pnid], ["hn"])
                  TT("pool", hsq[:L, :], hn[:L, :], hn[:L, :], ALU.mult, ["hn"], ["hsq"])
                  RSUM(tm4[:L, 4:8], hn[:L, :].rearrange("p (h v) -> p h v", h=4), ["hn"], ["tm4"])
                  RSUM(tm4[:L, 8:12], hsq[:L, :].rearrange("p (h v) -> p h v", h=4), ["hsq"], ["tm4"])
                  TS("dve", tm4[:L, 4:8], tm4[:L, 4:8], 1.0 / 128, None, ALU.mult, None, ["tm4"], ["tm4"])
                  TS("dve", tm4[:L, 8:12], tm4[:L, 8:12], 1.0 / 128, None, ALU.mult, None, ["tm4"], ["tm4"])
                  TT("dve", tm4[:L, 12:16], tm4[:L, 4:8], tm4[:L, 4:8], ALU.mult, ["tm4"], ["tm4"])
                  TT("dve", tm4[:L, 8:12], tm4[:L, 8:12], tm4[:L, 12:16], ALU.subtract, ["tm4"], ["tm4"])
                  TT("dve", tm4[:L, 12:16], tm4[:L, 0:4], tm4[:L, 0:4], ALU.mult, ["tm4"], ["tm4"])
                  TT("dve", tm4[:L, 8:12], tm4[:L, 8:12], tm4[:L, 12:16], ALU.mult, ["tm4"], ["tm4"])
                  TS("dve", tm4[:L, 8:12], tm4[:L, 8:12], 0.0, None, ALU.max, None, ["tm4"], ["tm4"])
                  ACT(tm4[:L, 8:12], tm4[:L, 8:12], AF.Ln, ["tm4"], ["tm4"], bias=EPS)
                  ACT(tm4[:L, 8:12], tm4[:L, 8:12], AF.Exp, ["tm4"], ["tm4"], scale=-0.5)
                  TT("dve", tm4[:L, 8:12], tm4[:L, 8:12], tm4[:L, 0:4], ALU.mult, ["tm4"], ["tm4"])
                  hn3 = hn[:L, :].rearrange("p (h v) -> p h v", h=4)
                  TT("dve", hn3, hn3, tm4[:L, 4:8].unsqueeze(2).broadcast_to([L, 4, 128]), ALU.subtract, ["hn", "tm4"], ["hn"])
                  TT("dve", hn3, hn3, tm4[:L, 8:12].unsqueeze(2).broadcast_to([L, 4, 128]), ALU.mult, ["hn", "tm4"], ["hn"])
                  TT("pool", ytm[:L, :], hn[:L, :], Gml[:L, ti, :], ALU.mult, ["hn", "Gml"], ["ytm"])
                  for h in range(4):
                      TR(PS["psU"][:, h * 128:h * 128 + L], ytm[:L, h * 128:(h + 1) * 128], identf[:L, :L], ["ytm", "c128"], ["psU"])
                  CP("act", mixs[:, 0:4, col:col + L], PS["psU"][:, :].rearrange("p (j c) -> p j c", j=4)[:, :, :L],
                     ["psU"], ["mixs"])
              if has_s:
                  DMA(on_o[ly], ntmo[:], ["ntmo"], ())
              if last_pass:
                  DMA(pC_o[ly].rearrange("h d v -> d h v"), Caug[:, ly, :, 0:128], ["Caug"], ())
                  CP("dve", ncol[:, 0:4], Caug[:, ly, :, 128], ["Caug"], ["ncol"])
                  TR(PS["psU"][:4, :128], ncol[:, 0:4], identf, ["ncol", "c128"], ["psU"])
                  CP("dve", small_o[:4, 600:728], PS["psU"][:4, :128], ["psU"], ["small_o"])
                  DMA(pn_o[ly], small_o[:4, 600:728], ["small_o"], ())
              checkpoint()
              wout_part(mixs, 0)
              checkpoint()

              stage_reset()
              fa = [af([128, NMAX]) for _ in range(2)]
              xtm = af([128, NTM, 512])
              gzs = af([128, NTM, 512])
              gr = [af([8, NMAX]) for _ in range(6)]
              Wt = af([128, 512])
              Wv = af([128, 512])
              hn = af([128, 512])
              ytm = af([128, 256])
              Sin = af([64, NSQ, 128])
              cvin = af([NSQ * 3, 128])
              cvout = af([NSQ * 3, 128])
              ssdnwbc = af([128, BR])
              DMA(ssdnwbc, ssdnw_d[ly].partition_broadcast(128), (), ["ssdnwbc"])
              Btm = ab([128, NTM, 2, 128])
              BTb = ab([128, 2, NMAX])
              CTb = ab([128, 2, NMAX])
              Mh = ab([128, 512])
              Ct = ab([128, 512])
              Ctm = ab([128, NSQ, LS])
              xdt = ab([128, 256])
              xw = ab([128, 256])
              Ssb = ab([128, NSQ, 64])
              Bm = ab([128, NSQ, 128])
              mixs = ab([128, 4, NMAX])
              junk = ab([128, 256])
              CP("act", STb[:], STs[:, ly], ["STs"], ["STb"])
              for j in range(8):
                  if j % 2 == 0:
                      wt1, w1 = load_w(wpk_d[ly, 22 + j // 2])
                  conv_load_hist(8 + j)
                  if has_s:
                      sample_hist_tile(cvin, 4, j, sscv_d[ly])
                  proj_fm(wt1, w1, (j % 2) * 128, 128, lambda ps, pid, g0, n: fill_ub(ps, pid, g0, n))
                  conv_apply(fa[0], "fa0", 4, 4 * j, 32 + j)
                  conv_save_hist(8 + j)
                  if has_s:
                      sample_hist_out_tile(cvout, 4, j, oscv_o[ly])
                  ACT(fa[1][:, :N], fa[0][:, :N], AF.Exp, ["fa0"], ["fa1"], scale=-1.0)
                  sigmoid_from_exp(fa[1][:, :N], ["fa1"], ["fa1"])
                  TT("dve", fa[0][:, :N], fa[0][:, :N], fa[1][:, :N], ALU.mult, ["fa0", "fa1"], ["fa0"])
                  if j < 4:
                      for t in tiles:
                          L, ti, col = t["L"], t["ti"], t["col"]
                          TR(PS["psU"][:L, 0:128], fa[0][:, col:col + L], identf, ["fa0", "c128"], ["psU"])
                          CP("act", xtm[:L, ti, j * 128:(j + 1) * 128], PS["psU"][:L, 0:128], ["psU"], ["xtm"])
                  elif j < 6:
                      g = j - 4
                      CP("act", BTb[:, g, :N], fa[0][:, :N], ["fa0"], ["BTb"])
                      for t in tiles:
                          L, ti, col = t["L"], t["ti"], t["col"]
                          TR(PS["psU"][:L, 0:128], fa[0][:, col:col + L], identf, ["fa0", "c128"], ["psU"])
                          CP("act", Btm[:L, ti, g, :], PS["psU"][:L, 0:128], ["psU"], ["Btm"])
                  else:
                      g = j - 6
                      CP("act", CTb[:, g, :N], fa[0][:, :N], ["fa0"], ["CTb"])
              if last_pass:
                  prompt_hist_out(4, list(range(8, 16)), pscv_o[ly])
              for hp in range(2):
                  wt1, w1 = load_w(wpk_d[ly, 26 + hp])
                  for t in tiles:
                      L, ti = t["L"], t["ti"]

                      def ev_sz(ps, pid, L=L, ti=ti, hp=hp):
                          CP("act", ytm[:L, 0:256], ps[:L, 0:256], [pid], ["ytm"])
                          ACT(hn[:L, 0:256], ytm[:L, 0:256], AF.Exp, ["ytm"], ["hn"], scale=-1.0)
                          sigmoid_from_exp(hn[:L, 0:256], ["hn"], ["hn"])
                          TT("dve", gzs[:L, ti, hp * 256:(hp + 1) * 256], hn[:L, 0:256], ytm[:L, 0:256], ALU.mult,
                             ["hn", "ytm"], ["gzs"])
                      proj_tm(wt1, w1, 0, 256, t, ev_sz)
              for g in range(2):
                  gdt, gda, gacs, gwr, grr, gscr = gr
                  gna = gda
                  proj_fm(wsmb, "wsmb", 16 + 8 * g, 8,
                          lambda ps, pid, g0, n, g=g: ACT(gdt[:, g0:g0 + n], ps[:8, :n], AF.Exp, [pid, "g8"], ["gr0"],
                                                          bias=g8[:, 2 + g:3 + g]))
                  ACT(gdt[:, :N], gdt[:, :N], AF.Ln, ["gr0"], ["gr0"], bias=1.0)
                  TS("dve", gda[:, :N], gdt[:, :N], g8n[:, 4 + g:5 + g], None, ALU.mult, None, ["gr0", "g8n"], ["gr1", "gr3"])
                  SCAN(gacs[:, 0:NP], k8("keepP"), gda[:, 0:NP], 0.0, ALU.mult, ALU.add, ["c8", "gr1", "gr3"], ["gr2"])
                  if has_s:
                      SCAN(gacs[:, NP:NP + LS], k8("keepS"), gda[:, NP:NP + LS], 0.0, ALU.mult, ALU.add, ["c8", "gr1"], ["gr2"])
                  for kind in (("p", "s") if has_s else ("p",)):
                      ncn = NCH if kind == "p" else NSQ
                      T_ = 128 if kind == "p" else 4
                      base = 0 if kind == "p" else NP
                      seg = slice(base, base + ncn * T_)
                      al = gsm[:, 0:ncn]
                      CP("dve", al, chunk_ends(gacs, kind), ["gr2"], ["gsm"])
                      TT("dve", gwr[:, seg].rearrange("p (c t) -> p c t", t=T_), al.unsqueeze(2).broadcast_to([8, ncn, T_]),
                         gacs[:, seg].rearrange("p (c t) -> p c t", t=T_), ALU.subtract, ["gsm", "gr2"], ["gr4"])
                      ACT(small_o[:8, 128:128 + ncn], al, AF.Exp, ["gsm"], ["small_o"])
                      bcast_rows(small_o[:8, 128:128 + ncn], "small_o", ncn, 0 if kind == "p" else 1)
                  ACT(gwr[:, :N], gwr[:, :N], AF.Exp, ["gr4"], ["gr4"])
                  TT("dve", gwr[:, :N], gwr[:, :N], gdt[:, :N], ALU.mult, ["gr4", "gr0"], ["gr4"])
                  TS("dve", gna[:, :N], gacs[:, :N], -1.0, None, ALU.mult, None, ["gr2", "gr1"], ["gr3", "gr1"])
                  TS("dve", gna[:, :N], gna[:, :N], pm8, opm8, ALU.mult, ALU.add, ["gr3", "c8"], ["gr3"])
                  TS("dve", grr[:, :N], gacs[:, :N], opm8, pm8, ALU.mult, ALU.add, ["gr2", "c8"], ["gr5"])
                  for t in tiles:
                      L, ti, col, kind = t["L"], t["ti"], t["col"], t["kind"]
                      to_tok_major(gdt, "gr0", gwr, "gr4", t, 1, gscr, "gr6")
                      MM(PS["psS"][:L, :L], BTb[:, g, col:col + L], CTb[:, g, col:col + L], True, True, ["BTb", "CTb"], ["psS"])
                      expo(gna, "gr3", grr, "gr5", grr, "gr5", t, Wt, Wv)
                      TT("dve", Mh[:L, :4 * L].rearrange("p (h l) -> p h l", h=4),
                         PS["psS"][:L, :L].unsqueeze(1).broadcast_to([L, 4, L]),
                         Wt[:L, :4 * L].rearrange("p (h l) -> p h l", h=4), ALU.mult, ["psS", "Wt"], ["Mh"])
                      TT("pool", Ct[:, :4 * L].rearrange("p (h l) -> p h l", h=4),
                         CTb[:, g, col:col + L].unsqueeze(1).broadcast_to([128, 4, L]),
                         Wv[:, :4 * L].rearrange("p (h l) -> p h l", h=4), ALU.mult, ["CTb", "Wv"], ["Ct"])
                      xg = xtm[:L, ti, g * 256:(g + 1) * 256].rearrange("p (h c) -> p h c", h=4)
                      TT("dve", xdt[:L, :].rearrange("p (h c) -> p h c", h=4), xg,
                         scal[:L, ti, 1, 0:4].unsqueeze(2).broadcast_to([L, 4, 64]), ALU.mult, ["xtm", "scal"], ["xdt"])
                      TT("dve", xw[:L, :].rearrange("p (h c) -> p h c", h=4), xg,
                         scal[:L, ti, 1, 4:8].unsqueeze(2).broadcast_to([L, 4, 64]), ALU.mult, ["xtm", "scal"], ["xw"])
                      if kind == "s":
                          TT("pool", Bm[:L, :, :], Btm[:L, ti, g, :].unsqueeze(1).broadcast_to([L, NSQ, 128]),
                             smT[:L, :].unsqueeze(2).broadcast_to([L, NSQ, 128]), ALU.mult, ["Btm", "c128"], ["Bm"])
                      for h in range(4):
                          hh = g * 4 + h
                          if kind == "s":
                              TT("pool", Ctm, Ct[:, h * L:(h + 1) * L].unsqueeze(1).broadcast_to([128, NSQ, L]),
                                 smkb[:, :].rearrange("p (b l) -> p b l", b=NSQ), ALU.mult, ["Ct", "smkb"], ["Ctm"])
                              for b4 in range(0, NSQ, 8):
                                  DMA(Sin[:, b4:b4 + 8, :], sS_d[ly][b4:b4 + 8, hh].rearrange("b p n -> p b n"), (), ["Sin"])
                              for bq in range(NSQ // 8):
                                  for b in range(8):
                                      TR(PS["psU"][:, b * 64:(b + 1) * 64], Sin[:, bq * 8 + b, :], identf[:64, :64],
                                         ["Sin", "c128"], ["psU"])
                                  CP("act", Ssb[:, bq * 8:(bq + 1) * 8, :],
                                     PS["psU"][:, :].rearrange("p (b c) -> p b c", b=8), ["psU"], ["Ssb"])
                          MM(PS["psN0"][:L, h * 64:(h + 1) * 64], Mh[:L, h * L:(h + 1) * L], xdt[:L, h * 64:(h + 1) * 64],
                             True, False, ["Mh", "xdt"], ["psN0"])
                          if kind == "p":
                              MM(PS["psN0"][:L, h * 64:(h + 1) * 64], Ct[:, h * L:(h + 1) * L],
                                 STb[:, hh * 64:(hh + 1) * 64], False, True, ["Ct", "STb"], ["psN0"])
                          else:
                              for b in range(NSQ):
                                  MM(PS["psN0"][:L, h * 64:(h + 1) * 64], Ctm[:, b, :], Ssb[:, b, :],
                                     False, b == NSQ - 1, ["Ctm", "Ssb"], ["psN0"])
                              for bq in range(NSQ // 4):
                                  MM(PS["psV"][:64, :512], xw[:L, h * 64:(h + 1) * 64],
                                     Bm[:L, bq * 4:(bq + 1) * 4, :], True, True, ["xw", "Bm"], ["psV"])
                                  TT("dve", Sin[:, bq * 4:(bq + 1) * 4, :], Sin[:, bq * 4:(bq + 1) * 4, :],
                                     bcw[:64, 1, h, bq * 4:(bq + 1) * 4].unsqueeze(2).broadcast_to([64, 4, 128]), ALU.mult,
                                     ["Sin", "bcw"], ["Sin"])
                                  TT("dve", Sin[:, bq * 4:(bq + 1) * 4, :], Sin[:, bq * 4:(bq + 1) * 4, :],
                                     PS["psV"][:64, :512].rearrange("p (b n) -> p b n", b=4), ALU.add, ["Sin", "psV"], ["Sin"])
                              for b4 in range(0, NSQ, 8):
                                  DMA(oS_o[ly][b4:b4 + 8, hh].rearrange("b p n -> p b n"), Sin[:, b4:b4 + 8, :], ["Sin"], ())
                      TT("dve", hn[:L, 0:256].rearrange("p (h c) -> p h c", h=4), xg,
                         Dbc[:L, g * 4:(g + 1) * 4].unsqueeze(2).broadcast_to([L, 4, 64]), ALU.mult, ["xtm", "Dbc"], ["hn"])
                      TT("dve", hn[:L, 0:256], hn[:L, 0:256], PS["psN0"][:L, 0:256], ALU.add, ["hn", "psN0"], ["hn"])
                      TT("dve", hn[:L, 0:256], hn[:L, 0:256], gzs[:L, ti, g * 256:(g + 1) * 256], ALU.mult, ["hn", "gzs"], ["hn"])
                      ACT(junk[:L, 0:256], hn[:L, 0:256], AF.Square, ["hn"], ["junk", "tm4"], accum=tm4[:L, 0:1])
                      ACT(tm4[:L, 1:2], tm4[:L, 0:1], AF.Ln, ["tm4"], ["tm4"], bias=EPS, scale=1.0 / 256)
                      ACT(tm4[:L, 2:3], tm4[:L, 1:2], AF.Exp, ["tm4"], ["tm4"], scale=-0.5)
                      STT("dve", ytm[:L, 0:256], hn[:L, 0:256], tm4[:L, 2:3], ssdnwbc[:L, g * 256:(g + 1) * 256],
                          ALU.mult, ALU.mult, ["hn", "tm4", "ssdnwbc"], ["ytm"])
                      for h2 in range(2):
                          TR(PS["psU"][:, h2 * 128:h2 * 128 + L], ytm[:L, h2 * 128:(h2 + 1) * 128], identf[:L, :L],
                             ["ytm", "c128"], ["psU"])
                      CP("act", mixs[:, 2 * g:2 * g + 2, col:col + L],
                         PS["psU"][:, 0:256].rearrange("p (j c) -> p j c", j=2)[:, :, :L], ["psU"], ["mixs"])
                      if kind == "p":
                          psu, psuid = next_ps2()
                          MM(psu[:, 0:256], Btm[:L, ti, g, :], xw[:L, :], True, True, ["Btm", "xw"], [psuid])
                          sg = STs[:, ly, g * 256:(g + 1) * 256]
                          TT("pool", sg.rearrange("p (h c) -> p h c", h=4), sg.rearrange("p (h c) -> p h c", h=4),
                             bcw[:, 0, :, ti:ti + 1].broadcast_to([128, 4, 64]), ALU.mult, ["STs", "bcw"], ["STs"])
                          TT("dve", sg, sg, psu[:, 0:256], ALU.add, ["STs", psuid], ["STs"])
                          CP("act", STb[:, g * 256:(g + 1) * 256], sg, ["STs"], ["STb"])
              if last_pass:
                  for q4 in range(4):
                      TR(PS["psU"][:, q4 * 128:(q4 + 1) * 128], STs[:, ly, q4 * 128:(q4 + 1) * 128], identf, ["STs", "c128"], ["psU"])
                  CP("dve", hn[:, :], PS["psU"][:, :], ["psU"], ["hn"])
                  DMA(pS_o[ly].rearrange("(q p) n -> p q n", p=128), hn[:, :].rearrange("p (q n) -> p q n", q=4), ["hn"], ())
              checkpoint()
              wout_part(mixs, 4)
              checkpoint()

              if ly == L2 - 1:
                  stage_reset()
                  xn32 = af([128, D])
                  nwbc = af([128, D])
                  junk = ab([128, D])
                  DMA(nwbc, fnw_d.partition_broadcast(128), (), ["nwbc"])
                  for t in tiles:
                      L, ti = t["L"], t["ti"]
                      rmsnorm_tile(t, None, xn32, junk, nwbc)
                      dst = yp_o[t["tok"]:t["tok"] + 128, :] if t["kind"] == "p" else ys_o
                      DMA(dst, xn32[:L], ["xn32"], ())
    except _Stop:
        pass

    print("NOPS", len(R.ops), "arena max words f32/bf16", amax, flush=True)
    block = st.enter_context(nc.Block())
    R.emit(nc, block, st)
    st.close()
    return nc


_CACHE = {}


def _host_layout(inputs, NSQ, c):
    f = np.ascontiguousarray
    b0 = c * NSQ
    L2 = DEPTH
    w_in = inputs["w_in"]
    m = {}
    m["xp"] = f(inputs["x_prompt"][c % inputs["x_prompt"].shape[0]])
    m["xs"] = f(inputs["x_sample"][b0:b0 + NSQ].reshape(NSQ * 4, D))
    m["sC"] = f(inputs["state_mlstm_C"][:, b0:b0 + NSQ])
    m["sn"] = f(inputs["state_mlstm_n"][:, b0:b0 + NSQ].reshape(L2, NSQ, 512))
    smt = np.transpose(inputs["state_mlstm_m"][:, b0:b0 + NSQ], (0, 2, 1))
    m["sm"] = f(np.concatenate([smt, smt], axis=1))
    m["sS"] = f(inputs["state_ssd"][:, b0:b0 + NSQ])
    m["sscv"] = f(inputs["state_ssd_conv"][:, b0:b0 + NSQ].reshape(L2, NSQ * 3, 1024))
    m["sccv"] = f(inputs["state_sconv_conv"][:, b0:b0 + NSQ].reshape(L2, NSQ * 2, 512))
    m["srh"] = f(inputs["state_rglru_h"][:, b0:b0 + NSQ])
    m["srcv"] = f(inputs["state_rglru_conv"][:, b0:b0 + NSQ].reshape(L2, NSQ * 3, 512))
    return m


def _shared_layout(inputs, NCH, NSQ):
    f = np.ascontiguousarray
    L2 = DEPTH
    w_in = inputs["w_in"]
    s = {}
    def cols(o, n):
        return list(range(o, o + n))
    chunks = []
    for j in range(4):
        chunks.append(cols(OSB + j * 128, 128) + cols(OSC + j * 128, 128))
        chunks.append(cols(OSH + j * 128, 128) + cols(OSCZ + j * 128, 128))
    for j in range(4):
        chunks.append(cols(ORX + j * 128, 128) + cols(ORZ + j * 128, 128))
    for cg in range(4):
        chunks.append(cols(OO + cg * 128, 128) + cols(OZ + cg * 128, 128))
    for base in (OQ, OKK, OV):
        for hp in range(2):
            chunks.append(cols(base + hp * 256, 256))
    for jp in range(4):
        chunks.append(cols(OXBC + jp * 256, 256))
    for hp in range(2):
        chunks.append(cols(OSZ + hp * 256, 256))
    assert len(chunks) == 28

    def pack(W):
        K = W.shape[0] // 128
        return np.ascontiguousarray(W.reshape(K, 128, W.shape[1]).transpose(1, 0, 2).reshape(128, K * W.shape[1]))
    wpk = np.empty((L2, 28, 128, 4096), np.float32)
    for ly in range(L2):
        for k, cc in enumerate(chunks):
            wpk[ly, k] = pack(w_in[ly][:, cc])
    s["wpk"] = wpk
    wopk = np.empty((L2, 4, 8, 128, 1024), np.float32)
    for ly in range(L2):
        for g in range(4):
            for dg in range(8):
                wopk[ly, g, dg] = pack(inputs["w_out"][ly][g * 512:(g + 1) * 512, dg * 256:(dg + 1) * 256])
    s["wopk"] = wopk
    wi = w_in[:, :, OI:OI + 4]
    wf = w_in[:, :, OF:OF + 4]
    wd0 = w_in[:, :, ODT:ODT + 4]
    wd1 = w_in[:, :, ODT + 4:ODT + 8]
    wsm = np.concatenate([wi, wi, wf, wf, wd0, wd0, wd1, wd1], axis=2)
    s["wsm"] = np.stack([pack(wsm[ly]) for ly in range(L2)], axis=0)
    wabd = np.zeros((L2, 8, 128, 128), np.float32)
    for ly in range(L2):
        for j in range(4):
            for k in range(2):
                blk = 2 * j + k
                wabd[ly, j, k * 64:(k + 1) * 64, k * 64:(k + 1) * 64] = inputs["rg_wa"][ly, blk]
                wabd[ly, 4 + j, k * 64:(k + 1) * 64, k * 64:(k + 1) * 64] = inputs["rg_wx"][ly, blk]
    s["wabd"] = f(np.transpose(wabd, (0, 2, 1, 3)).reshape(L2, 128, 1024))
    cvec = np.zeros((L2, 128, 84), np.float32)

    def fm(v, nt):
        return v.reshape(nt, 128).T
    for ly in range(L2):
        cw = inputs["ssd_conv_w"][ly]
        cvec[ly, :, 0:32] = np.transpose(cw.reshape(4, 8, 128), (2, 1, 0)).reshape(128, 32)
        cvec[ly, :, 32:40] = fm(inputs["ssd_conv_b"][ly], 8)
        sw = inputs["sc_conv_w"][ly]
        cvec[ly, :, 40:52] = np.transpose(sw.reshape(3, 4, 128), (2, 1, 0)).reshape(128, 12)
        rw = inputs["rg_conv_w"][ly]
        cvec[ly, :, 52:68] = np.transpose(rw.reshape(4, 4, 128), (2, 1, 0)).reshape(128, 16)
        cvec[ly, :, 68:72] = fm(inputs["rg_conv_b"][ly], 4)
        cvec[ly, :, 72:76] = fm(inputs["rg_ba"][ly], 4)
        cvec[ly, :, 76:80] = fm(inputs["rg_bx"][ly], 4)
        cvec[ly, :, 80:84] = fm(inputs["rg_lambda"][ly], 4)
    s["cvec"] = cvec
    g8 = np.zeros((L2, 8, 8), np.float32)
    for ly in range(L2):
        d2 = lambda v: np.concatenate([v, v])
        g8[ly, :, 0] = d2(inputs["ml_i_bias"][ly])
        g8[ly, :, 1] = d2(inputs["ml_f_bias"][ly])
        g8[ly, :, 2] = d2(inputs["ssd_dt_bias"][ly][0:4])
        g8[ly, :, 3] = d2(inputs["ssd_dt_bias"][ly][4:8])
        g8[ly, :, 4] = d2(inputs["ssd_A_log"][ly][0:4])
        g8[ly, :, 5] = d2(inputs["ssd_A_log"][ly][4:8])
    s["g8"] = g8
    for k in ("norm_w", "ml_norm_w", "ssd_norm_w", "ssd_D", "final_norm_w"):
        s[k] = f(inputs[k])
    C128h, _, C8h, _ = host_consts(NCH, NSQ)
    s["c128"] = C128h
    s["c8"] = C8h
    return s


def run(inputs, NCH, n_cores=8):
    inputs = {k: np.asarray(v, dtype=np.float32) for k, v in inputs.items()}
    BP, TP = inputs["x_prompt"].shape[0], inputs["x_prompt"].shape[1]
    BS = inputs["x_sample"].shape[0]
    NSQ = BS // n_cores
    key = (TP, NCH, NSQ)
    if key not in _CACHE:
        _CACHE[key] = build(TP, NCH, NSQ)
    nc = _CACHE[key]
    shared = _shared_layout(inputs, NCH, NSQ)
    in_maps = []
    for c in range(n_cores):
        m = _host_layout(inputs, NSQ, c)
        m.update(shared)
        in_maps.append(m)
    res = run_bass_kernel_spmd(nc, in_maps, core_ids=list(range(n_cores)))
    r = res.results
    L2 = DEPTH
    cat = lambda k, ax: np.concatenate([r[c][k] for c in range(n_cores)], axis=ax)
    stk = lambda k: np.stack([r[c][k] for c in range(BP)], axis=1)
    y_prompt = np.stack([r[c]["yp"] for c in range(BP)], axis=0)
    y_sample = cat("ys", 0).reshape(BS, 4, D)
    p_C = stk("pC")
    p_n = stk("pn")
    p_m = stk("pm")
    p_ssd = stk("pS").reshape(L2, BP, 8, 64, 128)
    p_ssd_conv = stk("pscv")
    p_sc_conv = stk("pccv")
    p_rg_h = stk("prh").reshape(L2, BP, 512)
    p_rg_conv = stk("prcv")
    s_C = cat("oC", 1)
    s_n = cat("on", 1).reshape(L2, BS, 4, 128)
    s_m = np.transpose(cat("om", 2), (0, 2, 1))
    s_ssd = cat("oS", 1)
    s_ssd_conv = cat("oscv", 1).reshape(L2, BS, 3, 1024)
    s_sc_conv = cat("occv", 1).reshape(L2, BS, 2, 512)
    s_rg_h = cat("orh", 1)
    s_rg_conv = cat("orcv", 1).reshape(L2, BS, 3, 512)
    outs = (y_prompt, y_sample, p_C, p_n, p_m, p_ssd, p_ssd_conv, p_sc_conv, p_rg_h, p_rg_conv,
            s_C, s_n, s_m, s_ssd, s_ssd_conv, s_sc_conv, s_rg_h, s_rg_conv)
    return tuple(np.ascontiguousarray(o, dtype=np.float32) for o in outs)


def kernel(**inputs):
    return run(inputs, NCH=4)
```

```python
from contextlib import ExitStack

import numpy as np

import concourse.bass as bass
import concourse.mybir as mybir
from concourse.bass_utils import run_bass_kernel_spmd

F32 = mybir.dt.float32
BF16 = mybir.dt.bfloat16
ALU = mybir.AluOpType
AF = mybir.ActivationFunctionType

D = 2048
BR = 512
PO = 7184
DEPTH = 2
EPS = 1e-6
NEG = -30000.0
OQ, OKK, OV, OO, OZ, OI, OF = 0, 512, 1024, 1536, 2048, 2560, 2564
OSZ, OXBC, ODT = 2568, 3080, 4104
OSB, OSC, OSH, OSCZ = 4112, 4624, 5136, 5648
ORX, ORZ = 6160, 6672

ENGS = ("pe", "act", "dve", "pool", "sp")
NDMASEM = 8


class Op:
    __slots__ = ("eng", "fn", "reads", "writes", "dma", "deps", "sig", "tick", "dsem", "dcount", "idx", "prev")

    def __init__(self, eng, fn, reads, writes, dma):
        self.eng, self.fn, self.reads, self.writes, self.dma = eng, fn, reads, writes, dma
        self.deps = []
        self.sig = False
        self.tick = 0
        self.dsem = -1
        self.dcount = 0
        self.prev = 0


class _Stop(Exception):
    pass


class Rec:
    def __init__(self):
        self.ops = []
        self.last_w = {}
        self.readers = {}
        import os as _os
        self.maxops = int(_os.environ.get("KOPS", "100000000"))
        self.fence_ops = []
        self.pending = set()
        self.fence_from = 0

    PERSIST = frozenset(("c128", "c8", "identb", "maskPb", "maskSb", "smkb", "xnT", "st4", "st4b", "tm4", "Dbc", "cvec", "g8",
                         "g8n", "nsp8", "nba", "wb0", "wb1", "wb2", "wsmb", "wabd", "Caug", "Cbf", "mcar", "STs", "STb", "cvcar",
                         "hcar", "ubp", "ubs", "gsm", "RM", "RMv", "scal", "bcw", "ncol", "cvtmp", "h0s", "small_o"))

    def touches_arena(self, op):
        for b in op.reads + op.writes:
            if b in self.PERSIST or b.startswith("ps") or b.startswith("xres"):
                continue
            return True
        return False

    def fence(self):
        last = {}
        dmas = []
        for op in self.ops[self.fence_from:]:
            if not self.touches_arena(op):
                continue
            if op.dma:
                dmas.append(op.idx)
            else:
                last[op.eng] = op.idx
        self.fence_ops = sorted(set(self.fence_ops) | set(last.values()) | set(dmas)) if self.pending else \
            sorted(set(last.values()) | set(dmas))
        self.pending = set(ENGS)
        self.fence_from = len(self.ops)

    def add(self, eng, fn, reads=(), writes=(), dma=False):
        if len(self.ops) >= self.maxops:
            return None
        writes = tuple(writes) + tuple(b for b in reads if isinstance(b, str) and b.startswith("ps") and b not in writes)
        op = Op(eng, fn, tuple(reads), tuple(writes), dma)
        op.idx = len(self.ops)
        deps = set()
        if eng in self.pending and self.touches_arena(op):
            deps |= set(self.fence_ops)
            self.pending.discard(eng)
        for b in op.reads:
            w = self.last_w.get(b)
            if w is not None:
                deps.add(w)
        for b in op.writes:
            w = self.last_w.get(b)
            if w is not None:
                deps.add(w)
            for r in self.readers.get(b, ()):
                deps.add(r)
        deps.discard(op.idx)
        op.deps = sorted(deps)
        for b in op.reads:
            self.readers.setdefault(b, []).append(op.idx)
        for b in op.writes:
            self.last_w[b] = op.idx
            self.readers[b] = []
        self.ops.append(op)
        return op

    def emit(self, nc, block, stack):
        ops = self.ops
        for op in ops:
            for d in op.deps:
                p = ops[d]
                if p.dma:
                    continue
                if p.eng == "pe" and op.eng == "pe" and not op.dma:
                    continue
                p.sig = True
        sem = {e: stack.enter_context(nc.semaphore("s_" + e)) for e in ("pe", "act", "dve", "pool")}
        dsem = {q: [stack.enter_context(nc.semaphore("d_%s%d" % (q, i))) for i in range(NDMASEM)]
                for q in ("sp", "pool")}
        cnt = {e: 0 for e in sem}
        dn = {q: 0 for q in dsem}
        dc = {q: [0] * NDMASEM for q in dsem}
        for op in ops:
            if op.dma:
                q = op.eng
                i = dn[q] % NDMASEM
                dn[q] += 1
                op.dsem = i
                op.prev = dc[q][i]
                dc[q][i] += 1
                op.dcount = dc[q][i]
            elif op.sig:
                cnt[op.eng] += 1
                op.tick = cnt[op.eng]
        per = {e: [o for o in ops if o.eng == e] for e in ENGS}

        def run(eng_name, e):
            waited = {}

            def need(key, s, v):
                if waited.get(key, 0) >= v:
                    return
                waited[key] = v
                e.wait_ge(s, v)

            for op in per[eng_name]:
                for d in op.deps:
                    p = ops[d]
                    if p.dma:
                        need(("d", p.eng, p.dsem), dsem[p.eng][p.dsem], 16 * p.dcount)
                    else:
                        if p.eng == "pe" and eng_name == "pe" and not op.dma:
                            continue
                        need(("c", p.eng), sem[p.eng], p.tick)
                if op.dma:
                    if op.prev > 0:
                        need(("d", op.eng, op.dsem), dsem[op.eng][op.dsem], 16 * op.prev)
                    op.fn(e).then_inc(dsem[op.eng][op.dsem], 16)
                else:
                    ins = op.fn(e)
                    if op.sig:
                        ins.then_inc(sem[op.eng], 1)
            if eng_name in dsem:
                for i in range(NDMASEM):
                    if dc[eng_name][i] > 0:
                        need(("d", eng_name, i), dsem[eng_name][i], 16 * dc[eng_name][i])

        @block.tensor
        def _(e):
            run("pe", e)

        @block.scalar
        def _(e):
            run("act", e)

        @block.vector
        def _(e):
            run("dve", e)

        @block.gpsimd
        def _(e):
            run("pool", e)

        @block.sync
        def _(e):
            run("sp", e)


def host_consts(NCH, NSQ):
    LS = 4 * NSQ
    NP = 128 * NCH
    c128 = {}
    c128["ident"] = np.eye(128, dtype=np.float32)
    s = np.arange(128)[:, None]
    l = np.arange(128)[None, :]
    mp = np.where(s <= l, 0.0, NEG).astype(np.float32)
    c128["maskP"] = np.tile(mp[:, None, :], (1, 4, 1)).reshape(128, 512)
    ms = np.full((128, 128), NEG, np.float32)
    sl = np.arange(LS)
    okm = (sl[:, None] // 4 == sl[None, :] // 4) & (sl[:, None] <= sl[None, :])
    ms[:LS, :LS] = np.where(okm, 0.0, NEG)
    c128["maskS"] = np.tile(ms[:, None, :LS], (1, 4, 1)).reshape(128, 4 * LS)
    smk = (np.arange(NSQ)[:, None] == (sl[None, :] // 4)).astype(np.float32)
    c128["smk"] = np.tile(smk.reshape(1, NSQ * LS), (128, 1))
    smT = np.zeros((128, NSQ), np.float32)
    smT[:LS] = smk.T
    c128["smT"] = smT
    C128 = np.concatenate([c128[k] for k in ("ident", "maskP", "maskS", "smk", "smT")], axis=1)
    offs128 = {}
    o = 0
    for k in ("ident", "maskP", "maskS", "smk", "smT"):
        offs128[k] = (o, c128[k].shape[1])
        o += c128[k].shape[1]
    c8 = {}
    keepP = np.ones((8, NP), np.float32)
    keepP[:, ::128] = 0
    c8["keepP"] = keepP
    keepS = np.ones((8, LS), np.float32)
    keepS[:, ::4] = 0
    c8["keepS"] = keepS
    negS = np.zeros((8, LS), np.float32)
    negS[:, ::4] = -1e30
    c8["negS"] = negS
    pm = np.array([1, 1, 1, 1, 0, 0, 0, 0], np.float32)[:, None]
    c8["pm"] = pm
    c8["opm"] = 1 - pm
    mf = np.zeros((8, 4, 128), np.float32)
    for k in range(8):
        mf[k, k % 4, :] = 1
    c8["maskfull"] = mf.reshape(8, 512)
    m8 = np.zeros((8, 4), np.float32)
    for k in range(4):
        m8[k, k] = 1
    c8["mask8h"] = m8
    lv = np.zeros((8, 128), np.float32)
    lv[4:] = 1
    c8["LV"] = lv
    c8["ones8"] = np.ones((8, 128), np.float32)
    names8 = ("keepP", "keepS", "negS", "pm", "opm", "maskfull", "mask8h", "LV", "ones8")
    C8 = np.concatenate([c8[k] for k in names8], axis=1)
    offs8 = {}
    o = 0
    for k in names8:
        offs8[k] = (o, c8[k].shape[1])
        o += c8[k].shape[1]
    return C128, offs128, C8, offs8


def build(TP, NCH, NSQ):
    LS = 4 * NSQ
    NPASS = TP // (128 * NCH)
    assert NPASS * 128 * NCH == TP
    NTM = NCH + 1
    NMAX = 128 * NCH + LS
    NP = 128 * NCH
    L2 = DEPTH
    C128h, o128, C8h, o8 = host_consts(NCH, NSQ)
    nc = bass.Bass("TRN2", target_bir_lowering=False)

    def din(name, shape):
        return nc.dram_tensor(name, list(shape), F32, kind="ExternalInput").ap()

    def dout(name, shape):
        return nc.dram_tensor(name, list(shape), F32, kind="ExternalOutput").ap()

    xp_d = din("xp", [TP, D])
    xs_d = din("xs", [LS, D])
    sC_d = din("sC", [L2, NSQ, 4, 128, 128])
    sn_d = din("sn", [L2, NSQ, 512])
    sm_d = din("sm", [L2, 8, NSQ])
    sS_d = din("sS", [L2, NSQ, 8, 64, 128])
    sscv_d = din("sscv", [L2, NSQ * 3, 1024])
    sccv_d = din("sccv", [L2, NSQ * 2, 512])
    srh_d = din("srh", [L2, NSQ, 512])
    srcv_d = din("srcv", [L2, NSQ * 3, 512])
    wpk_d = din("wpk", [L2, 28, 128, 4096])
    wsm_d = din("wsm", [L2, 128, 512])
    wopk_d = din("wopk", [L2, 4, 8, 128, 1024])
    wabd_d = din("wabd", [L2, 128, 1024])
    cvec_d = din("cvec", [L2, 128, 84])
    g8_d = din("g8", [L2, 8, 8])
    normw_d = din("norm_w", [L2, D])
    mlnw_d = din("ml_norm_w", [L2, BR])
    ssdnw_d = din("ssd_norm_w", [L2, BR])
    ssdD_d = din("ssd_D", [L2, 8])
    fnw_d = din("final_norm_w", [D])
    c128_d = din("c128", list(C128h.shape))
    c8_d = din("c8", list(C8h.shape))

    yp_o = dout("yp", [TP, D])
    ys_o = dout("ys", [LS, D])
    pC_o = dout("pC", [L2, 4, 128, 128])
    pn_o = dout("pn", [L2, 4, 128])
    pm_o = dout("pm", [L2, 4])
    pS_o = dout("pS", [L2, 512, 128])
    pscv_o = dout("pscv", [L2, 3, 1024])
    pccv_o = dout("pccv", [L2, 2, 512])
    prh_o = dout("prh", [L2, 4, 128])
    prcv_o = dout("prcv", [L2, 3, 512])
    oC_o = dout("oC", [L2, NSQ, 4, 128, 128])
    on_o = dout("on", [L2, NSQ, 512])
    om_o = dout("om", [L2, 4, NSQ])
    oS_o = dout("oS", [L2, NSQ, 8, 64, 128])
    oscv_o = dout("oscv", [L2, NSQ * 3, 1024])
    occv_o = dout("occv", [L2, NSQ * 2, 512])
    orh_o = dout("orh", [L2, NSQ, 512])
    orcv_o = dout("orcv", [L2, NSQ * 3, 512])

    R = Rec()
    st = ExitStack()
    st.enter_context(nc.allow_non_contiguous_dma(reason="small strided state/constant transfers"))

    def sb(name, shape, dt=F32):
        return st.enter_context(nc.sbuf_tensor("sb_" + name, list(shape), dt))

    def psum(name):
        return st.enter_context(nc.psum_tensor(name, [128, 512], F32))

    def TT(eng, out, in0, in1, op, r, w):
        R.add(eng, lambda e: e.tensor_tensor(out=out, in0=in0, in1=in1, op=op), r, w)

    def TS(eng, out, in0, s1, s2, op0, op1, r, w):
        if op1 is None:
            R.add(eng, lambda e: e.tensor_scalar(out=out, in0=in0, scalar1=s1, scalar2=None, op0=op0), r, w)
        else:
            R.add(eng, lambda e: e.tensor_scalar(out=out, in0=in0, scalar1=s1, scalar2=s2, op0=op0, op1=op1), r, w)

    def STT(eng, out, in0, scalar, in1, op0, op1, r, w):
        R.add(eng, lambda e: e.scalar_tensor_tensor(out=out, in0=in0, scalar=scalar, in1=in1, op0=op0, op1=op1), r, w)

    def CP(eng, out, in_, r, w):
        if eng == "act":
            R.add(eng, lambda e: e.activation(out=out, in_=in_, func=AF.Copy), r, w)
        else:
            R.add(eng, lambda e: e.tensor_copy(out=out, in_=in_), r, w)

    def ACT(out, in_, func, r, w, bias=None, scale=1.0, accum=None):
        kw = {}
        if bias is not None:
            kw["bias"] = bias
        if accum is not None:
            kw["accum_out"] = accum
        R.add("act", lambda e: e.activation(out=out, in_=in_, func=func, scale=scale, **kw), r, w)

    def MM(out, lhsT, rhs, start, stop, r, w):
        R.add("pe", lambda e: e.matmul(out, lhsT=lhsT, rhs=rhs, start=start, stop=stop), r, w)

    def TR(out, in_, ident, r, w):
        R.add("pe", lambda e: e.transpose(out=out, in_=in_, identity=ident), r, w)

    def DMA(out, in_, r, w, q="sp"):
        R.add(q, lambda e: e.dma_start(out=out, in_=in_), r, w, dma=True)

    def SCAN(out, d0, d1, init, op0, op1, r, w):
        R.add("dve", lambda e: e.tensor_tensor_scan(out=out, data0=d0, data1=d1, initial=init, op0=op0, op1=op1), r, w)

    def MEMSET(eng, ap, val, w):
        R.add(eng, lambda e: e.memset(ap, val), (), w)

    def RSUM(out, in_, r, w):
        R.add("dve", lambda e: e.reduce_sum(out=out, in_=in_, axis=mybir.AxisListType.X), r, w)

    def RECIP(out, in_, r, w):
        R.add("dve", lambda e: e.reciprocal(out=out, in_=in_), r, w)

    def sigmoid_from_exp(t, r, w):
        ACT(t, t, AF.Ln, r, w, bias=1.0)
        ACT(t, t, AF.Exp, r, w, scale=-1.0)

    c128 = sb("c128", [128, 128 + NSQ])
    c8 = sb("c8", C8h.shape)
    identf = c128[:, 0:128]
    smT = c128[:, 128:128 + NSQ]

    def k8(name):
        o, n = o8[name]
        return c8[:, o:o + n]

    identb = sb("identb", [128, 128], BF16)
    maskPb = sb("maskPb", [128, 512], BF16)
    maskSb = sb("maskSb", [128, 4 * LS], BF16)
    smkb = sb("smkb", [128, NSQ * LS], BF16)
    pm8 = k8("pm")
    opm8 = k8("opm")
    xres = sb("xres", [128, NTM, D])
    xnT = sb("xnT", [128, 16, NMAX], BF16)
    st4 = sb("st4", [128, 8])
    tm4 = sb("tm4", [128, 16])
    Dbc = sb("Dbc", [128, 8])
    cvec = sb("cvec", [128, 84])
    g8 = sb("g8", [8, 8])
    g8n = sb("g8n", [8, 8])
    nsp8 = sb("nsp8", [128, 4])
    nba = sb("nba", [128, 8])
    NWB = 3
    wb = [sb("wb%d" % i, [128, 16, 256], BF16) for i in range(NWB)]
    wsmb = sb("wsmb", [128, 16, 32], BF16)
    wabd = sb("wabd", [128, 8, 128], BF16)
    PS = {k: psum(k) for k in ("psA", "psB", "psS", "psE", "psV", "psN0", "psN1", "psU")}
    Caug = sb("Caug", [128, L2, 4, 129])
    Cbf = sb("Cbf", [128, 4, 129], BF16)
    mcar = sb("mcar", [8, L2])
    STs = sb("STs", [128, L2, 512])
    STb = sb("STb", [128, 512], BF16)
    cvcar = sb("cvcar", [128, L2, 16, 3])
    hcar = sb("hcar", [128, L2, 4])
    ubp = sb("ubp", [128, 3 + NP])
    ubs = sb("ubs", [128, NSQ, 7])
    gsm = sb("gsm", [8, 64])
    RM = sb("RM", [8, 512])
    RMv = sb("RMv", [8, 512])
    scal = sb("scal", [128, NTM, 2, 8])
    bcw = sb("bcw", [128, 2, 4, 32])
    ncol = sb("ncol", [128, NSQ])
    cvtmp = sb("cvtmp", [128, NSQ * 3])
    h0s = sb("h0s", [128, NSQ])
    small_o = sb("small_o", [8, 768])
    AFW = 14480
    ABW = 17192
    arenaF = sb("arenaF", [128, AFW])
    arenaB = sb("arenaB", [128, ABW], BF16)
    aoff = [0, 0]
    amax = [0, 0]
    import os as _os
    print('SBUF remaining after persistent', nc.sbuf_bytes_remaining, flush=True)

    def stage_reset():
        R.fence()
        aoff[0] = 0
        aoff[1] = 0

    def carve(arena, k, cap, shape, pat):
        n = 1
        for d_ in shape[1:]:
            n *= d_
        o = aoff[k]
        aoff[k] += n
        amax[k] = max(amax[k], aoff[k])
        if _os.environ.get("KDRY"):
            o = 0
        else:
            assert aoff[k] <= cap, ("arena overflow", k, aoff[k], cap)
        v = arena[:shape[0], o:o + n]
        if len(shape) == 3:
            v = v.rearrange("p (a b) -> p a b", a=shape[1])
        elif len(shape) == 4:
            v = v.rearrange("p (a b c) -> p a b c", a=shape[1], b=shape[2])
        return v

    def af(shape):
        return carve(arenaF, 0, AFW, shape, None)

    def ab(shape):
        return carve(arenaB, 1, ABW, shape, None)

    DMA(c128[:, 0:128], c128_d[:, 0:128], (), ["c128"])
    o_, n_ = o128["smT"]
    DMA(c128[:, 128:128 + NSQ], c128_d[:, o_:o_ + n_], (), ["c128"])
    DMA(c8[:], c8_d, (), ["c8"])
    DMA(identb[:], c128_d[:, 0:128], (), ["identb"], q="pool")
    o_, n_ = o128["maskP"]
    DMA(maskPb[:], c128_d[:, o_:o_ + n_], (), ["maskPb"], q="pool")
    o_, n_ = o128["maskS"]
    DMA(maskSb[:], c128_d[:, o_:o_ + n_], (), ["maskSb"], q="pool")
    o_, n_ = o128["smk"]
    DMA(smkb[:], c128_d[:, o_:o_ + n_], (), ["smkb"], q="pool")
    MEMSET("pool", xnT[:], 0.0, ["xnT"])
    MEMSET("pool", Caug[:], 0.0, ["Caug"])
    MEMSET("pool", mcar[:], -1e30, ["mcar"])
    MEMSET("pool", STs[:], 0.0, ["STs"])
    MEMSET("pool", cvcar[:], 0.0, ["cvcar"])
    MEMSET("pool", hcar[:], 0.0, ["hcar"])
    MEMSET("pool", ubs[:], 0.0, ["ubs"])
    MEMSET("pool", ubp[:], 0.0, ["ubp"])
    MEMSET("pool", arenaF[:], 0.0, ["arenaF"])
    MEMSET("pool", arenaB[:], 0.0, ["arenaB"])
    stage_reset()

    wctr = [0]

    def load_w(src2d, nct=16):
        i = wctr[0] % NWB
        wctr[0] += 1
        DMA(wb[i][:, 0:nct, :].rearrange("p a b -> p (a b)"), src2d, (), ["wb%d" % i], q="pool")
        return wb[i], "wb%d" % i

    pctr = [0]

    def next_ps():
        k = ("psA", "psB", "psN1")[pctr[0] % 3]
        pctr[0] += 1
        return PS[k], k

    def next_ps2():
        k = ("psA", "psB")[pctr[0] % 2]
        pctr[0] += 1
        return PS[k], k

    import os as _os
    KSTOP = int(_os.environ.get("KSTOP", "99"))
    kst = [0]

    def checkpoint():
        kst[0] += 1
        if kst[0] > KSTOP:
            raise _Stop()

    try:
      for pi in range(NPASS):
          tiles = [dict(kind="p", L=128, col=128 * i, ti=i, tok=pi * NP + 128 * i) for i in range(NCH)]
          last_pass = pi == NPASS - 1
          if last_pass:
              tiles.append(dict(kind="s", L=LS, col=NP, ti=NCH, tok=0))
          NT = len(tiles)
          N = (NP + LS) if last_pass else NP
          groups = []
          c0 = 0
          while c0 < N:
              groups.append((c0, min(512, N - c0)))
              c0 += 512
          has_s = last_pass

          for t in tiles:
              src = xp_d[t["tok"]:t["tok"] + 128, :] if t["kind"] == "p" else xs_d
              DMA(xres[:t["L"], t["ti"], :], src, (), ["xres%d" % t["ti"]])

          for ly in range(L2):
              DMA(Dbc[:], ssdD_d[ly].partition_broadcast(128), (), ["Dbc"])
              DMA(cvec[:], cvec_d[ly], (), ["cvec"])
              DMA(g8[:], g8_d[ly], (), ["g8"])
              DMA(wsmb[:, :, :].rearrange("p a b -> p (a b)"), wsm_d[ly], (), ["wsmb"], q="pool")
              DMA(wabd[:, :, :].rearrange("p a b -> p (a b)"), wabd_d[ly], (), ["wabd"], q="pool")
              TS("dve", g8n[:, 0:2], g8[:, 0:2], -1.0, None, ALU.mult, None, ["g8"], ["g8n"])
              ACT(g8n[:, 4:6], g8[:, 4:6], AF.Exp, ["g8"], ["g8n"])
              TS("dve", g8n[:, 4:6], g8n[:, 4:6], -1.0, None, ALU.mult, None, ["g8n"], ["g8n"])
              ACT(nsp8[:], cvec[:, 80:84], AF.Exp, ["cvec"], ["nsp8"], scale=-1.0)
              ACT(nsp8[:], nsp8[:], AF.Ln, ["nsp8"], ["nsp8"], bias=1.0)
              TS("dve", nsp8[:], nsp8[:], -8.0, None, ALU.mult, None, ["nsp8"], ["nsp8"])
              TS("dve", nba[:], cvec[:, 72:80], -1.0, None, ALU.mult, None, ["cvec"], ["nba"])

              def rmsnorm_tile(t, wsrc_d, xn32s, junks, nwbc):
                  L, ti = t["L"], t["ti"]
                  pb = ti % 2
                  xn32, junk = xn32s[pb], junks[pb]
                  xid, jid, sid = "xn32_%d" % pb, "junk_%d" % pb, ("st4", "st4b")[pb]
                  so = 4 * pb
                  xr_id = "xres%d" % ti
                  ACT(junk[:L], xres[:L, ti, :], AF.Square, [xr_id], [jid, sid], accum=st4[:L, so:so + 1])
                  ACT(st4[:L, so + 1:so + 2], st4[:L, so:so + 1], AF.Ln, [sid], [sid], bias=EPS, scale=1.0 / D)
                  ACT(st4[:L, so + 2:so + 3], st4[:L, so + 1:so + 2], AF.Exp, [sid], [sid], scale=-0.5)
                  STT("dve", xn32[:L], xres[:L, ti, :], st4[:L, so + 2:so + 3], nwbc[:L], ALU.mult, ALU.mult,
                      [xr_id, sid, "nwbc"], [xid])
                  return xn32, xid

              stage_reset()
              xn32s = [af([128, D]), af([128, D])]
              nwbc = af([128, D])
              junks = [ab([128, D]), ab([128, D])]
              DMA(nwbc, normw_d[ly].partition_broadcast(128), (), ["nwbc"])
              for t in tiles:
                  L, ti, col = t["L"], t["ti"], t["col"]
                  xn32, xid = rmsnorm_tile(t, None, xn32s, junks, nwbc)
                  for q4 in range(4):
                      for j in range(4):
                          dt_ = q4 * 4 + j
                          TR(PS["psU"][:, j * 128:j * 128 + L], xn32[:L, dt_ * 128:(dt_ + 1) * 128], identf[:L, :L],
                             [xid, "c128"], ["psU"])
                      CP("act" if q4 % 2 else "dve", xnT[:, q4 * 4:q4 * 4 + 4, col:col + L],
                         PS["psU"][:, :].rearrange("p (j c) -> p j c", j=4)[:, :, :L], ["psU"], ["xnT"])

              checkpoint()
              def proj_fm(wt, wid, wo, M, evac):
                  for (g0, n) in groups:
                      ps, pid = next_ps()
                      for dt_ in range(16):
                          MM(ps[:M, :n], wt[:, dt_, wo:wo + M], xnT[:, dt_, g0:g0 + n], dt_ == 0, dt_ == 15,
                             [wid, "xnT"], [pid])
                      evac(ps, pid, g0, n)

              def proj_tm(wt, wid, wo, n, t, evac):
                  ps, pid = next_ps()
                  L, col = t["L"], t["col"]
                  for dt_ in range(16):
                      MM(ps[:L, :n], xnT[:, dt_, col:col + L], wt[:, dt_, wo:wo + n], dt_ == 0, dt_ == 15,
                         [wid, "xnT"], [pid])
                  evac(ps, pid)

              def wout_part(mixs, ct0):
                  for dg in range(8):
                      wt1, w1 = load_w(wopk_d[ly, ct0 // 4, dg], nct=4)
                      for t in tiles:
                          L, ti, col = t["L"], t["ti"], t["col"]
                          ps, pid = next_ps()
                          for ct in range(4):
                              MM(ps[:L, :256], mixs[:, ct, col:col + L], wt1[:, ct, :], ct == 0, ct == 3, ["mixs", w1], [pid])
                          TT("dve", xres[:L, ti, dg * 256:(dg + 1) * 256], xres[:L, ti, dg * 256:(dg + 1) * 256],
                             ps[:L, :256], ALU.add, ["xres%d" % ti, pid], ["xres%d" % ti])

              def conv_load_hist(stream):
                  CP("dve", ubp[:, 0:3], cvcar[:, ly, stream, :], ["cvcar"], ["ubp"])

              def conv_save_hist(stream):
                  CP("dve", cvcar[:, ly, stream, :], ubp[:, NP:NP + 3], ["ubp"], ["cvcar"])

              def conv_apply(out_fm, oid, W, wcol, bcol):
                  s0 = 3 - (W - 1)
                  views = [(out_fm[:, 0:NP], lambda j: ubp[:, s0 + j:s0 + j + NP])]
                  if has_s:
                      views.append((out_fm[:, NP:NP + LS].rearrange("p (b t) -> p b t", t=4),
                                    lambda j: ubs[:, :, s0 + j:s0 + j + 4]))
                  for (o, src) in views:
                      if bcol is None:
                          TS("dve", o, src(0), cvec[:, wcol:wcol + 1], None, ALU.mult, None,
                             ["ubp", "ubs", "cvec"], [oid])
                      else:
                          TS("dve", o, src(0), cvec[:, wcol:wcol + 1], cvec[:, bcol:bcol + 1], ALU.mult, ALU.add,
                             ["ubp", "ubs", "cvec"], [oid])
                      for j in range(1, W):
                          STT("dve", o, src(j), cvec[:, wcol + j:wcol + j + 1], o, ALU.mult, ALU.add,
                              ["ubp", "ubs", "cvec", oid], [oid])

              def sample_hist_tile(cvin, W, ctile, src_d):
                  rows = NSQ * (W - 1)
                  DMA(cvin[:rows, :], src_d[:, ctile * 128:(ctile + 1) * 128], (), ["cvin"])
                  TR(PS["psU"][:, :rows], cvin[:rows, :], identf[:rows, :rows],
                     ["cvin", "c128"], ["psU"])
                  CP("dve", ubs[:, :, 3 - (W - 1):3], PS["psU"][:, :rows].rearrange("p (b r) -> p b r", r=W - 1),
                     ["psU"], ["ubs"])

              def sample_hist_out_tile(cvout, W, ctile, dst_d):
                  rows = NSQ * (W - 1)
                  CP("dve", cvtmp[:, :rows].rearrange("p (b r) -> p b r", r=W - 1), ubs[:, :, 7 - (W - 1):7],
                     ["ubs"], ["cvtmp"])
                  TR(PS["psU"][:rows, :128], cvtmp[:, :rows], identf, ["cvtmp", "c128"], ["psU"])
                  CP("dve", cvout[:rows, :], PS["psU"][:rows, :128], ["psU"], ["cvout"])
                  DMA(dst_d[:, ctile * 128:(ctile + 1) * 128], cvout[:rows, :], ["cvout"], ())

              def prompt_hist_out(W, streams, dst):
                  for c0 in range(0, len(streams), 4):
                      sub = streams[c0:c0 + 4]
                      for i, s_ in enumerate(sub):
                          CP("dve", cvtmp[:, 0:W - 1], cvcar[:, ly, s_, 3 - (W - 1):3], ["cvcar"], ["cvtmp"])
                          TR(PS["psU"][:W - 1, :128], cvtmp[:, 0:W - 1], identf, ["cvtmp", "c128"], ["psU"])
                          CP("dve", small_o[:W - 1, i * 128:(i + 1) * 128], PS["psU"][:W - 1, :128], ["psU"], ["small_o"])
                      DMA(dst[:, c0 * 128:(c0 + len(sub)) * 128], small_o[:W - 1, :128 * len(sub)], ["small_o"], ())

              def fill_ub(ps, pid, g0, n, eng="act"):
                  pe_ = min(g0 + n, NP)
                  if pe_ > g0:
                      CP(eng, ubp[:, 3 + g0:3 + pe_], ps[:, 0:pe_ - g0], [pid], ["ubp"])
                  if has_s and g0 + n > NP:
                      a0 = max(g0, NP)
                      CP(eng, ubs[:, :, 3:7], ps[:, a0 - g0:a0 - g0 + LS].rearrange("p (b t) -> p b t", t=4),
                         [pid], ["ubs"])

              stage_reset()
              fa = [af([128, NMAX]) for _ in range(6)]
              cvin = af([NSQ * 3, 128])
              cvout = af([NSQ * 3, 128])
              htm = af([NSQ, 512])
              htmo = af([NSQ, 512])
              mixs = ab([128, 4, NMAX])
              xrb = ab([128, NMAX])
              for j in range(4):
                  wt1, w1 = load_w(wpk_d[ly, 2 * j])
                  wt2, w2 = load_w(wpk_d[ly, 2 * j + 1])
                  conv_load_hist(j)
                  if has_s:
                      sample_hist_tile(cvin, 3, j, sccv_d[ly])
                  proj_fm(wt1, w1, 0, 128, lambda ps, pid, g0, n: CP("act", fa[0][:, g0:g0 + n], ps[:, :n], [pid], ["fa0"]))
                  proj_fm(wt1, w1, 128, 128, lambda ps, pid, g0, n: CP("act", fa[1][:, g0:g0 + n], ps[:, :n], [pid], ["fa1"]))

                  def ev_h(ps, pid, g0, n):
                      TT("dve", fa[1][:, g0:g0 + n], fa[1][:, g0:g0 + n], ps[:, :n], ALU.mult, ["fa1", pid], ["fa1"])
                  proj_fm(wt2, w2, 0, 128, ev_h)

                  def ev_z(ps, pid, g0, n):
                      CP("act", fa[2][:, g0:g0 + n], ps[:, :n], [pid], ["fa2"])
                      ACT(fa[3][:, g0:g0 + n], fa[2][:, g0:g0 + n], AF.Exp, ["fa2"], ["fa3"], scale=-1.0)
                  proj_fm(wt2, w2, 128, 128, ev_z)
                  CP("dve", ubp[:, 3:3 + NP], fa[1][:, 0:NP], ["fa1"], ["ubp"])
                  if has_s:
                      CP("dve", ubs[:, :, 3:7], fa[1][:, NP:NP + LS].rearrange("p (b t) -> p b t", t=4), ["fa1"], ["ubs"])
                  conv_apply(fa[4], "fa4", 3, 40 + 3 * j, None)
                  conv_save_hist(j)
                  if has_s:
                      sample_hist_out_tile(cvout, 3, j, occv_o[ly])
                  sigmoid_from_exp(fa[3][:, :N], ["fa3"], ["fa3"])
                  TT("dve", fa[2][:, :N], fa[2][:, :N], fa[3][:, :N], ALU.mult, ["fa2", "fa3"], ["fa2"])
                  TT("dve", fa[2][:, :N], fa[2][:, :N], fa[0][:, :N], ALU.mult, ["fa2", "fa0"], ["fa2"])
                  TT("dve", mixs[:, j, :N], fa[2][:, :N], fa[4][:, :N], ALU.mult, ["fa2", "fa4"], ["mixs"])
              if last_pass:
                  prompt_hist_out(3, [0, 1, 2, 3], pccv_o[ly])
              checkpoint()
              wout_part(mixs, 8)
              checkpoint()

              if has_s:
                  DMA(htm[:], srh_d[ly], (), ["htm"])
              for j in range(4):
                  wt1, w1 = load_w(wpk_d[ly, 8 + j])
                  conv_load_hist(4 + j)
                  if has_s:
                      sample_hist_tile(cvin, 4, j, srcv_d[ly])
                      TR(PS["psU"][:, :NSQ], htm[:NSQ, j * 128:(j + 1) * 128], identf[:NSQ, :NSQ], ["htm", "c128"], ["psU"])
                      CP("dve", h0s[:], PS["psU"][:, :NSQ], ["psU"], ["h0s"])
                  proj_fm(wt1, w1, 0, 128, lambda ps, pid, g0, n: fill_ub(ps, pid, g0, n))

                  def ev_rz(ps, pid, g0, n):
                      CP("act", fa[2][:, g0:g0 + n], ps[:, :n], [pid], ["fa2"])
                      ACT(fa[3][:, g0:g0 + n], fa[2][:, g0:g0 + n], AF.Exp, ["fa2"], ["fa3"], scale=-1.0)
                  proj_fm(wt1, w1, 128, 128, ev_rz)
                  conv_apply(fa[0], "fa0", 4, 52 + 4 * j, 68 + j)
                  conv_save_hist(4 + j)
                  if has_s:
                      sample_hist_out_tile(cvout, 4, j, orcv_o[ly])
                  CP("act", xrb[:, :N], fa[0][:, :N], ["fa0"], ["xrb"])
                  for (g0, n) in groups:
                      ps, pid = next_ps()
                      MM(ps[:, :n], wabd[:, j, :], xrb[:, g0:g0 + n], True, True, ["wabd", "xrb"], [pid])
                      ACT(fa[1][:, g0:g0 + n], ps[:, :n], AF.Exp, [pid, "nba"], ["fa1"], bias=nba[:, j:j + 1], scale=-1.0)
                      ps, pid = next_ps()
                      MM(ps[:, :n], wabd[:, 4 + j, :], xrb[:, g0:g0 + n], True, True, ["wabd", "xrb"], [pid])
                      ACT(fa[4][:, g0:g0 + n], ps[:, :n], AF.Exp, [pid, "nba"], ["fa4"], bias=nba[:, 4 + j:5 + j], scale=-1.0)
                  sigmoid_from_exp(fa[1][:, :N], ["fa1"], ["fa1"])
                  sigmoid_from_exp(fa[4][:, :N], ["fa4"], ["fa4"])
                  ACT(fa[1][:, :N], fa[1][:, :N], AF.Exp, ["fa1", "nsp8"], ["fa1"], scale=nsp8[:, j:j + 1])
                  TT("dve", fa[5][:, :N], fa[1][:, :N], fa[1][:, :N], ALU.mult, ["fa1"], ["fa5"])
                  TS("dve", fa[5][:, :N], fa[5][:, :N], -1.0, 1.0, ALU.mult, ALU.add, ["fa5"], ["fa5"])
                  TS("dve", fa[5][:, :N], fa[5][:, :N], 1e-30, None, ALU.max, None, ["fa5"], ["fa5"])
                  ACT(fa[5][:, :N], fa[5][:, :N], AF.Ln, ["fa5"], ["fa5"])
                  ACT(fa[5][:, :N], fa[5][:, :N], AF.Exp, ["fa5"], ["fa5"], scale=0.5)
                  TT("dve", fa[4][:, :N], fa[4][:, :N], fa[0][:, :N], ALU.mult, ["fa4", "fa0"], ["fa4"])
                  TT("dve", fa[4][:, :N], fa[4][:, :N], fa[5][:, :N], ALU.mult, ["fa4", "fa5"], ["fa4"])
                  STT("dve", fa[4][:, 0:1], fa[1][:, 0:1], hcar[:, ly, j:j + 1], fa[4][:, 0:1], ALU.mult, ALU.add,
                      ["fa1", "fa4", "hcar"], ["fa4"])
                  MEMSET("dve", fa[1][:, 0:1], 0.0, ["fa1"])
                  SCAN(fa[0][:, 0:NP], fa[1][:, 0:NP], fa[4][:, 0:NP], 0.0, ALU.mult, ALU.add, ["fa1", "fa4"], ["fa0"])
                  CP("dve", hcar[:, ly, j:j + 1], fa[0][:, NP - 1:NP], ["fa0"], ["hcar"])
                  if has_s:
                      a0v = fa[1][:, NP:NP + LS].rearrange("p (b t) -> p b t", t=4)[:, :, 0]
                      b0v = fa[4][:, NP:NP + LS].rearrange("p (b t) -> p b t", t=4)[:, :, 0]
                      TT("dve", tm4[:, :NSQ], a0v, h0s[:], ALU.mult, ["fa1", "h0s"], ["tm4"])
                      TT("dve", b0v, b0v, tm4[:, :NSQ], ALU.add, ["fa4", "tm4"], ["fa4"])
                      MEMSET("dve", a0v, 0.0, ["fa1"])
                      SCAN(fa[0][:, NP:NP + LS], fa[1][:, NP:NP + LS], fa[4][:, NP:NP + LS], 0.0, ALU.mult, ALU.add,
                           ["fa1", "fa4"], ["fa0"])
                      CP("dve", ncol[:, :NSQ], fa[0][:, NP:NP + LS].rearrange("p (b t) -> p b t", t=4)[:, :, 3], ["fa0"], ["ncol"])
                      TR(PS["psU"][:NSQ, :128], ncol[:, :NSQ], identf, ["ncol", "c128"], ["psU"])
                      CP("dve", htmo[:NSQ, j * 128:(j + 1) * 128], PS["psU"][:NSQ, :128], ["psU"], ["htmo"])
                  sigmoid_from_exp(fa[3][:, :N], ["fa3"], ["fa3"])
                  TT("dve", fa[2][:, :N], fa[2][:, :N], fa[3][:, :N], ALU.mult, ["fa2", "fa3"], ["fa2"])
                  TT("dve", mixs[:, j, :N], fa[2][:, :N], fa[0][:, :N], ALU.mult, ["fa2", "fa0"], ["mixs"])
              if has_s:
                  DMA(orh_o[ly], htmo[:], ["htmo"], ())
              if last_pass:
                  prompt_hist_out(4, [4, 5, 6, 7], prcv_o[ly])
                  for j in range(4):
                      TR(PS["psU"][:1, j * 128:(j + 1) * 128], hcar[:, ly, j:j + 1], identf, ["hcar", "c128"], ["psU"])
                  CP("dve", small_o[:1, 0:512], PS["psU"][:1, 0:512], ["psU"], ["small_o"])
                  DMA(prh_o[ly].rearrange("(o t) c -> o (t c)", o=1), small_o[:1, 0:512], ["small_o"], ())
              checkpoint()
              wout_part(mixs, 12)
              checkpoint()

              def chunk_ends(row, t_kind):
                  if t_kind == "p":
                      return row[:, 0:NP].rearrange("p (c t) -> p c t", t=128)[:, :, 127]
                  return row[:, NP:NP + LS].rearrange("p (b t) -> p b t", t=4)[:, :, 3]

              def expo(a_row, aid, c_row, cid, cv_row, cvid, t, Wt, Wv):
                  L, col = t["L"], t["col"]
                  mf = k8("maskfull").rearrange("p (h l) -> p h l", h=4)[:, :, :L]
                  RMl = RM[:, :4 * L].rearrange("p (h l) -> p h l", h=4)
                  RMvl = RMv[:, :4 * L].rearrange("p (h l) -> p h l", h=4)
                  TT("dve", RMl, c_row[:, col:col + L].unsqueeze(1).broadcast_to([8, 4, L]), mf, ALU.mult,
                     [cid, "c8"], ["RM"])
                  TT("dve", RMvl, cv_row[:, col:col + L].unsqueeze(1).broadcast_to([8, 4, L]), mf, ALU.mult,
                     [cvid, "c8"], ["RMv"])
                  msk = (maskPb if t["kind"] == "p" else maskSb)
                  MM(PS["psE"][:L, :4 * L], a_row[:, col:col + L], RM[:, :4 * L], True, False, [aid, "RM"], ["psE"])
                  MM(PS["psE"][:L, :4 * L], identb[:L, :L], msk[:L, :4 * L], False, True, ["identb", "maskPb", "maskSb"], ["psE"])
                  MM(PS["psV"][:, :4 * L], k8("LV"), RMv[:, :4 * L], True, True, ["c8", "RMv"], ["psV"])
                  ACT(Wt[:L, :4 * L], PS["psE"][:L, :4 * L], AF.Exp, ["psE"], ["Wt"])
                  ACT(Wv[:, :4 * L], PS["psV"][:, :4 * L], AF.Exp, ["psV"], ["Wv"])

              def bcast_rows(src_row, sid, ncols, slot):
                  w8 = gsm[:, :4 * ncols].rearrange("p (h c) -> p h c", h=4)
                  TT("dve", w8, src_row.unsqueeze(1).broadcast_to([8, 4, ncols]),
                     k8("mask8h").unsqueeze(2).broadcast_to([8, 4, ncols]), ALU.mult, [sid, "c8"], ["gsm"])
                  MM(PS["psU"][:, :4 * ncols], k8("ones8"), gsm[:, :4 * ncols], True, True, ["c8", "gsm"], ["psU"])
                  CP("dve", bcw[:, slot, :, :ncols], PS["psU"][:, :4 * ncols].rearrange("p (h c) -> p h c", h=4),
                     ["psU"], ["bcw"])

              def to_tok_major(rowA, aid, rowB, bid, t, slot, scr, scrid):
                  L, col, ti = t["L"], t["col"], t["ti"]
                  TS("dve", scr[:, :L], rowA[:, col:col + L], pm8, None, ALU.mult, None, [aid, "c8"], [scrid])
                  STT("dve", scr[:, :L], rowB[:, col:col + L], opm8, scr[:, :L], ALU.mult, ALU.add,
                      [bid, "c8", scrid], [scrid])
                  TR(PS["psU"][:L, :8], scr[:, :L], identf[:8, :8], [scrid, "c128"], ["psU"])
                  CP("dve", scal[:L, ti, slot, :], PS["psU"][:L, :8], ["psU"], ["scal"])

              stage_reset()
              Gml = af([128, NTM, 512])
              gr = [af([8, NMAX]) for _ in range(10)]
              mlnwbc = af([128, BR])
              DMA(mlnwbc, mlnw_d[ly].partition_broadcast(128), (), ["mlnwbc"])
              Wt = af([128, 512])
              Wv = af([128, 512])
              hn = af([128, 512])
              ytm = af([128, 512])
              Cs = af([128, NSQ, 129])
              ntm = af([NSQ, 512])
              ntmo = af([NSQ, 512])
              qT = ab([128, 4, NMAX])
              kT = ab([128, 4, NMAX])
              ktm = ab([128, NTM, 512])
              vaug = ab([128, NTM, 4, 129])
              SwT = ab([128, 512])
              qv = ab([128, 512])
              kw = ab([128, 512])
              qvm = ab([128, NSQ, LS])
              Csb = ab([128, NSQ, 129])
              vm = ab([128, 3, 129])
              mixs = ab([128, 4, NMAX])
              junk = ab([128, 128])
              hsq = af([128, 512])
              MEMSET("pool", vaug, 1.0, ["vaug"])
              CP("act", Cbf[:], Caug[:, ly], ["Caug"], ["Cbf"])
              for cg in range(4):
                  wt1, w1 = load_w(wpk_d[ly, 12 + cg])
                  for t in tiles:
                      L, ti = t["L"], t["ti"]

                      def ev_g(ps, pid, L=L, ti=ti, cg=cg):
                          CP("act", ytm[:L, 0:256], ps[:L, 0:256], [pid], ["ytm"])
                          ACT(hn[:L, 0:256], ytm[:L, 0:256], AF.Exp, ["ytm"], ["hn"], scale=-1.0)
                          ACT(hn[:L, 0:256], hn[:L, 0:256], AF.Ln, ["hn"], ["hn"], bias=1.0)
                          TT("dve", hn[:L, 0:128], hn[:L, 0:128], hn[:L, 128:256], ALU.add, ["hn"], ["hn"])
                          ACT(hn[:L, 0:128], hn[:L, 0:128], AF.Exp, ["hn"], ["hn"], scale=-1.0)
                          TT("dve", hn[:L, 0:128], hn[:L, 0:128], ytm[:L, 128:256], ALU.mult, ["hn", "ytm"], ["hn"])
                          TT("dve", Gml[:L, ti, cg * 128:(cg + 1) * 128], hn[:L, 0:128], mlnwbc[:L, cg * 128:(cg + 1) * 128],
                             ALU.mult, ["hn", "mlnwbc"], ["Gml"])
                      proj_tm(wt1, w1, 0, 256, t, ev_g)
              for hp in range(2):
                  wt1, w1 = load_w(wpk_d[ly, 16 + hp])
                  for hh in range(2):
                      h = hp * 2 + hh
                      proj_fm(wt1, w1, hh * 128, 128,
                              lambda ps, pid, g0, n, h=h: CP("act", qT[:, h, g0:g0 + n], ps[:, :n], [pid], ["qT"]))
              for hp in range(2):
                  wt1, w1 = load_w(wpk_d[ly, 18 + hp])
                  for hh in range(2):
                      h = hp * 2 + hh
                      proj_fm(wt1, w1, hh * 128, 128,
                              lambda ps, pid, g0, n, h=h: ACT(kT[:, h, g0:g0 + n], ps[:, :n], AF.Copy, [pid], ["kT"],
                                                             scale=128.0 ** -0.5))
                  for t in tiles:
                      L, ti = t["L"], t["ti"]
                      proj_tm(wt1, w1, 0, 256, t,
                              lambda ps, pid, L=L, ti=ti, hp=hp: ACT(ktm[:L, ti, hp * 256:(hp + 1) * 256], ps[:L, :256],
                                                                    AF.Copy, [pid], ["ktm"], scale=128.0 ** -0.5))
              for hp in range(2):
                  wt1, w1 = load_w(wpk_d[ly, 20 + hp])
                  for t in tiles:
                      L, ti = t["L"], t["ti"]
                      proj_tm(wt1, w1, 0, 256, t,
                              lambda ps, pid, L=L, ti=ti, hp=hp: CP("dve", vaug[:L, ti, 2 * hp:2 * hp + 2, 0:128],
                                                                   ps[:L, :256].rearrange("p (h v) -> p h v", h=2),
                                                                   [pid], ["vaug"]))
              gi, gf, gb, gm, ga, gc, gcv, gwg, genm, gmp = gr
              glf = gf
              proj_fm(wsmb, "wsmb", 0, 8,
                      lambda ps, pid, g0, n: TS("dve", gi[:, g0:g0 + n], ps[:8, :n], g8[:, 0:1], None, ALU.add, None,
                                                [pid, "g8"], ["gr0"]))
              proj_fm(wsmb, "wsmb", 8, 8,
                      lambda ps, pid, g0, n: ACT(gf[:, g0:g0 + n], ps[:8, :n], AF.Exp, [pid, "g8n"], ["gr1"],
                                                 bias=g8n[:, 1:2], scale=-1.0))
              ACT(glf[:, :N], gf[:, :N], AF.Ln, ["gr1", "gr2"], ["gr1", "gr2"], bias=1.0)
              TS("dve", glf[:, :N], glf[:, :N], -1.0, None, ALU.mult, None, ["gr2"], ["gr2"])
              SCAN(gb[:, 0:NP], k8("keepP"), glf[:, 0:NP], 0.0, ALU.mult, ALU.add, ["c8", "gr2"], ["gr3"])
              SCAN(gm[:, 0:NP], glf[:, 0:NP], gi[:, 0:NP], mcar[:, ly:ly + 1], ALU.add, ALU.max, ["gr2", "gr0", "mcar"], ["gr4"])
              CP("dve", gmp[:, 1:NP], gm[:, 0:NP - 1], ["gr4"], ["gr10"])
              CP("dve", gmp[:, 0:1], mcar[:, ly:ly + 1], ["mcar"], ["gr10"])
              CP("dve", mcar[:, ly:ly + 1], gm[:, NP - 1:NP], ["gr4"], ["mcar"])
              m0 = small_o[:8, 0:NSQ]
              if has_s:
                  sl_ = slice(NP, NP + LS)
                  DMA(m0, sm_d[ly], (), ["small_o"])
                  SCAN(gb[:, sl_], k8("keepS"), glf[:, sl_], 0.0, ALU.mult, ALU.add, ["c8", "gr2"], ["gr3"])
                  TT("dve", gmp[:, sl_], glf[:, sl_], k8("keepS"), ALU.mult, ["gr2", "c8"], ["gr10"])
                  TT("dve", gmp[:, sl_], gmp[:, sl_], k8("negS"), ALU.add, ["gr10", "c8"], ["gr10"])
                  CP("dve", gwg[:, sl_], gi[:, sl_], ["gr0"], ["gr8"])
                  lf0 = glf[:, sl_].rearrange("p (b t) -> p b t", t=4)[:, :, 0]
                  i0 = gwg[:, sl_].rearrange("p (b t) -> p b t", t=4)[:, :, 0]
                  TT("dve", small_o[:8, 64:64 + NSQ], lf0, m0, ALU.add, ["gr2", "small_o"], ["small_o"])
                  TT("dve", i0, i0, small_o[:8, 64:64 + NSQ], ALU.max, ["gr8", "small_o"], ["gr8"])
                  SCAN(gm[:, sl_], gmp[:, sl_], gwg[:, sl_], 0.0, ALU.add, ALU.max, ["gr10", "gr8"], ["gr4"])
              TT("dve", gc[:, :N], gb[:, :N], gm[:, :N], ALU.subtract, ["gr3", "gr4"], ["gr6"])
              TT("dve", ga[:, :N], gi[:, :N], gb[:, :N], ALU.subtract, ["gr0", "gr3"], ["gr5"])
              mprev_p = gmp[:, 0:NP].rearrange("p (c t) -> p c t", t=128)[:, :, 0]
              TT("dve", gcv[:, 0:NP].rearrange("p (c t) -> p c t", t=128), gc[:, 0:NP].rearrange("p (c t) -> p c t", t=128),
                 mprev_p.unsqueeze(2).broadcast_to([8, NCH, 128]), ALU.add, ["gr6", "gr10"], ["gr7"])
              if has_s:
                  TT("dve", gcv[:, sl_].rearrange("p (b t) -> p b t", t=4), gc[:, sl_].rearrange("p (b t) -> p b t", t=4),
                     m0.unsqueeze(2).broadcast_to([8, NSQ, 4]), ALU.add, ["gr6", "small_o"], ["gr7"])
              for kind in (("p", "s") if has_s else ("p",)):
                  ncn = NCH if kind == "p" else NSQ
                  T_ = 128 if kind == "p" else 4
                  base = 0 if kind == "p" else NP
                  gcst = gsm[:, 0:ncn]
                  TT("dve", gcst, chunk_ends(gb, kind), chunk_ends(gm, kind), ALU.subtract, ["gr3", "gr4"], ["gsm"])
                  seg = slice(base, base + ncn * T_)
                  TT("dve", gwg[:, seg].rearrange("p (c t) -> p c t", t=T_), ga[:, seg].rearrange("p (c t) -> p c t", t=T_),
                     gcst.unsqueeze(2).broadcast_to([8, ncn, T_]), ALU.add, ["gr5", "gsm"], ["gr8"])
                  mpv = mprev_p if kind == "p" else m0
                  TT("dve", small_o[:8, 128:128 + ncn], gcst, mpv, ALU.add, ["gsm", "gr10", "small_o"], ["small_o"])
                  ACT(small_o[:8, 128:128 + ncn], small_o[:8, 128:128 + ncn], AF.Exp, ["small_o"], ["small_o"])
                  bcast_rows(small_o[:8, 128:128 + ncn], "small_o", ncn, 0 if kind == "p" else 1)
              ACT(gwg[:, :N], gwg[:, :N], AF.Exp, ["gr8"], ["gr8"])
              ACT(genm[:, :N], gm[:, :N], AF.Exp, ["gr4"], ["gr9"], scale=-1.0)
              TS("dve", ga[:, :N], ga[:, :N], pm8, opm8, ALU.mult, ALU.add, ["gr5", "c8"], ["gr5"])
              TS("dve", gc[:, :N], gc[:, :N], opm8, pm8, ALU.mult, ALU.add, ["gr6", "c8"], ["gr6"])
              TS("dve", gcv[:, :N], gcv[:, :N], opm8, pm8, ALU.mult, ALU.add, ["gr7", "c8"], ["gr7"])
              if has_s:
                  DMA(ntm[:], sn_d[ly], (), ["ntm"])
              if last_pass:
                  DMA(pm_o[ly].rearrange("(h o) -> h o", o=1), mcar[0:4, ly:ly + 1], ["mcar"], ())
              if has_s:
                  CP("dve", small_o[:8, 512:512 + NSQ], chunk_ends(gm, "s"), ["gr4"], ["small_o"])
                  DMA(om_o[ly], small_o[0:4, 512:512 + NSQ], ["small_o"], ())

              for t in tiles:
                  L, ti, col, kind = t["L"], t["ti"], t["col"], t["kind"]
                  to_tok_major(gwg, "gr8", genm, "gr9", t, 0, gmp, "gr10")
                  for h in range(4):
                      MM(PS["psS"][:L, h * L:(h + 1) * L], kT[:, h, col:col + L], qT[:, h, col:col + L], True, True,
                         ["kT", "qT"], ["psS"])
                  expo(ga, "gr5", gc, "gr6", gcv, "gr7", t, Wt, Wv)
                  TT("dve", SwT[:L, :4 * L], PS["psS"][:L, :4 * L], Wt[:L, :4 * L], ALU.mult, ["psS", "Wt"], ["SwT"])
                  TT("pool", qv[:, :4 * L].rearrange("p (h l) -> p h l", h=4), qT[:, :, col:col + L],
                     Wv[:, :4 * L].rearrange("p (h l) -> p h l", h=4), ALU.mult, ["qT", "Wv"], ["qv"])
                  TT("pool", kw[:L, :].rearrange("p (h d) -> p h d", h=4), ktm[:L, ti, :].rearrange("p (h d) -> p h d", h=4),
                     scal[:L, ti, 0, 0:4].unsqueeze(2).broadcast_to([L, 4, 128]), ALU.mult, ["ktm", "scal"], ["kw"])
                  for h in range(4):
                      pn = PS["psN0"] if h < 2 else PS["psN1"]
                      pnid = "psN0" if h < 2 else "psN1"
                      o_ = (h % 2) * 129
                      if kind == "s":
                          for b4 in range(0, NSQ, 4):
                              DMA(Cs[:, b4:b4 + 4, 0:128], sC_d[ly][b4:b4 + 4, h].rearrange("b d v -> d b v"), (), ["Cs"])
                          TR(PS["psU"][:, :NSQ], ntm[:NSQ, h * 128:(h + 1) * 128], identf[:NSQ, :NSQ], ["ntm", "c128"], ["psU"])
                          CP("dve", Cs[:, :, 128], PS["psU"][:, :NSQ], ["psU"], ["Cs"])
                          CP("act", Csb, Cs, ["Cs"], ["Csb"])
                          TT("pool", qvm, qv[:, h * L:(h + 1) * L].unsqueeze(1).broadcast_to([128, NSQ, L]),
                             smkb[:, :].rearrange("p (b l) -> p b l", b=NSQ), ALU.mult, ["qv", "smkb"], ["qvm"])
                      MM(pn[:L, o_:o_ + 129], SwT[:L, h * L:(h + 1) * L], vaug[:L, ti, h, :], True, False,
                         ["SwT", "vaug"], [pnid])
                      if kind == "p":
                          MM(pn[:L, o_:o_ + 129], qv[:, h * L:(h + 1) * L], Cbf[:, h, :], False, True, ["qv", "Cbf"], [pnid])
                      else:
                          for b in range(NSQ):
                              MM(pn[:L, o_:o_ + 129], qvm[:, b, :], Csb[:, b, :], False, b == NSQ - 1,
                                 ["qvm", "Csb"], [pnid])
                      kwh = kw[:L, h * 128:(h + 1) * 128]
                      if kind == "p":
                          psu, psuid = next_ps2()
                          MM(psu[:, 0:129], kwh, vaug[:L, ti, h, :], True, True, ["kw", "vaug"], [psuid])
                          STT("dve", Caug[:, ly, h, :], Caug[:, ly, h, :], bcw[:, 0, h, ti:ti + 1], psu[:, 0:129],
                              ALU.mult, ALU.add, ["Caug", "bcw", psuid], ["Caug"])
                      else:
                          b0 = 0
                          while b0 < NSQ:
                              nb = min(3, NSQ - b0)
                              TT("pool", vm[:L, 0:nb, :], vaug[:L, ti, h, :].unsqueeze(1).broadcast_to([L, nb, 129]),
                                 smT[:L, b0:b0 + nb].unsqueeze(2).broadcast_to([L, nb, 129]), ALU.mult, ["vaug", "c128"], ["vm"])
                              MM(PS["psU"][:, 0:nb * 129], kwh, vm[:L, 0:nb, :], True, True, ["kw", "vm"], ["psU"])
                              TT("dve", Cs[:, b0:b0 + nb, :], Cs[:, b0:b0 + nb, :],
                                 bcw[:, 1, h, b0:b0 + nb].unsqueeze(2).broadcast_to([128, nb, 129]), ALU.mult,
                                 ["Cs", "bcw"], ["Cs"])
                              TT("dve", Cs[:, b0:b0 + nb, :], Cs[:, b0:b0 + nb, :],
                                 PS["psU"][:, 0:nb * 129].rearrange("p (b v) -> p b v", b=nb), ALU.add, ["Cs", "psU"], ["Cs"])
                              b0 += nb
                          for b4 in range(0, NSQ, 4):
                              DMA(oC_o[ly][b4:b4 + 4, h].rearrange("b d v -> d b v"), Cs[:, b4:b4 + 4, 0:128], ["Cs"], ())
                          CP("dve", ncol[:, :NSQ], Cs[:, :, 128], ["Cs"], ["ncol"])
                          TR(PS["psU"][:NSQ, :128], ncol[:, :NSQ], identf, ["ncol", "c128"], ["psU"])
                          CP("dve", ntmo[:NSQ, h * 128:(h + 1) * 128], PS["psU"][:NSQ, :128], ["psU"], ["ntmo"])
                  if kind == "p":
                      CP("act", Cbf[:], Caug[:, ly], ["Caug"], ["Cbf"])
                  for half in range(2):
                      pn = PS["psN0"] if half == 0 else PS["psN1"]
                      pnid = "psN0" if half == 0 else "psN1"
                      den = pn[:L, 0:258].rearrange("p (h v) -> p h v", h=2)[:, :, 128]
                      ACT(tm4[:L, 2 * half:2 * half + 2], den, AF.Abs, [pnid], ["tm4"])
                  TT("dve", tm4[:L, 0:4], tm4[:L, 0:4], scal[:L, ti, 0, 4:8], ALU.max, ["tm4", "scal"], ["tm4"])
                  RECIP(tm4[:L, 0:4], tm4[:L, 0:4], ["tm4"], ["tm4"])
                  for half in range(2):
                      pn = PS["psN0"] if half == 0 else PS["psN1"]
                      pnid = "psN0" if half == 0 else "psN1"
                      CP("act", hn[:L, half * 256:(half + 1) * 256].rearrange("p (h v) -> p h v", h=2),
                         pn[:L, 0:258].rearrange("p (h v) -> p h v", h=2)[:, :, 0:128], [pnid], ["hn"])
                  TT("pool", hsq[:L, :], hn[:L, :], hn[:L, :], ALU.mult, ["hn"], ["hsq"])
                  RSUM(tm4[:L, 4:8], hn[:L, :].rearrange("p (h v) -> p h v", h=4), ["hn"], ["tm4"])
                  RSUM(tm4[:L, 8:12], hsq[:L, :].rearrange("p (h v) -> p h v", h=4), ["hsq"], ["tm4"])
                  TS("dve", tm4[:L, 4:8], tm4[:L, 4:8], 1.0 / 128, None, ALU.mult, None, ["tm4"], ["tm4"])
                  TS("dve", tm4[:L, 8:12], tm4[:L, 8:12], 1.0 / 128, None, ALU.mult, None, ["tm4"], ["tm4"])
                  TT("dve", tm4[:L, 12:16], tm4[:L, 4:8], tm4[:L, 4:8], ALU.mult, ["tm4"], ["tm4"])
                  TT("dve", tm4[:L, 8:12], tm4[:L, 8:12], tm4[:L, 12:16], ALU.subtract, ["tm4"], ["tm4"])
                  TT("dve", tm4[:L, 12:16], tm4[:L, 0:4], tm4[:L, 0:4], ALU.mult, ["tm4"], ["tm4"])
                  TT("dve", tm4[:L, 8:12], tm4[:L, 8:12], tm4[:L, 12:16], ALU.mult, ["tm4"], ["tm4"])
                  TS("dve", tm4[:L, 8:12], tm4[:L, 8:12], 0.0, None, ALU.max, None, ["tm4"], ["tm4"])
                  ACT(tm4[:L, 8:12], tm4[:L, 8:12], AF.Ln, ["tm4"], ["tm4"], bias=EPS)
                  ACT(tm4[:L, 8:12], tm4[:L, 8:12], AF.Exp, ["tm4"], ["tm4"], scale=-0.5)
                  TT("dve", tm4[:L, 8:12], tm4[:L, 8:12], tm4[:L, 0:4], ALU.mult, ["tm4"], ["tm4"])
                  hn3 = hn[:L, :].rearrange("p (h v) -> p h v", h=4)
                  TT("dve", hn3, hn3, tm4[:L, 4:8].unsqueeze(2).broadcast_to([L, 4, 128]), ALU.subtract, ["hn", "tm4"], ["hn"])
                  TT("dve", hn3, hn3, tm4[:L, 8:12].unsqueeze(2).broadcast_to([L, 4, 128]), ALU.mult, ["hn", "tm4"], ["hn"])
                  TT("pool", ytm[:L, :], hn[:L, :], Gml[:L, ti, :], ALU.mult, ["hn", "Gml"], ["ytm"])
                  for h in range(4):
                      TR(PS["psU"][:, h * 128:h * 128 + L], ytm[:L, h * 128:(h + 1) * 128], identf[:L, :L], ["ytm", "c128"], ["psU"])
                  CP("act", mixs[:, 0:4, col:col + L], PS["psU"][:, :].rearrange("p (j c) -> p j c", j=4)[:, :, :L],
                     ["psU"], ["mixs"])
              if has_s:
                  DMA(on_o[ly], ntmo[:], ["ntmo"], ())
              if last_pass:
                  DMA(pC_o[ly].rearrange("h d v -> d h v"), Caug[:, ly, :, 0:128], ["Caug"], ())
                  CP("dve", ncol[:, 0:4], Caug[:, ly, :, 128], ["Caug"], ["ncol"])
                  TR(PS["psU"][:4, :128], ncol[:, 0:4], identf, ["ncol", "c128"], ["psU"])
                  CP("dve", small_o[:4, 600:728], PS["psU"][:4, :128], ["psU"], ["small_o"])
                  DMA(pn_o[ly], small_o[:4, 600:728], ["small_o"], ())
              checkpoint()
              wout_part(mixs, 0)
              checkpoint()

              stage_reset()
              fa = [af([128, NMAX]) for _ in range(2)]
              xtm = af([128, NTM, 512])
              gzs = af([128, NTM, 512])
              gr = [af([8, NMAX]) for _ in range(6)]
              Wt = af([128, 512])
              Wv = af([128, 512])
              hn = af([128, 512])
              ytm = af([128, 256])
              Sin = af([64, NSQ, 128])
              cvin = af([NSQ * 3, 128])
              cvout = af([NSQ * 3, 128])
              ssdnwbc = af([128, BR])
              DMA(ssdnwbc, ssdnw_d[ly].partition_broadcast(128), (), ["ssdnwbc"])
              Btm = ab([128, NTM, 2, 128])
              BTb = ab([128, 2, NMAX])
              CTb = ab([128, 2, NMAX])
              Mh = ab([128, 512])
              Ct = ab([128, 512])
              Ctm = ab([128, NSQ, LS])
              xdt = ab([128, 256])
              xw = ab([128, 256])
              Ssb = ab([128, NSQ, 64])
              Bm = ab([128, NSQ, 128])
              mixs = ab([128, 4, NMAX])
              junk = ab([128, 256])
              CP("act", STb[:], STs[:, ly], ["STs"], ["STb"])
              for j in range(8):
                  if j % 2 == 0:
                      wt1, w1 = load_w(wpk_d[ly, 22 + j // 2])
                  conv_load_hist(8 + j)
                  if has_s:
                      sample_hist_tile(cvin, 4, j, sscv_d[ly])
                  proj_fm(wt1, w1, (j % 2) * 128, 128, lambda ps, pid, g0, n: fill_ub(ps, pid, g0, n))
                  conv_apply(fa[0], "fa0", 4, 4 * j, 32 + j)
                  conv_save_hist(8 + j)
                  if has_s:
                      sample_hist_out_tile(cvout, 4, j, oscv_o[ly])
                  ACT(fa[1][:, :N], fa[0][:, :N], AF.Exp, ["fa0"], ["fa1"], scale=-1.0)
                  sigmoid_from_exp(fa[1][:, :N], ["fa1"], ["fa1"])
                  TT("dve", fa[0][:, :N], fa[0][:, :N], fa[1][:, :N], ALU.mult, ["fa0", "fa1"], ["fa0"])
                  if j < 4:
                      for t in tiles:
                          L, ti, col = t["L"], t["ti"], t["col"]
                          TR(PS["psU"][:L, 0:128], fa[0][:, col:col + L], identf, ["fa0", "c128"], ["psU"])
                          CP("act", xtm[:L, ti, j * 128:(j + 1) * 128], PS["psU"][:L, 0:128], ["psU"], ["xtm"])
                  elif j < 6:
                      g = j - 4
                      CP("act", BTb[:, g, :N], fa[0][:, :N], ["fa0"], ["BTb"])
                      for t in tiles:
                          L, ti, col = t["L"], t["ti"], t["col"]
                          TR(PS["psU"][:L, 0:128], fa[0][:, col:col + L], identf, ["fa0", "c128"], ["psU"])
                          CP("act", Btm[:L, ti, g, :], PS["psU"][:L, 0:128], ["psU"], ["Btm"])
                  else:
                      g = j - 6
                      CP("act", CTb[:, g, :N], fa[0][:, :N], ["fa0"], ["CTb"])
              if last_pass:
                  prompt_hist_out(4, list(range(8, 16)), pscv_o[ly])
              for hp in range(2):
                  wt1, w1 = load_w(wpk_d[ly, 26 + hp])
                  for t in tiles:
                      L, ti = t["L"], t["ti"]

                      def ev_sz(ps, pid, L=L, ti=ti, hp=hp):
                          CP("act", ytm[:L, 0:256], ps[:L, 0:256], [pid], ["ytm"])
                          ACT(hn[:L, 0:256], ytm[:L, 0:256], AF.Exp, ["ytm"], ["hn"], scale=-1.0)
                          sigmoid_from_exp(hn[:L, 0:256], ["hn"], ["hn"])
                          TT("dve", gzs[:L, ti, hp * 256:(hp + 1) * 256], hn[:L, 0:256], ytm[:L, 0:256], ALU.mult,
                             ["hn", "ytm"], ["gzs"])
                      proj_tm(wt1, w1, 0, 256, t, ev_sz)
              for g in range(2):
                  gdt, gda, gacs, gwr, grr, gscr = gr
                  gna = gda
                  proj_fm(wsmb, "wsmb", 16 + 8 * g, 8,
                          lambda ps, pid, g0, n, g=g: ACT(gdt[:, g0:g0 + n], ps[:8, :n], AF.Exp, [pid, "g8"], ["gr0"],
                                                          bias=g8[:, 2 + g:3 + g]))
                  ACT(gdt[:, :N], gdt[:, :N], AF.Ln, ["gr0"], ["gr0"], bias=1.0)
                  TS("dve", gda[:, :N], gdt[:, :N], g8n[:, 4 + g:5 + g], None, ALU.mult, None, ["gr0", "g8n"], ["gr1", "gr3"])
                  SCAN(gacs[:, 0:NP], k8("keepP"), gda[:, 0:NP], 0.0, ALU.mult, ALU.add, ["c8", "gr1", "gr3"], ["gr2"])
                  if has_s:
                      SCAN(gacs[:, NP:NP + LS], k8("keepS"), gda[:, NP:NP + LS], 0.0, ALU.mult, ALU.add, ["c8", "gr1"], ["gr2"])
                  for kind in (("p", "s") if has_s else ("p",)):
                      ncn = NCH if kind == "p" else NSQ
                      T_ = 128 if kind == "p" else 4
                      base = 0 if kind == "p" else NP
                      seg = slice(base, base + ncn * T_)
                      al = gsm[:, 0:ncn]
                      CP("dve", al, chunk_ends(gacs, kind), ["gr2"], ["gsm"])
                      TT("dve", gwr[:, seg].rearrange("p (c t) -> p c t", t=T_), al.unsqueeze(2).broadcast_to([8, ncn, T_]),
                         gacs[:, seg].rearrange("p (c t) -> p c t", t=T_), ALU.subtract, ["gsm", "gr2"], ["gr4"])
                      ACT(small_o[:8, 128:128 + ncn], al, AF.Exp, ["gsm"], ["small_o"])
                      bcast_rows(small_o[:8, 128:128 + ncn], "small_o", ncn, 0 if kind == "p" else 1)
                  ACT(gwr[:, :N], gwr[:, :N], AF.Exp, ["gr4"], ["gr4"])
                  TT("dve", gwr[:, :N], gwr[:, :N], gdt[:, :N], ALU.mult, ["gr4", "gr0"], ["gr4"])
                  TS("dve", gna[:, :N], gacs[:, :N], -1.0, None, ALU.mult, None, ["gr2", "gr1"], ["gr3", "gr1"])
                  TS("dve", gna[:, :N], gna[:, :N], pm8, opm8, ALU.mult, ALU.add, ["gr3", "c8"], ["gr3"])
                  TS("dve", grr[:, :N], gacs[:, :N], opm8, pm8, ALU.mult, ALU.add, ["gr2", "c8"], ["gr5"])
                  for t in tiles:
                      L, ti, col, kind = t["L"], t["ti"], t["col"], t["kind"]
                      to_tok_major(gdt, "gr0", gwr, "gr4", t, 1, gscr, "gr6")
                      MM(PS["psS"][:L, :L], BTb[:, g, col:col + L], CTb[:, g, col:col + L], True, True, ["BTb", "CTb"], ["psS"])
                      expo(gna, "gr3", grr, "gr5", grr, "gr5", t, Wt, Wv)
                      TT("dve", Mh[:L, :4 * L].rearrange("p (h l) -> p h l", h=4),
                         PS["psS"][:L, :L].unsqueeze(1).broadcast_to([L, 4, L]),
                         Wt[:L, :4 * L].rearrange("p (h l) -> p h l", h=4), ALU.mult, ["psS", "Wt"], ["Mh"])
                      TT("pool", Ct[:, :4 * L].rearrange("p (h l) -> p h l", h=4),
                         CTb[:, g, col:col + L].unsqueeze(1).broadcast_to([128, 4, L]),
                         Wv[:, :4 * L].rearrange("p (h l) -> p h l", h=4), ALU.mult, ["CTb", "Wv"], ["Ct"])
                      xg = xtm[:L, ti, g * 256:(g + 1) * 256].rearrange("p (h c) -> p h c", h=4)
                      TT("dve", xdt[:L, :].rearrange("p (h c) -> p h c", h=4), xg,
                         scal[:L, ti, 1, 0:4].unsqueeze(2).broadcast_to([L, 4, 64]), ALU.mult, ["xtm", "scal"], ["xdt"])
                      TT("dve", xw[:L, :].rearrange("p (h c) -> p h c", h=4), xg,
                         scal[:L, ti, 1, 4:8].unsqueeze(2).broadcast_to([L, 4, 64]), ALU.mult, ["xtm", "scal"], ["xw"])
                      if kind == "s":
                          TT("pool", Bm[:L, :, :], Btm[:L, ti, g, :].unsqueeze(1).broadcast_to([L, NSQ, 128]),
                             smT[:L, :].unsqueeze(2).broadcast_to([L, NSQ, 128]), ALU.mult, ["Btm", "c128"], ["Bm"])
                      for h in range(4):
                          hh = g * 4 + h
                          if kind == "s":
                              TT("pool", Ctm, Ct[:, h * L:(h + 1) * L].unsqueeze(1).broadcast_to([128, NSQ, L]),
                                 smkb[:, :].rearrange("p (b l) -> p b l", b=NSQ), ALU.mult, ["Ct", "smkb"], ["Ctm"])
                              for b4 in range(0, NSQ, 8):
                                  DMA(Sin[:, b4:b4 + 8, :], sS_d[ly][b4:b4 + 8, hh].rearrange("b p n -> p b n"), (), ["Sin"])
                              for bq in range(NSQ // 8):
                                  for b in range(8):
                                      TR(PS["psU"][:, b * 64:(b + 1) * 64], Sin[:, bq * 8 + b, :], identf[:64, :64],
                                         ["Sin", "c128"], ["psU"])
                                  CP("act", Ssb[:, bq * 8:(bq + 1) * 8, :],
                                     PS["psU"][:, :].rearrange("p (b c) -> p b c", b=8), ["psU"], ["Ssb"])
                          MM(PS["psN0"][:L, h * 64:(h + 1) * 64], Mh[:L, h * L:(h + 1) * L], xdt[:L, h * 64:(h + 1) * 64],
                             True, False, ["Mh", "xdt"], ["psN0"])
                          if kind == "p":
                              MM(PS["psN0"][:L, h * 64:(h + 1) * 64], Ct[:, h * L:(h + 1) * L],
                                 STb[:, hh * 64:(hh + 1) * 64], False, True, ["Ct", "STb"], ["psN0"])
                          else:
                              for b in range(NSQ):
                                  MM(PS["psN0"][:L, h * 64:(h + 1) * 64], Ctm[:, b, :], Ssb[:, b, :],
                                     False, b == NSQ - 1, ["Ctm", "Ssb"], ["psN0"])
                              for bq in range(NSQ // 4):
                                  MM(PS["psV"][:64, :512], xw[:L, h * 64:(h + 1) * 64],
                                     Bm[:L, bq * 4:(bq + 1) * 4, :], True, True, ["xw", "Bm"], ["psV"])
                                  TT("dve", Sin[:, bq * 4:(bq + 1) * 4, :], Sin[:, bq * 4:(bq + 1) * 4, :],
                                     bcw[:64, 1, h, bq * 4:(bq + 1) * 4].unsqueeze(2).broadcast_to([64, 4, 128]), ALU.mult,
                                     ["Sin", "bcw"], ["Sin"])
                                  TT("dve", Sin[:, bq * 4:(bq + 1) * 4, :], Sin[:, bq * 4:(bq + 1) * 4, :],
                                     PS["psV"][:64, :512].rearrange("p (b n) -> p b n", b=4), ALU.add, ["Sin", "psV"], ["Sin"])
                              for b4 in range(0, NSQ, 8):
                                  DMA(oS_o[ly][b4:b4 + 8, hh].rearrange("b p n -> p b n"), Sin[:, b4:b4 + 8, :], ["Sin"], ())
                      TT("dve", hn[:L, 0:256].rearrange("p (h c) -> p h c", h=4), xg,
                         Dbc[:L, g * 4:(g + 1) * 4].unsqueeze(2).broadcast_to([L, 4, 64]), ALU.mult, ["xtm", "Dbc"], ["hn"])
                      TT("dve", hn[:L, 0:256], hn[:L, 0:256], PS["psN0"][:L, 0:256], ALU.add, ["hn", "psN0"], ["hn"])
                      TT("dve", hn[:L, 0:256], hn[:L, 0:256], gzs[:L, ti, g * 256:(g + 1) * 256], ALU.mult, ["hn", "gzs"], ["hn"])
                      ACT(junk[:L, 0:256], hn[:L, 0:256], AF.Square, ["hn"], ["junk", "tm4"], accum=tm4[:L, 0:1])
                      ACT(tm4[:L, 1:2], tm4[:L, 0:1], AF.Ln, ["tm4"], ["tm4"], bias=EPS, scale=1.0 / 256)
                      ACT(tm4[:L, 2:3], tm4[:L, 1:2], AF.Exp, ["tm4"], ["tm4"], scale=-0.5)
                      STT("dve", ytm[:L, 0:256], hn[:L, 0:256], tm4[:L, 2:3], ssdnwbc[:L, g * 256:(g + 1) * 256],
                          ALU.mult, ALU.mult, ["hn", "tm4", "ssdnwbc"], ["ytm"])
                      for h2 in range(2):
                          TR(PS["psU"][:, h2 * 128:h2 * 128 + L], ytm[:L, h2 * 128:(h2 + 1) * 128], identf[:L, :L],
                             ["ytm", "c128"], ["psU"])
                      CP("act", mixs[:, 2 * g:2 * g + 2, col:col + L],
                         PS["psU"][:, 0:256].rearrange("p (j c) -> p j c", j=2)[:, :, :L], ["psU"], ["mixs"])
                      if kind == "p":
                          psu, psuid = next_ps2()
                          MM(psu[:, 0:256], Btm[:L, ti, g, :], xw[:L, :], True, True, ["Btm", "xw"], [psuid])
                          sg = STs[:, ly, g * 256:(g + 1) * 256]
                          TT("pool", sg.rearrange("p (h c) -> p h c", h=4), sg.rearrange("p (h c) -> p h c", h=4),
                             bcw[:, 0, :, ti:ti + 1].broadcast_to([128, 4, 64]), ALU.mult, ["STs", "bcw"], ["STs"])
                          TT("dve", sg, sg, psu[:, 0:256], ALU.add, ["STs", psuid], ["STs"])
                          CP("act", STb[:, g * 256:(g + 1) * 256], sg, ["STs"], ["STb"])
              if last_pass:
                  for q4 in range(4):
                      TR(PS["psU"][:, q4 * 128:(q4 + 1) * 128], STs[:, ly, q4 * 128:(q4 + 1) * 128], identf, ["STs", "c128"], ["psU"])
                  CP("dve", hn[:, :], PS["psU"][:, :], ["psU"], ["hn"])
                  DMA(pS_o[ly].rearrange("(q p) n -> p q n", p=128), hn[:, :].rearrange("p (q n) -> p q n", q=4), ["hn"], ())
              checkpoint()
              wout_part(mixs, 4)
              checkpoint()

              if ly == L2 - 1:
                  stage_reset()
                  xn32s = [af([128, D]), af([128, D])]
                  nwbc = af([128, D])
                  junks = [ab([128, D]), ab([128, D])]
                  DMA(nwbc, fnw_d.partition_broadcast(128), (), ["nwbc"])
                  for t in tiles:
                      L, ti = t["L"], t["ti"]
                      xn32, xid = rmsnorm_tile(t, None, xn32s, junks, nwbc)
                      dst = yp_o[t["tok"]:t["tok"] + 128, :] if t["kind"] == "p" else ys_o
                      DMA(dst, xn32[:L], [xid], ())
    except _Stop:
        pass

    print("NOPS", len(R.ops), "arena max words f32/bf16", amax, flush=True)
    block = st.enter_context(nc.Block())
    R.emit(nc, block, st)
    st.close()
    return nc


_CACHE = {}


def _host_layout(inputs, NSQ, c):
    f = np.ascontiguousarray
    b0 = c * NSQ
    L2 = DEPTH
    w_in = inputs["w_in"]
    m = {}
    m["xp"] = f(inputs["x_prompt"][c % inputs["x_prompt"].shape[0]])
    m["xs"] = f(inputs["x_sample"][b0:b0 + NSQ].reshape(NSQ * 4, D))
    m["sC"] = f(inputs["state_mlstm_C"][:, b0:b0 + NSQ])
    m["sn"] = f(inputs["state_mlstm_n"][:, b0:b0 + NSQ].reshape(L2, NSQ, 512))
    smt = np.transpose(inputs["state_mlstm_m"][:, b0:b0 + NSQ], (0, 2, 1))
    m["sm"] = f(np.concatenate([smt, smt], axis=1))
    m["sS"] = f(inputs["state_ssd"][:, b0:b0 + NSQ])
    m["sscv"] = f(inputs["state_ssd_conv"][:, b0:b0 + NSQ].reshape(L2, NSQ * 3, 1024))
    m["sccv"] = f(inputs["state_sconv_conv"][:, b0:b0 + NSQ].reshape(L2, NSQ * 2, 512))
    m["srh"] = f(inputs["state_rglru_h"][:, b0:b0 + NSQ])
    m["srcv"] = f(inputs["state_rglru_conv"][:, b0:b0 + NSQ].reshape(L2, NSQ * 3, 512))
    return m


def _shared_layout(inputs, NCH, NSQ):
    f = np.ascontiguousarray
    L2 = DEPTH
    w_in = inputs["w_in"]
    s = {}
    def cols(o, n):
        return list(range(o, o + n))
    chunks = []
    for j in range(4):
        chunks.append(cols(OSB + j * 128, 128) + cols(OSC + j * 128, 128))
        chunks.append(cols(OSH + j * 128, 128) + cols(OSCZ + j * 128, 128))
    for j in range(4):
        chunks.append(cols(ORX + j * 128, 128) + cols(ORZ + j * 128, 128))
    for cg in range(4):
        chunks.append(cols(OO + cg * 128, 128) + cols(OZ + cg * 128, 128))
    for base in (OQ, OKK, OV):
        for hp in range(2):
            chunks.append(cols(base + hp * 256, 256))
    for jp in range(4):
        chunks.append(cols(OXBC + jp * 256, 256))
    for hp in range(2):
        chunks.append(cols(OSZ + hp * 256, 256))
    assert len(chunks) == 28

    def pack(W):
        K = W.shape[0] // 128
        return np.ascontiguousarray(W.reshape(K, 128, W.shape[1]).transpose(1, 0, 2).reshape(128, K * W.shape[1]))
    wpk = np.empty((L2, 28, 128, 4096), np.float32)
    for ly in range(L2):
        for k, cc in enumerate(chunks):
            wpk[ly, k] = pack(w_in[ly][:, cc])
    s["wpk"] = wpk
    wopk = np.empty((L2, 4, 8, 128, 1024), np.float32)
    for ly in range(L2):
        for g in range(4):
            for dg in range(8):
                wopk[ly, g, dg] = pack(inputs["w_out"][ly][g * 512:(g + 1) * 512, dg * 256:(dg + 1) * 256])
    s["wopk"] = wopk
    wi = w_in[:, :, OI:OI + 4]
    wf = w_in[:, :, OF:OF + 4]
    wd0 = w_in[:, :, ODT:ODT + 4]
    wd1 = w_in[:, :, ODT + 4:ODT + 8]
    wsm = np.concatenate([wi, wi, wf, wf, wd0, wd0, wd1, wd1], axis=2)
    s["wsm"] = np.stack([pack(wsm[ly]) for ly in range(L2)], axis=0)
    wabd = np.zeros((L2, 8, 128, 128), np.float32)
    for ly in range(L2):
        for j in range(4):
            for k in range(2):
                blk = 2 * j + k
                wabd[ly, j, k * 64:(k + 1) * 64, k * 64:(k + 1) * 64] = inputs["rg_wa"][ly, blk]
                wabd[ly, 4 + j, k * 64:(k + 1) * 64, k * 64:(k + 1) * 64] = inputs["rg_wx"][ly, blk]
    s["wabd"] = f(np.transpose(wabd, (0, 2, 1, 3)).reshape(L2, 128, 1024))
    cvec = np.zeros((L2, 128, 84), np.float32)

    def fm(v, nt):
        return v.reshape(nt, 128).T
    for ly in range(L2):
        cw = inputs["ssd_conv_w"][ly]
        cvec[ly, :, 0:32] = np.transpose(cw.reshape(4, 8, 128), (2, 1, 0)).reshape(128, 32)
        cvec[ly, :, 32:40] = fm(inputs["ssd_conv_b"][ly], 8)
        sw = inputs["sc_conv_w"][ly]
        cvec[ly, :, 40:52] = np.transpose(sw.reshape(3, 4, 128), (2, 1, 0)).reshape(128, 12)
        rw = inputs["rg_conv_w"][ly]
        cvec[ly, :, 52:68] = np.transpose(rw.reshape(4, 4, 128), (2, 1, 0)).reshape(128, 16)
        cvec[ly, :, 68:72] = fm(inputs["rg_conv_b"][ly], 4)
        cvec[ly, :, 72:76] = fm(inputs["rg_ba"][ly], 4)
        cvec[ly, :, 76:80] = fm(inputs["rg_bx"][ly], 4)
        cvec[ly, :, 80:84] = fm(inputs["rg_lambda"][ly], 4)
    s["cvec"] = cvec
    g8 = np.zeros((L2, 8, 8), np.float32)
    for ly in range(L2):
        d2 = lambda v: np.concatenate([v, v])
        g8[ly, :, 0] = d2(inputs["ml_i_bias"][ly])
        g8[ly, :, 1] = d2(inputs["ml_f_bias"][ly])
        g8[ly, :, 2] = d2(inputs["ssd_dt_bias"][ly][0:4])
        g8[ly, :, 3] = d2(inputs["ssd_dt_bias"][ly][4:8])
        g8[ly, :, 4] = d2(inputs["ssd_A_log"][ly][0:4])
        g8[ly, :, 5] = d2(inputs["ssd_A_log"][ly][4:8])
    s["g8"] = g8
    for k in ("norm_w", "ml_norm_w", "ssd_norm_w", "ssd_D", "final_norm_w"):
        s[k] = f(inputs[k])
    C128h, _, C8h, _ = host_consts(NCH, NSQ)
    s["c128"] = C128h
    s["c8"] = C8h
    return s


def run(inputs, NCH, n_cores=8):
    inputs = {k: np.asarray(v, dtype=np.float32) for k, v in inputs.items()}
    BP, TP = inputs["x_prompt"].shape[0], inputs["x_prompt"].shape[1]
    BS = inputs["x_sample"].shape[0]
    NSQ = BS // n_cores
    key = (TP, NCH, NSQ)
    if key not in _CACHE:
        _CACHE[key] = build(TP, NCH, NSQ)
    nc = _CACHE[key]
    shared = _shared_layout(inputs, NCH, NSQ)
    in_maps = []
    for c in range(n_cores):
        m = _host_layout(inputs, NSQ, c)
        m.update(shared)
        in_maps.append(m)
    res = run_bass_kernel_spmd(nc, in_maps, core_ids=list(range(n_cores)))
    r = res.results
    L2 = DEPTH
    cat = lambda k, ax: np.concatenate([r[c][k] for c in range(n_cores)], axis=ax)
    stk = lambda k: np.stack([r[c][k] for c in range(BP)], axis=1)
    y_prompt = np.stack([r[c]["yp"] for c in range(BP)], axis=0)
    y_sample = cat("ys", 0).reshape(BS, 4, D)
    p_C = stk("pC")
    p_n = stk("pn")
    p_m = stk("pm")
    p_ssd = stk("pS").reshape(L2, BP, 8, 64, 128)
    p_ssd_conv = stk("pscv")
    p_sc_conv = stk("pccv")
    p_rg_h = stk("prh").reshape(L2, BP, 512)
    p_rg_conv = stk("prcv")
    s_C = cat("oC", 1)
    s_n = cat("on", 1).reshape(L2, BS, 4, 128)
    s_m = np.transpose(cat("om", 2), (0, 2, 1))
    s_ssd = cat("oS", 1)
    s_ssd_conv = cat("oscv", 1).reshape(L2, BS, 3, 1024)
    s_sc_conv = cat("occv", 1).reshape(L2, BS, 2, 512)
    s_rg_h = cat("orh", 1)
    s_rg_conv = cat("orcv", 1).reshape(L2, BS, 3, 512)
    outs = (y_prompt, y_sample, p_C, p_n, p_m, p_ssd, p_ssd_conv, p_sc_conv, p_rg_h, p_rg_conv,
            s_C, s_n, s_m, s_ssd, s_ssd_conv, s_sc_conv, s_rg_h, s_rg_conv)
    return tuple(np.ascontiguousarray(o, dtype=np.float32) for o in outs)


def kernel(**inputs):
    return run(inputs, NCH=4)
```

```python
from contextlib import ExitStack

import numpy as np

import concourse.bass as bass
import concourse.mybir as mybir
from concourse.bass_utils import run_bass_kernel_spmd

F32 = mybir.dt.float32
BF16 = mybir.dt.bfloat16
ALU = mybir.AluOpType
AF = mybir.ActivationFunctionType

D = 2048
BR = 512
PO = 7184
DEPTH = 2
EPS = 1e-6
NEG = -30000.0
OQ, OKK, OV, OO, OZ, OI, OF = 0, 512, 1024, 1536, 2048, 2560, 2564
OSZ, OXBC, ODT = 2568, 3080, 4104
OSB, OSC, OSH, OSCZ = 4112, 4624, 5136, 5648
ORX, ORZ = 6160, 6672

ENGS = ("pe", "act", "dve", "pool", "sp")
NDMASEM = 8


class Op:
    __slots__ = ("eng", "fn", "reads", "writes", "dma", "deps", "sig", "tick", "dsem", "dcount", "idx", "prev")

    def __init__(self, eng, fn, reads, writes, dma):
        self.eng, self.fn, self.reads, self.writes, self.dma = eng, fn, reads, writes, dma
        self.deps = []
        self.sig = False
        self.tick = 0
        self.dsem = -1
        self.dcount = 0
        self.prev = 0


class _Stop(Exception):
    pass


class Rec:
    def __init__(self):
        self.ops = []
        self.last_w = {}
        self.readers = {}
        import os as _os
        self.maxops = int(_os.environ.get("KOPS", "100000000"))
        self.fence_ops = []
        self.pending = set()
        self.fence_from = 0

    PERSIST = frozenset(("c128", "c8", "identb", "maskPb", "maskSb", "smkb", "xnT", "st4", "st4b", "tm4", "Dbc", "cvec", "g8",
                         "g8n", "nsp8", "nba", "wb0", "wb1", "wb2", "wsmb", "wabd", "Caug", "Cbf", "mcar", "STs", "STb", "cvcar",
                         "hcar", "ubp", "ubs", "gsm", "RM", "RMv", "scal", "bcw", "ncol", "cvtmp", "h0s", "small_o"))

    def touches_arena(self, op):
        for b in op.reads + op.writes:
            if b in self.PERSIST or b.startswith("ps") or b.startswith("xres"):
                continue
            return True
        return False

    def fence(self):
        last = {}
        dmas = []
        for op in self.ops[self.fence_from:]:
            if not self.touches_arena(op):
                continue
            if op.dma:
                dmas.append(op.idx)
            else:
                last[op.eng] = op.idx
        self.fence_ops = sorted(set(self.fence_ops) | set(last.values()) | set(dmas)) if self.pending else \
            sorted(set(last.values()) | set(dmas))
        self.pending = set(ENGS)
        self.fence_from = len(self.ops)

    def add(self, eng, fn, reads=(), writes=(), dma=False):
        if len(self.ops) >= self.maxops:
            return None
        writes = tuple(writes) + tuple(b for b in reads if isinstance(b, str) and b.startswith("ps") and b not in writes)
        op = Op(eng, fn, tuple(reads), tuple(writes), dma)
        op.idx = len(self.ops)
        deps = set()
        if eng in self.pending and self.touches_arena(op):
            deps |= set(self.fence_ops)
            self.pending.discard(eng)
        for b in op.reads:
            w = self.last_w.get(b)
            if w is not None:
                deps.add(w)
        for b in op.writes:
            w = self.last_w.get(b)
            if w is not None:
                deps.add(w)
            for r in self.readers.get(b, ()):
                deps.add(r)
        deps.discard(op.idx)
        op.deps = sorted(deps)
        for b in op.reads:
            self.readers.setdefault(b, []).append(op.idx)
        for b in op.writes:
            self.last_w[b] = op.idx
            self.readers[b] = []
        self.ops.append(op)
        return op

    def emit(self, nc, block, stack):
        ops = self.ops
        for op in ops:
            for d in op.deps:
                p = ops[d]
                if p.dma:
                    continue
                if p.eng == "pe" and op.eng == "pe" and not op.dma:
                    continue
                p.sig = True
        sem = {e: stack.enter_context(nc.semaphore("s_" + e)) for e in ("pe", "act", "dve", "pool")}
        dsem = {q: [stack.enter_context(nc.semaphore("d_%s%d" % (q, i))) for i in range(NDMASEM)]
                for q in ("sp", "pool")}
        cnt = {e: 0 for e in sem}
        dn = {q: 0 for q in dsem}
        dc = {q: [0] * NDMASEM for q in dsem}
        for op in ops:
            if op.dma:
                q = op.eng
                i = dn[q] % NDMASEM
                dn[q] += 1
                op.dsem = i
                op.prev = dc[q][i]
                dc[q][i] += 1
                op.dcount = dc[q][i]
            elif op.sig:
                cnt[op.eng] += 1
                op.tick = cnt[op.eng]
        per = {e: [o for o in ops if o.eng == e] for e in ENGS}

        def run(eng_name, e):
            waited = {}

            def need(key, s, v):
                if waited.get(key, 0) >= v:
                    return
                waited[key] = v
                e.wait_ge(s, v)

            for op in per[eng_name]:
                for d in op.deps:
                    p = ops[d]
                    if p.dma:
                        need(("d", p.eng, p.dsem), dsem[p.eng][p.dsem], 16 * p.dcount)
                    else:
                        if p.eng == "pe" and eng_name == "pe" and not op.dma:
                            continue
                        need(("c", p.eng), sem[p.eng], p.tick)
                if op.dma:
                    if op.prev > 0:
                        need(("d", op.eng, op.dsem), dsem[op.eng][op.dsem], 16 * op.prev)
                    op.fn(e).then_inc(dsem[op.eng][op.dsem], 16)
                else:
                    ins = op.fn(e)
                    if op.sig:
                        ins.then_inc(sem[op.eng], 1)
            if eng_name in dsem:
                for i in range(NDMASEM):
                    if dc[eng_name][i] > 0:
                        need(("d", eng_name, i), dsem[eng_name][i], 16 * dc[eng_name][i])

        @block.tensor
        def _(e):
            run("pe", e)

        @block.scalar
        def _(e):
            run("act", e)

        @block.vector
        def _(e):
            run("dve", e)

        @block.gpsimd
        def _(e):
            run("pool", e)

        @block.sync
        def _(e):
            run("sp", e)


def host_consts(NCH, NSQ):
    LS = 4 * NSQ
    NP = 128 * NCH
    c128 = {}
    c128["ident"] = np.eye(128, dtype=np.float32)
    s = np.arange(128)[:, None]
    l = np.arange(128)[None, :]
    mp = np.where(s <= l, 0.0, NEG).astype(np.float32)
    c128["maskP"] = np.tile(mp[:, None, :], (1, 4, 1)).reshape(128, 512)
    ms = np.full((128, 128), NEG, np.float32)
    sl = np.arange(LS)
    okm = (sl[:, None] // 4 == sl[None, :] // 4) & (sl[:, None] <= sl[None, :])
    ms[:LS, :LS] = np.where(okm, 0.0, NEG)
    c128["maskS"] = np.tile(ms[:, None, :LS], (1, 4, 1)).reshape(128, 4 * LS)
    smk = (np.arange(NSQ)[:, None] == (sl[None, :] // 4)).astype(np.float32)
    c128["smk"] = np.tile(smk.reshape(1, NSQ * LS), (128, 1))
    smT = np.zeros((128, NSQ), np.float32)
    smT[:LS] = smk.T
    c128["smT"] = smT
    C128 = np.concatenate([c128[k] for k in ("ident", "maskP", "maskS", "smk", "smT")], axis=1)
    offs128 = {}
    o = 0
    for k in ("ident", "maskP", "maskS", "smk", "smT"):
        offs128[k] = (o, c128[k].shape[1])
        o += c128[k].shape[1]
    c8 = {}
    keepP = np.ones((8, NP), np.float32)
    keepP[:, ::128] = 0
    c8["keepP"] = keepP
    keepS = np.ones((8, LS), np.float32)
    keepS[:, ::4] = 0
    c8["keepS"] = keepS
    negS = np.zeros((8, LS), np.float32)
    negS[:, ::4] = -1e30
    c8["negS"] = negS
    pm = np.array([1, 1, 1, 1, 0, 0, 0, 0], np.float32)[:, None]
    c8["pm"] = pm
    c8["opm"] = 1 - pm
    mf = np.zeros((8, 4, 128), np.float32)
    for k in range(8):
        mf[k, k % 4, :] = 1
    c8["maskfull"] = mf.reshape(8, 512)
    m8 = np.zeros((8, 4), np.float32)
    for k in range(4):
        m8[k, k] = 1
    c8["mask8h"] = m8
    lv = np.zeros((8, 128), np.float32)
    lv[4:] = 1
    c8["LV"] = lv
    c8["ones8"] = np.ones((8, 128), np.float32)
    names8 = ("keepP", "keepS", "negS", "pm", "opm", "maskfull", "mask8h", "LV", "ones8")
    C8 = np.concatenate([c8[k] for k in names8], axis=1)
    offs8 = {}
    o = 0
    for k in names8:
        offs8[k] = (o, c8[k].shape[1])
        o += c8[k].shape[1]
    return C128, offs128, C8, offs8


def build(TP, NCH, NSQ):
    LS = 4 * NSQ
    NPASS = TP // (128 * NCH)
    assert NPASS * 128 * NCH == TP
    NTM = NCH + 1
    NMAX = 128 * NCH + LS
    NP = 128 * NCH
    L2 = DEPTH
    C128h, o128, C8h, o8 = host_consts(NCH, NSQ)
    nc = bass.Bass("TRN2", target_bir_lowering=False)

    def din(name, shape):
        return nc.dram_tensor(name, list(shape), F32, kind="ExternalInput").ap()

    def dout(name, shape):
        return nc.dram_tensor(name, list(shape), F32, kind="ExternalOutput").ap()

    xp_d = din("xp", [TP, D])
    xs_d = din("xs", [LS, D])
    sC_d = din("sC", [L2, NSQ, 4, 128, 128])
    sn_d = din("sn", [L2, NSQ, 512])
    sm_d = din("sm", [L2, 8, NSQ])
    sS_d = din("sS", [L2, NSQ, 8, 64, 128])
    sscv_d = din("sscv", [L2, NSQ * 3, 1024])
    sccv_d = din("sccv", [L2, NSQ * 2, 512])
    srh_d = din("srh", [L2, NSQ, 512])
    srcv_d = din("srcv", [L2, NSQ * 3, 512])
    wpk_d = din("wpk", [L2, 28, 128, 4096])
    wsm_d = din("wsm", [L2, 128, 512])
    wopk_d = din("wopk", [L2, 4, 8, 128, 1024])
    wabd_d = din("wabd", [L2, 128, 1024])
    cvec_d = din("cvec", [L2, 128, 84])
    g8_d = din("g8", [L2, 8, 8])
    normw_d = din("norm_w", [L2, D])
    mlnw_d = din("ml_norm_w", [L2, BR])
    ssdnw_d = din("ssd_norm_w", [L2, BR])
    ssdD_d = din("ssd_D", [L2, 8])
    fnw_d = din("final_norm_w", [D])
    c128_d = din("c128", list(C128h.shape))
    c8_d = din("c8", list(C8h.shape))

    yp_o = dout("yp", [TP, D])
    ys_o = dout("ys", [LS, D])
    pC_o = dout("pC", [L2, 4, 128, 128])
    pn_o = dout("pn", [L2, 4, 128])
    pm_o = dout("pm", [L2, 4])
    pS_o = dout("pS", [L2, 512, 128])
    pscv_o = dout("pscv", [L2, 3, 1024])
    pccv_o = dout("pccv", [L2, 2, 512])
    prh_o = dout("prh", [L2, 4, 128])
    prcv_o = dout("prcv", [L2, 3, 512])
    oC_o = dout("oC", [L2, NSQ, 4, 128, 128])
    on_o = dout("on", [L2, NSQ, 512])
    om_o = dout("om", [L2, 4, NSQ])
    oS_o = dout("oS", [L2, NSQ, 8, 64, 128])
    oscv_o = dout("oscv", [L2, NSQ * 3, 1024])
    occv_o = dout("occv", [L2, NSQ * 2, 512])
    orh_o = dout("orh", [L2, NSQ, 512])
    orcv_o = dout("orcv", [L2, NSQ * 3, 512])

    R = Rec()
    st = ExitStack()
    st.enter_context(nc.allow_non_contiguous_dma(reason="small strided state/constant transfers"))

    def sb(name, shape, dt=F32):
        return st.enter_context(nc.sbuf_tensor("sb_" + name, list(shape), dt))

    def psum(name):
        return st.enter_context(nc.psum_tensor(name, [128, 512], F32))

    def TT(eng, out, in0, in1, op, r, w):
        R.add(eng, lambda e: e.tensor_tensor(out=out, in0=in0, in1=in1, op=op), r, w)

    def TS(eng, out, in0, s1, s2, op0, op1, r, w):
        if op1 is None:
            R.add(eng, lambda e: e.tensor_scalar(out=out, in0=in0, scalar1=s1, scalar2=None, op0=op0), r, w)
        else:
            R.add(eng, lambda e: e.tensor_scalar(out=out, in0=in0, scalar1=s1, scalar2=s2, op0=op0, op1=op1), r, w)

    def STT(eng, out, in0, scalar, in1, op0, op1, r, w):
        R.add(eng, lambda e: e.scalar_tensor_tensor(out=out, in0=in0, scalar=scalar, in1=in1, op0=op0, op1=op1), r, w)

    def CP(eng, out, in_, r, w):
        if eng == "act":
            R.add(eng, lambda e: e.activation(out=out, in_=in_, func=AF.Copy), r, w)
        else:
            R.add(eng, lambda e: e.tensor_copy(out=out, in_=in_), r, w)

    def ACT(out, in_, func, r, w, bias=None, scale=1.0, accum=None):
        kw = {}
        if bias is not None:
            kw["bias"] = bias
        if accum is not None:
            kw["accum_out"] = accum
        R.add("act", lambda e: e.activation(out=out, in_=in_, func=func, scale=scale, **kw), r, w)

    def MM(out, lhsT, rhs, start, stop, r, w):
        R.add("pe", lambda e: e.matmul(out, lhsT=lhsT, rhs=rhs, start=start, stop=stop), r, w)

    def TR(out, in_, ident, r, w):
        R.add("pe", lambda e: e.transpose(out=out, in_=in_, identity=ident), r, w)

    def DMA(out, in_, r, w, q="sp"):
        R.add(q, lambda e: e.dma_start(out=out, in_=in_), r, w, dma=True)

    def SCAN(out, d0, d1, init, op0, op1, r, w):
        R.add("dve", lambda e: e.tensor_tensor_scan(out=out, data0=d0, data1=d1, initial=init, op0=op0, op1=op1), r, w)

    def MEMSET(eng, ap, val, w):
        R.add(eng, lambda e: e.memset(ap, val), (), w)

    def RSUM(out, in_, r, w):
        R.add("dve", lambda e: e.reduce_sum(out=out, in_=in_, axis=mybir.AxisListType.X), r, w)

    def RECIP(out, in_, r, w):
        R.add("dve", lambda e: e.reciprocal(out=out, in_=in_), r, w)

    def sigmoid_from_exp(t, r, w):
        ACT(t, t, AF.Ln, r, w, bias=1.0)
        ACT(t, t, AF.Exp, r, w, scale=-1.0)

    c128 = sb("c128", [128, 128 + NSQ])
    c8 = sb("c8", C8h.shape)
    identf = c128[:, 0:128]
    smT = c128[:, 128:128 + NSQ]

    def k8(name):
        o, n = o8[name]
        return c8[:, o:o + n]

    identb = sb("identb", [128, 128], BF16)
    maskPb = sb("maskPb", [128, 512], BF16)
    maskSb = sb("maskSb", [128, 4 * LS], BF16)
    smkb = sb("smkb", [128, NSQ * LS], BF16)
    pm8 = k8("pm")
    opm8 = k8("opm")
    xres = sb("xres", [128, NTM, D])
    xnT = sb("xnT", [128, 16, NMAX], BF16)
    st4 = sb("st4", [128, 8])
    tm4 = sb("tm4", [128, 16])
    Dbc = sb("Dbc", [128, 8])
    cvec = sb("cvec", [128, 84])
    g8 = sb("g8", [8, 8])
    g8n = sb("g8n", [8, 8])
    nsp8 = sb("nsp8", [128, 4])
    nba = sb("nba", [128, 8])
    NWB = 3
    wb = [sb("wb%d" % i, [128, 16, 256], BF16) for i in range(NWB)]
    wsmb = sb("wsmb", [128, 16, 32], BF16)
    wabd = sb("wabd", [128, 8, 128], BF16)
    PS = {k: psum(k) for k in ("psA", "psB", "psS", "psE", "psV", "psN0", "psN1", "psU")}
    Caug = sb("Caug", [128, L2, 4, 129])
    Cbf = sb("Cbf", [128, 4, 129], BF16)
    mcar = sb("mcar", [8, L2])
    STs = sb("STs", [128, L2, 512])
    STb = sb("STb", [128, 512], BF16)
    cvcar = sb("cvcar", [128, L2, 16, 3])
    hcar = sb("hcar", [128, L2, 4])
    ubp = sb("ubp", [128, 3 + NP])
    ubs = sb("ubs", [128, NSQ, 7])
    gsm = sb("gsm", [8, 64])
    RM = sb("RM", [8, 512])
    RMv = sb("RMv", [8, 512])
    scal = sb("scal", [128, NTM, 2, 8])
    bcw = sb("bcw", [128, 2, 4, 32])
    ncol = sb("ncol", [128, NSQ])
    cvtmp = sb("cvtmp", [128, NSQ * 3])
    h0s = sb("h0s", [128, NSQ])
    small_o = sb("small_o", [8, 768])
    AFW = 14480
    ABW = 17192
    arenaF = sb("arenaF", [128, AFW])
    arenaB = sb("arenaB", [128, ABW], BF16)
    aoff = [0, 0]
    amax = [0, 0]
    import os as _os
    print('SBUF remaining after persistent', nc.sbuf_bytes_remaining, flush=True)

    def stage_reset():
        R.fence()
        aoff[0] = 0
        aoff[1] = 0

    def carve(arena, k, cap, shape, pat):
        n = 1
        for d_ in shape[1:]:
            n *= d_
        o = aoff[k]
        aoff[k] += n
        amax[k] = max(amax[k], aoff[k])
        if _os.environ.get("KDRY"):
            o = 0
        else:
            assert aoff[k] <= cap, ("arena overflow", k, aoff[k], cap)
        v = arena[:shape[0], o:o + n]
        if len(shape) == 3:
            v = v.rearrange("p (a b) -> p a b", a=shape[1])
        elif len(shape) == 4:
            v = v.rearrange("p (a b c) -> p a b c", a=shape[1], b=shape[2])
        return v

    def af(shape):
        return carve(arenaF, 0, AFW, shape, None)

    def ab(shape):
        return carve(arenaB, 1, ABW, shape, None)

    DMA(c128[:, 0:128], c128_d[:, 0:128], (), ["c128"])
    o_, n_ = o128["smT"]
    DMA(c128[:, 128:128 + NSQ], c128_d[:, o_:o_ + n_], (), ["c128"])
    DMA(c8[:], c8_d, (), ["c8"])
    DMA(identb[:], c128_d[:, 0:128], (), ["identb"], q="pool")
    o_, n_ = o128["maskP"]
    DMA(maskPb[:], c128_d[:, o_:o_ + n_], (), ["maskPb"], q="pool")
    o_, n_ = o128["maskS"]
    DMA(maskSb[:], c128_d[:, o_:o_ + n_], (), ["maskSb"], q="pool")
    o_, n_ = o128["smk"]
    DMA(smkb[:], c128_d[:, o_:o_ + n_], (), ["smkb"], q="pool")
    MEMSET("pool", xnT[:], 0.0, ["xnT"])
    MEMSET("pool", Caug[:], 0.0, ["Caug"])
    MEMSET("pool", mcar[:], -1e30, ["mcar"])
    MEMSET("pool", STs[:], 0.0, ["STs"])
    MEMSET("pool", cvcar[:], 0.0, ["cvcar"])
    MEMSET("pool", hcar[:], 0.0, ["hcar"])
    MEMSET("pool", ubs[:], 0.0, ["ubs"])
    MEMSET("pool", ubp[:], 0.0, ["ubp"])
    MEMSET("pool", arenaF[:], 0.0, ["arenaF"])
    MEMSET("pool", arenaB[:], 0.0, ["arenaB"])
    stage_reset()

    wctr = [0]

    def load_w(src2d, nct=16):
        i = wctr[0] % NWB
        wctr[0] += 1
        DMA(wb[i][:, 0:nct, :].rearrange("p a b -> p (a b)"), src2d, (), ["wb%d" % i], q="pool")
        return wb[i], "wb%d" % i

    pctr = [0]

    def next_ps():
        k = ("psA", "psB", "psN1")[pctr[0] % 3]
        pctr[0] += 1
        return PS[k], k

    def next_ps2():
        k = ("psA", "psB")[pctr[0] % 2]
        pctr[0] += 1
        return PS[k], k

    import os as _os
    KSTOP = int(_os.environ.get("KSTOP", "99"))
    kst = [0]

    def checkpoint():
        kst[0] += 1
        if kst[0] > KSTOP:
            raise _Stop()

    try:
      for pi in range(NPASS):
          tiles = [dict(kind="p", L=128, col=128 * i, ti=i, tok=pi * NP + 128 * i) for i in range(NCH)]
          last_pass = pi == NPASS - 1
          if last_pass:
              tiles.append(dict(kind="s", L=LS, col=NP, ti=NCH, tok=0))
          NT = len(tiles)
          N = (NP + LS) if last_pass else NP
          groups = []
          c0 = 0
          while c0 < N:
              groups.append((c0, min(512, N - c0)))
              c0 += 512
          has_s = last_pass

          for t in tiles:
              src = xp_d[t["tok"]:t["tok"] + 128, :] if t["kind"] == "p" else xs_d
              DMA(xres[:t["L"], t["ti"], :], src, (), ["xres%d" % t["ti"]])

          for ly in range(L2):
              DMA(Dbc[:], ssdD_d[ly].partition_broadcast(128), (), ["Dbc"])
              DMA(cvec[:], cvec_d[ly], (), ["cvec"])
              DMA(g8[:], g8_d[ly], (), ["g8"])
              DMA(wsmb[:, :, :].rearrange("p a b -> p (a b)"), wsm_d[ly], (), ["wsmb"], q="pool")
              DMA(wabd[:, :, :].rearrange("p a b -> p (a b)"), wabd_d[ly], (), ["wabd"], q="pool")
              TS("dve", g8n[:, 0:2], g8[:, 0:2], -1.0, None, ALU.mult, None, ["g8"], ["g8n"])
              ACT(g8n[:, 4:6], g8[:, 4:6], AF.Exp, ["g8"], ["g8n"])
              TS("dve", g8n[:, 4:6], g8n[:, 4:6], -1.0, None, ALU.mult, None, ["g8n"], ["g8n"])
              ACT(nsp8[:], cvec[:, 80:84], AF.Exp, ["cvec"], ["nsp8"], scale=-1.0)
              ACT(nsp8[:], nsp8[:], AF.Ln, ["nsp8"], ["nsp8"], bias=1.0)
              TS("dve", nsp8[:], nsp8[:], -8.0, None, ALU.mult, None, ["nsp8"], ["nsp8"])
              TS("dve", nba[:], cvec[:, 72:80], -1.0, None, ALU.mult, None, ["cvec"], ["nba"])

              def rmsnorm_tile(t, wsrc_d, xn32s, junks, nwbc):
                  L, ti = t["L"], t["ti"]
                  pb = ti % 2
                  xn32, junk = xn32s[pb], junks[pb]
                  xid, jid, sid = "xn32_%d" % pb, "junk_%d" % pb, ("st4", "st4b")[pb]
                  so = 4 * pb
                  xr_id = "xres%d" % ti
                  ACT(junk[:L], xres[:L, ti, :], AF.Square, [xr_id], [jid, sid], accum=st4[:L, so:so + 1])
                  ACT(st4[:L, so + 1:so + 2], st4[:L, so:so + 1], AF.Ln, [sid], [sid], bias=EPS, scale=1.0 / D)
                  ACT(st4[:L, so + 2:so + 3], st4[:L, so + 1:so + 2], AF.Exp, [sid], [sid], scale=-0.5)
                  STT("dve", xn32[:L], xres[:L, ti, :], st4[:L, so + 2:so + 3], nwbc[:L], ALU.mult, ALU.mult,
                      [xr_id, sid, "nwbc"], [xid])
                  return xn32, xid

              stage_reset()
              xn32s = [af([128, D]), af([128, D])]
              nwbc = af([128, D])
              junks = [ab([128, D]), ab([128, D])]
              DMA(nwbc, normw_d[ly].partition_broadcast(128), (), ["nwbc"])
              for t in tiles:
                  L, ti, col = t["L"], t["ti"], t["col"]
                  xn32, xid = rmsnorm_tile(t, None, xn32s, junks, nwbc)
                  for q4 in range(4):
                      for j in range(4):
                          dt_ = q4 * 4 + j
                          TR(PS["psU"][:, j * 128:j * 128 + L], xn32[:L, dt_ * 128:(dt_ + 1) * 128], identf[:L, :L],
                             [xid, "c128"], ["psU"])
                      CP("act" if q4 % 2 else "dve", xnT[:, q4 * 4:q4 * 4 + 4, col:col + L],
                         PS["psU"][:, :].rearrange("p (j c) -> p j c", j=4)[:, :, :L], ["psU"], ["xnT"])

              checkpoint()
              def proj_fm(wt, wid, wo, M, evac):
                  for (g0, n) in groups:
                      ps, pid = next_ps()
                      for dt_ in range(16):
                          MM(ps[:M, :n], wt[:, dt_, wo:wo + M], xnT[:, dt_, g0:g0 + n], dt_ == 0, dt_ == 15,
                             [wid, "xnT"], [pid])
                      evac(ps, pid, g0, n)

              def proj_tm(wt, wid, wo, n, t, evac):
                  ps, pid = next_ps()
                  L, col = t["L"], t["col"]
                  for dt_ in range(16):
                      MM(ps[:L, :n], xnT[:, dt_, col:col + L], wt[:, dt_, wo:wo + n], dt_ == 0, dt_ == 15,
                         [wid, "xnT"], [pid])
                  evac(ps, pid)

              def wout_part(mixs, ct0):
                  for dg in range(8):
                      wt1, w1 = load_w(wopk_d[ly, ct0 // 4, dg], nct=4)
                      for t in tiles:
                          L, ti, col = t["L"], t["ti"], t["col"]
                          ps, pid = next_ps()
                          for ct in range(4):
                              MM(ps[:L, :256], mixs[:, ct, col:col + L], wt1[:, ct, :], ct == 0, ct == 3, ["mixs", w1], [pid])
                          TT("dve", xres[:L, ti, dg * 256:(dg + 1) * 256], xres[:L, ti, dg * 256:(dg + 1) * 256],
                             ps[:L, :256], ALU.add, ["xres%d" % ti, pid], ["xres%d" % ti])

              def conv_load_hist(stream):
                  CP("dve", ubp[:, 0:3], cvcar[:, ly, stream, :], ["cvcar"], ["ubp"])

              def conv_save_hist(stream):
                  CP("dve", cvcar[:, ly, stream, :], ubp[:, NP:NP + 3], ["ubp"], ["cvcar"])

              def conv_apply(out_fm, oid, W, wcol, bcol):
                  s0 = 3 - (W - 1)
                  views = [(out_fm[:, 0:NP], lambda j: ubp[:, s0 + j:s0 + j + NP])]
                  if has_s:
                      views.append((out_fm[:, NP:NP + LS].rearrange("p (b t) -> p b t", t=4),
                                    lambda j: ubs[:, :, s0 + j:s0 + j + 4]))
                  for (o, src) in views:
                      if bcol is None:
                          TS("dve", o, src(0), cvec[:, wcol:wcol + 1], None, ALU.mult, None,
                             ["ubp", "ubs", "cvec"], [oid])
                      else:
                          TS("dve", o, src(0), cvec[:, wcol:wcol + 1], cvec[:, bcol:bcol + 1], ALU.mult, ALU.add,
                             ["ubp", "ubs", "cvec"], [oid])
                      for j in range(1, W):
                          STT("dve", o, src(j), cvec[:, wcol + j:wcol + j + 1], o, ALU.mult, ALU.add,
                              ["ubp", "ubs", "cvec", oid], [oid])

              def sample_hist_tile(cvin, W, ctile, src_d):
                  rows = NSQ * (W - 1)
                  DMA(cvin[:rows, :], src_d[:, ctile * 128:(ctile + 1) * 128], (), ["cvin"])
                  TR(PS["psU"][:, :rows], cvin[:rows, :], identf[:rows, :rows],
                     ["cvin", "c128"], ["psU"])
                  CP("dve", ubs[:, :, 3 - (W - 1):3], PS["psU"][:, :rows].rearrange("p (b r) -> p b r", r=W - 1),
                     ["psU"], ["ubs"])

              def sample_hist_out_tile(cvout, W, ctile, dst_d):
                  rows = NSQ * (W - 1)
                  CP("dve", cvtmp[:, :rows].rearrange("p (b r) -> p b r", r=W - 1), ubs[:, :, 7 - (W - 1):7],
                     ["ubs"], ["cvtmp"])
                  TR(PS["psU"][:rows, :128], cvtmp[:, :rows], identf, ["cvtmp", "c128"], ["psU"])
                  CP("dve", cvout[:rows, :], PS["psU"][:rows, :128], ["psU"], ["cvout"])
                  DMA(dst_d[:, ctile * 128:(ctile + 1) * 128], cvout[:rows, :], ["cvout"], ())

              def prompt_hist_out(W, streams, dst):
                  for c0 in range(0, len(streams), 4):
                      sub = streams[c0:c0 + 4]
                      for i, s_ in enumerate(sub):
                          CP("dve", cvtmp[:, 0:W - 1], cvcar[:, ly, s_, 3 - (W - 1):3], ["cvcar"], ["cvtmp"])
                          TR(PS["psU"][:W - 1, :128], cvtmp[:, 0:W - 1], identf, ["cvtmp", "c128"], ["psU"])
                          CP("dve", small_o[:W - 1, i * 128:(i + 1) * 128], PS["psU"][:W - 1, :128], ["psU"], ["small_o"])
                      DMA(dst[:, c0 * 128:(c0 + len(sub)) * 128], small_o[:W - 1, :128 * len(sub)], ["small_o"], ())

              def fill_ub(ps, pid, g0, n, eng="act"):
                  pe_ = min(g0 + n, NP)
                  if pe_ > g0:
                      CP(eng, ubp[:, 3 + g0:3 + pe_], ps[:, 0:pe_ - g0], [pid], ["ubp"])
                  if has_s and g0 + n > NP:
                      a0 = max(g0, NP)
                      CP(eng, ubs[:, :, 3:7], ps[:, a0 - g0:a0 - g0 + LS].rearrange("p (b t) -> p b t", t=4),
                         [pid], ["ubs"])

              stage_reset()
              fa = [af([128, NMAX]) for _ in range(6)]
              cvin = af([NSQ * 3, 128])
              cvout = af([NSQ * 3, 128])
              htm = af([NSQ, 512])
              htmo = af([NSQ, 512])
              mixs = ab([128, 4, NMAX])
              xrb = ab([128, NMAX])
              for j in range(4):
                  wt1, w1 = load_w(wpk_d[ly, 2 * j])
                  wt2, w2 = load_w(wpk_d[ly, 2 * j + 1])
                  conv_load_hist(j)
                  if has_s:
                      sample_hist_tile(cvin, 3, j, sccv_d[ly])
                  proj_fm(wt1, w1, 0, 128, lambda ps, pid, g0, n: CP("act", fa[0][:, g0:g0 + n], ps[:, :n], [pid], ["fa0"]))
                  proj_fm(wt1, w1, 128, 128, lambda ps, pid, g0, n: CP("act", fa[1][:, g0:g0 + n], ps[:, :n], [pid], ["fa1"]))

                  def ev_h(ps, pid, g0, n):
                      TT("dve", fa[1][:, g0:g0 + n], fa[1][:, g0:g0 + n], ps[:, :n], ALU.mult, ["fa1", pid], ["fa1"])
                  proj_fm(wt2, w2, 0, 128, ev_h)

                  def ev_z(ps, pid, g0, n):
                      CP("act", fa[2][:, g0:g0 + n], ps[:, :n], [pid], ["fa2"])
                      ACT(fa[3][:, g0:g0 + n], fa[2][:, g0:g0 + n], AF.Exp, ["fa2"], ["fa3"], scale=-1.0)
                  proj_fm(wt2, w2, 128, 128, ev_z)
                  CP("dve", ubp[:, 3:3 + NP], fa[1][:, 0:NP], ["fa1"], ["ubp"])
                  if has_s:
                      CP("dve", ubs[:, :, 3:7], fa[1][:, NP:NP + LS].rearrange("p (b t) -> p b t", t=4), ["fa1"], ["ubs"])
                  conv_apply(fa[4], "fa4", 3, 40 + 3 * j, None)
                  conv_save_hist(j)
                  if has_s:
                      sample_hist_out_tile(cvout, 3, j, occv_o[ly])
                  sigmoid_from_exp(fa[3][:, :N], ["fa3"], ["fa3"])
                  TT("dve", fa[2][:, :N], fa[2][:, :N], fa[3][:, :N], ALU.mult, ["fa2", "fa3"], ["fa2"])
                  TT("dve", fa[2][:, :N], fa[2][:, :N], fa[0][:, :N], ALU.mult, ["fa2", "fa0"], ["fa2"])
                  TT("dve", mixs[:, j, :N], fa[2][:, :N], fa[4][:, :N], ALU.mult, ["fa2", "fa4"], ["mixs"])
              if last_pass:
                  prompt_hist_out(3, [0, 1, 2, 3], pccv_o[ly])
              checkpoint()
              wout_part(mixs, 8)
              checkpoint()

              if has_s:
                  DMA(htm[:], srh_d[ly], (), ["htm"])
              for j in range(4):
                  wt1, w1 = load_w(wpk_d[ly, 8 + j])
                  conv_load_hist(4 + j)
                  if has_s:
                      sample_hist_tile(cvin, 4, j, srcv_d[ly])
                      TR(PS["psU"][:, :NSQ], htm[:NSQ, j * 128:(j + 1) * 128], identf[:NSQ, :NSQ], ["htm", "c128"], ["psU"])
                      CP("dve", h0s[:], PS["psU"][:, :NSQ], ["psU"], ["h0s"])
                  proj_fm(wt1, w1, 0, 128, lambda ps, pid, g0, n: fill_ub(ps, pid, g0, n))

                  def ev_rz(ps, pid, g0, n):
                      CP("act", fa[2][:, g0:g0 + n], ps[:, :n], [pid], ["fa2"])
                      ACT(fa[3][:, g0:g0 + n], fa[2][:, g0:g0 + n], AF.Exp, ["fa2"], ["fa3"], scale=-1.0)
                  proj_fm(wt1, w1, 128, 128, ev_rz)
                  conv_apply(fa[0], "fa0", 4, 52 + 4 * j, 68 + j)
                  conv_save_hist(4 + j)
                  if has_s:
                      sample_hist_out_tile(cvout, 4, j, orcv_o[ly])
                  CP("act", xrb[:, :N], fa[0][:, :N], ["fa0"], ["xrb"])
                  for (g0, n) in groups:
                      ps, pid = next_ps()
                      MM(ps[:, :n], wabd[:, j, :], xrb[:, g0:g0 + n], True, True, ["wabd", "xrb"], [pid])
                      ACT(fa[1][:, g0:g0 + n], ps[:, :n], AF.Exp, [pid, "nba"], ["fa1"], bias=nba[:, j:j + 1], scale=-1.0)
                      ps, pid = next_ps()
                      MM(ps[:, :n], wabd[:, 4 + j, :], xrb[:, g0:g0 + n], True, True, ["wabd", "xrb"], [pid])
                      ACT(fa[4][:, g0:g0 + n], ps[:, :n], AF.Exp, [pid, "nba"], ["fa4"], bias=nba[:, 4 + j:5 + j], scale=-1.0)
                  sigmoid_from_exp(fa[1][:, :N], ["fa1"], ["fa1"])
                  sigmoid_from_exp(fa[4][:, :N], ["fa4"], ["fa4"])
                  ACT(fa[1][:, :N], fa[1][:, :N], AF.Exp, ["fa1", "nsp8"], ["fa1"], scale=nsp8[:, j:j + 1])
                  TT("dve", fa[5][:, :N], fa[1][:, :N], fa[1][:, :N], ALU.mult, ["fa1"], ["fa5"])
                  TS("dve", fa[5][:, :N], fa[5][:, :N], -1.0, 1.0, ALU.mult, ALU.add, ["fa5"], ["fa5"])
                  TS("dve", fa[5][:, :N], fa[5][:, :N], 1e-30, None, ALU.max, None, ["fa5"], ["fa5"])
                  ACT(fa[5][:, :N], fa[5][:, :N], AF.Ln, ["fa5"], ["fa5"])
                  ACT(fa[5][:, :N], fa[5][:, :N], AF.Exp, ["fa5"], ["fa5"], scale=0.5)
                  TT("dve", fa[4][:, :N], fa[4][:, :N], fa[0][:, :N], ALU.mult, ["fa4", "fa0"], ["fa4"])
                  TT("dve", fa[4][:, :N], fa[4][:, :N], fa[5][:, :N], ALU.mult, ["fa4", "fa5"], ["fa4"])
                  STT("dve", fa[4][:, 0:1], fa[1][:, 0:1], hcar[:, ly, j:j + 1], fa[4][:, 0:1], ALU.mult, ALU.add,
                      ["fa1", "fa4", "hcar"], ["fa4"])
                  MEMSET("dve", fa[1][:, 0:1], 0.0, ["fa1"])
                  SCAN(fa[0][:, 0:NP], fa[1][:, 0:NP], fa[4][:, 0:NP], 0.0, ALU.mult, ALU.add, ["fa1", "fa4"], ["fa0"])
                  CP("dve", hcar[:, ly, j:j + 1], fa[0][:, NP - 1:NP], ["fa0"], ["hcar"])
                  if has_s:
                      a0v = fa[1][:, NP:NP + LS].rearrange("p (b t) -> p b t", t=4)[:, :, 0]
                      b0v = fa[4][:, NP:NP + LS].rearrange("p (b t) -> p b t", t=4)[:, :, 0]
                      TT("dve", tm4[:, :NSQ], a0v, h0s[:], ALU.mult, ["fa1", "h0s"], ["tm4"])
                      TT("dve", b0v, b0v, tm4[:, :NSQ], ALU.add, ["fa4", "tm4"], ["fa4"])
                      MEMSET("dve", a0v, 0.0, ["fa1"])
                      SCAN(fa[0][:, NP:NP + LS], fa[1][:, NP:NP + LS], fa[4][:, NP:NP + LS], 0.0, ALU.mult, ALU.add,
                           ["fa1", "fa4"], ["fa0"])
                      CP("dve", ncol[:, :NSQ], fa[0][:, NP:NP + LS].rearrange("p (b t) -> p b t", t=4)[:, :, 3], ["fa0"], ["ncol"])
                      TR(PS["psU"][:NSQ, :128], ncol[:, :NSQ], identf, ["ncol", "c128"], ["psU"])
                      CP("dve", htmo[:NSQ, j * 128:(j + 1) * 128], PS["psU"][:NSQ, :128], ["psU"], ["htmo"])
                  sigmoid_from_exp(fa[3][:, :N], ["fa3"], ["fa3"])
                  TT("dve", fa[2][:, :N], fa[2][:, :N], fa[3][:, :N], ALU.mult, ["fa2", "fa3"], ["fa2"])
                  TT("dve", mixs[:, j, :N], fa[2][:, :N], fa[0][:, :N], ALU.mult, ["fa2", "fa0"], ["mixs"])
              if has_s:
                  DMA(orh_o[ly], htmo[:], ["htmo"], ())
              if last_pass:
                  prompt_hist_out(4, [4, 5, 6, 7], prcv_o[ly])
                  for j in range(4):
                      TR(PS["psU"][:1, j * 128:(j + 1) * 128], hcar[:, ly, j:j + 1], identf, ["hcar", "c128"], ["psU"])
                  CP("dve", small_o[:1, 0:512], PS["psU"][:1, 0:512], ["psU"], ["small_o"])
                  DMA(prh_o[ly].rearrange("(o t) c -> o (t c)", o=1), small_o[:1, 0:512], ["small_o"], ())
              checkpoint()
              wout_part(mixs, 12)
              checkpoint()

              def chunk_ends(row, t_kind):
                  if t_kind == "p":
                      return row[:, 0:NP].rearrange("p (c t) -> p c t", t=128)[:, :, 127]
                  return row[:, NP:NP + LS].rearrange("p (b t) -> p b t", t=4)[:, :, 3]

              def expo(a_row, aid, c_row, cid, cv_row, cvid, t, Wt, Wv):
                  L, col = t["L"], t["col"]
                  mf = k8("maskfull").rearrange("p (h l) -> p h l", h=4)[:, :, :L]
                  RMl = RM[:, :4 * L].rearrange("p (h l) -> p h l", h=4)
                  RMvl = RMv[:, :4 * L].rearrange("p (h l) -> p h l", h=4)
                  TT("dve", RMl, c_row[:, col:col + L].unsqueeze(1).broadcast_to([8, 4, L]), mf, ALU.mult,
                     [cid, "c8"], ["RM"])
                  TT("dve", RMvl, cv_row[:, col:col + L].unsqueeze(1).broadcast_to([8, 4, L]), mf, ALU.mult,
                     [cvid, "c8"], ["RMv"])
                  msk = (maskPb if t["kind"] == "p" else maskSb)
                  MM(PS["psE"][:L, :4 * L], a_row[:, col:col + L], RM[:, :4 * L], True, False, [aid, "RM"], ["psE"])
                  MM(PS["psE"][:L, :4 * L], identb[:L, :L], msk[:L, :4 * L], False, True, ["identb", "maskPb", "maskSb"], ["psE"])
                  MM(PS["psV"][:, :4 * L], k8("LV"), RMv[:, :4 * L], True, True, ["c8", "RMv"], ["psV"])
                  ACT(Wt[:L, :4 * L], PS["psE"][:L, :4 * L], AF.Exp, ["psE"], ["Wt"])
                  ACT(Wv[:, :4 * L], PS["psV"][:, :4 * L], AF.Exp, ["psV"], ["Wv"])

              def bcast_rows(src_row, sid, ncols, slot):
                  w8 = gsm[:, :4 * ncols].rearrange("p (h c) -> p h c", h=4)
                  TT("dve", w8, src_row.unsqueeze(1).broadcast_to([8, 4, ncols]),
                     k8("mask8h").unsqueeze(2).broadcast_to([8, 4, ncols]), ALU.mult, [sid, "c8"], ["gsm"])
                  MM(PS["psU"][:, :4 * ncols], k8("ones8"), gsm[:, :4 * ncols], True, True, ["c8", "gsm"], ["psU"])
                  CP("dve", bcw[:, slot, :, :ncols], PS["psU"][:, :4 * ncols].rearrange("p (h c) -> p h c", h=4),
                     ["psU"], ["bcw"])

              def to_tok_major(rowA, aid, rowB, bid, t, slot, scr, scrid):
                  L, col, ti = t["L"], t["col"], t["ti"]
                  TS("dve", scr[:, :L], rowA[:, col:col + L], pm8, None, ALU.mult, None, [aid, "c8"], [scrid])
                  STT("dve", scr[:, :L], rowB[:, col:col + L], opm8, scr[:, :L], ALU.mult, ALU.add,
                      [bid, "c8", scrid], [scrid])
                  TR(PS["psU"][:L, :8], scr[:, :L], identf[:8, :8], [scrid, "c128"], ["psU"])
                  CP("dve", scal[:L, ti, slot, :], PS["psU"][:L, :8], ["psU"], ["scal"])

              stage_reset()
              Gml = af([128, NTM, 512])
              gr = [af([8, NMAX]) for _ in range(10)]
              mlnwbc = af([128, BR])
              DMA(mlnwbc, mlnw_d[ly].partition_broadcast(128), (), ["mlnwbc"])
              Wt = af([128, 512])
              Wv = af([128, 512])
              hn = af([128, 512])
              ytm = af([128, 512])
              Cs = af([128, NSQ, 129])
              ntm = af([NSQ, 512])
              ntmo = af([NSQ, 512])
              qT = ab([128, 4, NMAX])
              kT = ab([128, 4, NMAX])
              ktm = ab([128, NTM, 512])
              vaug = ab([128, NTM, 4, 129])
              SwT = ab([128, 512])
              qv = ab([128, 512])
              kw = ab([128, 512])
              qvm = ab([128, NSQ, LS])
              Csb = ab([128, NSQ, 129])
              vm = ab([128, 3, 129])
              mixs = ab([128, 4, NMAX])
              junk = ab([128, 128])
              hsq = af([128, 512])
              MEMSET("pool", vaug, 1.0, ["vaug"])
              CP("act", Cbf[:], Caug[:, ly], ["Caug"], ["Cbf"])
              for cg in range(4):
                  wt1, w1 = load_w(wpk_d[ly, 12 + cg])
                  for t in tiles:
                      L, ti = t["L"], t["ti"]

                      def ev_g(ps, pid, L=L, ti=ti, cg=cg):
                          CP("act", ytm[:L, 0:256], ps[:L, 0:256], [pid], ["ytm"])
                          ACT(hn[:L, 0:256], ytm[:L, 0:256], AF.Exp, ["ytm"], ["hn"], scale=-1.0)
                          ACT(hn[:L, 0:256], hn[:L, 0:256], AF.Ln, ["hn"], ["hn"], bias=1.0)
                          TT("dve", hn[:L, 0:128], hn[:L, 0:128], hn[:L, 128:256], ALU.add, ["hn"], ["hn"])
                          ACT(hn[:L, 0:128], hn[:L, 0:128], AF.Exp, ["hn"], ["hn"], scale=-1.0)
                          TT("dve", hn[:L, 0:128], hn[:L, 0:128], ytm[:L, 128:256], ALU.mult, ["hn", "ytm"], ["hn"])
                          TT("dve", Gml[:L, ti, cg * 128:(cg + 1) * 128], hn[:L, 0:128], mlnwbc[:L, cg * 128:(cg + 1) * 128],
                             ALU.mult, ["hn", "mlnwbc"], ["Gml"])
                      proj_tm(wt1, w1, 0, 256, t, ev_g)
              for hp in range(2):
                  wt1, w1 = load_w(wpk_d[ly, 16 + hp])
                  for hh in range(2):
                      h = hp * 2 + hh
                      proj_fm(wt1, w1, hh * 128, 128,
                              lambda ps, pid, g0, n, h=h: CP("act", qT[:, h, g0:g0 + n], ps[:, :n], [pid], ["qT"]))
              for hp in range(2):
                  wt1, w1 = load_w(wpk_d[ly, 18 + hp])
                  for hh in range(2):
                      h = hp * 2 + hh
                      proj_fm(wt1, w1, hh * 128, 128,
                              lambda ps, pid, g0, n, h=h: ACT(kT[:, h, g0:g0 + n], ps[:, :n], AF.Copy, [pid], ["kT"],
                                                             scale=128.0 ** -0.5))
                  for t in tiles:
                      L, ti = t["L"], t["ti"]
                      proj_tm(wt1, w1, 0, 256, t,
                              lambda ps, pid, L=L, ti=ti, hp=hp: ACT(ktm[:L, ti, hp * 256:(hp + 1) * 256], ps[:L, :256],
                                                                    AF.Copy, [pid], ["ktm"], scale=128.0 ** -0.5))
              for hp in range(2):
                  wt1, w1 = load_w(wpk_d[ly, 20 + hp])
                  for t in tiles:
                      L, ti = t["L"], t["ti"]
                      proj_tm(wt1, w1, 0, 256, t,
                              lambda ps, pid, L=L, ti=ti, hp=hp: CP("dve", vaug[:L, ti, 2 * hp:2 * hp + 2, 0:128],
                                                                   ps[:L, :256].rearrange("p (h v) -> p h v", h=2),
                                                                   [pid], ["vaug"]))
              gi, gf, gb, gm, ga, gc, gcv, gwg, genm, gmp = gr
              glf = gf
              proj_fm(wsmb, "wsmb", 0, 8,
                      lambda ps, pid, g0, n: TS("dve", gi[:, g0:g0 + n], ps[:8, :n], g8[:, 0:1], None, ALU.add, None,
                                                [pid, "g8"], ["gr0"]))
              proj_fm(wsmb, "wsmb", 8, 8,
                      lambda ps, pid, g0, n: ACT(gf[:, g0:g0 + n], ps[:8, :n], AF.Exp, [pid, "g8n"], ["gr1"],
                                                 bias=g8n[:, 1:2], scale=-1.0))
              ACT(glf[:, :N], gf[:, :N], AF.Ln, ["gr1", "gr2"], ["gr1", "gr2"], bias=1.0)
              TS("dve", glf[:, :N], glf[:, :N], -1.0, None, ALU.mult, None, ["gr2"], ["gr2"])
              SCAN(gb[:, 0:NP], k8("keepP"), glf[:, 0:NP], 0.0, ALU.mult, ALU.add, ["c8", "gr2"], ["gr3"])
              SCAN(gm[:, 0:NP], glf[:, 0:NP], gi[:, 0:NP], mcar[:, ly:ly + 1], ALU.add, ALU.max, ["gr2", "gr0", "mcar"], ["gr4"])
              CP("dve", gmp[:, 1:NP], gm[:, 0:NP - 1], ["gr4"], ["gr10"])
              CP("dve", gmp[:, 0:1], mcar[:, ly:ly + 1], ["mcar"], ["gr10"])
              CP("dve", mcar[:, ly:ly + 1], gm[:, NP - 1:NP], ["gr4"], ["mcar"])
              m0 = small_o[:8, 0:NSQ]
              if has_s:
                  sl_ = slice(NP, NP + LS)
                  DMA(m0, sm_d[ly], (), ["small_o"])
                  SCAN(gb[:, sl_], k8("keepS"), glf[:, sl_], 0.0, ALU.mult, ALU.add, ["c8", "gr2"], ["gr3"])
                  TT("dve", gmp[:, sl_], glf[:, sl_], k8("keepS"), ALU.mult, ["gr2", "c8"], ["gr10"])
                  TT("dve", gmp[:, sl_], gmp[:, sl_], k8("negS"), ALU.add, ["gr10", "c8"], ["gr10"])
                  CP("dve", gwg[:, sl_], gi[:, sl_], ["gr0"], ["gr8"])
                  lf0 = glf[:, sl_].rearrange("p (b t) -> p b t", t=4)[:, :, 0]
                  i0 = gwg[:, sl_].rearrange("p (b t) -> p b t", t=4)[:, :, 0]
                  TT("dve", small_o[:8, 64:64 + NSQ], lf0, m0, ALU.add, ["gr2", "small_o"], ["small_o"])
                  TT("dve", i0, i0, small_o[:8, 64:64 + NSQ], ALU.max, ["gr8", "small_o"], ["gr8"])
                  SCAN(gm[:, sl_], gmp[:, sl_], gwg[:, sl_], 0.0, ALU.add, ALU.max, ["gr10", "gr8"], ["gr4"])
              TT("dve", gc[:, :N], gb[:, :N], gm[:, :N], ALU.subtract, ["gr3", "gr4"], ["gr6"])
              TT("dve", ga[:, :N], gi[:, :N], gb[:, :N], ALU.subtract, ["gr0", "gr3"], ["gr5"])
              mprev_p = gmp[:, 0:NP].rearrange("p (c t) -> p c t", t=128)[:, :, 0]
              TT("dve", gcv[:, 0:NP].rearrange("p (c t) -> p c t", t=128), gc[:, 0:NP].rearrange("p (c t) -> p c t", t=128),
                 mprev_p.unsqueeze(2).broadcast_to([8, NCH, 128]), ALU.add, ["gr6", "gr10"], ["gr7"])
              if has_s:
                  TT("dve", gcv[:, sl_].rearrange("p (b t) -> p b t", t=4), gc[:, sl_].rearrange("p (b t) -> p b t", t=4),
                     m0.unsqueeze(2).broadcast_to([8, NSQ, 4]), ALU.add, ["gr6", "small_o"], ["gr7"])
              for kind in (("p", "s") if has_s else ("p",)):
                  ncn = NCH if kind == "p" else NSQ
                  T_ = 128 if kind == "p" else 4
                  base = 0 if kind == "p" else NP
                  gcst = gsm[:, 0:ncn]
                  TT("dve", gcst, chunk_ends(gb, kind), chunk_ends(gm, kind), ALU.subtract, ["gr3", "gr4"], ["gsm"])
                  seg = slice(base, base + ncn * T_)
                  TT("dve", gwg[:, seg].rearrange("p (c t) -> p c t", t=T_), ga[:, seg].rearrange("p (c t) -> p c t", t=T_),
                     gcst.unsqueeze(2).broadcast_to([8, ncn, T_]), ALU.add, ["gr5", "gsm"], ["gr8"])
                  mpv = mprev_p if kind == "p" else m0
                  TT("dve", small_o[:8, 128:128 + ncn], gcst, mpv, ALU.add, ["gsm", "gr10", "small_o"], ["small_o"])
                  ACT(small_o[:8, 128:128 + ncn], small_o[:8, 128:128 + ncn], AF.Exp, ["small_o"], ["small_o"])
                  bcast_rows(small_o[:8, 128:128 + ncn], "small_o", ncn, 0 if kind == "p" else 1)
              ACT(gwg[:, :N], gwg[:, :N], AF.Exp, ["gr8"], ["gr8"])
              ACT(genm[:, :N], gm[:, :N], AF.Exp, ["gr4"], ["gr9"], scale=-1.0)
              TS("dve", ga[:, :N], ga[:, :N], pm8, opm8, ALU.mult, ALU.add, ["gr5", "c8"], ["gr5"])
              TS("dve", gc[:, :N], gc[:, :N], opm8, pm8, ALU.mult, ALU.add, ["gr6", "c8"], ["gr6"])
              TS("dve", gcv[:, :N], gcv[:, :N], opm8, pm8, ALU.mult, ALU.add, ["gr7", "c8"], ["gr7"])
              if has_s:
                  DMA(ntm[:], sn_d[ly], (), ["ntm"])
              if last_pass:
                  DMA(pm_o[ly].rearrange("(h o) -> h o", o=1), mcar[0:4, ly:ly + 1], ["mcar"], ())
              if has_s:
                  CP("dve", small_o[:8, 512:512 + NSQ], chunk_ends(gm, "s"), ["gr4"], ["small_o"])
                  DMA(om_o[ly], small_o[0:4, 512:512 + NSQ], ["small_o"], ())

              for t in tiles:
                  L, ti, col, kind = t["L"], t["ti"], t["col"], t["kind"]
                  to_tok_major(gwg, "gr8", genm, "gr9", t, 0, gmp, "gr10")
                  for h in range(4):
                      MM(PS["psS"][:L, h * L:(h + 1) * L], kT[:, h, col:col + L], qT[:, h, col:col + L], True, True,
                         ["kT", "qT"], ["psS"])
                  expo(ga, "gr5", gc, "gr6", gcv, "gr7", t, Wt, Wv)
                  TT("dve", SwT[:L, :4 * L], PS["psS"][:L, :4 * L], Wt[:L, :4 * L], ALU.mult, ["psS", "Wt"], ["SwT"])
                  TT("pool", qv[:, :4 * L].rearrange("p (h l) -> p h l", h=4), qT[:, :, col:col + L],
                     Wv[:, :4 * L].rearrange("p (h l) -> p h l", h=4), ALU.mult, ["qT", "Wv"], ["qv"])
                  TT("pool", kw[:L, :].rearrange("p (h d) -> p h d", h=4), ktm[:L, ti, :].rearrange("p (h d) -> p h d", h=4),
                     scal[:L, ti, 0, 0:4].unsqueeze(2).broadcast_to([L, 4, 128]), ALU.mult, ["ktm", "scal"], ["kw"])
                  for h in range(4):
                      pn = PS["psN0"] if h < 2 else PS["psN1"]
                      pnid = "psN0" if h < 2 else "psN1"
                      o_ = (h % 2) * 129
                      if kind == "s":
                          for b4 in range(0, NSQ, 4):
                              DMA(Cs[:, b4:b4 + 4, 0:128], sC_d[ly][b4:b4 + 4, h].rearrange("b d v -> d b v"), (), ["Cs"])
                          TR(PS["psU"][:, :NSQ], ntm[:NSQ, h * 128:(h + 1) * 128], identf[:NSQ, :NSQ], ["ntm", "c128"], ["psU"])
                          CP("dve", Cs[:, :, 128], PS["psU"][:, :NSQ], ["psU"], ["Cs"])
                          CP("act", Csb, Cs, ["Cs"], ["Csb"])
                          TT("pool", qvm, qv[:, h * L:(h + 1) * L].unsqueeze(1).broadcast_to([128, NSQ, L]),
                             smkb[:, :].rearrange("p (b l) -> p b l", b=NSQ), ALU.mult, ["qv", "smkb"], ["qvm"])
                      MM(pn[:L, o_:o_ + 129], SwT[:L, h * L:(h + 1) * L], vaug[:L, ti, h, :], True, False,
                         ["SwT", "vaug"], [pnid])
                      if kind == "p":
                          MM(pn[:L, o_:o_ + 129], qv[:, h * L:(h + 1) * L], Cbf[:, h, :], False, True, ["qv", "Cbf"], [pnid])
                      else:
                          for b in range(NSQ):
                              MM(pn[:L, o_:o_ + 129], qvm[:, b, :], Csb[:, b, :], False, b == NSQ - 1,
                                 ["qvm", "Csb"], [pnid])
                      kwh = kw[:L, h * 128:(h + 1) * 128]
                      if kind == "p":
                          psu, psuid = next_ps2()
                          MM(psu[:, 0:129], kwh, vaug[:L, ti, h, :], True, True, ["kw", "vaug"], [psuid])
                          STT("dve", Caug[:, ly, h, :], Caug[:, ly, h, :], bcw[:, 0, h, ti:ti + 1], psu[:, 0:129],
                              ALU.mult, ALU.add, ["Caug", "bcw", psuid], ["Caug"])
                      else:
                          b0 = 0
                          while b0 < NSQ:
                              nb = min(3, NSQ - b0)
                              TT("pool", vm[:L, 0:nb, :], vaug[:L, ti, h, :].unsqueeze(1).broadcast_to([L, nb, 129]),
                                 smT[:L, b0:b0 + nb].unsqueeze(2).broadcast_to([L, nb, 129]), ALU.mult, ["vaug", "c128"], ["vm"])
                              MM(PS["psU"][:, 0:nb * 129], kwh, vm[:L, 0:nb, :], True, True, ["kw", "vm"], ["psU"])
                              TT("dve", Cs[:, b0:b0 + nb, :], Cs[:, b0:b0 + nb, :],
                                 bcw[:, 1, h, b0:b0 + nb].unsqueeze(2).broadcast_to([128, nb, 129]), ALU.mult,
                                 ["Cs", "bcw"], ["Cs"])
                              TT("dve", Cs[:, b0:b0 + nb, :], Cs[:, b0:b0 + nb, :],
                                 PS["psU"][:, 0:nb * 129].rearrange("p (b v) -> p b v", b=nb), ALU.add, ["Cs", "psU"], ["Cs"])
                              b0 += nb
                          for b4 in range(0, NSQ, 4):
                              DMA(oC_o[ly][b4:b4 + 4, h].rearrange("b d v -> d b v"), Cs[:, b4:b4 + 4, 0:128], ["Cs"], ())
                          CP("dve", ncol[:, :NSQ], Cs[:, :, 128], ["Cs"], ["ncol"])
                          TR(PS["psU"][:NSQ, :128], ncol[:, :NSQ], identf, ["ncol", "c128"], ["psU"])
                          CP("dve", ntmo[:NSQ, h * 128:(h + 1) * 128], PS["psU"][:NSQ, :128], ["psU"], ["ntmo"])
                  if kind == "p":
                      CP("act", Cbf[:], Caug[:, ly], ["Caug"], ["Cbf"])
                  for half in range(2):
                      pn = PS["psN0"] if half == 0 else PS["psN1"]
                      pnid = "psN0" if half == 0 else "psN1"
                      den = pn[:L, 0:258].rearrange("p (h v) -> p h v", h=2)[:, :, 128]
                      ACT(tm4[:L, 2 * half:2 * half + 2], den, AF.Abs, [pnid], ["tm4"])
                  TT("dve", tm4[:L, 0:4], tm4[:L, 0:4], scal[:L, ti, 0, 4:8], ALU.max, ["tm4", "scal"], ["tm4"])
                  RECIP(tm4[:L, 0:4], tm4[:L, 0:4], ["tm4"], ["tm4"])
                  for half in range(2):
                      pn = PS["psN0"] if half == 0 else PS["psN1"]
                      pnid = "psN0" if half == 0 else "psN1"
                      CP("act", hn[:L, half * 256:(half + 1) * 256].rearrange("p (h v) -> p h v", h=2),
                         pn[:L, 0:258].rearrange("p (h v) -> p h v", h=2)[:, :, 0:128], [pnid], ["hn"])
                  TT("pool", hsq[:L, :], hn[:L, :], hn[:L, :], ALU.mult, ["hn"], ["hsq"])
                  RSUM(tm4[:L, 4:8], hn[:L, :].rearrange("p (h v) -> p h v", h=4), ["hn"], ["tm4"])
                  RSUM(tm4[:L, 8:12], hsq[:L, :].rearrange("p (h v) -> p h v", h=4), ["hsq"], ["tm4"])
                  TS("dve", tm4[:L, 4:8], tm4[:L, 4:8], 1.0 / 128, None, ALU.mult, None, ["tm4"], ["tm4"])
                  TS("dve", tm4[:L, 8:12], tm4[:L, 8:12], 1.0 / 128, None, ALU.mult, None, ["tm4"], ["tm4"])
                  TT("dve", tm4[:L, 12:16], tm4[:L, 4:8], tm4[:L, 4:8], ALU.mult, ["tm4"], ["tm4"])
                  TT("dve", tm4[:L, 8:12], tm4[:L, 8:12], tm4[:L, 12:16], ALU.subtract, ["tm4"], ["tm4"])
                  TT("dve", tm4[:L, 12:16], tm4[:L, 0:4], tm4[:L, 0:4], ALU.mult, ["tm4"], ["tm4"])
                  TT("dve", tm4[:L, 8:12], tm4[:L, 8:12], tm4[:L, 12:16], ALU.mult, ["tm4"], ["tm4"])
                  TS("dve", tm4[:L, 8:12], tm4[:L, 8:12], 0.0, None, ALU.max, None, ["tm4"], ["tm4"])
                  ACT(tm4[:L, 8:12], tm4[:L, 8:12], AF.Ln, ["tm4"], ["tm4"], bias=EPS)
                  ACT(tm4[:L, 8:12], tm4[:L, 8:12], AF.Exp, ["tm4"], ["tm4"], scale=-0.5)
                  TT("dve", tm4[:L, 8:12], tm4[:L, 8:12], tm4[:L, 0:4], ALU.mult, ["tm4"], ["tm4"])
                  hn3 = hn[:L, :].rearrange("p (h v) -> p h v", h=4)
                  TT("dve", hn3, hn3, tm4[:L, 4:8].unsqueeze(2).broadcast_to([L, 4, 128]), ALU.subtract, ["hn", "tm4"], ["hn"])
                  TT("dve", hn3, hn3, tm4[:L, 8:12].unsqueeze(2).broadcast_to([L, 4, 128]), ALU.mult, ["hn", "tm4"], ["hn"])
                  TT("pool", ytm[:L, :], hn[:L, :], Gml[:L, ti, :], ALU.mult, ["hn", "Gml"], ["ytm"])
                  for h in range(4):
                      TR(PS["psU"][:, h * 128:h * 128 + L], ytm[:L, h * 128:(h + 1) * 128], identf[:L, :L], ["ytm", "c128"], ["psU"])
                  CP("act", mixs[:, 0:4, col:col + L], PS["psU"][:, :].rearrange("p (j c) -> p j c", j=4)[:, :, :L],
                     ["psU"], ["mixs"])
              if has_s:
                  DMA(on_o[ly], ntmo[:], ["ntmo"], ())
              if last_pass:
                  DMA(pC_o[ly].rearrange("h d v -> d h v"), Caug[:, ly, :, 0:128], ["Caug"], ())
                  CP("dve", ncol[:, 0:4], Caug[:, ly, :, 128], ["Caug"], ["ncol"])
                  TR(PS["psU"][:4, :128], ncol[:, 0:4], identf, ["ncol", "c128"], ["psU"])
                  CP("dve", small_o[:4, 600:728], PS["psU"][:4, :128], ["psU"], ["small_o"])
                  DMA(pn_o[ly], small_o[:4, 600:728], ["small_o"], ())
              checkpoint()
              wout_part(mixs, 0)
              checkpoint()

              stage_reset()
              fa = [af([128, NMAX]) for _ in range(2)]
              xtm = af([128, NTM, 512])
              gzs = af([128, NTM, 512])
              gr = [af([8, NMAX]) for _ in range(6)]
              Wt = af([128, 512])
              Wv = af([128, 512])
              hn = af([128, 512])
              ytm = af([128, 256])
              Sin = af([64, NSQ, 128])
              cvin = af([NSQ * 3, 128])
              cvout = af([NSQ * 3, 128])
              ssdnwbc = af([128, BR])
              DMA(ssdnwbc, ssdnw_d[ly].partition_broadcast(128), (), ["ssdnwbc"])
              Btm = ab([128, NTM, 2, 128])
              BTb = ab([128, 2, NMAX])
              CTb = ab([128, 2, NMAX])
              Mh = ab([128, 512])
              Ct = ab([128, 512])
              Ctm = ab([128, NSQ, LS])
              xdt = ab([128, 256])
              xw = ab([128, 256])
              Mh2 = [Mh, ab([128, 512])]
              Ct2 = [Ct, ab([128, 512])]
              xdt2 = [xdt, ab([128, 256])]
              xw2 = [xw, ab([128, 256])]
              Ssb = ab([128, NSQ, 64])
              Bm = ab([128, NSQ, 128])
              mixs = ab([128, 4, NMAX])
              junk = ab([128, 256])
              CP("act", STb[:], STs[:, ly], ["STs"], ["STb"])
              for j in range(8):
                  if j % 2 == 0:
                      wt1, w1 = load_w(wpk_d[ly, 22 + j // 2])
                  conv_load_hist(8 + j)
                  if has_s:
                      sample_hist_tile(cvin, 4, j, sscv_d[ly])
                  proj_fm(wt1, w1, (j % 2) * 128, 128, lambda ps, pid, g0, n: fill_ub(ps, pid, g0, n))
                  conv_apply(fa[0], "fa0", 4, 4 * j, 32 + j)
                  conv_save_hist(8 + j)
                  if has_s:
                      sample_hist_out_tile(cvout, 4, j, oscv_o[ly])
                  ACT(fa[1][:, :N], fa[0][:, :N], AF.Exp, ["fa0"], ["fa1"], scale=-1.0)
                  sigmoid_from_exp(fa[1][:, :N], ["fa1"], ["fa1"])
                  TT("dve", fa[0][:, :N], fa[0][:, :N], fa[1][:, :N], ALU.mult, ["fa0", "fa1"], ["fa0"])
                  if j < 4:
                      for t in tiles:
                          L, ti, col = t["L"], t["ti"], t["col"]
                          TR(PS["psU"][:L, 0:128], fa[0][:, col:col + L], identf, ["fa0", "c128"], ["psU"])
                          CP("act", xtm[:L, ti, j * 128:(j + 1) * 128], PS["psU"][:L, 0:128], ["psU"], ["xtm"])
                  elif j < 6:
                      g = j - 4
                      CP("act", BTb[:, g, :N], fa[0][:, :N], ["fa0"], ["BTb"])
                      for t in tiles:
                          L, ti, col = t["L"], t["ti"], t["col"]
                          TR(PS["psU"][:L, 0:128], fa[0][:, col:col + L], identf, ["fa0", "c128"], ["psU"])
                          CP("act", Btm[:L, ti, g, :], PS["psU"][:L, 0:128], ["psU"], ["Btm"])
                  else:
                      g = j - 6
                      CP("act", CTb[:, g, :N], fa[0][:, :N], ["fa0"], ["CTb"])
              if last_pass:
                  prompt_hist_out(4, list(range(8, 16)), pscv_o[ly])
              for hp in range(2):
                  wt1, w1 = load_w(wpk_d[ly, 26 + hp])
                  for t in tiles:
                      L, ti = t["L"], t["ti"]

                      def ev_sz(ps, pid, L=L, ti=ti, hp=hp):
                          CP("act", ytm[:L, 0:256], ps[:L, 0:256], [pid], ["ytm"])
                          ACT(hn[:L, 0:256], ytm[:L, 0:256], AF.Exp, ["ytm"], ["hn"], scale=-1.0)
                          sigmoid_from_exp(hn[:L, 0:256], ["hn"], ["hn"])
                          TT("dve", gzs[:L, ti, hp * 256:(hp + 1) * 256], hn[:L, 0:256], ytm[:L, 0:256], ALU.mult,
                             ["hn", "ytm"], ["gzs"])
                      proj_tm(wt1, w1, 0, 256, t, ev_sz)
              for g in range(2):
                  gdt, gda, gacs, gwr, grr, gscr = gr
                  gna = gda
                  proj_fm(wsmb, "wsmb", 16 + 8 * g, 8,
                          lambda ps, pid, g0, n, g=g: ACT(gdt[:, g0:g0 + n], ps[:8, :n], AF.Exp, [pid, "g8"], ["gr0"],
                                                          bias=g8[:, 2 + g:3 + g]))
                  ACT(gdt[:, :N], gdt[:, :N], AF.Ln, ["gr0"], ["gr0"], bias=1.0)
                  TS("dve", gda[:, :N], gdt[:, :N], g8n[:, 4 + g:5 + g], None, ALU.mult, None, ["gr0", "g8n"], ["gr1", "gr3"])
                  SCAN(gacs[:, 0:NP], k8("keepP"), gda[:, 0:NP], 0.0, ALU.mult, ALU.add, ["c8", "gr1", "gr3"], ["gr2"])
                  if has_s:
                      SCAN(gacs[:, NP:NP + LS], k8("keepS"), gda[:, NP:NP + LS], 0.0, ALU.mult, ALU.add, ["c8", "gr1"], ["gr2"])
                  for kind in (("p", "s") if has_s else ("p",)):
                      ncn = NCH if kind == "p" else NSQ
                      T_ = 128 if kind == "p" else 4
                      base = 0 if kind == "p" else NP
                      seg = slice(base, base + ncn * T_)
                      al = gsm[:, 0:ncn]
                      CP("dve", al, chunk_ends(gacs, kind), ["gr2"], ["gsm"])
                      TT("dve", gwr[:, seg].rearrange("p (c t) -> p c t", t=T_), al.unsqueeze(2).broadcast_to([8, ncn, T_]),
                         gacs[:, seg].rearrange("p (c t) -> p c t", t=T_), ALU.subtract, ["gsm", "gr2"], ["gr4"])
                      ACT(small_o[:8, 128:128 + ncn], al, AF.Exp, ["gsm"], ["small_o"])
                      bcast_rows(small_o[:8, 128:128 + ncn], "small_o", ncn, 0 if kind == "p" else 1)
                  ACT(gwr[:, :N], gwr[:, :N], AF.Exp, ["gr4"], ["gr4"])
                  TT("dve", gwr[:, :N], gwr[:, :N], gdt[:, :N], ALU.mult, ["gr4", "gr0"], ["gr4"])
                  TS("dve", gna[:, :N], gacs[:, :N], -1.0, None, ALU.mult, None, ["gr2", "gr1"], ["gr3", "gr1"])
                  TS("dve", gna[:, :N], gna[:, :N], pm8, opm8, ALU.mult, ALU.add, ["gr3", "c8"], ["gr3"])
                  TS("dve", grr[:, :N], gacs[:, :N], opm8, pm8, ALU.mult, ALU.add, ["gr2", "c8"], ["gr5"])
                  def sd_front(t, g=g):
                      L, ti, col, kind = t["L"], t["ti"], t["col"], t["kind"]
                      pb = t["ti"] % 2
                      Mh, Ct, xdt, xw = Mh2[pb], Ct2[pb], xdt2[pb], xw2[pb]
                      idM, idC, idX, idW = "Mh%d" % pb, "Ct%d" % pb, "xdt%d" % pb, "xw%d" % pb
                      xg = xtm[:L, ti, g * 256:(g + 1) * 256].rearrange("p (h c) -> p h c", h=4)
                      to_tok_major(gdt, "gr0", gwr, "gr4", t, 1, gscr, "gr6")
                      MM(PS["psS"][:L, :L], BTb[:, g, col:col + L], CTb[:, g, col:col + L], True, True, ["BTb", "CTb"], ["psS"])
                      expo(gna, "gr3", grr, "gr5", grr, "gr5", t, Wt, Wv)
                      TT("dve", Mh[:L, :4 * L].rearrange("p (h l) -> p h l", h=4),
                         PS["psS"][:L, :L].unsqueeze(1).broadcast_to([L, 4, L]),
                         Wt[:L, :4 * L].rearrange("p (h l) -> p h l", h=4), ALU.mult, ["psS", "Wt"], [idM])
                      TT("pool", Ct[:, :4 * L].rearrange("p (h l) -> p h l", h=4),
                         CTb[:, g, col:col + L].unsqueeze(1).broadcast_to([128, 4, L]),
                         Wv[:, :4 * L].rearrange("p (h l) -> p h l", h=4), ALU.mult, ["CTb", "Wv"], [idC])
                      xg = xtm[:L, ti, g * 256:(g + 1) * 256].rearrange("p (h c) -> p h c", h=4)
                      TT("dve", xdt[:L, :].rearrange("p (h c) -> p h c", h=4), xg,
                         scal[:L, ti, 1, 0:4].unsqueeze(2).broadcast_to([L, 4, 64]), ALU.mult, ["xtm", "scal"], [idX])
                      TT("dve", xw[:L, :].rearrange("p (h c) -> p h c", h=4), xg,
                         scal[:L, ti, 1, 4:8].unsqueeze(2).broadcast_to([L, 4, 64]), ALU.mult, ["xtm", "scal"], [idW])
                      if kind == "s":
                          TT("pool", Bm[:L, :, :], Btm[:L, ti, g, :].unsqueeze(1).broadcast_to([L, NSQ, 128]),
                             smT[:L, :].unsqueeze(2).broadcast_to([L, NSQ, 128]), ALU.mult, ["Btm", "c128"], ["Bm"])

                  def sd_rest(t, g=g):
                      L, ti, col, kind = t["L"], t["ti"], t["col"], t["kind"]
                      pb = t["ti"] % 2
                      Mh, Ct, xdt, xw = Mh2[pb], Ct2[pb], xdt2[pb], xw2[pb]
                      idM, idC, idX, idW = "Mh%d" % pb, "Ct%d" % pb, "xdt%d" % pb, "xw%d" % pb
                      xg = xtm[:L, ti, g * 256:(g + 1) * 256].rearrange("p (h c) -> p h c", h=4)
                      for h in range(4):
                          hh = g * 4 + h
                          if kind == "s":
                              TT("pool", Ctm, Ct[:, h * L:(h + 1) * L].unsqueeze(1).broadcast_to([128, NSQ, L]),
                                 smkb[:, :].rearrange("p (b l) -> p b l", b=NSQ), ALU.mult, [idC, "smkb"], ["Ctm"])
                              for b4 in range(0, NSQ, 8):
                                  DMA(Sin[:, b4:b4 + 8, :], sS_d[ly][b4:b4 + 8, hh].rearrange("b p n -> p b n"), (), ["Sin"])
                              for bq in range(NSQ // 8):
                                  for b in range(8):
                                      TR(PS["psU"][:, b * 64:(b + 1) * 64], Sin[:, bq * 8 + b, :], identf[:64, :64],
                                         ["Sin", "c128"], ["psU"])
                                  CP("act", Ssb[:, bq * 8:(bq + 1) * 8, :],
                                     PS["psU"][:, :].rearrange("p (b c) -> p b c", b=8), ["psU"], ["Ssb"])
                          MM(PS["psN0"][:L, h * 64:(h + 1) * 64], Mh[:L, h * L:(h + 1) * L], xdt[:L, h * 64:(h + 1) * 64],
                             True, False, [idM, idX], ["psN0"])
                          if kind == "p":
                              MM(PS["psN0"][:L, h * 64:(h + 1) * 64], Ct[:, h * L:(h + 1) * L],
                                 STb[:, hh * 64:(hh + 1) * 64], False, True, [idC, "STb"], ["psN0"])
                          else:
                              for b in range(NSQ):
                                  MM(PS["psN0"][:L, h * 64:(h + 1) * 64], Ctm[:, b, :], Ssb[:, b, :],
                                     False, b == NSQ - 1, ["Ctm", "Ssb"], ["psN0"])
                              for bq in range(NSQ // 4):
                                  MM(PS["psV"][:64, :512], xw[:L, h * 64:(h + 1) * 64],
                                     Bm[:L, bq * 4:(bq + 1) * 4, :], True, True, [idW, "Bm"], ["psV"])
                                  TT("dve", Sin[:, bq * 4:(bq + 1) * 4, :], Sin[:, bq * 4:(bq + 1) * 4, :],
                                     bcw[:64, 1, h, bq * 4:(bq + 1) * 4].unsqueeze(2).broadcast_to([64, 4, 128]), ALU.mult,
                                     ["Sin", "bcw"], ["Sin"])
                                  TT("dve", Sin[:, bq * 4:(bq + 1) * 4, :], Sin[:, bq * 4:(bq + 1) * 4, :],
                                     PS["psV"][:64, :512].rearrange("p (b n) -> p b n", b=4), ALU.add, ["Sin", "psV"], ["Sin"])
                              for b4 in range(0, NSQ, 8):
                                  DMA(oS_o[ly][b4:b4 + 8, hh].rearrange("b p n -> p b n"), Sin[:, b4:b4 + 8, :], ["Sin"], ())
                      TT("dve", hn[:L, 0:256].rearrange("p (h c) -> p h c", h=4), xg,
                         Dbc[:L, g * 4:(g + 1) * 4].unsqueeze(2).broadcast_to([L, 4, 64]), ALU.mult, ["xtm", "Dbc"], ["hn"])
                      TT("dve", hn[:L, 0:256], hn[:L, 0:256], PS["psN0"][:L, 0:256], ALU.add, ["hn", "psN0"], ["hn"])
                      TT("dve", hn[:L, 0:256], hn[:L, 0:256], gzs[:L, ti, g * 256:(g + 1) * 256], ALU.mult, ["hn", "gzs"], ["hn"])
                      ACT(junk[:L, 0:256], hn[:L, 0:256], AF.Square, ["hn"], ["junk", "tm4"], accum=tm4[:L, 0:1])
                      ACT(tm4[:L, 1:2], tm4[:L, 0:1], AF.Ln, ["tm4"], ["tm4"], bias=EPS, scale=1.0 / 256)
                      ACT(tm4[:L, 2:3], tm4[:L, 1:2], AF.Exp, ["tm4"], ["tm4"], scale=-0.5)
                      STT("dve", ytm[:L, 0:256], hn[:L, 0:256], tm4[:L, 2:3], ssdnwbc[:L, g * 256:(g + 1) * 256],
                          ALU.mult, ALU.mult, ["hn", "tm4", "ssdnwbc"], ["ytm"])
                      for h2 in range(2):
                          TR(PS["psU"][:, h2 * 128:h2 * 128 + L], ytm[:L, h2 * 128:(h2 + 1) * 128], identf[:L, :L],
                             ["ytm", "c128"], ["psU"])
                      CP("act", mixs[:, 2 * g:2 * g + 2, col:col + L],
                         PS["psU"][:, 0:256].rearrange("p (j c) -> p j c", j=2)[:, :, :L], ["psU"], ["mixs"])
                      if kind == "p":
                          psu, psuid = next_ps2()
                          MM(psu[:, 0:256], Btm[:L, ti, g, :], xw[:L, :], True, True, ["Btm", idW], [psuid])
                          sg = STs[:, ly, g * 256:(g + 1) * 256]
                          TT("pool", sg.rearrange("p (h c) -> p h c", h=4), sg.rearrange("p (h c) -> p h c", h=4),
                             bcw[:, 0, :, ti:ti + 1].broadcast_to([128, 4, 64]), ALU.mult, ["STs", "bcw"], ["STs"])
                          TT("dve", sg, sg, psu[:, 0:256], ALU.add, ["STs", psuid], ["STs"])
                          CP("act", STb[:, g * 256:(g + 1) * 256], sg, ["STs"], ["STb"])

                  sd_front(tiles[0])
                  for i_, t in enumerate(tiles):
                      if i_ + 1 < len(tiles):
                          sd_front(tiles[i_ + 1])
                      sd_rest(t)
              if last_pass:
                  for q4 in range(4):
                      TR(PS["psU"][:, q4 * 128:(q4 + 1) * 128], STs[:, ly, q4 * 128:(q4 + 1) * 128], identf, ["STs", "c128"], ["psU"])
                  CP("dve", hn[:, :], PS["psU"][:, :], ["psU"], ["hn"])
                  DMA(pS_o[ly].rearrange("(q p) n -> p q n", p=128), hn[:, :].rearrange("p (q n) -> p q n", q=4), ["hn"], ())
              checkpoint()
              wout_part(mixs, 4)
              checkpoint()

              if ly == L2 - 1:
                  stage_reset()
                  xn32s = [af([128, D]), af([128, D])]
                  nwbc = af([128, D])
                  junks = [ab([128, D]), ab([128, D])]
                  DMA(nwbc, fnw_d.partition_broadcast(128), (), ["nwbc"])
                  for t in tiles:
                      L, ti = t["L"], t["ti"]
                      xn32, xid = rmsnorm_tile(t, None, xn32s, junks, nwbc)
                      dst = yp_o[t["tok"]:t["tok"] + 128, :] if t["kind"] == "p" else ys_o
                      DMA(dst, xn32[:L], [xid], ())
    except _Stop:
        pass

    print("NOPS", len(R.ops), "arena max words f32/bf16", amax, flush=True)
    block = st.enter_context(nc.Block())
    R.emit(nc, block, st)
    st.close()
    return nc


_CACHE = {}


def _host_layout(inputs, NSQ, c):
    f = np.ascontiguousarray
    b0 = c * NSQ
    L2 = DEPTH
    w_in = inputs["w_in"]
    m = {}
    m["xp"] = f(inputs["x_prompt"][c % inputs["x_prompt"].shape[0]])
    m["xs"] = f(inputs["x_sample"][b0:b0 + NSQ].reshape(NSQ * 4, D))
    m["sC"] = f(inputs["state_mlstm_C"][:, b0:b0 + NSQ])
    m["sn"] = f(inputs["state_mlstm_n"][:, b0:b0 + NSQ].reshape(L2, NSQ, 512))
    smt = np.transpose(inputs["state_mlstm_m"][:, b0:b0 + NSQ], (0, 2, 1))
    m["sm"] = f(np.concatenate([smt, smt], axis=1))
    m["sS"] = f(inputs["state_ssd"][:, b0:b0 + NSQ])
    m["sscv"] = f(inputs["state_ssd_conv"][:, b0:b0 + NSQ].reshape(L2, NSQ * 3, 1024))
    m["sccv"] = f(inputs["state_sconv_conv"][:, b0:b0 + NSQ].reshape(L2, NSQ * 2, 512))
    m["srh"] = f(inputs["state_rglru_h"][:, b0:b0 + NSQ])
    m["srcv"] = f(inputs["state_rglru_conv"][:, b0:b0 + NSQ].reshape(L2, NSQ * 3, 512))
    return m


def _shared_layout(inputs, NCH, NSQ):
    f = np.ascontiguousarray
    L2 = DEPTH
    w_in = inputs["w_in"]
    s = {}
    def cols(o, n):
        return list(range(o, o + n))
    chunks = []
    for j in range(4):
        chunks.append(cols(OSB + j * 128, 128) + cols(OSC + j * 128, 128))
        chunks.append(cols(OSH + j * 128, 128) + cols(OSCZ + j * 128, 128))
    for j in range(4):
        chunks.append(cols(ORX + j * 128, 128) + cols(ORZ + j * 128, 128))
    for cg in range(4):
        chunks.append(cols(OO + cg * 128, 128) + cols(OZ + cg * 128, 128))
    for base in (OQ, OKK, OV):
        for hp in range(2):
            chunks.append(cols(base + hp * 256, 256))
    for jp in range(4):
        chunks.append(cols(OXBC + jp * 256, 256))
    for hp in range(2):
        chunks.append(cols(OSZ + hp * 256, 256))
    assert len(chunks) == 28

    def pack(W):
        K = W.shape[0] // 128
        return np.ascontiguousarray(W.reshape(K, 128, W.shape[1]).transpose(1, 0, 2).reshape(128, K * W.shape[1]))
    wpk = np.empty((L2, 28, 128, 4096), np.float32)
    for ly in range(L2):
        for k, cc in enumerate(chunks):
            wpk[ly, k] = pack(w_in[ly][:, cc])
    s["wpk"] = wpk
    wopk = np.empty((L2, 4, 8, 128, 1024), np.float32)
    for ly in range(L2):
        for g in range(4):
            for dg in range(8):
                wopk[ly, g, dg] = pack(inputs["w_out"][ly][g * 512:(g + 1) * 512, dg * 256:(dg + 1) * 256])
    s["wopk"] = wopk
    wi = w_in[:, :, OI:OI + 4]
    wf = w_in[:, :, OF:OF + 4]
    wd0 = w_in[:, :, ODT:ODT + 4]
    wd1 = w_in[:, :, ODT + 4:ODT + 8]
    wsm = np.concatenate([wi, wi, wf, wf, wd0, wd0, wd1, wd1], axis=2)
    s["wsm"] = np.stack([pack(wsm[ly]) for ly in range(L2)], axis=0)
    wabd = np.zeros((L2, 8, 128, 128), np.float32)
    for ly in range(L2):
        for j in range(4):
            for k in range(2):
                blk = 2 * j + k
                wabd[ly, j, k * 64:(k + 1) * 64, k * 64:(k + 1) * 64] = inputs["rg_wa"][ly, blk]
                wabd[ly, 4 + j, k * 64:(k + 1) * 64, k * 64:(k + 1) * 64] = inputs["rg_wx"][ly, blk]
    s["wabd"] = f(np.transpose(wabd, (0, 2, 1, 3)).reshape(L2, 128, 1024))
    cvec = np.zeros((L2, 128, 84), np.float32)

    def fm(v, nt):
        return v.reshape(nt, 128).T
    for ly in range(L2):
        cw = inputs["ssd_conv_w"][ly]
        cvec[ly, :, 0:32] = np.transpose(cw.reshape(4, 8, 128), (2, 1, 0)).reshape(128, 32)
        cvec[ly, :, 32:40] = fm(inputs["ssd_conv_b"][ly], 8)
        sw = inputs["sc_conv_w"][ly]
        cvec[ly, :, 40:52] = np.transpose(sw.reshape(3, 4, 128), (2, 1, 0)).reshape(128, 12)
        rw = inputs["rg_conv_w"][ly]
        cvec[ly, :, 52:68] = np.transpose(rw.reshape(4, 4, 128), (2, 1, 0)).reshape(128, 16)
        cvec[ly, :, 68:72] = fm(inputs["rg_conv_b"][ly], 4)
        cvec[ly, :, 72:76] = fm(inputs["rg_ba"][ly], 4)
        cvec[ly, :, 76:80] = fm(inputs["rg_bx"][ly], 4)
        cvec[ly, :, 80:84] = fm(inputs["rg_lambda"][ly], 4)
    s["cvec"] = cvec
    g8 = np.zeros((L2, 8, 8), np.float32)
    for ly in range(L2):
        d2 = lambda v: np.concatenate([v, v])
        g8[ly, :, 0] = d2(inputs["ml_i_bias"][ly])
        g8[ly, :, 1] = d2(inputs["ml_f_bias"][ly])
        g8[ly, :, 2] = d2(inputs["ssd_dt_bias"][ly][0:4])
        g8[ly, :, 3] = d2(inputs["ssd_dt_bias"][ly][4:8])
        g8[ly, :, 4] = d2(inputs["ssd_A_log"][ly][0:4])
        g8[ly, :, 5] = d2(inputs["ssd_A_log"][ly][4:8])
    s["g8"] = g8
    for k in ("norm_w", "ml_norm_w", "ssd_norm_w", "ssd_D", "final_norm_w"):
        s[k] = f(inputs[k])
    C128h, _, C8h, _ = host_consts(NCH, NSQ)
    s["c128"] = C128h
    s["c8"] = C8h
    return s


def run(inputs, NCH, n_cores=8):
    inputs = {k: np.asarray(v, dtype=np.float32) for k, v in inputs.items()}
    BP, TP = inputs["x_prompt"].shape[0], inputs["x_prompt"].shape[1]
    BS = inputs["x_sample"].shape[0]
    NSQ = BS // n_cores
    key = (TP, NCH, NSQ)
    if key not in _CACHE:
        _CACHE[key] = build(TP, NCH, NSQ)
    nc = _CACHE[key]
    shared = _shared_layout(inputs, NCH, NSQ)
    in_maps = []
    for c in range(n_cores):
        m = _host_layout(inputs, NSQ, c)
        m.update(shared)
        in_maps.append(m)
    res = run_bass_kernel_spmd(nc, in_maps, core_ids=list(range(n_cores)))
    r = res.results
    L2 = DEPTH
    cat = lambda k, ax: np.concatenate([r[c][k] for c in range(n_cores)], axis=ax)
    stk = lambda k: np.stack([r[c][k] for c in range(BP)], axis=1)
    y_prompt = np.stack([r[c]["yp"] for c in range(BP)], axis=0)
    y_sample = cat("ys", 0).reshape(BS, 4, D)
    p_C = stk("pC")
    p_n = stk("pn")
    p_m = stk("pm")
    p_ssd = stk("pS").reshape(L2, BP, 8, 64, 128)
    p_ssd_conv = stk("pscv")
    p_sc_conv = stk("pccv")
    p_rg_h = stk("prh").reshape(L2, BP, 512)
    p_rg_conv = stk("prcv")
    s_C = cat("oC", 1)
    s_n = cat("on", 1).reshape(L2, BS, 4, 128)
    s_m = np.transpose(cat("om", 2), (0, 2, 1))
    s_ssd = cat("oS", 1)
    s_ssd_conv = cat("oscv", 1).reshape(L2, BS, 3, 1024)
    s_sc_conv = cat("occv", 1).reshape(L2, BS, 2, 512)
    s_rg_h = cat("orh", 1)
    s_rg_conv = cat("orcv", 1).reshape(L2, BS, 3, 512)
    outs = (y_prompt, y_sample, p_C, p_n, p_m, p_ssd, p_ssd_conv, p_sc_conv, p_rg_h, p_rg_conv,
            s_C, s_n, s_m, s_ssd, s_ssd_conv, s_sc_conv, s_rg_h, s_rg_conv)
    return tuple(np.ascontiguousarray(o, dtype=np.float32) for o in outs)


def kernel(**inputs):
    return run(inputs, NCH=4)
```

```python
from contextlib import ExitStack

import numpy as np

import concourse.bass as bass
import concourse.mybir as mybir
from concourse.bass_utils import run_bass_kernel_spmd

F32 = mybir.dt.float32
BF16 = mybir.dt.bfloat16
ALU = mybir.AluOpType
AF = mybir.ActivationFunctionType

D = 2048
BR = 512
PO = 7184
DEPTH = 2
EPS = 1e-6
NEG = -30000.0
OQ, OKK, OV, OO, OZ, OI, OF = 0, 512, 1024, 1536, 2048, 2560, 2564
OSZ, OXBC, ODT = 2568, 3080, 4104
OSB, OSC, OSH, OSCZ = 4112, 4624, 5136, 5648
ORX, ORZ = 6160, 6672

ENGS = ("pe", "act", "dve", "pool", "sp")
NDMASEM = 8


class Op:
    __slots__ = ("eng", "fn", "reads", "writes", "dma", "deps", "sig", "tick", "dsem", "dcount", "idx", "prev")

    def __init__(self, eng, fn, reads, writes, dma):
        self.eng, self.fn, self.reads, self.writes, self.dma = eng, fn, reads, writes, dma
        self.deps = []
        self.sig = False
        self.tick = 0
        self.dsem = -1
        self.dcount = 0
        self.prev = 0


class _Stop(Exception):
    pass


class Rec:
    def __init__(self):
        self.ops = []
        self.last_w = {}
        self.readers = {}
        import os as _os
        self.maxops = int(_os.environ.get("KOPS", "100000000"))
        self.fence_ops = []
        self.pending = set()
        self.fence_from = 0

    PERSIST = frozenset(("c128", "c8", "identb", "maskPb", "maskSb", "smkb", "xnT", "st4", "st4b", "tm4", "Dbc", "cvec", "g8",
                         "g8n", "nsp8", "nba", "wb0", "wb1", "wb2", "wsmb", "wabd", "Caug", "Cbf", "mcar", "STs", "STb", "cvcar",
                         "hcar", "ubp", "ubs", "gsm", "RM", "RMv", "scal", "bcw", "ncol", "cvtmp", "h0s", "small_o"))

    def touches_arena(self, op):
        for b in op.reads + op.writes:
            if b in self.PERSIST or b.startswith("ps") or b.startswith("xres"):
                continue
            return True
        return False

    def fence(self):
        last = {}
        dmas = []
        for op in self.ops[self.fence_from:]:
            if not self.touches_arena(op):
                continue
            if op.dma:
                dmas.append(op.idx)
            else:
                last[op.eng] = op.idx
        self.fence_ops = sorted(set(self.fence_ops) | set(last.values()) | set(dmas)) if self.pending else \
            sorted(set(last.values()) | set(dmas))
        self.pending = set(ENGS)
        self.fence_from = len(self.ops)

    def add(self, eng, fn, reads=(), writes=(), dma=False):
        if len(self.ops) >= self.maxops:
            return None
        writes = tuple(writes) + tuple(b for b in reads if isinstance(b, str) and b.startswith("ps") and b not in writes)
        op = Op(eng, fn, tuple(reads), tuple(writes), dma)
        op.idx = len(self.ops)
        deps = set()
        if eng in self.pending and self.touches_arena(op):
            deps |= set(self.fence_ops)
            self.pending.discard(eng)
        for b in op.reads:
            w = self.last_w.get(b)
            if w is not None:
                deps.add(w)
        for b in op.writes:
            w = self.last_w.get(b)
            if w is not None:
                deps.add(w)
            for r in self.readers.get(b, ()):
                deps.add(r)
        deps.discard(op.idx)
        op.deps = sorted(deps)
        for b in op.reads:
            self.readers.setdefault(b, []).append(op.idx)
        for b in op.writes:
            self.last_w[b] = op.idx
            self.readers[b] = []
        self.ops.append(op)
        return op

    def emit(self, nc, block, stack):
        ops = self.ops
        for op in ops:
            for d in op.deps:
                p = ops[d]
                if p.dma:
                    continue
                if p.eng == "pe" and op.eng == "pe" and not op.dma:
                    continue
                p.sig = True
        sem = {e: stack.enter_context(nc.semaphore("s_" + e)) for e in ("pe", "act", "dve", "pool")}
        dsem = {q: [stack.enter_context(nc.semaphore("d_%s%d" % (q, i))) for i in range(NDMASEM)]
                for q in ("sp", "pool")}
        cnt = {e: 0 for e in sem}
        dn = {q: 0 for q in dsem}
        dc = {q: [0] * NDMASEM for q in dsem}
        for op in ops:
            if op.dma:
                q = op.eng
                i = dn[q] % NDMASEM
                dn[q] += 1
                op.dsem = i
                op.prev = dc[q][i]
                dc[q][i] += 1
                op.dcount = dc[q][i]
            elif op.sig:
                cnt[op.eng] += 1
                op.tick = cnt[op.eng]
        per = {e: [o for o in ops if o.eng == e] for e in ENGS}

        def run(eng_name, e):
            waited = {}

            def need(key, s, v):
                if waited.get(key, 0) >= v:
                    return
                waited[key] = v
                e.wait_ge(s, v)

            for op in per[eng_name]:
                for d in op.deps:
                    p = ops[d]
                    if p.dma:
                        need(("d", p.eng, p.dsem), dsem[p.eng][p.dsem], 16 * p.dcount)
                    else:
                        if p.eng == "pe" and eng_name == "pe" and not op.dma:
                            continue
                        need(("c", p.eng), sem[p.eng], p.tick)
                if op.dma:
                    if op.prev > 0:
                        need(("d", op.eng, op.dsem), dsem[op.eng][op.dsem], 16 * op.prev)
                    op.fn(e).then_inc(dsem[op.eng][op.dsem], 16)
                else:
                    ins = op.fn(e)
                    if op.sig:
                        ins.then_inc(sem[op.eng], 1)
            if eng_name in dsem:
                for i in range(NDMASEM):
                    if dc[eng_name][i] > 0:
                        need(("d", eng_name, i), dsem[eng_name][i], 16 * dc[eng_name][i])

        @block.tensor
        def _(e):
            run("pe", e)

        @block.scalar
        def _(e):
            run("act", e)

        @block.vector
        def _(e):
            run("dve", e)

        @block.gpsimd
        def _(e):
            run("pool", e)

        @block.sync
        def _(e):
            run("sp", e)


def host_consts(NCH, NSQ):
    LS = 4 * NSQ
    NP = 128 * NCH
    c128 = {}
    c128["ident"] = np.eye(128, dtype=np.float32)
    s = np.arange(128)[:, None]
    l = np.arange(128)[None, :]
    mp = np.where(s <= l, 0.0, NEG).astype(np.float32)
    c128["maskP"] = np.tile(mp[:, None, :], (1, 4, 1)).reshape(128, 512)
    ms = np.full((128, 128), NEG, np.float32)
    sl = np.arange(LS)
    okm = (sl[:, None] // 4 == sl[None, :] // 4) & (sl[:, None] <= sl[None, :])
    ms[:LS, :LS] = np.where(okm, 0.0, NEG)
    c128["maskS"] = np.tile(ms[:, None, :LS], (1, 4, 1)).reshape(128, 4 * LS)
    smk = (np.arange(NSQ)[:, None] == (sl[None, :] // 4)).astype(np.float32)
    c128["smk"] = np.tile(smk.reshape(1, NSQ * LS), (128, 1))
    smT = np.zeros((128, NSQ), np.float32)
    smT[:LS] = smk.T
    c128["smT"] = smT
    C128 = np.concatenate([c128[k] for k in ("ident", "maskP", "maskS", "smk", "smT")], axis=1)
    offs128 = {}
    o = 0
    for k in ("ident", "maskP", "maskS", "smk", "smT"):
        offs128[k] = (o, c128[k].shape[1])
        o += c128[k].shape[1]
    c8 = {}
    keepP = np.ones((8, NP), np.float32)
    keepP[:, ::128] = 0
    c8["keepP"] = keepP
    keepS = np.ones((8, LS), np.float32)
    keepS[:, ::4] = 0
    c8["keepS"] = keepS
    negS = np.zeros((8, LS), np.float32)
    negS[:, ::4] = -1e30
    c8["negS"] = negS
    pm = np.array([1, 1, 1, 1, 0, 0, 0, 0], np.float32)[:, None]
    c8["pm"] = pm
    c8["opm"] = 1 - pm
    mf = np.zeros((8, 4, 128), np.float32)
    for k in range(8):
        mf[k, k % 4, :] = 1
    c8["maskfull"] = mf.reshape(8, 512)
    m8 = np.zeros((8, 4), np.float32)
    for k in range(4):
        m8[k, k] = 1
    c8["mask8h"] = m8
    lv = np.zeros((8, 128), np.float32)
    lv[4:] = 1
    c8["LV"] = lv
    c8["ones8"] = np.ones((8, 128), np.float32)
    names8 = ("keepP", "keepS", "negS", "pm", "opm", "maskfull", "mask8h", "LV", "ones8")
    C8 = np.concatenate([c8[k] for k in names8], axis=1)
    offs8 = {}
    o = 0
    for k in names8:
        offs8[k] = (o, c8[k].shape[1])
        o += c8[k].shape[1]
    return C128, offs128, C8, offs8


def build(TP, NCH, NSQ):
    LS = 4 * NSQ
    NPASS = TP // (128 * NCH)
    assert NPASS * 128 * NCH == TP
    NTM = NCH + 1
    NMAX = 128 * NCH + LS
    NP = 128 * NCH
    L2 = DEPTH
    C128h, o128, C8h, o8 = host_consts(NCH, NSQ)
    nc = bass.Bass("TRN2", target_bir_lowering=False)

    def din(name, shape):
        return nc.dram_tensor(name, list(shape), F32, kind="ExternalInput").ap()

    def dout(name, shape):
        return nc.dram_tensor(name, list(shape), F32, kind="ExternalOutput").ap()

    xp_d = din("xp", [TP, D])
    xs_d = din("xs", [LS, D])
    sC_d = din("sC", [L2, NSQ, 4, 128, 128])
    sn_d = din("sn", [L2, NSQ, 512])
    sm_d = din("sm", [L2, 8, NSQ])
    sS_d = din("sS", [L2, NSQ, 8, 64, 128])
    sscv_d = din("sscv", [L2, NSQ * 3, 1024])
    sccv_d = din("sccv", [L2, NSQ * 2, 512])
    srh_d = din("srh", [L2, NSQ, 512])
    srcv_d = din("srcv", [L2, NSQ * 3, 512])
    wpk_d = din("wpk", [L2, 28, 128, 4096])
    wsm_d = din("wsm", [L2, 128, 512])
    wopk_d = din("wopk", [L2, 4, 8, 128, 1024])
    wabd_d = din("wabd", [L2, 128, 1024])
    cvec_d = din("cvec", [L2, 128, 84])
    g8_d = din("g8", [L2, 8, 8])
    normw_d = din("norm_w", [L2, D])
    mlnw_d = din("ml_norm_w", [L2, BR])
    ssdnw_d = din("ssd_norm_w", [L2, BR])
    ssdD_d = din("ssd_D", [L2, 8])
    fnw_d = din("final_norm_w", [D])
    c128_d = din("c128", list(C128h.shape))
    c8_d = din("c8", list(C8h.shape))

    yp_o = dout("yp", [TP, D])
    ys_o = dout("ys", [LS, D])
    pC_o = dout("pC", [L2, 4, 128, 128])
    pn_o = dout("pn", [L2, 4, 128])
    pm_o = dout("pm", [L2, 4])
    pS_o = dout("pS", [L2, 512, 128])
    pscv_o = dout("pscv", [L2, 3, 1024])
    pccv_o = dout("pccv", [L2, 2, 512])
    prh_o = dout("prh", [L2, 4, 128])
    prcv_o = dout("prcv", [L2, 3, 512])
    oC_o = dout("oC", [L2, NSQ, 4, 128, 128])
    on_o = dout("on", [L2, NSQ, 512])
    om_o = dout("om", [L2, 4, NSQ])
    oS_o = dout("oS", [L2, NSQ, 8, 64, 128])
    oscv_o = dout("oscv", [L2, NSQ * 3, 1024])
    occv_o = dout("occv", [L2, NSQ * 2, 512])
    orh_o = dout("orh", [L2, NSQ, 512])
    orcv_o = dout("orcv", [L2, NSQ * 3, 512])

    R = Rec()
    st = ExitStack()
    st.enter_context(nc.allow_non_contiguous_dma(reason="small strided state/constant transfers"))

    def sb(name, shape, dt=F32):
        return st.enter_context(nc.sbuf_tensor("sb_" + name, list(shape), dt))

    def psum(name):
        return st.enter_context(nc.psum_tensor(name, [128, 512], F32))

    def TT(eng, out, in0, in1, op, r, w):
        R.add(eng, lambda e: e.tensor_tensor(out=out, in0=in0, in1=in1, op=op), r, w)

    def TS(eng, out, in0, s1, s2, op0, op1, r, w):
        if op1 is None:
            R.add(eng, lambda e: e.tensor_scalar(out=out, in0=in0, scalar1=s1, scalar2=None, op0=op0), r, w)
        else:
            R.add(eng, lambda e: e.tensor_scalar(out=out, in0=in0, scalar1=s1, scalar2=s2, op0=op0, op1=op1), r, w)

    def STT(eng, out, in0, scalar, in1, op0, op1, r, w):
        R.add(eng, lambda e: e.scalar_tensor_tensor(out=out, in0=in0, scalar=scalar, in1=in1, op0=op0, op1=op1), r, w)

    def CP(eng, out, in_, r, w):
        if eng == "act":
            R.add(eng, lambda e: e.activation(out=out, in_=in_, func=AF.Copy), r, w)
        else:
            R.add(eng, lambda e: e.tensor_copy(out=out, in_=in_), r, w)

    def ACT(out, in_, func, r, w, bias=None, scale=1.0, accum=None):
        kw = {}
        if bias is not None:
            kw["bias"] = bias
        if accum is not None:
            kw["accum_out"] = accum
        R.add("act", lambda e: e.activation(out=out, in_=in_, func=func, scale=scale, **kw), r, w)

    def MM(out, lhsT, rhs, start, stop, r, w):
        R.add("pe", lambda e: e.matmul(out, lhsT=lhsT, rhs=rhs, start=start, stop=stop), r, w)

    def TR(out, in_, ident, r, w):
        R.add("pe", lambda e: e.transpose(out=out, in_=in_, identity=ident), r, w)

    def DMA(out, in_, r, w, q="sp"):
        R.add(q, lambda e: e.dma_start(out=out, in_=in_), r, w, dma=True)

    def SCAN(out, d0, d1, init, op0, op1, r, w):
        R.add("dve", lambda e: e.tensor_tensor_scan(out=out, data0=d0, data1=d1, initial=init, op0=op0, op1=op1), r, w)

    def MEMSET(eng, ap, val, w):
        R.add(eng, lambda e: e.memset(ap, val), (), w)

    def RSUM(out, in_, r, w):
        R.add("dve", lambda e: e.reduce_sum(out=out, in_=in_, axis=mybir.AxisListType.X), r, w)

    def RECIP(out, in_, r, w):
        R.add("dve", lambda e: e.reciprocal(out=out, in_=in_), r, w)

    def sigmoid_from_exp(t, r, w):
        ACT(t, t, AF.Ln, r, w, bias=1.0)
        ACT(t, t, AF.Exp, r, w, scale=-1.0)

    c128 = sb("c128", [128, 128 + NSQ])
    c8 = sb("c8", C8h.shape)
    identf = c128[:, 0:128]
    smT = c128[:, 128:128 + NSQ]

    def k8(name):
        o, n = o8[name]
        return c8[:, o:o + n]

    identb = sb("identb", [128, 128], BF16)
    maskPb = sb("maskPb", [128, 512], BF16)
    maskSb = sb("maskSb", [128, 4 * LS], BF16)
    smkb = sb("smkb", [128, NSQ * LS], BF16)
    pm8 = k8("pm")
    opm8 = k8("opm")
    xres = sb("xres", [128, NTM, D])
    xnT = sb("xnT", [128, 16, NMAX], BF16)
    st4 = sb("st4", [128, 8])
    tm4 = sb("tm4", [128, 16])
    Dbc = sb("Dbc", [128, 8])
    cvec = sb("cvec", [128, 84])
    g8 = sb("g8", [8, 8])
    g8n = sb("g8n", [8, 8])
    nsp8 = sb("nsp8", [128, 4])
    nba = sb("nba", [128, 8])
    NWB = 3
    wb = [sb("wb%d" % i, [128, 16, 256], BF16) for i in range(NWB)]
    wsmb = sb("wsmb", [128, 16, 32], BF16)
    wabd = sb("wabd", [128, 8, 128], BF16)
    PS = {k: psum(k) for k in ("psA", "psB", "psS", "psE", "psV", "psN0", "psN1", "psU")}
    Caug = sb("Caug", [128, L2, 4, 129])
    Cbf = sb("Cbf", [128, 4, 129], BF16)
    mcar = sb("mcar", [8, L2])
    STs = sb("STs", [128, L2, 512])
    STb = sb("STb", [128, 512], BF16)
    cvcar = sb("cvcar", [128, L2, 16, 3])
    hcar = sb("hcar", [128, L2, 4])
    ubp = sb("ubp", [128, 3 + NP])
    ubs = sb("ubs", [128, NSQ, 7])
    gsm = sb("gsm", [8, 64])
    RM = sb("RM", [8, 512])
    RMv = sb("RMv", [8, 512])
    scal = sb("scal", [128, NTM, 2, 8])
    bcw = sb("bcw", [128, 2, 4, 32])
    ncol = sb("ncol", [128, NSQ])
    cvtmp = sb("cvtmp", [128, NSQ * 3])
    h0s = sb("h0s", [128, NSQ])
    small_o = sb("small_o", [8, 768])
    AFW = 14480
    ABW = 17192
    arenaF = sb("arenaF", [128, AFW])
    arenaB = sb("arenaB", [128, ABW], BF16)
    aoff = [0, 0]
    amax = [0, 0]
    import os as _os
    print('SBUF remaining after persistent', nc.sbuf_bytes_remaining, flush=True)

    def stage_reset():
        R.fence()
        aoff[0] = 0
        aoff[1] = 0

    def carve(arena, k, cap, shape, pat):
        n = 1
        for d_ in shape[1:]:
            n *= d_
        o = aoff[k]
        aoff[k] += n
        amax[k] = max(amax[k], aoff[k])
        if _os.environ.get("KDRY"):
            o = 0
        else:
            assert aoff[k] <= cap, ("arena overflow", k, aoff[k], cap)
        v = arena[:shape[0], o:o + n]
        if len(shape) == 3:
            v = v.rearrange("p (a b) -> p a b", a=shape[1])
        elif len(shape) == 4:
            v = v.rearrange("p (a b c) -> p a b c", a=shape[1], b=shape[2])
        return v

    def af(shape):
        return carve(arenaF, 0, AFW, shape, None)

    def ab(shape):
        return carve(arenaB, 1, ABW, shape, None)

    DMA(c128[:, 0:128], c128_d[:, 0:128], (), ["c128"])
    o_, n_ = o128["smT"]
    DMA(c128[:, 128:128 + NSQ], c128_d[:, o_:o_ + n_], (), ["c128"])
    DMA(c8[:], c8_d, (), ["c8"])
    DMA(identb[:], c128_d[:, 0:128], (), ["identb"], q="pool")
    o_, n_ = o128["maskP"]
    DMA(maskPb[:], c128_d[:, o_:o_ + n_], (), ["maskPb"], q="pool")
    o_, n_ = o128["maskS"]
    DMA(maskSb[:], c128_d[:, o_:o_ + n_], (), ["maskSb"], q="pool")
    o_, n_ = o128["smk"]
    DMA(smkb[:], c128_d[:, o_:o_ + n_], (), ["smkb"], q="pool")
    MEMSET("pool", xnT[:], 0.0, ["xnT"])
    MEMSET("pool", Caug[:], 0.0, ["Caug"])
    MEMSET("pool", mcar[:], -1e30, ["mcar"])
    MEMSET("pool", STs[:], 0.0, ["STs"])
    MEMSET("pool", cvcar[:], 0.0, ["cvcar"])
    MEMSET("pool", hcar[:], 0.0, ["hcar"])
    MEMSET("pool", ubs[:], 0.0, ["ubs"])
    MEMSET("pool", ubp[:], 0.0, ["ubp"])
    MEMSET("pool", arenaF[:], 0.0, ["arenaF"])
    MEMSET("pool", arenaB[:], 0.0, ["arenaB"])
    stage_reset()

    wctr = [0]

    def load_w(src2d, nct=16):
        i = wctr[0] % NWB
        wctr[0] += 1
        DMA(wb[i][:, 0:nct, :].rearrange("p a b -> p (a b)"), src2d, (), ["wb%d" % i], q="pool")
        return wb[i], "wb%d" % i

    pctr = [0]

    def next_ps():
        k = ("psA", "psB", "psN1", "psN0", "psS", "psE")[pctr[0] % 6]
        pctr[0] += 1
        return PS[k], k

    def next_ps2():
        k = ("psA", "psB")[pctr[0] % 2]
        pctr[0] += 1
        return PS[k], k

    import os as _os
    KSTOP = int(_os.environ.get("KSTOP", "99"))
    kst = [0]

    def checkpoint():
        kst[0] += 1
        if kst[0] > KSTOP:
            raise _Stop()

    try:
      for pi in range(NPASS):
          tiles = [dict(kind="p", L=128, col=128 * i, ti=i, tok=pi * NP + 128 * i) for i in range(NCH)]
          last_pass = pi == NPASS - 1
          if last_pass:
              tiles.append(dict(kind="s", L=LS, col=NP, ti=NCH, tok=0))
          NT = len(tiles)
          N = (NP + LS) if last_pass else NP
          groups = []
          c0 = 0
          while c0 < N:
              groups.append((c0, min(512, N - c0)))
              c0 += 512
          has_s = last_pass

          for t in tiles:
              src = xp_d[t["tok"]:t["tok"] + 128, :] if t["kind"] == "p" else xs_d
              DMA(xres[:t["L"], t["ti"], :], src, (), ["xres%d" % t["ti"]])

          for ly in range(L2):
              DMA(Dbc[:], ssdD_d[ly].partition_broadcast(128), (), ["Dbc"])
              DMA(cvec[:], cvec_d[ly], (), ["cvec"])
              DMA(g8[:], g8_d[ly], (), ["g8"])
              DMA(wsmb[:, :, :].rearrange("p a b -> p (a b)"), wsm_d[ly], (), ["wsmb"], q="pool")
              DMA(wabd[:, :, :].rearrange("p a b -> p (a b)"), wabd_d[ly], (), ["wabd"], q="pool")
              TS("dve", g8n[:, 0:2], g8[:, 0:2], -1.0, None, ALU.mult, None, ["g8"], ["g8n"])
              ACT(g8n[:, 4:6], g8[:, 4:6], AF.Exp, ["g8"], ["g8n"])
              TS("dve", g8n[:, 4:6], g8n[:, 4:6], -1.0, None, ALU.mult, None, ["g8n"], ["g8n"])
              ACT(nsp8[:], cvec[:, 80:84], AF.Exp, ["cvec"], ["nsp8"], scale=-1.0)
              ACT(nsp8[:], nsp8[:], AF.Ln, ["nsp8"], ["nsp8"], bias=1.0)
              TS("dve", nsp8[:], nsp8[:], -8.0, None, ALU.mult, None, ["nsp8"], ["nsp8"])
              TS("dve", nba[:], cvec[:, 72:80], -1.0, None, ALU.mult, None, ["cvec"], ["nba"])

              def rmsnorm_tile(t, wsrc_d, xn32s, junks, nwbc):
                  L, ti = t["L"], t["ti"]
                  pb = ti % 2
                  xn32, junk = xn32s[pb], junks[pb]
                  xid, jid, sid = "xn32_%d" % pb, "junk_%d" % pb, ("st4", "st4b")[pb]
                  so = 4 * pb
                  xr_id = "xres%d" % ti
                  ACT(junk[:L], xres[:L, ti, :], AF.Square, [xr_id], [jid, sid], accum=st4[:L, so:so + 1])
                  ACT(st4[:L, so + 1:so + 2], st4[:L, so:so + 1], AF.Ln, [sid], [sid], bias=EPS, scale=1.0 / D)
                  ACT(st4[:L, so + 2:so + 3], st4[:L, so + 1:so + 2], AF.Exp, [sid], [sid], scale=-0.5)
                  STT("dve", xn32[:L], xres[:L, ti, :], st4[:L, so + 2:so + 3], nwbc[:L], ALU.mult, ALU.mult,
                      [xr_id, sid, "nwbc"], [xid])
                  return xn32, xid

              stage_reset()
              xn32s = [af([128, D]), af([128, D])]
              nwbc = af([128, D])
              junks = [ab([128, D]), ab([128, D])]
              DMA(nwbc, normw_d[ly].partition_broadcast(128), (), ["nwbc"])
              for t in tiles:
                  L, ti, col = t["L"], t["ti"], t["col"]
                  xn32, xid = rmsnorm_tile(t, None, xn32s, junks, nwbc)
                  for q4 in range(4):
                      for j in range(4):
                          dt_ = q4 * 4 + j
                          TR(PS["psU"][:, j * 128:j * 128 + L], xn32[:L, dt_ * 128:(dt_ + 1) * 128], identf[:L, :L],
                             [xid, "c128"], ["psU"])
                      CP("act" if q4 % 2 else "dve", xnT[:, q4 * 4:q4 * 4 + 4, col:col + L],
                         PS["psU"][:, :].rearrange("p (j c) -> p j c", j=4)[:, :, :L], ["psU"], ["xnT"])

              checkpoint()
              def proj_fm(wt, wid, wo, M, evac):
                  for (g0, n) in groups:
                      ps, pid = next_ps()
                      for dt_ in range(16):
                          MM(ps[:M, :n], wt[:, dt_, wo:wo + M], xnT[:, dt_, g0:g0 + n], dt_ == 0, dt_ == 15,
                             [wid, "xnT"], [pid])
                      evac(ps, pid, g0, n)

              def proj_tm(wt, wid, wo, n, t, evac):
                  ps, pid = next_ps()
                  L, col = t["L"], t["col"]
                  for dt_ in range(16):
                      MM(ps[:L, :n], xnT[:, dt_, col:col + L], wt[:, dt_, wo:wo + n], dt_ == 0, dt_ == 15,
                         [wid, "xnT"], [pid])
                  evac(ps, pid)

              def wout_part(mixs, ct0):
                  for dg in range(8):
                      wt1, w1 = load_w(wopk_d[ly, ct0 // 4, dg], nct=4)
                      for t in tiles:
                          L, ti, col = t["L"], t["ti"], t["col"]
                          ps, pid = next_ps()
                          for ct in range(4):
                              MM(ps[:L, :256], mixs[:, ct, col:col + L], wt1[:, ct, :], ct == 0, ct == 3, ["mixs", w1], [pid])
                          TT("dve", xres[:L, ti, dg * 256:(dg + 1) * 256], xres[:L, ti, dg * 256:(dg + 1) * 256],
                             ps[:L, :256], ALU.add, ["xres%d" % ti, pid], ["xres%d" % ti])

              def conv_load_hist(stream):
                  CP("dve", ubp[:, 0:3], cvcar[:, ly, stream, :], ["cvcar"], ["ubp"])

              def conv_save_hist(stream):
                  CP("dve", cvcar[:, ly, stream, :], ubp[:, NP:NP + 3], ["ubp"], ["cvcar"])

              def conv_apply(out_fm, oid, W, wcol, bcol):
                  s0 = 3 - (W - 1)
                  views = [(out_fm[:, 0:NP], lambda j: ubp[:, s0 + j:s0 + j + NP])]
                  if has_s:
                      views.append((out_fm[:, NP:NP + LS].rearrange("p (b t) -> p b t", t=4),
                                    lambda j: ubs[:, :, s0 + j:s0 + j + 4]))
                  for (o, src) in views:
                      if bcol is None:
                          TS("dve", o, src(0), cvec[:, wcol:wcol + 1], None, ALU.mult, None,
                             ["ubp", "ubs", "cvec"], [oid])
                      else:
                          TS("dve", o, src(0), cvec[:, wcol:wcol + 1], cvec[:, bcol:bcol + 1], ALU.mult, ALU.add,
                             ["ubp", "ubs", "cvec"], [oid])
                      for j in range(1, W):
                          STT("dve", o, src(j), cvec[:, wcol + j:wcol + j + 1], o, ALU.mult, ALU.add,
                              ["ubp", "ubs", "cvec", oid], [oid])

              def sample_hist_tile(cvin, W, ctile, src_d):
                  rows = NSQ * (W - 1)
                  DMA(cvin[:rows, :], src_d[:, ctile * 128:(ctile + 1) * 128], (), ["cvin"])
                  TR(PS["psU"][:, :rows], cvin[:rows, :], identf[:rows, :rows],
                     ["cvin", "c128"], ["psU"])
                  CP("dve", ubs[:, :, 3 - (W - 1):3], PS["psU"][:, :rows].rearrange("p (b r) -> p b r", r=W - 1),
                     ["psU"], ["ubs"])

              def sample_hist_out_tile(cvout, W, ctile, dst_d):
                  rows = NSQ * (W - 1)
                  CP("dve", cvtmp[:, :rows].rearrange("p (b r) -> p b r", r=W - 1), ubs[:, :, 7 - (W - 1):7],
                     ["ubs"], ["cvtmp"])
                  TR(PS["psU"][:rows, :128], cvtmp[:, :rows], identf, ["cvtmp", "c128"], ["psU"])
                  CP("dve", cvout[:rows, :], PS["psU"][:rows, :128], ["psU"], ["cvout"])
                  DMA(dst_d[:, ctile * 128:(ctile + 1) * 128], cvout[:rows, :], ["cvout"], ())

              def prompt_hist_out(W, streams, dst):
                  for c0 in range(0, len(streams), 4):
                      sub = streams[c0:c0 + 4]
                      for i, s_ in enumerate(sub):
                          CP("dve", cvtmp[:, 0:W - 1], cvcar[:, ly, s_, 3 - (W - 1):3], ["cvcar"], ["cvtmp"])
                          TR(PS["psU"][:W - 1, :128], cvtmp[:, 0:W - 1], identf, ["cvtmp", "c128"], ["psU"])
                          CP("dve", small_o[:W - 1, i * 128:(i + 1) * 128], PS["psU"][:W - 1, :128], ["psU"], ["small_o"])
                      DMA(dst[:, c0 * 128:(c0 + len(sub)) * 128], small_o[:W - 1, :128 * len(sub)], ["small_o"], ())

              def fill_ub(ps, pid, g0, n, eng="act"):
                  pe_ = min(g0 + n, NP)
                  if pe_ > g0:
                      CP(eng, ubp[:, 3 + g0:3 + pe_], ps[:, 0:pe_ - g0], [pid], ["ubp"])
                  if has_s and g0 + n > NP:
                      a0 = max(g0, NP)
                      CP(eng, ubs[:, :, 3:7], ps[:, a0 - g0:a0 - g0 + LS].rearrange("p (b t) -> p b t", t=4),
                         [pid], ["ubs"])

              stage_reset()
              fa = [af([128, NMAX]) for _ in range(6)]
              cvin = af([NSQ * 3, 128])
              cvout = af([NSQ * 3, 128])
              htm = af([NSQ, 512])
              htmo = af([NSQ, 512])
              mixs = ab([128, 4, NMAX])
              xrb = ab([128, NMAX])
              for j in range(4):
                  wt1, w1 = load_w(wpk_d[ly, 2 * j])
                  wt2, w2 = load_w(wpk_d[ly, 2 * j + 1])
                  conv_load_hist(j)
                  if has_s:
                      sample_hist_tile(cvin, 3, j, sccv_d[ly])
                  proj_fm(wt1, w1, 0, 128, lambda ps, pid, g0, n: CP("act", fa[0][:, g0:g0 + n], ps[:, :n], [pid], ["fa0"]))
                  proj_fm(wt1, w1, 128, 128, lambda ps, pid, g0, n: CP("act", fa[1][:, g0:g0 + n], ps[:, :n], [pid], ["fa1"]))

                  def ev_h(ps, pid, g0, n):
                      TT("dve", fa[1][:, g0:g0 + n], fa[1][:, g0:g0 + n], ps[:, :n], ALU.mult, ["fa1", pid], ["fa1"])
                  proj_fm(wt2, w2, 0, 128, ev_h)

                  def ev_z(ps, pid, g0, n):
                      CP("act", fa[2][:, g0:g0 + n], ps[:, :n], [pid], ["fa2"])
                      ACT(fa[3][:, g0:g0 + n], fa[2][:, g0:g0 + n], AF.Exp, ["fa2"], ["fa3"], scale=-1.0)
                  proj_fm(wt2, w2, 128, 128, ev_z)
                  CP("dve", ubp[:, 3:3 + NP], fa[1][:, 0:NP], ["fa1"], ["ubp"])
                  if has_s:
                      CP("dve", ubs[:, :, 3:7], fa[1][:, NP:NP + LS].rearrange("p (b t) -> p b t", t=4), ["fa1"], ["ubs"])
                  conv_apply(fa[4], "fa4", 3, 40 + 3 * j, None)
                  conv_save_hist(j)
                  if has_s:
                      sample_hist_out_tile(cvout, 3, j, occv_o[ly])
                  sigmoid_from_exp(fa[3][:, :N], ["fa3"], ["fa3"])
                  TT("dve", fa[2][:, :N], fa[2][:, :N], fa[3][:, :N], ALU.mult, ["fa2", "fa3"], ["fa2"])
                  TT("dve", fa[2][:, :N], fa[2][:, :N], fa[0][:, :N], ALU.mult, ["fa2", "fa0"], ["fa2"])
                  TT("dve", mixs[:, j, :N], fa[2][:, :N], fa[4][:, :N], ALU.mult, ["fa2", "fa4"], ["mixs"])
              if last_pass:
                  prompt_hist_out(3, [0, 1, 2, 3], pccv_o[ly])
              checkpoint()
              wout_part(mixs, 8)
              checkpoint()

              if has_s:
                  DMA(htm[:], srh_d[ly], (), ["htm"])
              for j in range(4):
                  wt1, w1 = load_w(wpk_d[ly, 8 + j])
                  conv_load_hist(4 + j)
                  if has_s:
                      sample_hist_tile(cvin, 4, j, srcv_d[ly])
                      TR(PS["psU"][:, :NSQ], htm[:NSQ, j * 128:(j + 1) * 128], identf[:NSQ, :NSQ], ["htm", "c128"], ["psU"])
                      CP("dve", h0s[:], PS["psU"][:, :NSQ], ["psU"], ["h0s"])
                  proj_fm(wt1, w1, 0, 128, lambda ps, pid, g0, n: fill_ub(ps, pid, g0, n))

                  def ev_rz(ps, pid, g0, n):
                      CP("act", fa[2][:, g0:g0 + n], ps[:, :n], [pid], ["fa2"])
                      ACT(fa[3][:, g0:g0 + n], fa[2][:, g0:g0 + n], AF.Exp, ["fa2"], ["fa3"], scale=-1.0)
                  proj_fm(wt1, w1, 128, 128, ev_rz)
                  conv_apply(fa[0], "fa0", 4, 52 + 4 * j, 68 + j)
                  conv_save_hist(4 + j)
                  if has_s:
                      sample_hist_out_tile(cvout, 4, j, orcv_o[ly])
                  CP("act", xrb[:, :N], fa[0][:, :N], ["fa0"], ["xrb"])
                  for (g0, n) in groups:
                      ps, pid = next_ps()
                      MM(ps[:, :n], wabd[:, j, :], xrb[:, g0:g0 + n], True, True, ["wabd", "xrb"], [pid])
                      ACT(fa[1][:, g0:g0 + n], ps[:, :n], AF.Exp, [pid, "nba"], ["fa1"], bias=nba[:, j:j + 1], scale=-1.0)
                      ps, pid = next_ps()
                      MM(ps[:, :n], wabd[:, 4 + j, :], xrb[:, g0:g0 + n], True, True, ["wabd", "xrb"], [pid])
                      ACT(fa[4][:, g0:g0 + n], ps[:, :n], AF.Exp, [pid, "nba"], ["fa4"], bias=nba[:, 4 + j:5 + j], scale=-1.0)
                  sigmoid_from_exp(fa[1][:, :N], ["fa1"], ["fa1"])
                  sigmoid_from_exp(fa[4][:, :N], ["fa4"], ["fa4"])
                  ACT(fa[1][:, :N], fa[1][:, :N], AF.Exp, ["fa1", "nsp8"], ["fa1"], scale=nsp8[:, j:j + 1])
                  TT("dve", fa[5][:, :N], fa[1][:, :N], fa[1][:, :N], ALU.mult, ["fa1"], ["fa5"])
                  TS("dve", fa[5][:, :N], fa[5][:, :N], -1.0, 1.0, ALU.mult, ALU.add, ["fa5"], ["fa5"])
                  TS("dve", fa[5][:, :N], fa[5][:, :N], 1e-30, None, ALU.max, None, ["fa5"], ["fa5"])
                  ACT(fa[5][:, :N], fa[5][:, :N], AF.Ln, ["fa5"], ["fa5"])
                  ACT(fa[5][:, :N], fa[5][:, :N], AF.Exp, ["fa5"], ["fa5"], scale=0.5)
                  TT("dve", fa[4][:, :N], fa[4][:, :N], fa[0][:, :N], ALU.mult, ["fa4", "fa0"], ["fa4"])
                  TT("dve", fa[4][:, :N], fa[4][:, :N], fa[5][:, :N], ALU.mult, ["fa4", "fa5"], ["fa4"])
                  STT("dve", fa[4][:, 0:1], fa[1][:, 0:1], hcar[:, ly, j:j + 1], fa[4][:, 0:1], ALU.mult, ALU.add,
                      ["fa1", "fa4", "hcar"], ["fa4"])
                  MEMSET("dve", fa[1][:, 0:1], 0.0, ["fa1"])
                  SCAN(fa[0][:, 0:NP], fa[1][:, 0:NP], fa[4][:, 0:NP], 0.0, ALU.mult, ALU.add, ["fa1", "fa4"], ["fa0"])
                  CP("dve", hcar[:, ly, j:j + 1], fa[0][:, NP - 1:NP], ["fa0"], ["hcar"])
                  if has_s:
                      a0v = fa[1][:, NP:NP + LS].rearrange("p (b t) -> p b t", t=4)[:, :, 0]
                      b0v = fa[4][:, NP:NP + LS].rearrange("p (b t) -> p b t", t=4)[:, :, 0]
                      TT("dve", tm4[:, :NSQ], a0v, h0s[:], ALU.mult, ["fa1", "h0s"], ["tm4"])
                      TT("dve", b0v, b0v, tm4[:, :NSQ], ALU.add, ["fa4", "tm4"], ["fa4"])
                      MEMSET("dve", a0v, 0.0, ["fa1"])
                      SCAN(fa[0][:, NP:NP + LS], fa[1][:, NP:NP + LS], fa[4][:, NP:NP + LS], 0.0, ALU.mult, ALU.add,
                           ["fa1", "fa4"], ["fa0"])
                      CP("dve", ncol[:, :NSQ], fa[0][:, NP:NP + LS].rearrange("p (b t) -> p b t", t=4)[:, :, 3], ["fa0"], ["ncol"])
                      TR(PS["psU"][:NSQ, :128], ncol[:, :NSQ], identf, ["ncol", "c128"], ["psU"])
                      CP("dve", htmo[:NSQ, j * 128:(j + 1) * 128], PS["psU"][:NSQ, :128], ["psU"], ["htmo"])
                  sigmoid_from_exp(fa[3][:, :N], ["fa3"], ["fa3"])
                  TT("dve", fa[2][:, :N], fa[2][:, :N], fa[3][:, :N], ALU.mult, ["fa2", "fa3"], ["fa2"])
                  TT("dve", mixs[:, j, :N], fa[2][:, :N], fa[0][:, :N], ALU.mult, ["fa2", "fa0"], ["mixs"])
              if has_s:
                  DMA(orh_o[ly], htmo[:], ["htmo"], ())
              if last_pass:
                  prompt_hist_out(4, [4, 5, 6, 7], prcv_o[ly])
                  for j in range(4):
                      TR(PS["psU"][:1, j * 128:(j + 1) * 128], hcar[:, ly, j:j + 1], identf, ["hcar", "c128"], ["psU"])
                  CP("dve", small_o[:1, 0:512], PS["psU"][:1, 0:512], ["psU"], ["small_o"])
                  DMA(prh_o[ly].rearrange("(o t) c -> o (t c)", o=1), small_o[:1, 0:512], ["small_o"], ())
              checkpoint()
              wout_part(mixs, 12)
              checkpoint()

              def chunk_ends(row, t_kind):
                  if t_kind == "p":
                      return row[:, 0:NP].rearrange("p (c t) -> p c t", t=128)[:, :, 127]
                  return row[:, NP:NP + LS].rearrange("p (b t) -> p b t", t=4)[:, :, 3]

              def expo(a_row, aid, c_row, cid, cv_row, cvid, t, Wt, Wv):
                  L, col = t["L"], t["col"]
                  mf = k8("maskfull").rearrange("p (h l) -> p h l", h=4)[:, :, :L]
                  RMl = RM[:, :4 * L].rearrange("p (h l) -> p h l", h=4)
                  RMvl = RMv[:, :4 * L].rearrange("p (h l) -> p h l", h=4)
                  TT("dve", RMl, c_row[:, col:col + L].unsqueeze(1).broadcast_to([8, 4, L]), mf, ALU.mult,
                     [cid, "c8"], ["RM"])
                  TT("dve", RMvl, cv_row[:, col:col + L].unsqueeze(1).broadcast_to([8, 4, L]), mf, ALU.mult,
                     [cvid, "c8"], ["RMv"])
                  msk = (maskPb if t["kind"] == "p" else maskSb)
                  MM(PS["psE"][:L, :4 * L], a_row[:, col:col + L], RM[:, :4 * L], True, False, [aid, "RM"], ["psE"])
                  MM(PS["psE"][:L, :4 * L], identb[:L, :L], msk[:L, :4 * L], False, True, ["identb", "maskPb", "maskSb"], ["psE"])
                  MM(PS["psV"][:, :4 * L], k8("LV"), RMv[:, :4 * L], True, True, ["c8", "RMv"], ["psV"])
                  ACT(Wt[:L, :4 * L], PS["psE"][:L, :4 * L], AF.Exp, ["psE"], ["Wt"])
                  ACT(Wv[:, :4 * L], PS["psV"][:, :4 * L], AF.Exp, ["psV"], ["Wv"])

              def bcast_rows(src_row, sid, ncols, slot):
                  w8 = gsm[:, :4 * ncols].rearrange("p (h c) -> p h c", h=4)
                  TT("dve", w8, src_row.unsqueeze(1).broadcast_to([8, 4, ncols]),
                     k8("mask8h").unsqueeze(2).broadcast_to([8, 4, ncols]), ALU.mult, [sid, "c8"], ["gsm"])
                  MM(PS["psU"][:, :4 * ncols], k8("ones8"), gsm[:, :4 * ncols], True, True, ["c8", "gsm"], ["psU"])
                  CP("dve", bcw[:, slot, :, :ncols], PS["psU"][:, :4 * ncols].rearrange("p (h c) -> p h c", h=4),
                     ["psU"], ["bcw"])

              def to_tok_major(rowA, aid, rowB, bid, t, slot, scr, scrid):
                  L, col, ti = t["L"], t["col"], t["ti"]
                  TS("dve", scr[:, :L], rowA[:, col:col + L], pm8, None, ALU.mult, None, [aid, "c8"], [scrid])
                  STT("dve", scr[:, :L], rowB[:, col:col + L], opm8, scr[:, :L], ALU.mult, ALU.add,
                      [bid, "c8", scrid], [scrid])
                  TR(PS["psU"][:L, :8], scr[:, :L], identf[:8, :8], [scrid, "c128"], ["psU"])
                  CP("dve", scal[:L, ti, slot, :], PS["psU"][:L, :8], ["psU"], ["scal"])

              stage_reset()
              Gml = af([128, NTM, 512])
              gr = [af([8, NMAX]) for _ in range(10)]
              mlnwbc = af([128, BR])
              DMA(mlnwbc, mlnw_d[ly].partition_broadcast(128), (), ["mlnwbc"])
              Wt = af([128, 512])
              Wv = af([128, 512])
              hn = af([128, 512])
              ytm = af([128, 512])
              Cs = af([128, NSQ, 129])
              ntm = af([NSQ, 512])
              ntmo = af([NSQ, 512])
              qT = ab([128, 4, NMAX])
              kT = ab([128, 4, NMAX])
              ktm = ab([128, NTM, 512])
              vaug = ab([128, NTM, 4, 129])
              SwT = ab([128, 512])
              qv = ab([128, 512])
              kw = ab([128, 512])
              qvm = ab([128, NSQ, LS])
              Csb = ab([128, NSQ, 129])
              vm = ab([128, 3, 129])
              mixs = ab([128, 4, NMAX])
              junk = ab([128, 128])
              hsq = af([128, 512])
              MEMSET("pool", vaug, 1.0, ["vaug"])
              CP("act", Cbf[:], Caug[:, ly], ["Caug"], ["Cbf"])
              for cg in range(4):
                  wt1, w1 = load_w(wpk_d[ly, 12 + cg])
                  for t in tiles:
                      L, ti = t["L"], t["ti"]

                      def ev_g(ps, pid, L=L, ti=ti, cg=cg):
                          CP("act", ytm[:L, 0:256], ps[:L, 0:256], [pid], ["ytm"])
                          ACT(hn[:L, 0:256], ytm[:L, 0:256], AF.Exp, ["ytm"], ["hn"], scale=-1.0)
                          ACT(hn[:L, 0:256], hn[:L, 0:256], AF.Ln, ["hn"], ["hn"], bias=1.0)
                          TT("dve", hn[:L, 0:128], hn[:L, 0:128], hn[:L, 128:256], ALU.add, ["hn"], ["hn"])
                          ACT(hn[:L, 0:128], hn[:L, 0:128], AF.Exp, ["hn"], ["hn"], scale=-1.0)
                          TT("dve", hn[:L, 0:128], hn[:L, 0:128], ytm[:L, 128:256], ALU.mult, ["hn", "ytm"], ["hn"])
                          TT("dve", Gml[:L, ti, cg * 128:(cg + 1) * 128], hn[:L, 0:128], mlnwbc[:L, cg * 128:(cg + 1) * 128],
                             ALU.mult, ["hn", "mlnwbc"], ["Gml"])
                      proj_tm(wt1, w1, 0, 256, t, ev_g)
              for hp in range(2):
                  wt1, w1 = load_w(wpk_d[ly, 16 + hp])
                  for hh in range(2):
                      h = hp * 2 + hh
                      proj_fm(wt1, w1, hh * 128, 128,
                              lambda ps, pid, g0, n, h=h: CP("act", qT[:, h, g0:g0 + n], ps[:, :n], [pid], ["qT"]))
              for hp in range(2):
                  wt1, w1 = load_w(wpk_d[ly, 18 + hp])
                  for hh in range(2):
                      h = hp * 2 + hh
                      proj_fm(wt1, w1, hh * 128, 128,
                              lambda ps, pid, g0, n, h=h: ACT(kT[:, h, g0:g0 + n], ps[:, :n], AF.Copy, [pid], ["kT"],
                                                             scale=128.0 ** -0.5))
                  for t in tiles:
                      L, ti = t["L"], t["ti"]
                      proj_tm(wt1, w1, 0, 256, t,
                              lambda ps, pid, L=L, ti=ti, hp=hp: ACT(ktm[:L, ti, hp * 256:(hp + 1) * 256], ps[:L, :256],
                                                                    AF.Copy, [pid], ["ktm"], scale=128.0 ** -0.5))
              for hp in range(2):
                  wt1, w1 = load_w(wpk_d[ly, 20 + hp])
                  for t in tiles:
                      L, ti = t["L"], t["ti"]
                      proj_tm(wt1, w1, 0, 256, t,
                              lambda ps, pid, L=L, ti=ti, hp=hp: CP("dve", vaug[:L, ti, 2 * hp:2 * hp + 2, 0:128],
                                                                   ps[:L, :256].rearrange("p (h v) -> p h v", h=2),
                                                                   [pid], ["vaug"]))
              gi, gf, gb, gm, ga, gc, gcv, gwg, genm, gmp = gr
              glf = gf
              proj_fm(wsmb, "wsmb", 0, 8,
                      lambda ps, pid, g0, n: TS("dve", gi[:, g0:g0 + n], ps[:8, :n], g8[:, 0:1], None, ALU.add, None,
                                                [pid, "g8"], ["gr0"]))
              proj_fm(wsmb, "wsmb", 8, 8,
                      lambda ps, pid, g0, n: ACT(gf[:, g0:g0 + n], ps[:8, :n], AF.Exp, [pid, "g8n"], ["gr1"],
                                                 bias=g8n[:, 1:2], scale=-1.0))
              ACT(glf[:, :N], gf[:, :N], AF.Ln, ["gr1", "gr2"], ["gr1", "gr2"], bias=1.0)
              TS("dve", glf[:, :N], glf[:, :N], -1.0, None, ALU.mult, None, ["gr2"], ["gr2"])
              SCAN(gb[:, 0:NP], k8("keepP"), glf[:, 0:NP], 0.0, ALU.mult, ALU.add, ["c8", "gr2"], ["gr3"])
              SCAN(gm[:, 0:NP], glf[:, 0:NP], gi[:, 0:NP], mcar[:, ly:ly + 1], ALU.add, ALU.max, ["gr2", "gr0", "mcar"], ["gr4"])
              CP("dve", gmp[:, 1:NP], gm[:, 0:NP - 1], ["gr4"], ["gr10"])
              CP("dve", gmp[:, 0:1], mcar[:, ly:ly + 1], ["mcar"], ["gr10"])
              CP("dve", mcar[:, ly:ly + 1], gm[:, NP - 1:NP], ["gr4"], ["mcar"])
              m0 = small_o[:8, 0:NSQ]
              if has_s:
                  sl_ = slice(NP, NP + LS)
                  DMA(m0, sm_d[ly], (), ["small_o"])
                  SCAN(gb[:, sl_], k8("keepS"), glf[:, sl_], 0.0, ALU.mult, ALU.add, ["c8", "gr2"], ["gr3"])
                  TT("dve", gmp[:, sl_], glf[:, sl_], k8("keepS"), ALU.mult, ["gr2", "c8"], ["gr10"])
                  TT("dve", gmp[:, sl_], gmp[:, sl_], k8("negS"), ALU.add, ["gr10", "c8"], ["gr10"])
                  CP("dve", gwg[:, sl_], gi[:, sl_], ["gr0"], ["gr8"])
                  lf0 = glf[:, sl_].rearrange("p (b t) -> p b t", t=4)[:, :, 0]
                  i0 = gwg[:, sl_].rearrange("p (b t) -> p b t", t=4)[:, :, 0]
                  TT("dve", small_o[:8, 64:64 + NSQ], lf0, m0, ALU.add, ["gr2", "small_o"], ["small_o"])
                  TT("dve", i0, i0, small_o[:8, 64:64 + NSQ], ALU.max, ["gr8", "small_o"], ["gr8"])
                  SCAN(gm[:, sl_], gmp[:, sl_], gwg[:, sl_], 0.0, ALU.add, ALU.max, ["gr10", "gr8"], ["gr4"])
              TT("dve", gc[:, :N], gb[:, :N], gm[:, :N], ALU.subtract, ["gr3", "gr4"], ["gr6"])
              TT("dve", ga[:, :N], gi[:, :N], gb[:, :N], ALU.subtract, ["gr0", "gr3"], ["gr5"])
              mprev_p = gmp[:, 0:NP].rearrange("p (c t) -> p c t", t=128)[:, :, 0]
              TT("dve", gcv[:, 0:NP].rearrange("p (c t) -> p c t", t=128), gc[:, 0:NP].rearrange("p (c t) -> p c t", t=128),
                 mprev_p.unsqueeze(2).broadcast_to([8, NCH, 128]), ALU.add, ["gr6", "gr10"], ["gr7"])
              if has_s:
                  TT("dve", gcv[:, sl_].rearrange("p (b t) -> p b t", t=4), gc[:, sl_].rearrange("p (b t) -> p b t", t=4),
                     m0.unsqueeze(2).broadcast_to([8, NSQ, 4]), ALU.add, ["gr6", "small_o"], ["gr7"])
              for kind in (("p", "s") if has_s else ("p",)):
                  ncn = NCH if kind == "p" else NSQ
                  T_ = 128 if kind == "p" else 4
                  base = 0 if kind == "p" else NP
                  gcst = gsm[:, 0:ncn]
                  TT("dve", gcst, chunk_ends(gb, kind), chunk_ends(gm, kind), ALU.subtract, ["gr3", "gr4"], ["gsm"])
                  seg = slice(base, base + ncn * T_)
                  TT("dve", gwg[:, seg].rearrange("p (c t) -> p c t", t=T_), ga[:, seg].rearrange("p (c t) -> p c t", t=T_),
                     gcst.unsqueeze(2).broadcast_to([8, ncn, T_]), ALU.add, ["gr5", "gsm"], ["gr8"])
                  mpv = mprev_p if kind == "p" else m0
                  TT("dve", small_o[:8, 128:128 + ncn], gcst, mpv, ALU.add, ["gsm", "gr10", "small_o"], ["small_o"])
                  ACT(small_o[:8, 128:128 + ncn], small_o[:8, 128:128 + ncn], AF.Exp, ["small_o"], ["small_o"])
                  bcast_rows(small_o[:8, 128:128 + ncn], "small_o", ncn, 0 if kind == "p" else 1)
              ACT(gwg[:, :N], gwg[:, :N], AF.Exp, ["gr8"], ["gr8"])
              ACT(genm[:, :N], gm[:, :N], AF.Exp, ["gr4"], ["gr9"], scale=-1.0)
              TS("dve", ga[:, :N], ga[:, :N], pm8, opm8, ALU.mult, ALU.add, ["gr5", "c8"], ["gr5"])
              TS("dve", gc[:, :N], gc[:, :N], opm8, pm8, ALU.mult, ALU.add, ["gr6", "c8"], ["gr6"])
              TS("dve", gcv[:, :N], gcv[:, :N], opm8, pm8, ALU.mult, ALU.add, ["gr7", "c8"], ["gr7"])
              if has_s:
                  DMA(ntm[:], sn_d[ly], (), ["ntm"])
              if last_pass:
                  DMA(pm_o[ly].rearrange("(h o) -> h o", o=1), mcar[0:4, ly:ly + 1], ["mcar"], ())
              if has_s:
                  CP("dve", small_o[:8, 512:512 + NSQ], chunk_ends(gm, "s"), ["gr4"], ["small_o"])
                  DMA(om_o[ly], small_o[0:4, 512:512 + NSQ], ["small_o"], ())

              for t in tiles:
                  L, ti, col, kind = t["L"], t["ti"], t["col"], t["kind"]
                  to_tok_major(gwg, "gr8", genm, "gr9", t, 0, gmp, "gr10")
                  for h in range(4):
                      MM(PS["psS"][:L, h * L:(h + 1) * L], kT[:, h, col:col + L], qT[:, h, col:col + L], True, True,
                         ["kT", "qT"], ["psS"])
                  expo(ga, "gr5", gc, "gr6", gcv, "gr7", t, Wt, Wv)
                  TT("dve", SwT[:L, :4 * L], PS["psS"][:L, :4 * L], Wt[:L, :4 * L], ALU.mult, ["psS", "Wt"], ["SwT"])
                  TT("pool", qv[:, :4 * L].rearrange("p (h l) -> p h l", h=4), qT[:, :, col:col + L],
                     Wv[:, :4 * L].rearrange("p (h l) -> p h l", h=4), ALU.mult, ["qT", "Wv"], ["qv"])
                  TT("pool", kw[:L, :].rearrange("p (h d) -> p h d", h=4), ktm[:L, ti, :].rearrange("p (h d) -> p h d", h=4),
                     scal[:L, ti, 0, 0:4].unsqueeze(2).broadcast_to([L, 4, 128]), ALU.mult, ["ktm", "scal"], ["kw"])
                  for h in range(4):
                      pn = PS["psN0"] if h < 2 else PS["psN1"]
                      pnid = "psN0" if h < 2 else "psN1"
                      o_ = (h % 2) * 129
                      if kind == "s":
                          for b4 in range(0, NSQ, 4):
                              DMA(Cs[:, b4:b4 + 4, 0:128], sC_d[ly][b4:b4 + 4, h].rearrange("b d v -> d b v"), (), ["Cs"])
                          TR(PS["psU"][:, :NSQ], ntm[:NSQ, h * 128:(h + 1) * 128], identf[:NSQ, :NSQ], ["ntm", "c128"], ["psU"])
                          CP("dve", Cs[:, :, 128], PS["psU"][:, :NSQ], ["psU"], ["Cs"])
                          CP("act", Csb, Cs, ["Cs"], ["Csb"])
                          TT("pool", qvm, qv[:, h * L:(h + 1) * L].unsqueeze(1).broadcast_to([128, NSQ, L]),
                             smkb[:, :].rearrange("p (b l) -> p b l", b=NSQ), ALU.mult, ["qv", "smkb"], ["qvm"])
                      MM(pn[:L, o_:o_ + 129], SwT[:L, h * L:(h + 1) * L], vaug[:L, ti, h, :], True, False,
                         ["SwT", "vaug"], [pnid])
                      if kind == "p":
                          MM(pn[:L, o_:o_ + 129], qv[:, h * L:(h + 1) * L], Cbf[:, h, :], False, True, ["qv", "Cbf"], [pnid])
                      else:
                          for b in range(NSQ):
                              MM(pn[:L, o_:o_ + 129], qvm[:, b, :], Csb[:, b, :], False, b == NSQ - 1,
                                 ["qvm", "Csb"], [pnid])
                      kwh = kw[:L, h * 128:(h + 1) * 128]
                      if kind == "p":
                          psu, psuid = next_ps2()
                          MM(psu[:, 0:129], kwh, vaug[:L, ti, h, :], True, True, ["kw", "vaug"], [psuid])
                          STT("dve", Caug[:, ly, h, :], Caug[:, ly, h, :], bcw[:, 0, h, ti:ti + 1], psu[:, 0:129],
                              ALU.mult, ALU.add, ["Caug", "bcw", psuid], ["Caug"])
                      else:
                          b0 = 0
                          while b0 < NSQ:
                              nb = min(3, NSQ - b0)
                              TT("pool", vm[:L, 0:nb, :], vaug[:L, ti, h, :].unsqueeze(1).broadcast_to([L, nb, 129]),
                                 smT[:L, b0:b0 + nb].unsqueeze(2).broadcast_to([L, nb, 129]), ALU.mult, ["vaug", "c128"], ["vm"])
                              MM(PS["psU"][:, 0:nb * 129], kwh, vm[:L, 0:nb, :], True, True, ["kw", "vm"], ["psU"])
                              TT("dve", Cs[:, b0:b0 + nb, :], Cs[:, b0:b0 + nb, :],
                                 bcw[:, 1, h, b0:b0 + nb].unsqueeze(2).broadcast_to([128, nb, 129]), ALU.mult,
                                 ["Cs", "bcw"], ["Cs"])
                              TT("dve", Cs[:, b0:b0 + nb, :], Cs[:, b0:b0 + nb, :],
                                 PS["psU"][:, 0:nb * 129].rearrange("p (b v) -> p b v", b=nb), ALU.add, ["Cs", "psU"], ["Cs"])
                              b0 += nb
                          for b4 in range(0, NSQ, 4):
                              DMA(oC_o[ly][b4:b4 + 4, h].rearrange("b d v -> d b v"), Cs[:, b4:b4 + 4, 0:128], ["Cs"], ())
                          CP("dve", ncol[:, :NSQ], Cs[:, :, 128], ["Cs"], ["ncol"])
                          TR(PS["psU"][:NSQ, :128], ncol[:, :NSQ], identf, ["ncol", "c128"], ["psU"])
                          CP("dve", ntmo[:NSQ, h * 128:(h + 1) * 128], PS["psU"][:NSQ, :128], ["psU"], ["ntmo"])
                  if kind == "p":
                      CP("act", Cbf[:], Caug[:, ly], ["Caug"], ["Cbf"])
                  for half in range(2):
                      pn = PS["psN0"] if half == 0 else PS["psN1"]
                      pnid = "psN0" if half == 0 else "psN1"
                      den = pn[:L, 0:258].rearrange("p (h v) -> p h v", h=2)[:, :, 128]
                      ACT(tm4[:L, 2 * half:2 * half + 2], den, AF.Abs, [pnid], ["tm4"])
                  TT("dve", tm4[:L, 0:4], tm4[:L, 0:4], scal[:L, ti, 0, 4:8], ALU.max, ["tm4", "scal"], ["tm4"])
                  RECIP(tm4[:L, 0:4], tm4[:L, 0:4], ["tm4"], ["tm4"])
                  for half in range(2):
                      pn = PS["psN0"] if half == 0 else PS["psN1"]
                      pnid = "psN0" if half == 0 else "psN1"
                      CP("act", hn[:L, half * 256:(half + 1) * 256].rearrange("p (h v) -> p h v", h=2),
                         pn[:L, 0:258].rearrange("p (h v) -> p h v", h=2)[:, :, 0:128], [pnid], ["hn"])
                  TT("pool", hsq[:L, :], hn[:L, :], hn[:L, :], ALU.mult, ["hn"], ["hsq"])
                  RSUM(tm4[:L, 4:8], hn[:L, :].rearrange("p (h v) -> p h v", h=4), ["hn"], ["tm4"])
                  RSUM(tm4[:L, 8:12], hsq[:L, :].rearrange("p (h v) -> p h v", h=4), ["hsq"], ["tm4"])
                  TS("dve", tm4[:L, 4:8], tm4[:L, 4:8], 1.0 / 128, None, ALU.mult, None, ["tm4"], ["tm4"])
                  TS("dve", tm4[:L, 8:12], tm4[:L, 8:12], 1.0 / 128, None, ALU.mult, None, ["tm4"], ["tm4"])
                  TT("dve", tm4[:L, 12:16], tm4[:L, 4:8], tm4[:L, 4:8], ALU.mult, ["tm4"], ["tm4"])
                  TT("dve", tm4[:L, 8:12], tm4[:L, 8:12], tm4[:L, 12:16], ALU.subtract, ["tm4"], ["tm4"])
                  TT("dve", tm4[:L, 12:16], tm4[:L, 0:4], tm4[:L, 0:4], ALU.mult, ["tm4"], ["tm4"])
                  TT("dve", tm4[:L, 8:12], tm4[:L, 8:12], tm4[:L, 12:16], ALU.mult, ["tm4"], ["tm4"])
                  TS("dve", tm4[:L, 8:12], tm4[:L, 8:12], 0.0, None, ALU.max, None, ["tm4"], ["tm4"])
                  ACT(tm4[:L, 8:12], tm4[:L, 8:12], AF.Ln, ["tm4"], ["tm4"], bias=EPS)
                  ACT(tm4[:L, 8:12], tm4[:L, 8:12], AF.Exp, ["tm4"], ["tm4"], scale=-0.5)
                  TT("dve", tm4[:L, 8:12], tm4[:L, 8:12], tm4[:L, 0:4], ALU.mult, ["tm4"], ["tm4"])
                  hn3 = hn[:L, :].rearrange("p (h v) -> p h v", h=4)
                  TT("dve", hn3, hn3, tm4[:L, 4:8].unsqueeze(2).broadcast_to([L, 4, 128]), ALU.subtract, ["hn", "tm4"], ["hn"])
                  TT("dve", hn3, hn3, tm4[:L, 8:12].unsqueeze(2).broadcast_to([L, 4, 128]), ALU.mult, ["hn", "tm4"], ["hn"])
                  TT("pool", ytm[:L, :], hn[:L, :], Gml[:L, ti, :], ALU.mult, ["hn", "Gml"], ["ytm"])
                  for h in range(4):
                      TR(PS["psU"][:, h * 128:h * 128 + L], ytm[:L, h * 128:(h + 1) * 128], identf[:L, :L], ["ytm", "c128"], ["psU"])
                  CP("act", mixs[:, 0:4, col:col + L], PS["psU"][:, :].rearrange("p (j c) -> p j c", j=4)[:, :, :L],
                     ["psU"], ["mixs"])
              if has_s:
                  DMA(on_o[ly], ntmo[:], ["ntmo"], ())
              if last_pass:
                  DMA(pC_o[ly].rearrange("h d v -> d h v"), Caug[:, ly, :, 0:128], ["Caug"], ())
                  CP("dve", ncol[:, 0:4], Caug[:, ly, :, 128], ["Caug"], ["ncol"])
                  TR(PS["psU"][:4, :128], ncol[:, 0:4], identf, ["ncol", "c128"], ["psU"])
                  CP("dve", small_o[:4, 600:728], PS["psU"][:4, :128], ["psU"], ["small_o"])
                  DMA(pn_o[ly], small_o[:4, 600:728], ["small_o"], ())
              checkpoint()
              wout_part(mixs, 0)
              checkpoint()

              stage_reset()
              fa = [af([128, NMAX]) for _ in range(2)]
              xtm = af([128, NTM, 512])
              gzs = af([128, NTM, 512])
              gr = [af([8, NMAX]) for _ in range(6)]
              Wt = af([128, 512])
              Wv = af([128, 512])
              hn = af([128, 512])
              ytm = af([128, 256])
              Sin = af([64, NSQ, 128])
              cvin = af([NSQ * 3, 128])
              cvout = af([NSQ * 3, 128])
              ssdnwbc = af([128, BR])
              DMA(ssdnwbc, ssdnw_d[ly].partition_broadcast(128), (), ["ssdnwbc"])
              Btm = ab([128, NTM, 2, 128])
              BTb = ab([128, 2, NMAX])
              CTb = ab([128, 2, NMAX])
              Mh = ab([128, 512])
              Ct = ab([128, 512])
              Ctm = ab([128, NSQ, LS])
              xdt = ab([128, 256])
              xw = ab([128, 256])
              Mh2 = [Mh, ab([128, 512])]
              Ct2 = [Ct, ab([128, 512])]
              xdt2 = [xdt, ab([128, 256])]
              xw2 = [xw, ab([128, 256])]
              Ssb = ab([128, NSQ, 64])
              Bm = ab([128, NSQ, 128])
              mixs = ab([128, 4, NMAX])
              junk = ab([128, 256])
              CP("act", STb[:], STs[:, ly], ["STs"], ["STb"])
              for j in range(8):
                  if j % 2 == 0:
                      wt1, w1 = load_w(wpk_d[ly, 22 + j // 2])
                  conv_load_hist(8 + j)
                  if has_s:
                      sample_hist_tile(cvin, 4, j, sscv_d[ly])
                  proj_fm(wt1, w1, (j % 2) * 128, 128, lambda ps, pid, g0, n: fill_ub(ps, pid, g0, n))
                  conv_apply(fa[0], "fa0", 4, 4 * j, 32 + j)
                  conv_save_hist(8 + j)
                  if has_s:
                      sample_hist_out_tile(cvout, 4, j, oscv_o[ly])
                  ACT(fa[1][:, :N], fa[0][:, :N], AF.Exp, ["fa0"], ["fa1"], scale=-1.0)
                  sigmoid_from_exp(fa[1][:, :N], ["fa1"], ["fa1"])
                  TT("dve", fa[0][:, :N], fa[0][:, :N], fa[1][:, :N], ALU.mult, ["fa0", "fa1"], ["fa0"])
                  if j < 4:
                      for t in tiles:
                          L, ti, col = t["L"], t["ti"], t["col"]
                          TR(PS["psU"][:L, 0:128], fa[0][:, col:col + L], identf, ["fa0", "c128"], ["psU"])
                          CP("act", xtm[:L, ti, j * 128:(j + 1) * 128], PS["psU"][:L, 0:128], ["psU"], ["xtm"])
                  elif j < 6:
                      g = j - 4
                      CP("act", BTb[:, g, :N], fa[0][:, :N], ["fa0"], ["BTb"])
                      for t in tiles:
                          L, ti, col = t["L"], t["ti"], t["col"]
                          TR(PS["psU"][:L, 0:128], fa[0][:, col:col + L], identf, ["fa0", "c128"], ["psU"])
                          CP("act", Btm[:L, ti, g, :], PS["psU"][:L, 0:128], ["psU"], ["Btm"])
                  else:
                      g = j - 6
                      CP("act", CTb[:, g, :N], fa[0][:, :N], ["fa0"], ["CTb"])
              if last_pass:
                  prompt_hist_out(4, list(range(8, 16)), pscv_o[ly])
              for hp in range(2):
                  wt1, w1 = load_w(wpk_d[ly, 26 + hp])
                  for t in tiles:
                      L, ti = t["L"], t["ti"]

                      def ev_sz(ps, pid, L=L, ti=ti, hp=hp):
                          CP("act", ytm[:L, 0:256], ps[:L, 0:256], [pid], ["ytm"])
                          ACT(hn[:L, 0:256], ytm[:L, 0:256], AF.Exp, ["ytm"], ["hn"], scale=-1.0)
                          sigmoid_from_exp(hn[:L, 0:256], ["hn"], ["hn"])
                          TT("dve", gzs[:L, ti, hp * 256:(hp + 1) * 256], hn[:L, 0:256], ytm[:L, 0:256], ALU.mult,
                             ["hn", "ytm"], ["gzs"])
                      proj_tm(wt1, w1, 0, 256, t, ev_sz)
              for g in range(2):
                  gdt, gda, gacs, gwr, grr, gscr = gr
                  gna = gda
                  proj_fm(wsmb, "wsmb", 16 + 8 * g, 8,
                          lambda ps, pid, g0, n, g=g: ACT(gdt[:, g0:g0 + n], ps[:8, :n], AF.Exp, [pid, "g8"], ["gr0"],
                                                          bias=g8[:, 2 + g:3 + g]))
                  ACT(gdt[:, :N], gdt[:, :N], AF.Ln, ["gr0"], ["gr0"], bias=1.0)
                  TS("dve", gda[:, :N], gdt[:, :N], g8n[:, 4 + g:5 + g], None, ALU.mult, None, ["gr0", "g8n"], ["gr1", "gr3"])
                  SCAN(gacs[:, 0:NP], k8("keepP"), gda[:, 0:NP], 0.0, ALU.mult, ALU.add, ["c8", "gr1", "gr3"], ["gr2"])
                  if has_s:
                      SCAN(gacs[:, NP:NP + LS], k8("keepS"), gda[:, NP:NP + LS], 0.0, ALU.mult, ALU.add, ["c8", "gr1"], ["gr2"])
                  for kind in (("p", "s") if has_s else ("p",)):
                      ncn = NCH if kind == "p" else NSQ
                      T_ = 128 if kind == "p" else 4
                      base = 0 if kind == "p" else NP
                      seg = slice(base, base + ncn * T_)
                      al = gsm[:, 0:ncn]
                      CP("dve", al, chunk_ends(gacs, kind), ["gr2"], ["gsm"])
                      TT("dve", gwr[:, seg].rearrange("p (c t) -> p c t", t=T_), al.unsqueeze(2).broadcast_to([8, ncn, T_]),
                         gacs[:, seg].rearrange("p (c t) -> p c t", t=T_), ALU.subtract, ["gsm", "gr2"], ["gr4"])
                      ACT(small_o[:8, 128:128 + ncn], al, AF.Exp, ["gsm"], ["small_o"])
                      bcast_rows(small_o[:8, 128:128 + ncn], "small_o", ncn, 0 if kind == "p" else 1)
                  ACT(gwr[:, :N], gwr[:, :N], AF.Exp, ["gr4"], ["gr4"])
                  TT("dve", gwr[:, :N], gwr[:, :N], gdt[:, :N], ALU.mult, ["gr4", "gr0"], ["gr4"])
                  TS("dve", gna[:, :N], gacs[:, :N], -1.0, None, ALU.mult, None, ["gr2", "gr1"], ["gr3", "gr1"])
                  TS("dve", gna[:, :N], gna[:, :N], pm8, opm8, ALU.mult, ALU.add, ["gr3", "c8"], ["gr3"])
                  TS("dve", grr[:, :N], gacs[:, :N], opm8, pm8, ALU.mult, ALU.add, ["gr2", "c8"], ["gr5"])
                  def sd_front(t, g=g):
                      L, ti, col, kind = t["L"], t["ti"], t["col"], t["kind"]
                      pb = t["ti"] % 2
                      Mh, Ct, xdt, xw = Mh2[pb], Ct2[pb], xdt2[pb], xw2[pb]
                      idM, idC, idX, idW = "Mh%d" % pb, "Ct%d" % pb, "xdt%d" % pb, "xw%d" % pb
                      xg = xtm[:L, ti, g * 256:(g + 1) * 256].rearrange("p (h c) -> p h c", h=4)
                      to_tok_major(gdt, "gr0", gwr, "gr4", t, 1, gscr, "gr6")
                      MM(PS["psS"][:L, :L], BTb[:, g, col:col + L], CTb[:, g, col:col + L], True, True, ["BTb", "CTb"], ["psS"])
                      expo(gna, "gr3", grr, "gr5", grr, "gr5", t, Wt, Wv)
                      TT("dve", Mh[:L, :4 * L].rearrange("p (h l) -> p h l", h=4),
                         PS["psS"][:L, :L].unsqueeze(1).broadcast_to([L, 4, L]),
                         Wt[:L, :4 * L].rearrange("p (h l) -> p h l", h=4), ALU.mult, ["psS", "Wt"], [idM])
                      TT("pool", Ct[:, :4 * L].rearrange("p (h l) -> p h l", h=4),
                         CTb[:, g, col:col + L].unsqueeze(1).broadcast_to([128, 4, L]),
                         Wv[:, :4 * L].rearrange("p (h l) -> p h l", h=4), ALU.mult, ["CTb", "Wv"], [idC])
                      xg = xtm[:L, ti, g * 256:(g + 1) * 256].rearrange("p (h c) -> p h c", h=4)
                      TT("dve", xdt[:L, :].rearrange("p (h c) -> p h c", h=4), xg,
                         scal[:L, ti, 1, 0:4].unsqueeze(2).broadcast_to([L, 4, 64]), ALU.mult, ["xtm", "scal"], [idX])
                      TT("dve", xw[:L, :].rearrange("p (h c) -> p h c", h=4), xg,
                         scal[:L, ti, 1, 4:8].unsqueeze(2).broadcast_to([L, 4, 64]), ALU.mult, ["xtm", "scal"], [idW])
                      if kind == "s":
                          TT("pool", Bm[:L, :, :], Btm[:L, ti, g, :].unsqueeze(1).broadcast_to([L, NSQ, 128]),
                             smT[:L, :].unsqueeze(2).broadcast_to([L, NSQ, 128]), ALU.mult, ["Btm", "c128"], ["Bm"])

                  def sd_rest(t, g=g):
                      L, ti, col, kind = t["L"], t["ti"], t["col"], t["kind"]
                      pb = t["ti"] % 2
                      Mh, Ct, xdt, xw = Mh2[pb], Ct2[pb], xdt2[pb], xw2[pb]
                      idM, idC, idX, idW = "Mh%d" % pb, "Ct%d" % pb, "xdt%d" % pb, "xw%d" % pb
                      xg = xtm[:L, ti, g * 256:(g + 1) * 256].rearrange("p (h c) -> p h c", h=4)
                      for h in range(4):
                          hh = g * 4 + h
                          if kind == "s":
                              TT("pool", Ctm, Ct[:, h * L:(h + 1) * L].unsqueeze(1).broadcast_to([128, NSQ, L]),
                                 smkb[:, :].rearrange("p (b l) -> p b l", b=NSQ), ALU.mult, [idC, "smkb"], ["Ctm"])
                              for b4 in range(0, NSQ, 8):
                                  DMA(Sin[:, b4:b4 + 8, :], sS_d[ly][b4:b4 + 8, hh].rearrange("b p n -> p b n"), (), ["Sin"])
                              for bq in range(NSQ // 8):
                                  for b in range(8):
                                      TR(PS["psU"][:, b * 64:(b + 1) * 64], Sin[:, bq * 8 + b, :], identf[:64, :64],
                                         ["Sin", "c128"], ["psU"])
                                  CP("act", Ssb[:, bq * 8:(bq + 1) * 8, :],
                                     PS["psU"][:, :].rearrange("p (b c) -> p b c", b=8), ["psU"], ["Ssb"])
                          MM(PS["psN0"][:L, h * 64:(h + 1) * 64], Mh[:L, h * L:(h + 1) * L], xdt[:L, h * 64:(h + 1) * 64],
                             True, False, [idM, idX], ["psN0"])
                          if kind == "p":
                              MM(PS["psN0"][:L, h * 64:(h + 1) * 64], Ct[:, h * L:(h + 1) * L],
                                 STb[:, hh * 64:(hh + 1) * 64], False, True, [idC, "STb"], ["psN0"])
                          else:
                              for b in range(NSQ):
                                  MM(PS["psN0"][:L, h * 64:(h + 1) * 64], Ctm[:, b, :], Ssb[:, b, :],
                                     False, b == NSQ - 1, ["Ctm", "Ssb"], ["psN0"])
                              for bq in range(NSQ // 4):
                                  MM(PS["psV"][:64, :512], xw[:L, h * 64:(h + 1) * 64],
                                     Bm[:L, bq * 4:(bq + 1) * 4, :], True, True, [idW, "Bm"], ["psV"])
                                  TT("dve", Sin[:, bq * 4:(bq + 1) * 4, :], Sin[:, bq * 4:(bq + 1) * 4, :],
                                     bcw[:64, 1, h, bq * 4:(bq + 1) * 4].unsqueeze(2).broadcast_to([64, 4, 128]), ALU.mult,
                                     ["Sin", "bcw"], ["Sin"])
                                  TT("dve", Sin[:, bq * 4:(bq + 1) * 4, :], Sin[:, bq * 4:(bq + 1) * 4, :],
                                     PS["psV"][:64, :512].rearrange("p (b n) -> p b n", b=4), ALU.add, ["Sin", "psV"], ["Sin"])
                              for b4 in range(0, NSQ, 8):
                                  DMA(oS_o[ly][b4:b4 + 8, hh].rearrange("b p n -> p b n"), Sin[:, b4:b4 + 8, :], ["Sin"], ())
                      TT("dve", hn[:L, 0:256].rearrange("p (h c) -> p h c", h=4), xg,
                         Dbc[:L, g * 4:(g + 1) * 4].unsqueeze(2).broadcast_to([L, 4, 64]), ALU.mult, ["xtm", "Dbc"], ["hn"])
                      TT("dve", hn[:L, 0:256], hn[:L, 0:256], PS["psN0"][:L, 0:256], ALU.add, ["hn", "psN0"], ["hn"])
                      TT("dve", hn[:L, 0:256], hn[:L, 0:256], gzs[:L, ti, g * 256:(g + 1) * 256], ALU.mult, ["hn", "gzs"], ["hn"])
                      ACT(junk[:L, 0:256], hn[:L, 0:256], AF.Square, ["hn"], ["junk", "tm4"], accum=tm4[:L, 0:1])
                      ACT(tm4[:L, 1:2], tm4[:L, 0:1], AF.Ln, ["tm4"], ["tm4"], bias=EPS, scale=1.0 / 256)
                      ACT(tm4[:L, 2:3], tm4[:L, 1:2], AF.Exp, ["tm4"], ["tm4"], scale=-0.5)
                      STT("dve", ytm[:L, 0:256], hn[:L, 0:256], tm4[:L, 2:3], ssdnwbc[:L, g * 256:(g + 1) * 256],
                          ALU.mult, ALU.mult, ["hn", "tm4", "ssdnwbc"], ["ytm"])
                      for h2 in range(2):
                          TR(PS["psU"][:, h2 * 128:h2 * 128 + L], ytm[:L, h2 * 128:(h2 + 1) * 128], identf[:L, :L],
                             ["ytm", "c128"], ["psU"])
                      CP("act", mixs[:, 2 * g:2 * g + 2, col:col + L],
                         PS["psU"][:, 0:256].rearrange("p (j c) -> p j c", j=2)[:, :, :L], ["psU"], ["mixs"])
                      if kind == "p":
                          psu, psuid = next_ps2()
                          MM(psu[:, 0:256], Btm[:L, ti, g, :], xw[:L, :], True, True, ["Btm", idW], [psuid])
                          sg = STs[:, ly, g * 256:(g + 1) * 256]
                          TT("pool", sg.rearrange("p (h c) -> p h c", h=4), sg.rearrange("p (h c) -> p h c", h=4),
                             bcw[:, 0, :, ti:ti + 1].broadcast_to([128, 4, 64]), ALU.mult, ["STs", "bcw"], ["STs"])
                          TT("dve", sg, sg, psu[:, 0:256], ALU.add, ["STs", psuid], ["STs"])
                          CP("act", STb[:, g * 256:(g + 1) * 256], sg, ["STs"], ["STb"])

                  sd_front(tiles[0])
                  for i_, t in enumerate(tiles):
                      if i_ + 1 < len(tiles):
                          sd_front(tiles[i_ + 1])
                      sd_rest(t)
              if last_pass:
                  for q4 in range(4):
                      TR(PS["psU"][:, q4 * 128:(q4 + 1) * 128], STs[:, ly, q4 * 128:(q4 + 1) * 128], identf, ["STs", "c128"], ["psU"])
                  CP("dve", hn[:, :], PS["psU"][:, :], ["psU"], ["hn"])
                  DMA(pS_o[ly].rearrange("(q p) n -> p q n", p=128), hn[:, :].rearrange("p (q n) -> p q n", q=4), ["hn"], ())
              checkpoint()
              wout_part(mixs, 4)
              checkpoint()

              if ly == L2 - 1:
                  stage_reset()
                  xn32s = [af([128, D]), af([128, D])]
                  nwbc = af([128, D])
                  junks = [ab([128, D]), ab([128, D])]
                  DMA(nwbc, fnw_d.partition_broadcast(128), (), ["nwbc"])
                  for t in tiles:
                      L, ti = t["L"], t["ti"]
                      xn32, xid = rmsnorm_tile(t, None, xn32s, junks, nwbc)
                      dst = yp_o[t["tok"]:t["tok"] + 128, :] if t["kind"] == "p" else ys_o
                      DMA(dst, xn32[:L], [xid], ())
    except _Stop:
        pass

    print("NOPS", len(R.ops), "arena max words f32/bf16", amax, flush=True)
    block = st.enter_context(nc.Block())
    R.emit(nc, block, st)
    st.close()
    return nc


_CACHE = {}


def _host_layout(inputs, NSQ, c):
    f = np.ascontiguousarray
    b0 = c * NSQ
    L2 = DEPTH
    w_in = inputs["w_in"]
    m = {}
    m["xp"] = f(inputs["x_prompt"][c % inputs["x_prompt"].shape[0]])
    m["xs"] = f(inputs["x_sample"][b0:b0 + NSQ].reshape(NSQ * 4, D))
    m["sC"] = f(inputs["state_mlstm_C"][:, b0:b0 + NSQ])
    m["sn"] = f(inputs["state_mlstm_n"][:, b0:b0 + NSQ].reshape(L2, NSQ, 512))
    smt = np.transpose(inputs["state_mlstm_m"][:, b0:b0 + NSQ], (0, 2, 1))
    m["sm"] = f(np.concatenate([smt, smt], axis=1))
    m["sS"] = f(inputs["state_ssd"][:, b0:b0 + NSQ])
    m["sscv"] = f(inputs["state_ssd_conv"][:, b0:b0 + NSQ].reshape(L2, NSQ * 3, 1024))
    m["sccv"] = f(inputs["state_sconv_conv"][:, b0:b0 + NSQ].reshape(L2, NSQ * 2, 512))
    m["srh"] = f(inputs["state_rglru_h"][:, b0:b0 + NSQ])
    m["srcv"] = f(inputs["state_rglru_conv"][:, b0:b0 + NSQ].reshape(L2, NSQ * 3, 512))
    return m


def _shared_layout(inputs, NCH, NSQ):
    f = np.ascontiguousarray
    L2 = DEPTH
    w_in = inputs["w_in"]
    s = {}
    def cols(o, n):
        return list(range(o, o + n))
    chunks = []
    for j in range(4):
        chunks.append(cols(OSB + j * 128, 128) + cols(OSC + j * 128, 128))
        chunks.append(cols(OSH + j * 128, 128) + cols(OSCZ + j * 128, 128))
    for j in range(4):
        chunks.append(cols(ORX + j * 128, 128) + cols(ORZ + j * 128, 128))
    for cg in range(4):
        chunks.append(cols(OO + cg * 128, 128) + cols(OZ + cg * 128, 128))
    for base in (OQ, OKK, OV):
        for hp in range(2):
            chunks.append(cols(base + hp * 256, 256))
    for jp in range(4):
        chunks.append(cols(OXBC + jp * 256, 256))
    for hp in range(2):
        chunks.append(cols(OSZ + hp * 256, 256))
    assert len(chunks) == 28

    def pack(W):
        K = W.shape[0] // 128
        return np.ascontiguousarray(W.reshape(K, 128, W.shape[1]).transpose(1, 0, 2).reshape(128, K * W.shape[1]))
    wpk = np.empty((L2, 28, 128, 4096), np.float32)
    for ly in range(L2):
        for k, cc in enumerate(chunks):
            wpk[ly, k] = pack(w_in[ly][:, cc])
    s["wpk"] = wpk
    wopk = np.empty((L2, 4, 8, 128, 1024), np.float32)
    for ly in range(L2):
        for g in range(4):
            for dg in range(8):
                wopk[ly, g, dg] = pack(inputs["w_out"][ly][g * 512:(g + 1) * 512, dg * 256:(dg + 1) * 256])
    s["wopk"] = wopk
    wi = w_in[:, :, OI:OI + 4]
    wf = w_in[:, :, OF:OF + 4]
    wd0 = w_in[:, :, ODT:ODT + 4]
    wd1 = w_in[:, :, ODT + 4:ODT + 8]
    wsm = np.concatenate([wi, wi, wf, wf, wd0, wd0, wd1, wd1], axis=2)
    s["wsm"] = np.stack([pack(wsm[ly]) for ly in range(L2)], axis=0)
    wabd = np.zeros((L2, 8, 128, 128), np.float32)
    for ly in range(L2):
        for j in range(4):
            for k in range(2):
                blk = 2 * j + k
                wabd[ly, j, k * 64:(k + 1) * 64, k * 64:(k + 1) * 64] = inputs["rg_wa"][ly, blk]
                wabd[ly, 4 + j, k * 64:(k + 1) * 64, k * 64:(k + 1) * 64] = inputs["rg_wx"][ly, blk]
    s["wabd"] = f(np.transpose(wabd, (0, 2, 1, 3)).reshape(L2, 128, 1024))
    cvec = np.zeros((L2, 128, 84), np.float32)

    def fm(v, nt):
        return v.reshape(nt, 128).T
    for ly in range(L2):
        cw = inputs["ssd_conv_w"][ly]
        cvec[ly, :, 0:32] = np.transpose(cw.reshape(4, 8, 128), (2, 1, 0)).reshape(128, 32)
        cvec[ly, :, 32:40] = fm(inputs["ssd_conv_b"][ly], 8)
        sw = inputs["sc_conv_w"][ly]
        cvec[ly, :, 40:52] = np.transpose(sw.reshape(3, 4, 128), (2, 1, 0)).reshape(128, 12)
        rw = inputs["rg_conv_w"][ly]
        cvec[ly, :, 52:68] = np.transpose(rw.reshape(4, 4, 128), (2, 1, 0)).reshape(128, 16)
        cvec[ly, :, 68:72] = fm(inputs["rg_conv_b"][ly], 4)
        cvec[ly, :, 72:76] = fm(inputs["rg_ba"][ly], 4)
        cvec[ly, :, 76:80] = fm(inputs["rg_bx"][ly], 4)
        cvec[ly, :, 80:84] = fm(inputs["rg_lambda"][ly], 4)
    s["cvec"] = cvec
    g8 = np.zeros((L2, 8, 8), np.float32)
    for ly in range(L2):
        d2 = lambda v: np.concatenate([v, v])
        g8[ly, :, 0] = d2(inputs["ml_i_bias"][ly])
        g8[ly, :, 1] = d2(inputs["ml_f_bias"][ly])
        g8[ly, :, 2] = d2(inputs["ssd_dt_bias"][ly][0:4])
        g8[ly, :, 3] = d2(inputs["ssd_dt_bias"][ly][4:8])
        g8[ly, :, 4] = d2(inputs["ssd_A_log"][ly][0:4])
        g8[ly, :, 5] = d2(inputs["ssd_A_log"][ly][4:8])
    s["g8"] = g8
    for k in ("norm_w", "ml_norm_w", "ssd_norm_w", "ssd_D", "final_norm_w"):
        s[k] = f(inputs[k])
    C128h, _, C8h, _ = host_consts(NCH, NSQ)
    s["c128"] = C128h
    s["c8"] = C8h
    return s


def run(inputs, NCH, n_cores=8):
    inputs = {k: np.asarray(v, dtype=np.float32) for k, v in inputs.items()}
    BP, TP = inputs["x_prompt"].shape[0], inputs["x_prompt"].shape[1]
    BS = inputs["x_sample"].shape[0]
    NSQ = BS // n_cores
    key = (TP, NCH, NSQ)
    if key not in _CACHE:
        _CACHE[key] = build(TP, NCH, NSQ)
    nc = _CACHE[key]
    shared = _shared_layout(inputs, NCH, NSQ)
    in_maps = []
    for c in range(n_cores):
        m = _host_layout(inputs, NSQ, c)
        m.update(shared)
        in_maps.append(m)
    res = run_bass_kernel_spmd(nc, in_maps, core_ids=list(range(n_cores)))
    r = res.results
    L2 = DEPTH
    cat = lambda k, ax: np.concatenate([r[c][k] for c in range(n_cores)], axis=ax)
    stk = lambda k: np.stack([r[c][k] for c in range(BP)], axis=1)
    y_prompt = np.stack([r[c]["yp"] for c in range(BP)], axis=0)
    y_sample = cat("ys", 0).reshape(BS, 4, D)
    p_C = stk("pC")
    p_n = stk("pn")
    p_m = stk("pm")
    p_ssd = stk("pS").reshape(L2, BP, 8, 64, 128)
    p_ssd_conv = stk("pscv")
    p_sc_conv = stk("pccv")
    p_rg_h = stk("prh").reshape(L2, BP, 512)
    p_rg_conv = stk("prcv")
    s_C = cat("oC", 1)
    s_n = cat("on", 1).reshape(L2, BS, 4, 128)
    s_m = np.transpose(cat("om", 2), (0, 2, 1))
    s_ssd = cat("oS", 1)
    s_ssd_conv = cat("oscv", 1).reshape(L2, BS, 3, 1024)
    s_sc_conv = cat("occv", 1).reshape(L2, BS, 2, 512)
    s_rg_h = cat("orh", 1)
    s_rg_conv = cat("orcv", 1).reshape(L2, BS, 3, 512)
    outs = (y_prompt, y_sample, p_C, p_n, p_m, p_ssd, p_ssd_conv, p_sc_conv, p_rg_h, p_rg_conv,
            s_C, s_n, s_m, s_ssd, s_ssd_conv, s_sc_conv, s_rg_h, s_rg_conv)
    return tuple(np.ascontiguousarray(o, dtype=np.float32) for o in outs)


def kernel(**inputs):
    return run(inputs, NCH=4)
```

```python
from contextlib import ExitStack

import numpy as np

import concourse.bass as bass
import concourse.mybir as mybir
from concourse.bass_utils import run_bass_kernel_spmd

F32 = mybir.dt.float32
BF16 = mybir.dt.bfloat16
ALU = mybir.AluOpType
AF = mybir.ActivationFunctionType

D = 2048
BR = 512
PO = 7184
DEPTH = 2
EPS = 1e-6
NEG = -30000.0
OQ, OKK, OV, OO, OZ, OI, OF = 0, 512, 1024, 1536, 2048, 2560, 2564
OSZ, OXBC, ODT = 2568, 3080, 4104
OSB, OSC, OSH, OSCZ = 4112, 4624, 5136, 5648
ORX, ORZ = 6160, 6672

ENGS = ("pe", "act", "dve", "pool", "sp")
NDMASEM = 8


class Op:
    __slots__ = ("eng", "fn", "reads", "writes", "dma", "deps", "sig", "tick", "dsem", "dcount", "idx", "prev")

    def __init__(self, eng, fn, reads, writes, dma):
        self.eng, self.fn, self.reads, self.writes, self.dma = eng, fn, reads, writes, dma
        self.deps = []
        self.sig = False
        self.tick = 0
        self.dsem = -1
        self.dcount = 0
        self.prev = 0


class _Stop(Exception):
    pass


class Rec:
    def __init__(self):
        self.ops = []
        self.last_w = {}
        self.readers = {}
        import os as _os
        self.maxops = int(_os.environ.get("KOPS", "100000000"))
        self.fence_ops = []
        self.pending = set()
        self.fence_from = 0

    PERSIST = frozenset(("c128", "c8", "identb", "maskPb", "maskSb", "smkb", "xnT", "st4", "st4b", "tm4", "Dbc", "cvec", "g8",
                         "g8n", "nsp8", "nba", "wb0", "wb1", "wb2", "wsmb", "wabd", "Caug", "Cbf", "mcar", "STs", "STb", "cvcar",
                         "hcar", "ubp", "ubs", "gsm", "RM", "RMv", "scal", "bcw", "ncol", "cvtmp", "h0s", "small_o"))

    def touches_arena(self, op):
        for b in op.reads + op.writes:
            if b in self.PERSIST or b.startswith("ps") or b.startswith("xres"):
                continue
            return True
        return False

    def fence(self):
        last = {}
        dmas = []
        for op in self.ops[self.fence_from:]:
            if not self.touches_arena(op):
                continue
            if op.dma:
                dmas.append(op.idx)
            else:
                last[op.eng] = op.idx
        self.fence_ops = sorted(set(self.fence_ops) | set(last.values()) | set(dmas)) if self.pending else \
            sorted(set(last.values()) | set(dmas))
        self.pending = set(ENGS)
        self.fence_from = len(self.ops)

    def add(self, eng, fn, reads=(), writes=(), dma=False):
        if len(self.ops) >= self.maxops:
            return None
        writes = tuple(writes) + tuple(b for b in reads if isinstance(b, str) and b.startswith("ps") and b not in writes)
        op = Op(eng, fn, tuple(reads), tuple(writes), dma)
        op.idx = len(self.ops)
        deps = set()
        if eng in self.pending and self.touches_arena(op):
            deps |= set(self.fence_ops)
            self.pending.discard(eng)
        for b in op.reads:
            w = self.last_w.get(b)
            if w is not None:
                deps.add(w)
        for b in op.writes:
            w = self.last_w.get(b)
            if w is not None:
                deps.add(w)
            for r in self.readers.get(b, ()):
                deps.add(r)
        deps.discard(op.idx)
        op.deps = sorted(deps)
        for b in op.reads:
            self.readers.setdefault(b, []).append(op.idx)
        for b in op.writes:
            self.last_w[b] = op.idx
            self.readers[b] = []
        self.ops.append(op)
        return op

    def emit(self, nc, block, stack):
        ops = self.ops
        for op in ops:
            for d in op.deps:
                p = ops[d]
                if p.dma:
                    continue
                if p.eng == "pe" and op.eng == "pe" and not op.dma:
                    continue
                p.sig = True
        sem = {e: stack.enter_context(nc.semaphore("s_" + e)) for e in ("pe", "act", "dve", "pool")}
        dsem = {q: [stack.enter_context(nc.semaphore("d_%s%d" % (q, i))) for i in range(NDMASEM)]
                for q in ("sp", "pool")}
        cnt = {e: 0 for e in sem}
        dn = {q: 0 for q in dsem}
        dc = {q: [0] * NDMASEM for q in dsem}
        for op in ops:
            if op.dma:
                q = op.eng
                i = dn[q] % NDMASEM
                dn[q] += 1
                op.dsem = i
                op.prev = dc[q][i]
                dc[q][i] += 1
                op.dcount = dc[q][i]
            elif op.sig:
                cnt[op.eng] += 1
                op.tick = cnt[op.eng]
        per = {e: [o for o in ops if o.eng == e] for e in ENGS}

        def run(eng_name, e):
            waited = {}

            def need(key, s, v):
                if waited.get(key, 0) >= v:
                    return
                waited[key] = v
                e.wait_ge(s, v)

            for op in per[eng_name]:
                for d in op.deps:
                    p = ops[d]
                    if p.dma:
                        need(("d", p.eng, p.dsem), dsem[p.eng][p.dsem], 16 * p.dcount)
                    else:
                        if p.eng == "pe" and eng_name == "pe" and not op.dma:
                            continue
                        need(("c", p.eng), sem[p.eng], p.tick)
                if op.dma:
                    if op.prev > 0:
                        need(("d", op.eng, op.dsem), dsem[op.eng][op.dsem], 16 * op.prev)
                    op.fn(e).then_inc(dsem[op.eng][op.dsem], 16)
                else:
                    ins = op.fn(e)
                    if op.sig:
                        ins.then_inc(sem[op.eng], 1)
            if eng_name in dsem:
                for i in range(NDMASEM):
                    if dc[eng_name][i] > 0:
                        need(("d", eng_name, i), dsem[eng_name][i], 16 * dc[eng_name][i])

        @block.tensor
        def _(e):
            run("pe", e)

        @block.scalar
        def _(e):
            run("act", e)

        @block.vector
        def _(e):
            run("dve", e)

        @block.gpsimd
        def _(e):
            run("pool", e)

        @block.sync
        def _(e):
            run("sp", e)


def host_consts(NCH, NSQ):
    LS = 4 * NSQ
    NP = 128 * NCH
    c128 = {}
    c128["ident"] = np.eye(128, dtype=np.float32)
    s = np.arange(128)[:, None]
    l = np.arange(128)[None, :]
    mp = np.where(s <= l, 0.0, NEG).astype(np.float32)
    c128["maskP"] = np.tile(mp[:, None, :], (1, 4, 1)).reshape(128, 512)
    ms = np.full((128, 128), NEG, np.float32)
    sl = np.arange(LS)
    okm = (sl[:, None] // 4 == sl[None, :] // 4) & (sl[:, None] <= sl[None, :])
    ms[:LS, :LS] = np.where(okm, 0.0, NEG)
    c128["maskS"] = np.tile(ms[:, None, :LS], (1, 4, 1)).reshape(128, 4 * LS)
    smk = (np.arange(NSQ)[:, None] == (sl[None, :] // 4)).astype(np.float32)
    c128["smk"] = np.tile(smk.reshape(1, NSQ * LS), (128, 1))
    smT = np.zeros((128, NSQ), np.float32)
    smT[:LS] = smk.T
    c128["smT"] = smT
    C128 = np.concatenate([c128[k] for k in ("ident", "maskP", "maskS", "smk", "smT")], axis=1)
    offs128 = {}
    o = 0
    for k in ("ident", "maskP", "maskS", "smk", "smT"):
        offs128[k] = (o, c128[k].shape[1])
        o += c128[k].shape[1]
    c8 = {}
    keepP = np.ones((8, NP), np.float32)
    keepP[:, ::128] = 0
    c8["keepP"] = keepP
    keepS = np.ones((8, LS), np.float32)
    keepS[:, ::4] = 0
    c8["keepS"] = keepS
    negS = np.zeros((8, LS), np.float32)
    negS[:, ::4] = -1e30
    c8["negS"] = negS
    pm = np.array([1, 1, 1, 1, 0, 0, 0, 0], np.float32)[:, None]
    c8["pm"] = pm
    c8["opm"] = 1 - pm
    mf = np.zeros((8, 4, 128), np.float32)
    for k in range(8):
        mf[k, k % 4, :] = 1
    c8["maskfull"] = mf.reshape(8, 512)
    m8 = np.zeros((8, 4), np.float32)
    for k in range(4):
        m8[k, k] = 1
    c8["mask8h"] = m8
    lv = np.zeros((8, 128), np.float32)
    lv[4:] = 1
    c8["LV"] = lv
    c8["ones8"] = np.ones((8, 128), np.float32)
    names8 = ("keepP", "keepS", "negS", "pm", "opm", "maskfull", "mask8h", "LV", "ones8")
    C8 = np.concatenate([c8[k] for k in names8], axis=1)
    offs8 = {}
    o = 0
    for k in names8:
        offs8[k] = (o, c8[k].shape[1])
        o += c8[k].shape[1]
    return C128, offs128, C8, offs8


def build(TP, NCH, NSQ):
    LS = 4 * NSQ
    NPASS = TP // (128 * NCH)
    assert NPASS * 128 * NCH == TP
    NTM = NCH + 1
    NMAX = 128 * NCH + LS
    NP = 128 * NCH
    L2 = DEPTH
    C128h, o128, C8h, o8 = host_consts(NCH, NSQ)
    nc = bass.Bass("TRN2", target_bir_lowering=False)

    def din(name, shape):
        return nc.dram_tensor(name, list(shape), F32, kind="ExternalInput").ap()

    def dout(name, shape):
        return nc.dram_tensor(name, list(shape), F32, kind="ExternalOutput").ap()

    xp_d = din("xp", [TP, D])
    xs_d = din("xs", [LS, D])
    sC_d = din("sC", [L2, NSQ, 4, 128, 128])
    sn_d = din("sn", [L2, NSQ, 512])
    sm_d = din("sm", [L2, 8, NSQ])
    sS_d = din("sS", [L2, NSQ, 8, 64, 128])
    sscv_d = din("sscv", [L2, NSQ * 3, 1024])
    sccv_d = din("sccv", [L2, NSQ * 2, 512])
    srh_d = din("srh", [L2, NSQ, 512])
    srcv_d = din("srcv", [L2, NSQ * 3, 512])
    wpk_d = din("wpk", [L2, 28, 128, 4096])
    wsm_d = din("wsm", [L2, 128, 512])
    wopk_d = din("wopk", [L2, 4, 8, 128, 1024])
    wabd_d = din("wabd", [L2, 128, 1024])
    cvec_d = din("cvec", [L2, 128, 84])
    g8_d = din("g8", [L2, 8, 8])
    normw_d = din("norm_w", [L2, D])
    mlnw_d = din("ml_norm_w", [L2, BR])
    ssdnw_d = din("ssd_norm_w", [L2, BR])
    ssdD_d = din("ssd_D", [L2, 8])
    fnw_d = din("final_norm_w", [D])
    c128_d = din("c128", list(C128h.shape))
    c8_d = din("c8", list(C8h.shape))

    yp_o = dout("yp", [TP, D])
    ys_o = dout("ys", [LS, D])
    pC_o = dout("pC", [L2, 4, 128, 128])
    pn_o = dout("pn", [L2, 4, 128])
    pm_o = dout("pm", [L2, 4])
    pS_o = dout("pS", [L2, 512, 128])
    pscv_o = dout("pscv", [L2, 3, 1024])
    pccv_o = dout("pccv", [L2, 2, 512])
    prh_o = dout("prh", [L2, 4, 128])
    prcv_o = dout("prcv", [L2, 3, 512])
    oC_o = dout("oC", [L2, NSQ, 4, 128, 128])
    on_o = dout("on", [L2, NSQ, 512])
    om_o = dout("om", [L2, 4, NSQ])
    oS_o = dout("oS", [L2, NSQ, 8, 64, 128])
    oscv_o = dout("oscv", [L2, NSQ * 3, 1024])
    occv_o = dout("occv", [L2, NSQ * 2, 512])
    orh_o = dout("orh", [L2, NSQ, 512])
    orcv_o = dout("orcv", [L2, NSQ * 3, 512])

    R = Rec()
    st = ExitStack()
    st.enter_context(nc.allow_non_contiguous_dma(reason="small strided state/constant transfers"))

    def sb(name, shape, dt=F32):
        return st.enter_context(nc.sbuf_tensor("sb_" + name, list(shape), dt))

    def psum(name):
        return st.enter_context(nc.psum_tensor(name, [128, 512], F32))

    def TT(eng, out, in0, in1, op, r, w):
        R.add(eng, lambda e: e.tensor_tensor(out=out, in0=in0, in1=in1, op=op), r, w)

    def TS(eng, out, in0, s1, s2, op0, op1, r, w):
        if op1 is None:
            R.add(eng, lambda e: e.tensor_scalar(out=out, in0=in0, scalar1=s1, scalar2=None, op0=op0), r, w)
        else:
            R.add(eng, lambda e: e.tensor_scalar(out=out, in0=in0, scalar1=s1, scalar2=s2, op0=op0, op1=op1), r, w)

    def STT(eng, out, in0, scalar, in1, op0, op1, r, w):
        R.add(eng, lambda e: e.scalar_tensor_tensor(out=out, in0=in0, scalar=scalar, in1=in1, op0=op0, op1=op1), r, w)

    def CP(eng, out, in_, r, w):
        if eng == "act":
            R.add(eng, lambda e: e.activation(out=out, in_=in_, func=AF.Copy), r, w)
        else:
            R.add(eng, lambda e: e.tensor_copy(out=out, in_=in_), r, w)

    def ACT(out, in_, func, r, w, bias=None, scale=1.0, accum=None):
        kw = {}
        if bias is not None:
            kw["bias"] = bias
        if accum is not None:
            kw["accum_out"] = accum
        R.add("act", lambda e: e.activation(out=out, in_=in_, func=func, scale=scale, **kw), r, w)

    def MM(out, lhsT, rhs, start, stop, r, w):
        R.add("pe", lambda e: e.matmul(out, lhsT=lhsT, rhs=rhs, start=start, stop=stop), r, w)

    def TR(out, in_, ident, r, w):
        R.add("pe", lambda e: e.transpose(out=out, in_=in_, identity=ident), r, w)

    def DMA(out, in_, r, w, q="sp"):
        R.add(q, lambda e: e.dma_start(out=out, in_=in_), r, w, dma=True)

    def SCAN(out, d0, d1, init, op0, op1, r, w):
        R.add("dve", lambda e: e.tensor_tensor_scan(out=out, data0=d0, data1=d1, initial=init, op0=op0, op1=op1), r, w)

    def MEMSET(eng, ap, val, w):
        R.add(eng, lambda e: e.memset(ap, val), (), w)

    def RSUM(out, in_, r, w):
        R.add("dve", lambda e: e.reduce_sum(out=out, in_=in_, axis=mybir.AxisListType.X), r, w)

    def RECIP(out, in_, r, w):
        R.add("dve", lambda e: e.reciprocal(out=out, in_=in_), r, w)

    def sigmoid_from_exp(t, r, w):
        ACT(t, t, AF.Ln, r, w, bias=1.0)
        ACT(t, t, AF.Exp, r, w, scale=-1.0)

    c128 = sb("c128", [128, 128 + NSQ])
    c8 = sb("c8", C8h.shape)
    identf = c128[:, 0:128]
    smT = c128[:, 128:128 + NSQ]

    def k8(name):
        o, n = o8[name]
        return c8[:, o:o + n]

    identb = sb("identb", [128, 128], BF16)
    maskPb = sb("maskPb", [128, 512], BF16)
    maskSb = sb("maskSb", [128, 4 * LS], BF16)
    smkb = sb("smkb", [128, NSQ * LS], BF16)
    pm8 = k8("pm")
    opm8 = k8("opm")
    xres = sb("xres", [128, NTM, D])
    xnT = sb("xnT", [128, 16, NMAX], BF16)
    st4 = sb("st4", [128, 8])
    tm4 = sb("tm4", [128, 16])
    Dbc = sb("Dbc", [128, 8])
    cvec = sb("cvec", [128, 84])
    g8 = sb("g8", [8, 8])
    g8n = sb("g8n", [8, 8])
    nsp8 = sb("nsp8", [128, 4])
    nba = sb("nba", [128, 8])
    NWB = 3
    wb = [sb("wb%d" % i, [128, 16, 256], BF16) for i in range(NWB)]
    wsmb = sb("wsmb", [128, 16, 32], BF16)
    wabd = sb("wabd", [128, 8, 128], BF16)
    PS = {k: psum(k) for k in ("psA", "psB", "psS", "psE", "psV", "psN0", "psN1", "psU")}
    Caug = sb("Caug", [128, L2, 4, 129])
    Cbf = sb("Cbf", [128, 4, 129], BF16)
    mcar = sb("mcar", [8, L2])
    STs = sb("STs", [128, L2, 512])
    STb = sb("STb", [128, 512], BF16)
    cvcar = sb("cvcar", [128, L2, 16, 3])
    hcar = sb("hcar", [128, L2, 4])
    ubp = sb("ubp", [128, 3 + NP])
    ubs = sb("ubs", [128, NSQ, 7])
    gsm = sb("gsm", [8, 64])
    RM = sb("RM", [8, 512])
    RMv = sb("RMv", [8, 512])
    scal = sb("scal", [128, NTM, 2, 8])
    bcw = sb("bcw", [128, 2, 4, 32])
    ncol = sb("ncol", [128, NSQ])
    cvtmp = sb("cvtmp", [128, NSQ * 3])
    h0s = sb("h0s", [128, NSQ])
    small_o = sb("small_o", [8, 768])
    AFW = 14480
    ABW = 17192
    arenaF = sb("arenaF", [128, AFW])
    arenaB = sb("arenaB", [128, ABW], BF16)
    aoff = [0, 0]
    amax = [0, 0]
    import os as _os
    print('SBUF remaining after persistent', nc.sbuf_bytes_remaining, flush=True)

    def stage_reset():
        R.fence()
        aoff[0] = 0
        aoff[1] = 0

    def carve(arena, k, cap, shape, pat):
        n = 1
        for d_ in shape[1:]:
            n *= d_
        o = aoff[k]
        aoff[k] += n
        amax[k] = max(amax[k], aoff[k])
        if _os.environ.get("KDRY"):
            o = 0
        else:
            assert aoff[k] <= cap, ("arena overflow", k, aoff[k], cap)
        v = arena[:shape[0], o:o + n]
        if len(shape) == 3:
            v = v.rearrange("p (a b) -> p a b", a=shape[1])
        elif len(shape) == 4:
            v = v.rearrange("p (a b c) -> p a b c", a=shape[1], b=shape[2])
        return v

    def af(shape):
        return carve(arenaF, 0, AFW, shape, None)

    def ab(shape):
        return carve(arenaB, 1, ABW, shape, None)

    DMA(c128[:, 0:128], c128_d[:, 0:128], (), ["c128"])
    o_, n_ = o128["smT"]
    DMA(c128[:, 128:128 + NSQ], c128_d[:, o_:o_ + n_], (), ["c128"])
    DMA(c8[:], c8_d, (), ["c8"])
    DMA(identb[:], c128_d[:, 0:128], (), ["identb"], q="pool")
    o_, n_ = o128["maskP"]
    DMA(maskPb[:], c128_d[:, o_:o_ + n_], (), ["maskPb"], q="pool")
    o_, n_ = o128["maskS"]
    DMA(maskSb[:], c128_d[:, o_:o_ + n_], (), ["maskSb"], q="pool")
    o_, n_ = o128["smk"]
    DMA(smkb[:], c128_d[:, o_:o_ + n_], (), ["smkb"], q="pool")
    MEMSET("pool", xnT[:], 0.0, ["xnT"])
    MEMSET("pool", Caug[:], 0.0, ["Caug"])
    MEMSET("pool", mcar[:], -1e30, ["mcar"])
    MEMSET("pool", STs[:], 0.0, ["STs"])
    MEMSET("pool", cvcar[:], 0.0, ["cvcar"])
    MEMSET("pool", hcar[:], 0.0, ["hcar"])
    MEMSET("pool", ubs[:], 0.0, ["ubs"])
    MEMSET("pool", ubp[:], 0.0, ["ubp"])
    MEMSET("pool", arenaF[:], 0.0, ["arenaF"])
    MEMSET("pool", arenaB[:], 0.0, ["arenaB"])
    stage_reset()

    wctr = [0]

    def load_w(src2d, nct=16):
        i = wctr[0] % NWB
        wctr[0] += 1
        DMA(wb[i][:, 0:nct, :].rearrange("p a b -> p (a b)"), src2d, (), ["wb%d" % i], q="pool")
        return wb[i], "wb%d" % i

    pctr = [0]

    def next_ps():
        k = ("psA", "psB", "psN1", "psN0", "psS", "psE", "psV")[pctr[0] % 7]
        pctr[0] += 1
        return PS[k], k

    def next_ps2():
        k = ("psA", "psB")[pctr[0] % 2]
        pctr[0] += 1
        return PS[k], k

    import os as _os
    KSTOP = int(_os.environ.get("KSTOP", "99"))
    kst = [0]

    def checkpoint():
        kst[0] += 1
        if kst[0] > KSTOP:
            raise _Stop()

    try:
      for pi in range(NPASS):
          tiles = [dict(kind="p", L=128, col=128 * i, ti=i, tok=pi * NP + 128 * i) for i in range(NCH)]
          last_pass = pi == NPASS - 1
          if last_pass:
              tiles.append(dict(kind="s", L=LS, col=NP, ti=NCH, tok=0))
          NT = len(tiles)
          N = (NP + LS) if last_pass else NP
          groups = []
          c0 = 0
          while c0 < N:
              groups.append((c0, min(512, N - c0)))
              c0 += 512
          has_s = last_pass

          for t in tiles:
              src = xp_d[t["tok"]:t["tok"] + 128, :] if t["kind"] == "p" else xs_d
              DMA(xres[:t["L"], t["ti"], :], src, (), ["xres%d" % t["ti"]])

          for ly in range(L2):
              DMA(Dbc[:], ssdD_d[ly].partition_broadcast(128), (), ["Dbc"])
              DMA(cvec[:], cvec_d[ly], (), ["cvec"])
              DMA(g8[:], g8_d[ly], (), ["g8"])
              DMA(wsmb[:, :, :].rearrange("p a b -> p (a b)"), wsm_d[ly], (), ["wsmb"], q="pool")
              DMA(wabd[:, :, :].rearrange("p a b -> p (a b)"), wabd_d[ly], (), ["wabd"], q="pool")
              TS("dve", g8n[:, 0:2], g8[:, 0:2], -1.0, None, ALU.mult, None, ["g8"], ["g8n"])
              ACT(g8n[:, 4:6], g8[:, 4:6], AF.Exp, ["g8"], ["g8n"])
              TS("dve", g8n[:, 4:6], g8n[:, 4:6], -1.0, None, ALU.mult, None, ["g8n"], ["g8n"])
              ACT(nsp8[:], cvec[:, 80:84], AF.Exp, ["cvec"], ["nsp8"], scale=-1.0)
              ACT(nsp8[:], nsp8[:], AF.Ln, ["nsp8"], ["nsp8"], bias=1.0)
              TS("dve", nsp8[:], nsp8[:], -8.0, None, ALU.mult, None, ["nsp8"], ["nsp8"])
              TS("dve", nba[:], cvec[:, 72:80], -1.0, None, ALU.mult, None, ["cvec"], ["nba"])

              def rmsnorm_tile(t, wsrc_d, xn32s, junks, nwbc):
                  L, ti = t["L"], t["ti"]
                  pb = ti % 2
                  xn32, junk = xn32s[pb], junks[pb]
                  xid, jid, sid = "xn32_%d" % pb, "junk_%d" % pb, ("st4", "st4b")[pb]
                  so = 4 * pb
                  xr_id = "xres%d" % ti
                  ACT(junk[:L], xres[:L, ti, :], AF.Square, [xr_id], [jid, sid], accum=st4[:L, so:so + 1])
                  ACT(st4[:L, so + 1:so + 2], st4[:L, so:so + 1], AF.Ln, [sid], [sid], bias=EPS, scale=1.0 / D)
                  ACT(st4[:L, so + 2:so + 3], st4[:L, so + 1:so + 2], AF.Exp, [sid], [sid], scale=-0.5)
                  STT("dve", xn32[:L], xres[:L, ti, :], st4[:L, so + 2:so + 3], nwbc[:L], ALU.mult, ALU.mult,
                      [xr_id, sid, "nwbc"], [xid])
                  return xn32, xid

              stage_reset()
              xn32s = [af([128, D]), af([128, D])]
              nwbc = af([128, D])
              junks = [ab([128, D]), ab([128, D])]
              DMA(nwbc, normw_d[ly].partition_broadcast(128), (), ["nwbc"])
              for t in tiles:
                  L, ti, col = t["L"], t["ti"], t["col"]
                  xn32, xid = rmsnorm_tile(t, None, xn32s, junks, nwbc)
                  for q4 in range(4):
                      pbk = ("psU", "psV")[q4 % 2]
                      for j in range(4):
                          dt_ = q4 * 4 + j
                          TR(PS[pbk][:, j * 128:j * 128 + L], xn32[:L, dt_ * 128:(dt_ + 1) * 128], identf[:L, :L],
                             [xid, "c128"], [pbk])
                      CP("act" if q4 % 2 else "dve", xnT[:, q4 * 4:q4 * 4 + 4, col:col + L],
                         PS[pbk][:, :].rearrange("p (j c) -> p j c", j=4)[:, :, :L], [pbk], ["xnT"])

              checkpoint()
              def proj_fm(wt, wid, wo, M, evac):
                  for (g0, n) in groups:
                      ps, pid = next_ps()
                      for dt_ in range(16):
                          MM(ps[:M, :n], wt[:, dt_, wo:wo + M], xnT[:, dt_, g0:g0 + n], dt_ == 0, dt_ == 15,
                             [wid, "xnT"], [pid])
                      evac(ps, pid, g0, n)

              def proj_tm(wt, wid, wo, n, t, evac):
                  ps, pid = next_ps()
                  L, col = t["L"], t["col"]
                  for dt_ in range(16):
                      MM(ps[:L, :n], xnT[:, dt_, col:col + L], wt[:, dt_, wo:wo + n], dt_ == 0, dt_ == 15,
                         [wid, "xnT"], [pid])
                  evac(ps, pid)

              def wout_part(mixs, ct0):
                  for dg in range(8):
                      wt1, w1 = load_w(wopk_d[ly, ct0 // 4, dg], nct=4)
                      for t in tiles:
                          L, ti, col = t["L"], t["ti"], t["col"]
                          ps, pid = next_ps()
                          for ct in range(4):
                              MM(ps[:L, :256], mixs[:, ct, col:col + L], wt1[:, ct, :], ct == 0, ct == 3, ["mixs", w1], [pid])
                          TT("dve", xres[:L, ti, dg * 256:(dg + 1) * 256], xres[:L, ti, dg * 256:(dg + 1) * 256],
                             ps[:L, :256], ALU.add, ["xres%d" % ti, pid], ["xres%d" % ti])

              def conv_load_hist(stream):
                  CP("dve", ubp[:, 0:3], cvcar[:, ly, stream, :], ["cvcar"], ["ubp"])

              def conv_save_hist(stream):
                  CP("dve", cvcar[:, ly, stream, :], ubp[:, NP:NP + 3], ["ubp"], ["cvcar"])

              def conv_apply(out_fm, oid, W, wcol, bcol):
                  s0 = 3 - (W - 1)
                  views = [(out_fm[:, 0:NP], lambda j: ubp[:, s0 + j:s0 + j + NP])]
                  if has_s:
                      views.append((out_fm[:, NP:NP + LS].rearrange("p (b t) -> p b t", t=4),
                                    lambda j: ubs[:, :, s0 + j:s0 + j + 4]))
                  for (o, src) in views:
                      if bcol is None:
                          TS("dve", o, src(0), cvec[:, wcol:wcol + 1], None, ALU.mult, None,
                             ["ubp", "ubs", "cvec"], [oid])
                      else:
                          TS("dve", o, src(0), cvec[:, wcol:wcol + 1], cvec[:, bcol:bcol + 1], ALU.mult, ALU.add,
                             ["ubp", "ubs", "cvec"], [oid])
                      for j in range(1, W):
                          STT("dve", o, src(j), cvec[:, wcol + j:wcol + j + 1], o, ALU.mult, ALU.add,
                              ["ubp", "ubs", "cvec", oid], [oid])

              def sample_hist_tile(cvin, W, ctile, src_d):
                  rows = NSQ * (W - 1)
                  DMA(cvin[:rows, :], src_d[:, ctile * 128:(ctile + 1) * 128], (), ["cvin"])
                  TR(PS["psU"][:, :rows], cvin[:rows, :], identf[:rows, :rows],
                     ["cvin", "c128"], ["psU"])
                  CP("dve", ubs[:, :, 3 - (W - 1):3], PS["psU"][:, :rows].rearrange("p (b r) -> p b r", r=W - 1),
                     ["psU"], ["ubs"])

              def sample_hist_out_tile(cvout, W, ctile, dst_d):
                  rows = NSQ * (W - 1)
                  CP("dve", cvtmp[:, :rows].rearrange("p (b r) -> p b r", r=W - 1), ubs[:, :, 7 - (W - 1):7],
                     ["ubs"], ["cvtmp"])
                  TR(PS["psU"][:rows, :128], cvtmp[:, :rows], identf, ["cvtmp", "c128"], ["psU"])
                  CP("dve", cvout[:rows, :], PS["psU"][:rows, :128], ["psU"], ["cvout"])
                  DMA(dst_d[:, ctile * 128:(ctile + 1) * 128], cvout[:rows, :], ["cvout"], ())

              def prompt_hist_out(W, streams, dst):
                  for c0 in range(0, len(streams), 4):
                      sub = streams[c0:c0 + 4]
                      for i, s_ in enumerate(sub):
                          CP("dve", cvtmp[:, 0:W - 1], cvcar[:, ly, s_, 3 - (W - 1):3], ["cvcar"], ["cvtmp"])
                          TR(PS["psU"][:W - 1, :128], cvtmp[:, 0:W - 1], identf, ["cvtmp", "c128"], ["psU"])
                          CP("dve", small_o[:W - 1, i * 128:(i + 1) * 128], PS["psU"][:W - 1, :128], ["psU"], ["small_o"])
                      DMA(dst[:, c0 * 128:(c0 + len(sub)) * 128], small_o[:W - 1, :128 * len(sub)], ["small_o"], ())

              def fill_ub(ps, pid, g0, n, eng="act"):
                  pe_ = min(g0 + n, NP)
                  if pe_ > g0:
                      CP(eng, ubp[:, 3 + g0:3 + pe_], ps[:, 0:pe_ - g0], [pid], ["ubp"])
                  if has_s and g0 + n > NP:
                      a0 = max(g0, NP)
                      CP(eng, ubs[:, :, 3:7], ps[:, a0 - g0:a0 - g0 + LS].rearrange("p (b t) -> p b t", t=4),
                         [pid], ["ubs"])

              stage_reset()
              fa = [af([128, NMAX]) for _ in range(6)]
              cvin = af([NSQ * 3, 128])
              cvout = af([NSQ * 3, 128])
              htm = af([NSQ, 512])
              htmo = af([NSQ, 512])
              mixs = ab([128, 4, NMAX])
              xrb = ab([128, NMAX])
              for j in range(4):
                  wt1, w1 = load_w(wpk_d[ly, 2 * j])
                  wt2, w2 = load_w(wpk_d[ly, 2 * j + 1])
                  conv_load_hist(j)
                  if has_s:
                      sample_hist_tile(cvin, 3, j, sccv_d[ly])
                  proj_fm(wt1, w1, 0, 128, lambda ps, pid, g0, n: CP("act", fa[0][:, g0:g0 + n], ps[:, :n], [pid], ["fa0"]))
                  proj_fm(wt1, w1, 128, 128, lambda ps, pid, g0, n: CP("act", fa[1][:, g0:g0 + n], ps[:, :n], [pid], ["fa1"]))

                  def ev_h(ps, pid, g0, n):
                      TT("dve", fa[1][:, g0:g0 + n], fa[1][:, g0:g0 + n], ps[:, :n], ALU.mult, ["fa1", pid], ["fa1"])
                  proj_fm(wt2, w2, 0, 128, ev_h)

                  def ev_z(ps, pid, g0, n):
                      CP("act", fa[2][:, g0:g0 + n], ps[:, :n], [pid], ["fa2"])
                      ACT(fa[3][:, g0:g0 + n], fa[2][:, g0:g0 + n], AF.Exp, ["fa2"], ["fa3"], scale=-1.0)
                  proj_fm(wt2, w2, 128, 128, ev_z)
                  CP("dve", ubp[:, 3:3 + NP], fa[1][:, 0:NP], ["fa1"], ["ubp"])
                  if has_s:
                      CP("dve", ubs[:, :, 3:7], fa[1][:, NP:NP + LS].rearrange("p (b t) -> p b t", t=4), ["fa1"], ["ubs"])
                  conv_apply(fa[4], "fa4", 3, 40 + 3 * j, None)
                  conv_save_hist(j)
                  if has_s:
                      sample_hist_out_tile(cvout, 3, j, occv_o[ly])
                  sigmoid_from_exp(fa[3][:, :N], ["fa3"], ["fa3"])
                  TT("dve", fa[2][:, :N], fa[2][:, :N], fa[3][:, :N], ALU.mult, ["fa2", "fa3"], ["fa2"])
                  TT("dve", fa[2][:, :N], fa[2][:, :N], fa[0][:, :N], ALU.mult, ["fa2", "fa0"], ["fa2"])
                  TT("dve", mixs[:, j, :N], fa[2][:, :N], fa[4][:, :N], ALU.mult, ["fa2", "fa4"], ["mixs"])
              if last_pass:
                  prompt_hist_out(3, [0, 1, 2, 3], pccv_o[ly])
              checkpoint()
              wout_part(mixs, 8)
              checkpoint()

              if has_s:
                  DMA(htm[:], srh_d[ly], (), ["htm"])
              for j in range(4):
                  wt1, w1 = load_w(wpk_d[ly, 8 + j])
                  conv_load_hist(4 + j)
                  if has_s:
                      sample_hist_tile(cvin, 4, j, srcv_d[ly])
                      TR(PS["psU"][:, :NSQ], htm[:NSQ, j * 128:(j + 1) * 128], identf[:NSQ, :NSQ], ["htm", "c128"], ["psU"])
                      CP("dve", h0s[:], PS["psU"][:, :NSQ], ["psU"], ["h0s"])
                  proj_fm(wt1, w1, 0, 128, lambda ps, pid, g0, n: fill_ub(ps, pid, g0, n))

                  def ev_rz(ps, pid, g0, n):
                      CP("act", fa[2][:, g0:g0 + n], ps[:, :n], [pid], ["fa2"])
                      ACT(fa[3][:, g0:g0 + n], fa[2][:, g0:g0 + n], AF.Exp, ["fa2"], ["fa3"], scale=-1.0)
                  proj_fm(wt1, w1, 128, 128, ev_rz)
                  conv_apply(fa[0], "fa0", 4, 52 + 4 * j, 68 + j)
                  conv_save_hist(4 + j)
                  if has_s:
                      sample_hist_out_tile(cvout, 4, j, orcv_o[ly])
                  CP("act", xrb[:, :N], fa[0][:, :N], ["fa0"], ["xrb"])
                  for (g0, n) in groups:
                      ps, pid = next_ps()
                      MM(ps[:, :n], wabd[:, j, :], xrb[:, g0:g0 + n], True, True, ["wabd", "xrb"], [pid])
                      ACT(fa[1][:, g0:g0 + n], ps[:, :n], AF.Exp, [pid, "nba"], ["fa1"], bias=nba[:, j:j + 1], scale=-1.0)
                      ps, pid = next_ps()
                      MM(ps[:, :n], wabd[:, 4 + j, :], xrb[:, g0:g0 + n], True, True, ["wabd", "xrb"], [pid])
                      ACT(fa[4][:, g0:g0 + n], ps[:, :n], AF.Exp, [pid, "nba"], ["fa4"], bias=nba[:, 4 + j:5 + j], scale=-1.0)
                  sigmoid_from_exp(fa[1][:, :N], ["fa1"], ["fa1"])
                  sigmoid_from_exp(fa[4][:, :N], ["fa4"], ["fa4"])
                  ACT(fa[1][:, :N], fa[1][:, :N], AF.Exp, ["fa1", "nsp8"], ["fa1"], scale=nsp8[:, j:j + 1])
                  TT("dve", fa[5][:, :N], fa[1][:, :N], fa[1][:, :N], ALU.mult, ["fa1"], ["fa5"])
                  TS("dve", fa[5][:, :N], fa[5][:, :N], -1.0, 1.0, ALU.mult, ALU.add, ["fa5"], ["fa5"])
                  TS("dve", fa[5][:, :N], fa[5][:, :N], 1e-30, None, ALU.max, None, ["fa5"], ["fa5"])
                  ACT(fa[5][:, :N], fa[5][:, :N], AF.Ln, ["fa5"], ["fa5"])
                  ACT(fa[5][:, :N], fa[5][:, :N], AF.Exp, ["fa5"], ["fa5"], scale=0.5)
                  TT("dve", fa[4][:, :N], fa[4][:, :N], fa[0][:, :N], ALU.mult, ["fa4", "fa0"], ["fa4"])
                  TT("dve", fa[4][:, :N], fa[4][:, :N], fa[5][:, :N], ALU.mult, ["fa4", "fa5"], ["fa4"])
                  STT("dve", fa[4][:, 0:1], fa[1][:, 0:1], hcar[:, ly, j:j + 1], fa[4][:, 0:1], ALU.mult, ALU.add,
                      ["fa1", "fa4", "hcar"], ["fa4"])
                  MEMSET("dve", fa[1][:, 0:1], 0.0, ["fa1"])
                  SCAN(fa[0][:, 0:NP], fa[1][:, 0:NP], fa[4][:, 0:NP], 0.0, ALU.mult, ALU.add, ["fa1", "fa4"], ["fa0"])
                  CP("dve", hcar[:, ly, j:j + 1], fa[0][:, NP - 1:NP], ["fa0"], ["hcar"])
                  if has_s:
                      a0v = fa[1][:, NP:NP + LS].rearrange("p (b t) -> p b t", t=4)[:, :, 0]
                      b0v = fa[4][:, NP:NP + LS].rearrange("p (b t) -> p b t", t=4)[:, :, 0]
                      TT("dve", tm4[:, :NSQ], a0v, h0s[:], ALU.mult, ["fa1", "h0s"], ["tm4"])
                      TT("dve", b0v, b0v, tm4[:, :NSQ], ALU.add, ["fa4", "tm4"], ["fa4"])
                      MEMSET("dve", a0v, 0.0, ["fa1"])
                      SCAN(fa[0][:, NP:NP + LS], fa[1][:, NP:NP + LS], fa[4][:, NP:NP + LS], 0.0, ALU.mult, ALU.add,
                           ["fa1", "fa4"], ["fa0"])
                      CP("dve", ncol[:, :NSQ], fa[0][:, NP:NP + LS].rearrange("p (b t) -> p b t", t=4)[:, :, 3], ["fa0"], ["ncol"])
                      TR(PS["psU"][:NSQ, :128], ncol[:, :NSQ], identf, ["ncol", "c128"], ["psU"])
                      CP("dve", htmo[:NSQ, j * 128:(j + 1) * 128], PS["psU"][:NSQ, :128], ["psU"], ["htmo"])
                  sigmoid_from_exp(fa[3][:, :N], ["fa3"], ["fa3"])
                  TT("dve", fa[2][:, :N], fa[2][:, :N], fa[3][:, :N], ALU.mult, ["fa2", "fa3"], ["fa2"])
                  TT("dve", mixs[:, j, :N], fa[2][:, :N], fa[0][:, :N], ALU.mult, ["fa2", "fa0"], ["mixs"])
              if has_s:
                  DMA(orh_o[ly], htmo[:], ["htmo"], ())
              if last_pass:
                  prompt_hist_out(4, [4, 5, 6, 7], prcv_o[ly])
                  for j in range(4):
                      TR(PS["psU"][:1, j * 128:(j + 1) * 128], hcar[:, ly, j:j + 1], identf, ["hcar", "c128"], ["psU"])
                  CP("dve", small_o[:1, 0:512], PS["psU"][:1, 0:512], ["psU"], ["small_o"])
                  DMA(prh_o[ly].rearrange("(o t) c -> o (t c)", o=1), small_o[:1, 0:512], ["small_o"], ())
              checkpoint()
              wout_part(mixs, 12)
              checkpoint()

              def chunk_ends(row, t_kind):
                  if t_kind == "p":
                      return row[:, 0:NP].rearrange("p (c t) -> p c t", t=128)[:, :, 127]
                  return row[:, NP:NP + LS].rearrange("p (b t) -> p b t", t=4)[:, :, 3]

              def expo(a_row, aid, c_row, cid, cv_row, cvid, t, Wt, Wv):
                  L, col = t["L"], t["col"]
                  mf = k8("maskfull").rearrange("p (h l) -> p h l", h=4)[:, :, :L]
                  RMl = RM[:, :4 * L].rearrange("p (h l) -> p h l", h=4)
                  RMvl = RMv[:, :4 * L].rearrange("p (h l) -> p h l", h=4)
                  TT("dve", RMl, c_row[:, col:col + L].unsqueeze(1).broadcast_to([8, 4, L]), mf, ALU.mult,
                     [cid, "c8"], ["RM"])
                  TT("dve", RMvl, cv_row[:, col:col + L].unsqueeze(1).broadcast_to([8, 4, L]), mf, ALU.mult,
                     [cvid, "c8"], ["RMv"])
                  msk = (maskPb if t["kind"] == "p" else maskSb)
                  MM(PS["psE"][:L, :4 * L], a_row[:, col:col + L], RM[:, :4 * L], True, False, [aid, "RM"], ["psE"])
                  MM(PS["psE"][:L, :4 * L], identb[:L, :L], msk[:L, :4 * L], False, True, ["identb", "maskPb", "maskSb"], ["psE"])
                  MM(PS["psV"][:, :4 * L], k8("LV"), RMv[:, :4 * L], True, True, ["c8", "RMv"], ["psV"])
                  ACT(Wt[:L, :4 * L], PS["psE"][:L, :4 * L], AF.Exp, ["psE"], ["Wt"])
                  ACT(Wv[:, :4 * L], PS["psV"][:, :4 * L], AF.Exp, ["psV"], ["Wv"])

              def bcast_rows(src_row, sid, ncols, slot):
                  w8 = gsm[:, :4 * ncols].rearrange("p (h c) -> p h c", h=4)
                  TT("dve", w8, src_row.unsqueeze(1).broadcast_to([8, 4, ncols]),
                     k8("mask8h").unsqueeze(2).broadcast_to([8, 4, ncols]), ALU.mult, [sid, "c8"], ["gsm"])
                  MM(PS["psU"][:, :4 * ncols], k8("ones8"), gsm[:, :4 * ncols], True, True, ["c8", "gsm"], ["psU"])
                  CP("dve", bcw[:, slot, :, :ncols], PS["psU"][:, :4 * ncols].rearrange("p (h c) -> p h c", h=4),
                     ["psU"], ["bcw"])

              def to_tok_major(rowA, aid, rowB, bid, t, slot, scr, scrid):
                  L, col, ti = t["L"], t["col"], t["ti"]
                  TS("dve", scr[:, :L], rowA[:, col:col + L], pm8, None, ALU.mult, None, [aid, "c8"], [scrid])
                  STT("dve", scr[:, :L], rowB[:, col:col + L], opm8, scr[:, :L], ALU.mult, ALU.add,
                      [bid, "c8", scrid], [scrid])
                  TR(PS["psU"][:L, :8], scr[:, :L], identf[:8, :8], [scrid, "c128"], ["psU"])
                  CP("dve", scal[:L, ti, slot, :], PS["psU"][:L, :8], ["psU"], ["scal"])

              stage_reset()
              Gml = af([128, NTM, 512])
              gr = [af([8, NMAX]) for _ in range(10)]
              mlnwbc = af([128, BR])
              DMA(mlnwbc, mlnw_d[ly].partition_broadcast(128), (), ["mlnwbc"])
              Wt = af([128, 512])
              Wv = af([128, 512])
              hn = af([128, 512])
              ytm = af([128, 512])
              Cs = af([128, NSQ, 129])
              ntm = af([NSQ, 512])
              ntmo = af([NSQ, 512])
              qT = ab([128, 4, NMAX])
              kT = ab([128, 4, NMAX])
              ktm = ab([128, NTM, 512])
              vaug = ab([128, NTM, 4, 129])
              SwT = ab([128, 512])
              qv = ab([128, 512])
              kw = ab([128, 512])
              qvm = ab([128, NSQ, LS])
              Csb = ab([128, NSQ, 129])
              vm = ab([128, 3, 129])
              mixs = ab([128, 4, NMAX])
              junk = ab([128, 128])
              hsq = af([128, 512])
              MEMSET("pool", vaug, 1.0, ["vaug"])
              CP("act", Cbf[:], Caug[:, ly], ["Caug"], ["Cbf"])
              for cg in range(4):
                  wt1, w1 = load_w(wpk_d[ly, 12 + cg])
                  for t in tiles:
                      L, ti = t["L"], t["ti"]

                      def ev_g(ps, pid, L=L, ti=ti, cg=cg):
                          CP("act", ytm[:L, 0:256], ps[:L, 0:256], [pid], ["ytm"])
                          ACT(hn[:L, 0:256], ytm[:L, 0:256], AF.Exp, ["ytm"], ["hn"], scale=-1.0)
                          ACT(hn[:L, 0:256], hn[:L, 0:256], AF.Ln, ["hn"], ["hn"], bias=1.0)
                          TT("dve", hn[:L, 0:128], hn[:L, 0:128], hn[:L, 128:256], ALU.add, ["hn"], ["hn"])
                          ACT(hn[:L, 0:128], hn[:L, 0:128], AF.Exp, ["hn"], ["hn"], scale=-1.0)
                          TT("dve", hn[:L, 0:128], hn[:L, 0:128], ytm[:L, 128:256], ALU.mult, ["hn", "ytm"], ["hn"])
                          TT("dve", Gml[:L, ti, cg * 128:(cg + 1) * 128], hn[:L, 0:128], mlnwbc[:L, cg * 128:(cg + 1) * 128],
                             ALU.mult, ["hn", "mlnwbc"], ["Gml"])
                      proj_tm(wt1, w1, 0, 256, t, ev_g)
              for hp in range(2):
                  wt1, w1 = load_w(wpk_d[ly, 16 + hp])
                  for hh in range(2):
                      h = hp * 2 + hh
                      proj_fm(wt1, w1, hh * 128, 128,
                              lambda ps, pid, g0, n, h=h: CP("act", qT[:, h, g0:g0 + n], ps[:, :n], [pid], ["qT"]))
              for hp in range(2):
                  wt1, w1 = load_w(wpk_d[ly, 18 + hp])
                  for hh in range(2):
                      h = hp * 2 + hh
                      proj_fm(wt1, w1, hh * 128, 128,
                              lambda ps, pid, g0, n, h=h: ACT(kT[:, h, g0:g0 + n], ps[:, :n], AF.Copy, [pid], ["kT"],
                                                             scale=128.0 ** -0.5))
                  for t in tiles:
                      L, ti = t["L"], t["ti"]
                      proj_tm(wt1, w1, 0, 256, t,
                              lambda ps, pid, L=L, ti=ti, hp=hp: ACT(ktm[:L, ti, hp * 256:(hp + 1) * 256], ps[:L, :256],
                                                                    AF.Copy, [pid], ["ktm"], scale=128.0 ** -0.5))
              for hp in range(2):
                  wt1, w1 = load_w(wpk_d[ly, 20 + hp])
                  for t in tiles:
                      L, ti = t["L"], t["ti"]
                      proj_tm(wt1, w1, 0, 256, t,
                              lambda ps, pid, L=L, ti=ti, hp=hp: CP("dve", vaug[:L, ti, 2 * hp:2 * hp + 2, 0:128],
                                                                   ps[:L, :256].rearrange("p (h v) -> p h v", h=2),
                                                                   [pid], ["vaug"]))
              gi, gf, gb, gm, ga, gc, gcv, gwg, genm, gmp = gr
              glf = gf
              proj_fm(wsmb, "wsmb", 0, 8,
                      lambda ps, pid, g0, n: TS("dve", gi[:, g0:g0 + n], ps[:8, :n], g8[:, 0:1], None, ALU.add, None,
                                                [pid, "g8"], ["gr0"]))
              proj_fm(wsmb, "wsmb", 8, 8,
                      lambda ps, pid, g0, n: ACT(gf[:, g0:g0 + n], ps[:8, :n], AF.Exp, [pid, "g8n"], ["gr1"],
                                                 bias=g8n[:, 1:2], scale=-1.0))
              ACT(glf[:, :N], gf[:, :N], AF.Ln, ["gr1", "gr2"], ["gr1", "gr2"], bias=1.0)
              TS("dve", glf[:, :N], glf[:, :N], -1.0, None, ALU.mult, None, ["gr2"], ["gr2"])
              SCAN(gb[:, 0:NP], k8("keepP"), glf[:, 0:NP], 0.0, ALU.mult, ALU.add, ["c8", "gr2"], ["gr3"])
              SCAN(gm[:, 0:NP], glf[:, 0:NP], gi[:, 0:NP], mcar[:, ly:ly + 1], ALU.add, ALU.max, ["gr2", "gr0", "mcar"], ["gr4"])
              CP("dve", gmp[:, 1:NP], gm[:, 0:NP - 1], ["gr4"], ["gr10"])
              CP("dve", gmp[:, 0:1], mcar[:, ly:ly + 1], ["mcar"], ["gr10"])
              CP("dve", mcar[:, ly:ly + 1], gm[:, NP - 1:NP], ["gr4"], ["mcar"])
              m0 = small_o[:8, 0:NSQ]
              if has_s:
                  sl_ = slice(NP, NP + LS)
                  DMA(m0, sm_d[ly], (), ["small_o"])
                  SCAN(gb[:, sl_], k8("keepS"), glf[:, sl_], 0.0, ALU.mult, ALU.add, ["c8", "gr2"], ["gr3"])
                  TT("dve", gmp[:, sl_], glf[:, sl_], k8("keepS"), ALU.mult, ["gr2", "c8"], ["gr10"])
                  TT("dve", gmp[:, sl_], gmp[:, sl_], k8("negS"), ALU.add, ["gr10", "c8"], ["gr10"])
                  CP("dve", gwg[:, sl_], gi[:, sl_], ["gr0"], ["gr8"])
                  lf0 = glf[:, sl_].rearrange("p (b t) -> p b t", t=4)[:, :, 0]
                  i0 = gwg[:, sl_].rearrange("p (b t) -> p b t", t=4)[:, :, 0]
                  TT("dve", small_o[:8, 64:64 + NSQ], lf0, m0, ALU.add, ["gr2", "small_o"], ["small_o"])
                  TT("dve", i0, i0, small_o[:8, 64:64 + NSQ], ALU.max, ["gr8", "small_o"], ["gr8"])
                  SCAN(gm[:, sl_], gmp[:, sl_], gwg[:, sl_], 0.0, ALU.add, ALU.max, ["gr10", "gr8"], ["gr4"])
              TT("dve", gc[:, :N], gb[:, :N], gm[:, :N], ALU.subtract, ["gr3", "gr4"], ["gr6"])
              TT("dve", ga[:, :N], gi[:, :N], gb[:, :N], ALU.subtract, ["gr0", "gr3"], ["gr5"])
              mprev_p = gmp[:, 0:NP].rearrange("p (c t) -> p c t", t=128)[:, :, 0]
              TT("dve", gcv[:, 0:NP].rearrange("p (c t) -> p c t", t=128), gc[:, 0:NP].rearrange("p (c t) -> p c t", t=128),
                 mprev_p.unsqueeze(2).broadcast_to([8, NCH, 128]), ALU.add, ["gr6", "gr10"], ["gr7"])
              if has_s:
                  TT("dve", gcv[:, sl_].rearrange("p (b t) -> p b t", t=4), gc[:, sl_].rearrange("p (b t) -> p b t", t=4),
                     m0.unsqueeze(2).broadcast_to([8, NSQ, 4]), ALU.add, ["gr6", "small_o"], ["gr7"])
              for kind in (("p", "s") if has_s else ("p",)):
                  ncn = NCH if kind == "p" else NSQ
                  T_ = 128 if kind == "p" else 4
                  base = 0 if kind == "p" else NP
                  gcst = gsm[:, 0:ncn]
                  TT("dve", gcst, chunk_ends(gb, kind), chunk_ends(gm, kind), ALU.subtract, ["gr3", "gr4"], ["gsm"])
                  seg = slice(base, base + ncn * T_)
                  TT("dve", gwg[:, seg].rearrange("p (c t) -> p c t", t=T_), ga[:, seg].rearrange("p (c t) -> p c t", t=T_),
                     gcst.unsqueeze(2).broadcast_to([8, ncn, T_]), ALU.add, ["gr5", "gsm"], ["gr8"])
                  mpv = mprev_p if kind == "p" else m0
                  TT("dve", small_o[:8, 128:128 + ncn], gcst, mpv, ALU.add, ["gsm", "gr10", "small_o"], ["small_o"])
                  ACT(small_o[:8, 128:128 + ncn], small_o[:8, 128:128 + ncn], AF.Exp, ["small_o"], ["small_o"])
                  bcast_rows(small_o[:8, 128:128 + ncn], "small_o", ncn, 0 if kind == "p" else 1)
              ACT(gwg[:, :N], gwg[:, :N], AF.Exp, ["gr8"], ["gr8"])
              ACT(genm[:, :N], gm[:, :N], AF.Exp, ["gr4"], ["gr9"], scale=-1.0)
              TS("dve", ga[:, :N], ga[:, :N], pm8, opm8, ALU.mult, ALU.add, ["gr5", "c8"], ["gr5"])
              TS("dve", gc[:, :N], gc[:, :N], opm8, pm8, ALU.mult, ALU.add, ["gr6", "c8"], ["gr6"])
              TS("dve", gcv[:, :N], gcv[:, :N], opm8, pm8, ALU.mult, ALU.add, ["gr7", "c8"], ["gr7"])
              if has_s:
                  DMA(ntm[:], sn_d[ly], (), ["ntm"])
              if last_pass:
                  DMA(pm_o[ly].rearrange("(h o) -> h o", o=1), mcar[0:4, ly:ly + 1], ["mcar"], ())
              if has_s:
                  CP("dve", small_o[:8, 512:512 + NSQ], chunk_ends(gm, "s"), ["gr4"], ["small_o"])
                  DMA(om_o[ly], small_o[0:4, 512:512 + NSQ], ["small_o"], ())

              for t in tiles:
                  L, ti, col, kind = t["L"], t["ti"], t["col"], t["kind"]
                  to_tok_major(gwg, "gr8", genm, "gr9", t, 0, gmp, "gr10")
                  for h in range(4):
                      MM(PS["psS"][:L, h * L:(h + 1) * L], kT[:, h, col:col + L], qT[:, h, col:col + L], True, True,
                         ["kT", "qT"], ["psS"])
                  expo(ga, "gr5", gc, "gr6", gcv, "gr7", t, Wt, Wv)
                  TT("dve", SwT[:L, :4 * L], PS["psS"][:L, :4 * L], Wt[:L, :4 * L], ALU.mult, ["psS", "Wt"], ["SwT"])
                  TT("pool", qv[:, :4 * L].rearrange("p (h l) -> p h l", h=4), qT[:, :, col:col + L],
                     Wv[:, :4 * L].rearrange("p (h l) -> p h l", h=4), ALU.mult, ["qT", "Wv"], ["qv"])
                  TT("pool", kw[:L, :].rearrange("p (h d) -> p h d", h=4), ktm[:L, ti, :].rearrange("p (h d) -> p h d", h=4),
                     scal[:L, ti, 0, 0:4].unsqueeze(2).broadcast_to([L, 4, 128]), ALU.mult, ["ktm", "scal"], ["kw"])
                  for h in range(4):
                      pn = PS["psN0"] if h < 2 else PS["psN1"]
                      pnid = "psN0" if h < 2 else "psN1"
                      o_ = (h % 2) * 129
                      if kind == "s":
                          for b4 in range(0, NSQ, 4):
                              DMA(Cs[:, b4:b4 + 4, 0:128], sC_d[ly][b4:b4 + 4, h].rearrange("b d v -> d b v"), (), ["Cs"])
                          TR(PS["psU"][:, :NSQ], ntm[:NSQ, h * 128:(h + 1) * 128], identf[:NSQ, :NSQ], ["ntm", "c128"], ["psU"])
                          CP("dve", Cs[:, :, 128], PS["psU"][:, :NSQ], ["psU"], ["Cs"])
                          CP("act", Csb, Cs, ["Cs"], ["Csb"])
                          TT("pool", qvm, qv[:, h * L:(h + 1) * L].unsqueeze(1).broadcast_to([128, NSQ, L]),
                             smkb[:, :].rearrange("p (b l) -> p b l", b=NSQ), ALU.mult, ["qv", "smkb"], ["qvm"])
                      MM(pn[:L, o_:o_ + 129], SwT[:L, h * L:(h + 1) * L], vaug[:L, ti, h, :], True, False,
                         ["SwT", "vaug"], [pnid])
                      if kind == "p":
                          MM(pn[:L, o_:o_ + 129], qv[:, h * L:(h + 1) * L], Cbf[:, h, :], False, True, ["qv", "Cbf"], [pnid])
                      else:
                          for b in range(NSQ):
                              MM(pn[:L, o_:o_ + 129], qvm[:, b, :], Csb[:, b, :], False, b == NSQ - 1,
                                 ["qvm", "Csb"], [pnid])
                      kwh = kw[:L, h * 128:(h + 1) * 128]
                      if kind == "p":
                          psu, psuid = next_ps2()
                          MM(psu[:, 0:129], kwh, vaug[:L, ti, h, :], True, True, ["kw", "vaug"], [psuid])
                          STT("dve", Caug[:, ly, h, :], Caug[:, ly, h, :], bcw[:, 0, h, ti:ti + 1], psu[:, 0:129],
                              ALU.mult, ALU.add, ["Caug", "bcw", psuid], ["Caug"])
                      else:
                          b0 = 0
                          while b0 < NSQ:
                              nb = min(3, NSQ - b0)
                              TT("pool", vm[:L, 0:nb, :], vaug[:L, ti, h, :].unsqueeze(1).broadcast_to([L, nb, 129]),
                                 smT[:L, b0:b0 + nb].unsqueeze(2).broadcast_to([L, nb, 129]), ALU.mult, ["vaug", "c128"], ["vm"])
                              MM(PS["psU"][:, 0:nb * 129], kwh, vm[:L, 0:nb, :], True, True, ["kw", "vm"], ["psU"])
                              TT("dve", Cs[:, b0:b0 + nb, :], Cs[:, b0:b0 + nb, :],
                                 bcw[:, 1, h, b0:b0 + nb].unsqueeze(2).broadcast_to([128, nb, 129]), ALU.mult,
                                 ["Cs", "bcw"], ["Cs"])
                              TT("dve", Cs[:, b0:b0 + nb, :], Cs[:, b0:b0 + nb, :],
                                 PS["psU"][:, 0:nb * 129].rearrange("p (b v) -> p b v", b=nb), ALU.add, ["Cs", "psU"], ["Cs"])
                              b0 += nb
                          for b4 in range(0, NSQ, 4):
                              DMA(oC_o[ly][b4:b4 + 4, h].rearrange("b d v -> d b v"), Cs[:, b4:b4 + 4, 0:128], ["Cs"], ())
                          CP("dve", ncol[:, :NSQ], Cs[:, :, 128], ["Cs"], ["ncol"])
                          TR(PS["psU"][:NSQ, :128], ncol[:, :NSQ], identf, ["ncol", "c128"], ["psU"])
                          CP("dve", ntmo[:NSQ, h * 128:(h + 1) * 128], PS["psU"][:NSQ, :128], ["psU"], ["ntmo"])
                  if kind == "p":
                      CP("act", Cbf[:], Caug[:, ly], ["Caug"], ["Cbf"])
                  for half in range(2):
                      pn = PS["psN0"] if half == 0 else PS["psN1"]
                      pnid = "psN0" if half == 0 else "psN1"
                      den = pn[:L, 0:258].rearrange("p (h v) -> p h v", h=2)[:, :, 128]
                      ACT(tm4[:L, 2 * half:2 * half + 2], den, AF.Abs, [pnid], ["tm4"])
                  TT("dve", tm4[:L, 0:4], tm4[:L, 0:4], scal[:L, ti, 0, 4:8], ALU.max, ["tm4", "scal"], ["tm4"])
                  RECIP(tm4[:L, 0:4], tm4[:L, 0:4], ["tm4"], ["tm4"])
                  for half in range(2):
                      pn = PS["psN0"] if half == 0 else PS["psN1"]
                      pnid = "psN0" if half == 0 else "psN1"
                      CP("act", hn[:L, half * 256:(half + 1) * 256].rearrange("p (h v) -> p h v", h=2),
                         pn[:L, 0:258].rearrange("p (h v) -> p h v", h=2)[:, :, 0:128], [pnid], ["hn"])
                  TT("pool", hsq[:L, :], hn[:L, :], hn[:L, :], ALU.mult, ["hn"], ["hsq"])
                  RSUM(tm4[:L, 4:8], hn[:L, :].rearrange("p (h v) -> p h v", h=4), ["hn"], ["tm4"])
                  RSUM(tm4[:L, 8:12], hsq[:L, :].rearrange("p (h v) -> p h v", h=4), ["hsq"], ["tm4"])
                  TS("dve", tm4[:L, 4:8], tm4[:L, 4:8], 1.0 / 128, None, ALU.mult, None, ["tm4"], ["tm4"])
                  TS("dve", tm4[:L, 8:12], tm4[:L, 8:12], 1.0 / 128, None, ALU.mult, None, ["tm4"], ["tm4"])
                  TT("dve", tm4[:L, 12:16], tm4[:L, 4:8], tm4[:L, 4:8], ALU.mult, ["tm4"], ["tm4"])
                  TT("dve", tm4[:L, 8:12], tm4[:L, 8:12], tm4[:L, 12:16], ALU.subtract, ["tm4"], ["tm4"])
                  TT("dve", tm4[:L, 12:16], tm4[:L, 0:4], tm4[:L, 0:4], ALU.mult, ["tm4"], ["tm4"])
                  TT("dve", tm4[:L, 8:12], tm4[:L, 8:12], tm4[:L, 12:16], ALU.mult, ["tm4"], ["tm4"])
                  TS("dve", tm4[:L, 8:12], tm4[:L, 8:12], 0.0, None, ALU.max, None, ["tm4"], ["tm4"])
                  ACT(tm4[:L, 8:12], tm4[:L, 8:12], AF.Ln, ["tm4"], ["tm4"], bias=EPS)
                  ACT(tm4[:L, 8:12], tm4[:L, 8:12], AF.Exp, ["tm4"], ["tm4"], scale=-0.5)
                  TT("dve", tm4[:L, 8:12], tm4[:L, 8:12], tm4[:L, 0:4], ALU.mult, ["tm4"], ["tm4"])
                  hn3 = hn[:L, :].rearrange("p (h v) -> p h v", h=4)
                  TT("dve", hn3, hn3, tm4[:L, 4:8].unsqueeze(2).broadcast_to([L, 4, 128]), ALU.subtract, ["hn", "tm4"], ["hn"])
                  TT("dve", hn3, hn3, tm4[:L, 8:12].unsqueeze(2).broadcast_to([L, 4, 128]), ALU.mult, ["hn", "tm4"], ["hn"])
                  TT("pool", ytm[:L, :], hn[:L, :], Gml[:L, ti, :], ALU.mult, ["hn", "Gml"], ["ytm"])
                  for h in range(4):
                      TR(PS["psU"][:, h * 128:h * 128 + L], ytm[:L, h * 128:(h + 1) * 128], identf[:L, :L], ["ytm", "c128"], ["psU"])
                  CP("act", mixs[:, 0:4, col:col + L], PS["psU"][:, :].rearrange("p (j c) -> p j c", j=4)[:, :, :L],
                     ["psU"], ["mixs"])
              if has_s:
                  DMA(on_o[ly], ntmo[:], ["ntmo"], ())
              if last_pass:
                  DMA(pC_o[ly].rearrange("h d v -> d h v"), Caug[:, ly, :, 0:128], ["Caug"], ())
                  CP("dve", ncol[:, 0:4], Caug[:, ly, :, 128], ["Caug"], ["ncol"])
                  TR(PS["psU"][:4, :128], ncol[:, 0:4], identf, ["ncol", "c128"], ["psU"])
                  CP("dve", small_o[:4, 600:728], PS["psU"][:4, :128], ["psU"], ["small_o"])
                  DMA(pn_o[ly], small_o[:4, 600:728], ["small_o"], ())
              checkpoint()
              wout_part(mixs, 0)
              checkpoint()

              stage_reset()
              fa = [af([128, NMAX]) for _ in range(2)]
              xtm = af([128, NTM, 512])
              gzs = af([128, NTM, 512])
              gr = [af([8, NMAX]) for _ in range(6)]
              Wt = af([128, 512])
              Wv = af([128, 512])
              hn = af([128, 512])
              ytm = af([128, 256])
              Sin = af([64, NSQ, 128])
              cvin = af([NSQ * 3, 128])
              cvout = af([NSQ * 3, 128])
              ssdnwbc = af([128, BR])
              DMA(ssdnwbc, ssdnw_d[ly].partition_broadcast(128), (), ["ssdnwbc"])
              Btm = ab([128, NTM, 2, 128])
              BTb = ab([128, 2, NMAX])
              CTb = ab([128, 2, NMAX])
              Mh = ab([128, 512])
              Ct = ab([128, 512])
              Ctm = ab([128, NSQ, LS])
              xdt = ab([128, 256])
              xw = ab([128, 256])
              Mh2 = [Mh, ab([128, 512])]
              Ct2 = [Ct, ab([128, 512])]
              xdt2 = [xdt, ab([128, 256])]
              xw2 = [xw, ab([128, 256])]
              Ssb = ab([128, NSQ, 64])
              Bm = ab([128, NSQ, 128])
              mixs = ab([128, 4, NMAX])
              junk = ab([128, 256])
              CP("act", STb[:], STs[:, ly], ["STs"], ["STb"])
              for j in range(8):
                  if j % 2 == 0:
                      wt1, w1 = load_w(wpk_d[ly, 22 + j // 2])
                  conv_load_hist(8 + j)
                  if has_s:
                      sample_hist_tile(cvin, 4, j, sscv_d[ly])
                  proj_fm(wt1, w1, (j % 2) * 128, 128, lambda ps, pid, g0, n: fill_ub(ps, pid, g0, n))
                  conv_apply(fa[0], "fa0", 4, 4 * j, 32 + j)
                  conv_save_hist(8 + j)
                  if has_s:
                      sample_hist_out_tile(cvout, 4, j, oscv_o[ly])
                  ACT(fa[1][:, :N], fa[0][:, :N], AF.Exp, ["fa0"], ["fa1"], scale=-1.0)
                  sigmoid_from_exp(fa[1][:, :N], ["fa1"], ["fa1"])
                  TT("dve", fa[0][:, :N], fa[0][:, :N], fa[1][:, :N], ALU.mult, ["fa0", "fa1"], ["fa0"])
                  if j < 4:
                      for t in tiles:
                          L, ti, col = t["L"], t["ti"], t["col"]
                          TR(PS["psU"][:L, 0:128], fa[0][:, col:col + L], identf, ["fa0", "c128"], ["psU"])
                          CP("act", xtm[:L, ti, j * 128:(j + 1) * 128], PS["psU"][:L, 0:128], ["psU"], ["xtm"])
                  elif j < 6:
                      g = j - 4
                      CP("act", BTb[:, g, :N], fa[0][:, :N], ["fa0"], ["BTb"])
                      for t in tiles:
                          L, ti, col = t["L"], t["ti"], t["col"]
                          TR(PS["psU"][:L, 0:128], fa[0][:, col:col + L], identf, ["fa0", "c128"], ["psU"])
                          CP("act", Btm[:L, ti, g, :], PS["psU"][:L, 0:128], ["psU"], ["Btm"])
                  else:
                      g = j - 6
                      CP("act", CTb[:, g, :N], fa[0][:, :N], ["fa0"], ["CTb"])
              if last_pass:
                  prompt_hist_out(4, list(range(8, 16)), pscv_o[ly])
              for hp in range(2):
                  wt1, w1 = load_w(wpk_d[ly, 26 + hp])
                  for t in tiles:
                      L, ti = t["L"], t["ti"]

                      def ev_sz(ps, pid, L=L, ti=ti, hp=hp):
                          CP("act", ytm[:L, 0:256], ps[:L, 0:256], [pid], ["ytm"])
                          ACT(hn[:L, 0:256], ytm[:L, 0:256], AF.Exp, ["ytm"], ["hn"], scale=-1.0)
                          sigmoid_from_exp(hn[:L, 0:256], ["hn"], ["hn"])
                          TT("dve", gzs[:L, ti, hp * 256:(hp + 1) * 256], hn[:L, 0:256], ytm[:L, 0:256], ALU.mult,
                             ["hn", "ytm"], ["gzs"])
                      proj_tm(wt1, w1, 0, 256, t, ev_sz)
              for g in range(2):
                  gdt, gda, gacs, gwr, grr, gscr = gr
                  gna = gda
                  proj_fm(wsmb, "wsmb", 16 + 8 * g, 8,
                          lambda ps, pid, g0, n, g=g: ACT(gdt[:, g0:g0 + n], ps[:8, :n], AF.Exp, [pid, "g8"], ["gr0"],
                                                          bias=g8[:, 2 + g:3 + g]))
                  ACT(gdt[:, :N], gdt[:, :N], AF.Ln, ["gr0"], ["gr0"], bias=1.0)
                  TS("dve", gda[:, :N], gdt[:, :N], g8n[:, 4 + g:5 + g], None, ALU.mult, None, ["gr0", "g8n"], ["gr1", "gr3"])
                  SCAN(gacs[:, 0:NP], k8("keepP"), gda[:, 0:NP], 0.0, ALU.mult, ALU.add, ["c8", "gr1", "gr3"], ["gr2"])
                  if has_s:
                      SCAN(gacs[:, NP:NP + LS], k8("keepS"), gda[:, NP:NP + LS], 0.0, ALU.mult, ALU.add, ["c8", "gr1"], ["gr2"])
                  for kind in (("p", "s") if has_s else ("p",)):
                      ncn = NCH if kind == "p" else NSQ
                      T_ = 128 if kind == "p" else 4
                      base = 0 if kind == "p" else NP
                      seg = slice(base, base + ncn * T_)
                      al = gsm[:, 0:ncn]
                      CP("dve", al, chunk_ends(gacs, kind), ["gr2"], ["gsm"])
                      TT("dve", gwr[:, seg].rearrange("p (c t) -> p c t", t=T_), al.unsqueeze(2).broadcast_to([8, ncn, T_]),
                         gacs[:, seg].rearrange("p (c t) -> p c t", t=T_), ALU.subtract, ["gsm", "gr2"], ["gr4"])
                      ACT(small_o[:8, 128:128 + ncn], al, AF.Exp, ["gsm"], ["small_o"])
                      bcast_rows(small_o[:8, 128:128 + ncn], "small_o", ncn, 0 if kind == "p" else 1)
                  ACT(gwr[:, :N], gwr[:, :N], AF.Exp, ["gr4"], ["gr4"])
                  TT("dve", gwr[:, :N], gwr[:, :N], gdt[:, :N], ALU.mult, ["gr4", "gr0"], ["gr4"])
                  TS("dve", gna[:, :N], gacs[:, :N], -1.0, None, ALU.mult, None, ["gr2", "gr1"], ["gr3", "gr1"])
                  TS("dve", gna[:, :N], gna[:, :N], pm8, opm8, ALU.mult, ALU.add, ["gr3", "c8"], ["gr3"])
                  TS("dve", grr[:, :N], gacs[:, :N], opm8, pm8, ALU.mult, ALU.add, ["gr2", "c8"], ["gr5"])
                  def sd_front(t, g=g):
                      L, ti, col, kind = t["L"], t["ti"], t["col"], t["kind"]
                      pb = t["ti"] % 2
                      Mh, Ct, xdt, xw = Mh2[pb], Ct2[pb], xdt2[pb], xw2[pb]
                      idM, idC, idX, idW = "Mh%d" % pb, "Ct%d" % pb, "xdt%d" % pb, "xw%d" % pb
                      xg = xtm[:L, ti, g * 256:(g + 1) * 256].rearrange("p (h c) -> p h c", h=4)
                      to_tok_major(gdt, "gr0", gwr, "gr4", t, 1, gscr, "gr6")
                      MM(PS["psS"][:L, :L], BTb[:, g, col:col + L], CTb[:, g, col:col + L], True, True, ["BTb", "CTb"], ["psS"])
                      expo(gna, "gr3", grr, "gr5", grr, "gr5", t, Wt, Wv)
                      TT("dve", Mh[:L, :4 * L].rearrange("p (h l) -> p h l", h=4),
                         PS["psS"][:L, :L].unsqueeze(1).broadcast_to([L, 4, L]),
                         Wt[:L, :4 * L].rearrange("p (h l) -> p h l", h=4), ALU.mult, ["psS", "Wt"], [idM])
                      TT("pool", Ct[:, :4 * L].rearrange("p (h l) -> p h l", h=4),
                         CTb[:, g, col:col + L].unsqueeze(1).broadcast_to([128, 4, L]),
                         Wv[:, :4 * L].rearrange("p (h l) -> p h l", h=4), ALU.mult, ["CTb", "Wv"], [idC])
                      xg = xtm[:L, ti, g * 256:(g + 1) * 256].rearrange("p (h c) -> p h c", h=4)
                      TT("dve", xdt[:L, :].rearrange("p (h c) -> p h c", h=4), xg,
                         scal[:L, ti, 1, 0:4].unsqueeze(2).broadcast_to([L, 4, 64]), ALU.mult, ["xtm", "scal"], [idX])
                      TT("dve", xw[:L, :].rearrange("p (h c) -> p h c", h=4), xg,
                         scal[:L, ti, 1, 4:8].unsqueeze(2).broadcast_to([L, 4, 64]), ALU.mult, ["xtm", "scal"], [idW])
                      if kind == "s":
                          TT("pool", Bm[:L, :, :], Btm[:L, ti, g, :].unsqueeze(1).broadcast_to([L, NSQ, 128]),
                             smT[:L, :].unsqueeze(2).broadcast_to([L, NSQ, 128]), ALU.mult, ["Btm", "c128"], ["Bm"])

                  def sd_rest(t, g=g):
                      L, ti, col, kind = t["L"], t["ti"], t["col"], t["kind"]
                      pb = t["ti"] % 2
                      Mh, Ct, xdt, xw = Mh2[pb], Ct2[pb], xdt2[pb], xw2[pb]
                      idM, idC, idX, idW = "Mh%d" % pb, "Ct%d" % pb, "xdt%d" % pb, "xw%d" % pb
                      xg = xtm[:L, ti, g * 256:(g + 1) * 256].rearrange("p (h c) -> p h c", h=4)
                      for h in range(4):
                          hh = g * 4 + h
                          if kind == "s":
                              TT("pool", Ctm, Ct[:, h * L:(h + 1) * L].unsqueeze(1).broadcast_to([128, NSQ, L]),
                                 smkb[:, :].rearrange("p (b l) -> p b l", b=NSQ), ALU.mult, [idC, "smkb"], ["Ctm"])
                              for b4 in range(0, NSQ, 8):
                                  DMA(Sin[:, b4:b4 + 8, :], sS_d[ly][b4:b4 + 8, hh].rearrange("b p n -> p b n"), (), ["Sin"])
                              for bq in range(NSQ // 8):
                                  for b in range(8):
                                      TR(PS["psU"][:, b * 64:(b + 1) * 64], Sin[:, bq * 8 + b, :], identf[:64, :64],
                                         ["Sin", "c128"], ["psU"])
                                  CP("act", Ssb[:, bq * 8:(bq + 1) * 8, :],
                                     PS["psU"][:, :].rearrange("p (b c) -> p b c", b=8), ["psU"], ["Ssb"])
                          MM(PS["psN0"][:L, h * 64:(h + 1) * 64], Mh[:L, h * L:(h + 1) * L], xdt[:L, h * 64:(h + 1) * 64],
                             True, False, [idM, idX], ["psN0"])
                          if kind == "p":
                              MM(PS["psN0"][:L, h * 64:(h + 1) * 64], Ct[:, h * L:(h + 1) * L],
                                 STb[:, hh * 64:(hh + 1) * 64], False, True, [idC, "STb"], ["psN0"])
                          else:
                              for b in range(NSQ):
                                  MM(PS["psN0"][:L, h * 64:(h + 1) * 64], Ctm[:, b, :], Ssb[:, b, :],
                                     False, b == NSQ - 1, ["Ctm", "Ssb"], ["psN0"])
                              for bq in range(NSQ // 4):
                                  MM(PS["psV"][:64, :512], xw[:L, h * 64:(h + 1) * 64],
                                     Bm[:L, bq * 4:(bq + 1) * 4, :], True, True, [idW, "Bm"], ["psV"])
                                  TT("dve", Sin[:, bq * 4:(bq + 1) * 4, :], Sin[:, bq * 4:(bq + 1) * 4, :],
                                     bcw[:64, 1, h, bq * 4:(bq + 1) * 4].unsqueeze(2).broadcast_to([64, 4, 128]), ALU.mult,
                                     ["Sin", "bcw"], ["Sin"])
                                  TT("dve", Sin[:, bq * 4:(bq + 1) * 4, :], Sin[:, bq * 4:(bq + 1) * 4, :],
                                     PS["psV"][:64, :512].rearrange("p (b n) -> p b n", b=4), ALU.add, ["Sin", "psV"], ["Sin"])
                              for b4 in range(0, NSQ, 8):
                                  DMA(oS_o[ly][b4:b4 + 8, hh].rearrange("b p n -> p b n"), Sin[:, b4:b4 + 8, :], ["Sin"], ())
                      TT("dve", hn[:L, 0:256].rearrange("p (h c) -> p h c", h=4), xg,
                         Dbc[:L, g * 4:(g + 1) * 4].unsqueeze(2).broadcast_to([L, 4, 64]), ALU.mult, ["xtm", "Dbc"], ["hn"])
                      TT("dve", hn[:L, 0:256], hn[:L, 0:256], PS["psN0"][:L, 0:256], ALU.add, ["hn", "psN0"], ["hn"])
                      TT("dve", hn[:L, 0:256], hn[:L, 0:256], gzs[:L, ti, g * 256:(g + 1) * 256], ALU.mult, ["hn", "gzs"], ["hn"])
                      ACT(junk[:L, 0:256], hn[:L, 0:256], AF.Square, ["hn"], ["junk", "tm4"], accum=tm4[:L, 0:1])
                      ACT(tm4[:L, 1:2], tm4[:L, 0:1], AF.Ln, ["tm4"], ["tm4"], bias=EPS, scale=1.0 / 256)
                      ACT(tm4[:L, 2:3], tm4[:L, 1:2], AF.Exp, ["tm4"], ["tm4"], scale=-0.5)
                      STT("dve", ytm[:L, 0:256], hn[:L, 0:256], tm4[:L, 2:3], ssdnwbc[:L, g * 256:(g + 1) * 256],
                          ALU.mult, ALU.mult, ["hn", "tm4", "ssdnwbc"], ["ytm"])
                      for h2 in range(2):
                          TR(PS["psU"][:, h2 * 128:h2 * 128 + L], ytm[:L, h2 * 128:(h2 + 1) * 128], identf[:L, :L],
                             ["ytm", "c128"], ["psU"])
                      CP("act", mixs[:, 2 * g:2 * g + 2, col:col + L],
                         PS["psU"][:, 0:256].rearrange("p (j c) -> p j c", j=2)[:, :, :L], ["psU"], ["mixs"])
                      if kind == "p":
                          psu, psuid = next_ps2()
                          MM(psu[:, 0:256], Btm[:L, ti, g, :], xw[:L, :], True, True, ["Btm", idW], [psuid])
                          sg = STs[:, ly, g * 256:(g + 1) * 256]
                          TT("pool", sg.rearrange("p (h c) -> p h c", h=4), sg.rearrange("p (h c) -> p h c", h=4),
                             bcw[:, 0, :, ti:ti + 1].broadcast_to([128, 4, 64]), ALU.mult, ["STs", "bcw"], ["STs"])
                          TT("dve", sg, sg, psu[:, 0:256], ALU.add, ["STs", psuid], ["STs"])
                          CP("act", STb[:, g * 256:(g + 1) * 256], sg, ["STs"], ["STb"])

                  sd_front(tiles[0])
                  for i_, t in enumerate(tiles):
                      if i_ + 1 < len(tiles):
                          sd_front(tiles[i_ + 1])
                      sd_rest(t)
              if last_pass:
                  for q4 in range(4):
                      TR(PS["psU"][:, q4 * 128:(q4 + 1) * 128], STs[:, ly, q4 * 128:(q4 + 1) * 128], identf, ["STs", "c128"], ["psU"])
                  CP("dve", hn[:, :], PS["psU"][:, :], ["psU"], ["hn"])
                  DMA(pS_o[ly].rearrange("(q p) n -> p q n", p=128), hn[:, :].rearrange("p (q n) -> p q n", q=4), ["hn"], ())
              checkpoint()
              wout_part(mixs, 4)
              checkpoint()

              if ly == L2 - 1:
                  stage_reset()
                  xn32s = [af([128, D]), af([128, D])]
                  nwbc = af([128, D])
                  junks = [ab([128, D]), ab([128, D])]
                  DMA(nwbc, fnw_d.partition_broadcast(128), (), ["nwbc"])
                  for t in tiles:
                      L, ti = t["L"], t["ti"]
                      xn32, xid = rmsnorm_tile(t, None, xn32s, junks, nwbc)
                      dst = yp_o[t["tok"]:t["tok"] + 128, :] if t["kind"] == "p" else ys_o
                      DMA(dst, xn32[:L], [xid], ())
    except _Stop:
        pass

    print("NOPS", len(R.ops), "arena max words f32/bf16", amax, flush=True)
    block = st.enter_context(nc.Block())
    R.emit(nc, block, st)
    st.close()
    return nc


_CACHE = {}


def _host_layout(inputs, NSQ, c):
    f = np.ascontiguousarray
    b0 = c * NSQ
    L2 = DEPTH
    w_in = inputs["w_in"]
    m = {}
    m["xp"] = f(inputs["x_prompt"][c % inputs["x_prompt"].shape[0]])
    m["xs"] = f(inputs["x_sample"][b0:b0 + NSQ].reshape(NSQ * 4, D))
    m["sC"] = f(inputs["state_mlstm_C"][:, b0:b0 + NSQ])
    m["sn"] = f(inputs["state_mlstm_n"][:, b0:b0 + NSQ].reshape(L2, NSQ, 512))
    smt = np.transpose(inputs["state_mlstm_m"][:, b0:b0 + NSQ], (0, 2, 1))
    m["sm"] = f(np.concatenate([smt, smt], axis=1))
    m["sS"] = f(inputs["state_ssd"][:, b0:b0 + NSQ])
    m["sscv"] = f(inputs["state_ssd_conv"][:, b0:b0 + NSQ].reshape(L2, NSQ * 3, 1024))
    m["sccv"] = f(inputs["state_sconv_conv"][:, b0:b0 + NSQ].reshape(L2, NSQ * 2, 512))
    m["srh"] = f(inputs["state_rglru_h"][:, b0:b0 + NSQ])
    m["srcv"] = f(inputs["state_rglru_conv"][:, b0:b0 + NSQ].reshape(L2, NSQ * 3, 512))
    return m


def _shared_layout(inputs, NCH, NSQ):
    f = np.ascontiguousarray
    L2 = DEPTH
    w_in = inputs["w_in"]
    s = {}
    def cols(o, n):
        return list(range(o, o + n))
    chunks = []
    for j in range(4):
        chunks.append(cols(OSB + j * 128, 128) + cols(OSC + j * 128, 128))
        chunks.append(cols(OSH + j * 128, 128) + cols(OSCZ + j * 128, 128))
    for j in range(4):
        chunks.append(cols(ORX + j * 128, 128) + cols(ORZ + j * 128, 128))
    for cg in range(4):
        chunks.append(cols(OO + cg * 128, 128) + cols(OZ + cg * 128, 128))
    for base in (OQ, OKK, OV):
        for hp in range(2):
            chunks.append(cols(base + hp * 256, 256))
    for jp in range(4):
        chunks.append(cols(OXBC + jp * 256, 256))
    for hp in range(2):
        chunks.append(cols(OSZ + hp * 256, 256))
    assert len(chunks) == 28

    def pack(W):
        K = W.shape[0] // 128
        return np.ascontiguousarray(W.reshape(K, 128, W.shape[1]).transpose(1, 0, 2).reshape(128, K * W.shape[1]))
    wpk = np.empty((L2, 28, 128, 4096), np.float32)
    for ly in range(L2):
        for k, cc in enumerate(chunks):
            wpk[ly, k] = pack(w_in[ly][:, cc])
    s["wpk"] = wpk
    wopk = np.empty((L2, 4, 8, 128, 1024), np.float32)
    for ly in range(L2):
        for g in range(4):
            for dg in range(8):
                wopk[ly, g, dg] = pack(inputs["w_out"][ly][g * 512:(g + 1) * 512, dg * 256:(dg + 1) * 256])
    s["wopk"] = wopk
    wi = w_in[:, :, OI:OI + 4]
    wf = w_in[:, :, OF:OF + 4]
    wd0 = w_in[:, :, ODT:ODT + 4]
    wd1 = w_in[:, :, ODT + 4:ODT + 8]
    wsm = np.concatenate([wi, wi, wf, wf, wd0, wd0, wd1, wd1], axis=2)
    s["wsm"] = np.stack([pack(wsm[ly]) for ly in range(L2)], axis=0)
    wabd = np.zeros((L2, 8, 128, 128), np.float32)
    for ly in range(L2):
        for j in range(4):
            for k in range(2):
                blk = 2 * j + k
                wabd[ly, j, k * 64:(k + 1) * 64, k * 64:(k + 1) * 64] = inputs["rg_wa"][ly, blk]
                wabd[ly, 4 + j, k * 64:(k + 1) * 64, k * 64:(k + 1) * 64] = inputs["rg_wx"][ly, blk]
    s["wabd"] = f(np.transpose(wabd, (0, 2, 1, 3)).reshape(L2, 128, 1024))
    cvec = np.zeros((L2, 128, 84), np.float32)

    def fm(v, nt):
        return v.reshape(nt, 128).T
    for ly in range(L2):
        cw = inputs["ssd_conv_w"][ly]
        cvec[ly, :, 0:32] = np.transpose(cw.reshape(4, 8, 128), (2, 1, 0)).reshape(128, 32)
        cvec[ly, :, 32:40] = fm(inputs["ssd_conv_b"][ly], 8)
        sw = inputs["sc_conv_w"][ly]
        cvec[ly, :, 40:52] = np.transpose(sw.reshape(3, 4, 128), (2, 1, 0)).reshape(128, 12)
        rw = inputs["rg_conv_w"][ly]
        cvec[ly, :, 52:68] = np.transpose(rw.reshape(4, 4, 128), (2, 1, 0)).reshape(128, 16)
        cvec[ly, :, 68:72] = fm(inputs["rg_conv_b"][ly], 4)
        cvec[ly, :, 72:76] = fm(inputs["rg_ba"][ly], 4)
        cvec[ly, :, 76:80] = fm(inputs["rg_bx"][ly], 4)
        cvec[ly, :, 80:84] = fm(inputs["rg_lambda"][ly], 4)
    s["cvec"] = cvec
    g8 = np.zeros((L2, 8, 8), np.float32)
    for ly in range(L2):
        d2 = lambda v: np.concatenate([v, v])
        g8[ly, :, 0] = d2(inputs["ml_i_bias"][ly])
        g8[ly, :, 1] = d2(inputs["ml_f_bias"][ly])
        g8[ly, :, 2] = d2(inputs["ssd_dt_bias"][ly][0:4])
        g8[ly, :, 3] = d2(inputs["ssd_dt_bias"][ly][4:8])
        g8[ly, :, 4] = d2(inputs["ssd_A_log"][ly][0:4])
        g8[ly, :, 5] = d2(inputs["ssd_A_log"][ly][4:8])
    s["g8"] = g8
    for k in ("norm_w", "ml_norm_w", "ssd_norm_w", "ssd_D", "final_norm_w"):
        s[k] = f(inputs[k])
    C128h, _, C8h, _ = host_consts(NCH, NSQ)
    s["c128"] = C128h
    s["c8"] = C8h
    return s


def run(inputs, NCH, n_cores=8):
    inputs = {k: np.asarray(v, dtype=np.float32) for k, v in inputs.items()}
    BP, TP = inputs["x_prompt"].shape[0], inputs["x_prompt"].shape[1]
    BS = inputs["x_sample"].shape[0]
    NSQ = BS // n_cores
    key = (TP, NCH, NSQ)
    if key not in _CACHE:
        _CACHE[key] = build(TP, NCH, NSQ)
    nc = _CACHE[key]
    shared = _shared_layout(inputs, NCH, NSQ)
    in_maps = []
    for c in range(n_cores):
        m = _host_layout(inputs, NSQ, c)
        m.update(shared)
        in_maps.append(m)
    res = run_bass_kernel_spmd(nc, in_maps, core_ids=list(range(n_cores)))
    r = res.results
    L2 = DEPTH
    cat = lambda k, ax: np.concatenate([r[c][k] for c in range(n_cores)], axis=ax)
    stk = lambda k: np.stack([r[c][k] for c in range(BP)], axis=1)
    y_prompt = np.stack([r[c]["yp"] for c in range(BP)], axis=0)
    y_sample = cat("ys", 0).reshape(BS, 4, D)
    p_C = stk("pC")
    p_n = stk("pn")
    p_m = stk("pm")
    p_ssd = stk("pS").reshape(L2, BP, 8, 64, 128)
    p_ssd_conv = stk("pscv")
    p_sc_conv = stk("pccv")
    p_rg_h = stk("prh").reshape(L2, BP, 512)
    p_rg_conv = stk("prcv")
    s_C = cat("oC", 1)
    s_n = cat("on", 1).reshape(L2, BS, 4, 128)
    s_m = np.transpose(cat("om", 2), (0, 2, 1))
    s_ssd = cat("oS", 1)
    s_ssd_conv = cat("oscv", 1).reshape(L2, BS, 3, 1024)
    s_sc_conv = cat("occv", 1).reshape(L2, BS, 2, 512)
    s_rg_h = cat("orh", 1)
    s_rg_conv = cat("orcv", 1).reshape(L2, BS, 3, 512)
    outs = (y_prompt, y_sample, p_C, p_n, p_m, p_ssd, p_ssd_conv, p_sc_conv, p_rg_h, p_rg_conv,
            s_C, s_n, s_m, s_ssd, s_ssd_conv, s_sc_conv, s_rg_h, s_rg_conv)
    return tuple(np.ascontiguousarray(o, dtype=np.float32) for o in outs)


def kernel(**inputs):
    return run(inputs, NCH=4)
```

```python
from contextlib import ExitStack

import numpy as np

import concourse.bass as bass
import concourse.mybir as mybir
from concourse.bass_utils import run_bass_kernel_spmd

F32 = mybir.dt.float32
BF16 = mybir.dt.bfloat16
ALU = mybir.AluOpType
AF = mybir.ActivationFunctionType

D = 2048
BR = 512
PO = 7184
DEPTH = 2
EPS = 1e-6
NEG = -30000.0
OQ, OKK, OV, OO, OZ, OI, OF = 0, 512, 1024, 1536, 2048, 2560, 2564
OSZ, OXBC, ODT = 2568, 3080, 4104
OSB, OSC, OSH, OSCZ = 4112, 4624, 5136, 5648
ORX, ORZ = 6160, 6672

ENGS = ("pe", "act", "dve", "pool", "sp")
NDMASEM = 8


class Op:
    __slots__ = ("eng", "fn", "reads", "writes", "dma", "deps", "sig", "tick", "dsem", "dcount", "idx", "prev")

    def __init__(self, eng, fn, reads, writes, dma):
        self.eng, self.fn, self.reads, self.writes, self.dma = eng, fn, reads, writes, dma
        self.deps = []
        self.sig = False
        self.tick = 0
        self.dsem = -1
        self.dcount = 0
        self.prev = 0


class _Stop(Exception):
    pass


class Rec:
    def __init__(self):
        self.ops = []
        self.last_w = {}
        self.readers = {}
        import os as _os
        self.maxops = int(_os.environ.get("KOPS", "100000000"))
        self.fence_ops = []
        self.pending = set()
        self.fence_from = 0

    PERSIST = frozenset(("c128", "c8", "identb", "maskPb", "maskSb", "smkb", "xnT", "st4", "st4b", "tm4", "Dbc", "cvec", "g8",
                         "g8n", "nsp8", "nba", "wb0", "wb1", "wb2", "wsmb", "wabd", "Caug", "Cbf", "mcar", "STs", "STb", "cvcar",
                         "hcar", "ubp", "ubs", "gsm", "RM", "RMv", "scal", "bcw", "ncol", "cvtmp", "h0s", "small_o"))

    def touches_arena(self, op):
        for b in op.reads + op.writes:
            if b in self.PERSIST or b.startswith("ps") or b.startswith("xres"):
                continue
            return True
        return False

    def fence(self):
        last = {}
        dmas = []
        for op in self.ops[self.fence_from:]:
            if not self.touches_arena(op):
                continue
            if op.dma:
                dmas.append(op.idx)
            else:
                last[op.eng] = op.idx
        self.fence_ops = sorted(set(self.fence_ops) | set(last.values()) | set(dmas)) if self.pending else \
            sorted(set(last.values()) | set(dmas))
        self.pending = set(ENGS)
        self.fence_from = len(self.ops)

    def add(self, eng, fn, reads=(), writes=(), dma=False):
        if len(self.ops) >= self.maxops:
            return None
        writes = tuple(writes) + tuple(b for b in reads if isinstance(b, str) and b.startswith("ps") and b not in writes)
        op = Op(eng, fn, tuple(reads), tuple(writes), dma)
        op.idx = len(self.ops)
        deps = set()
        if eng in self.pending and self.touches_arena(op):
            deps |= set(self.fence_ops)
            self.pending.discard(eng)
        for b in op.reads:
            w = self.last_w.get(b)
            if w is not None:
                deps.add(w)
        for b in op.writes:
            w = self.last_w.get(b)
            if w is not None:
                deps.add(w)
            for r in self.readers.get(b, ()):
                deps.add(r)
        deps.discard(op.idx)
        op.deps = sorted(deps)
        for b in op.reads:
            self.readers.setdefault(b, []).append(op.idx)
        for b in op.writes:
            self.last_w[b] = op.idx
            self.readers[b] = []
        self.ops.append(op)
        return op

    def emit(self, nc, block, stack):
        ops = self.ops
        for op in ops:
            for d in op.deps:
                p = ops[d]
                if p.dma:
                    continue
                if p.eng == "pe" and op.eng == "pe" and not op.dma:
                    continue
                p.sig = True
        sem = {e: stack.enter_context(nc.semaphore("s_" + e)) for e in ("pe", "act", "dve", "pool")}
        dsem = {q: [stack.enter_context(nc.semaphore("d_%s%d" % (q, i))) for i in range(NDMASEM)]
                for q in ("sp", "pool")}
        cnt = {e: 0 for e in sem}
        dn = {q: 0 for q in dsem}
        dc = {q: [0] * NDMASEM for q in dsem}
        for op in ops:
            if op.dma:
                q = op.eng
                i = dn[q] % NDMASEM
                dn[q] += 1
                op.dsem = i
                op.prev = dc[q][i]
                dc[q][i] += 1
                op.dcount = dc[q][i]
            elif op.sig:
                cnt[op.eng] += 1
                op.tick = cnt[op.eng]
        per = {e: [o for o in ops if o.eng == e] for e in ENGS}

        def run(eng_name, e):
            waited = {}

            def need(key, s, v):
                if waited.get(key, 0) >= v:
                    return
                waited[key] = v
                e.wait_ge(s, v)

            for op in per[eng_name]:
                for d in op.deps:
                    p = ops[d]
                    if p.dma:
                        need(("d", p.eng, p.dsem), dsem[p.eng][p.dsem], 16 * p.dcount)
                    else:
                        if p.eng == "pe" and eng_name == "pe" and not op.dma:
                            continue
                        need(("c", p.eng), sem[p.eng], p.tick)
                if op.dma:
                    if op.prev > 0:
                        need(("d", op.eng, op.dsem), dsem[op.eng][op.dsem], 16 * op.prev)
                    op.fn(e).then_inc(dsem[op.eng][op.dsem], 16)
                else:
                    ins = op.fn(e)
                    if op.sig:
                        ins.then_inc(sem[op.eng], 1)
            if eng_name in dsem:
                for i in range(NDMASEM):
                    if dc[eng_name][i] > 0:
                        need(("d", eng_name, i), dsem[eng_name][i], 16 * dc[eng_name][i])

        @block.tensor
        def _(e):
            run("pe", e)

        @block.scalar
        def _(e):
            run("act", e)

        @block.vector
        def _(e):
            run("dve", e)

        @block.gpsimd
        def _(e):
            run("pool", e)

        @block.sync
        def _(e):
            run("sp", e)


def host_consts(NCH, NSQ):
    LS = 4 * NSQ
    NP = 128 * NCH
    c128 = {}
    c128["ident"] = np.eye(128, dtype=np.float32)
    s = np.arange(128)[:, None]
    l = np.arange(128)[None, :]
    mp = np.where(s <= l, 0.0, NEG).astype(np.float32)
    c128["maskP"] = np.tile(mp[:, None, :], (1, 4, 1)).reshape(128, 512)
    ms = np.full((128, 128), NEG, np.float32)
    sl = np.arange(LS)
    okm = (sl[:, None] // 4 == sl[None, :] // 4) & (sl[:, None] <= sl[None, :])
    ms[:LS, :LS] = np.where(okm, 0.0, NEG)
    c128["maskS"] = np.tile(ms[:, None, :LS], (1, 4, 1)).reshape(128, 4 * LS)
    smk = (np.arange(NSQ)[:, None] == (sl[None, :] // 4)).astype(np.float32)
    c128["smk"] = np.tile(smk.reshape(1, NSQ * LS), (128, 1))
    smT = np.zeros((128, NSQ), np.float32)
    smT[:LS] = smk.T
    c128["smT"] = smT
    C128 = np.concatenate([c128[k] for k in ("ident", "maskP", "maskS", "smk", "smT")], axis=1)
    offs128 = {}
    o = 0
    for k in ("ident", "maskP", "maskS", "smk", "smT"):
        offs128[k] = (o, c128[k].shape[1])
        o += c128[k].shape[1]
    c8 = {}
    keepP = np.ones((8, NP), np.float32)
    keepP[:, ::128] = 0
    c8["keepP"] = keepP
    keepS = np.ones((8, LS), np.float32)
    keepS[:, ::4] = 0
    c8["keepS"] = keepS
    negS = np.zeros((8, LS), np.float32)
    negS[:, ::4] = -1e30
    c8["negS"] = negS
    pm = np.array([1, 1, 1, 1, 0, 0, 0, 0], np.float32)[:, None]
    c8["pm"] = pm
    c8["opm"] = 1 - pm
    mf = np.zeros((8, 4, 128), np.float32)
    for k in range(8):
        mf[k, k % 4, :] = 1
    c8["maskfull"] = mf.reshape(8, 512)
    m8 = np.zeros((8, 4), np.float32)
    for k in range(4):
        m8[k, k] = 1
    c8["mask8h"] = m8
    lv = np.zeros((8, 128), np.float32)
    lv[4:] = 1
    c8["LV"] = lv
    c8["ones8"] = np.ones((8, 128), np.float32)
    names8 = ("keepP", "keepS", "negS", "pm", "opm", "maskfull", "mask8h", "LV", "ones8")
    C8 = np.concatenate([c8[k] for k in names8], axis=1)
    offs8 = {}
    o = 0
    for k in names8:
        offs8[k] = (o, c8[k].shape[1])
        o += c8[k].shape[1]
    return C128, offs128, C8, offs8


def build(TP, NCH, NSQ):
    LS = 4 * NSQ
    NPASS = TP // (128 * NCH)
    assert NPASS * 128 * NCH == TP
    NTM = NCH + 1
    NMAX = 128 * NCH + LS
    NP = 128 * NCH
    L2 = DEPTH
    C128h, o128, C8h, o8 = host_consts(NCH, NSQ)
    nc = bass.Bass("TRN2", target_bir_lowering=False)

    def din(name, shape):
        return nc.dram_tensor(name, list(shape), F32, kind="ExternalInput").ap()

    def dout(name, shape):
        return nc.dram_tensor(name, list(shape), F32, kind="ExternalOutput").ap()

    xp_d = din("xp", [TP, D])
    xs_d = din("xs", [LS, D])
    sC_d = din("sC", [L2, NSQ, 4, 128, 128])
    sn_d = din("sn", [L2, NSQ, 512])
    sm_d = din("sm", [L2, 8, NSQ])
    sS_d = din("sS", [L2, NSQ, 8, 64, 128])
    sscv_d = din("sscv", [L2, NSQ * 3, 1024])
    sccv_d = din("sccv", [L2, NSQ * 2, 512])
    srh_d = din("srh", [L2, NSQ, 512])
    srcv_d = din("srcv", [L2, NSQ * 3, 512])
    wpk_d = din("wpk", [L2, 28, 128, 4096])
    wsm_d = din("wsm", [L2, 128, 512])
    wopk_d = din("wopk", [L2, 4, 8, 128, 1024])
    wabd_d = din("wabd", [L2, 128, 1024])
    cvec_d = din("cvec", [L2, 128, 84])
    g8_d = din("g8", [L2, 8, 8])
    normw_d = din("norm_w", [L2, D])
    mlnw_d = din("ml_norm_w", [L2, BR])
    ssdnw_d = din("ssd_norm_w", [L2, BR])
    ssdD_d = din("ssd_D", [L2, 8])
    fnw_d = din("final_norm_w", [D])
    c128_d = din("c128", list(C128h.shape))
    c8_d = din("c8", list(C8h.shape))

    yp_o = dout("yp", [TP, D])
    ys_o = dout("ys", [LS, D])
    pC_o = dout("pC", [L2, 4, 128, 128])
    pn_o = dout("pn", [L2, 4, 128])
    pm_o = dout("pm", [L2, 4])
    pS_o = dout("pS", [L2, 512, 128])
    pscv_o = dout("pscv", [L2, 3, 1024])
    pccv_o = dout("pccv", [L2, 2, 512])
    prh_o = dout("prh", [L2, 4, 128])
    prcv_o = dout("prcv", [L2, 3, 512])
    oC_o = dout("oC", [L2, NSQ, 4, 128, 128])
    on_o = dout("on", [L2, NSQ, 512])
    om_o = dout("om", [L2, 4, NSQ])
    oS_o = dout("oS", [L2, NSQ, 8, 64, 128])
    oscv_o = dout("oscv", [L2, NSQ * 3, 1024])
    occv_o = dout("occv", [L2, NSQ * 2, 512])
    orh_o = dout("orh", [L2, NSQ, 512])
    orcv_o = dout("orcv", [L2, NSQ * 3, 512])

    R = Rec()
    st = ExitStack()
    st.enter_context(nc.allow_non_contiguous_dma(reason="small strided state/constant transfers"))

    def sb(name, shape, dt=F32):
        return st.enter_context(nc.sbuf_tensor("sb_" + name, list(shape), dt))

    def psum(name):
        return st.enter_context(nc.psum_tensor(name, [128, 512], F32))

    def TT(eng, out, in0, in1, op, r, w):
        R.add(eng, lambda e: e.tensor_tensor(out=out, in0=in0, in1=in1, op=op), r, w)

    def TS(eng, out, in0, s1, s2, op0, op1, r, w):
        if op1 is None:
            R.add(eng, lambda e: e.tensor_scalar(out=out, in0=in0, scalar1=s1, scalar2=None, op0=op0), r, w)
        else:
            R.add(eng, lambda e: e.tensor_scalar(out=out, in0=in0, scalar1=s1, scalar2=s2, op0=op0, op1=op1), r, w)

    def STT(eng, out, in0, scalar, in1, op0, op1, r, w):
        R.add(eng, lambda e: e.scalar_tensor_tensor(out=out, in0=in0, scalar=scalar, in1=in1, op0=op0, op1=op1), r, w)

    def CP(eng, out, in_, r, w):
        if eng == "act":
            R.add(eng, lambda e: e.activation(out=out, in_=in_, func=AF.Copy), r, w)
        else:
            R.add(eng, lambda e: e.tensor_copy(out=out, in_=in_), r, w)

    def ACT(out, in_, func, r, w, bias=None, scale=1.0, accum=None):
        kw = {}
        if bias is not None:
            kw["bias"] = bias
        if accum is not None:
            kw["accum_out"] = accum
        R.add("act", lambda e: e.activation(out=out, in_=in_, func=func, scale=scale, **kw), r, w)

    def MM(out, lhsT, rhs, start, stop, r, w):
        R.add("pe", lambda e: e.matmul(out, lhsT=lhsT, rhs=rhs, start=start, stop=stop), r, w)

    def TR(out, in_, ident, r, w):
        R.add("pe", lambda e: e.transpose(out=out, in_=in_, identity=ident), r, w)

    def DMA(out, in_, r, w, q="sp"):
        R.add(q, lambda e: e.dma_start(out=out, in_=in_), r, w, dma=True)

    def SCAN(out, d0, d1, init, op0, op1, r, w):
        R.add("dve", lambda e: e.tensor_tensor_scan(out=out, data0=d0, data1=d1, initial=init, op0=op0, op1=op1), r, w)

    def MEMSET(eng, ap, val, w):
        R.add(eng, lambda e: e.memset(ap, val), (), w)

    def RSUM(out, in_, r, w):
        R.add("dve", lambda e: e.reduce_sum(out=out, in_=in_, axis=mybir.AxisListType.X), r, w)

    def RECIP(out, in_, r, w):
        R.add("dve", lambda e: e.reciprocal(out=out, in_=in_), r, w)

    def sigmoid_from_exp(t, r, w):
        ACT(t, t, AF.Ln, r, w, bias=1.0)
        ACT(t, t, AF.Exp, r, w, scale=-1.0)

    c128 = sb("c128", [128, 128 + NSQ])
    c8 = sb("c8", C8h.shape)
    identf = c128[:, 0:128]
    smT = c128[:, 128:128 + NSQ]

    def k8(name):
        o, n = o8[name]
        return c8[:, o:o + n]

    identb = sb("identb", [128, 128], BF16)
    maskPb = sb("maskPb", [128, 512], BF16)
    maskSb = sb("maskSb", [128, 4 * LS], BF16)
    smkb = sb("smkb", [128, NSQ * LS], BF16)
    pm8 = k8("pm")
    opm8 = k8("opm")
    xres = sb("xres", [128, NTM, D])
    xnT = sb("xnT", [128, 16, NMAX], BF16)
    st4 = sb("st4", [128, 8])
    tm4 = sb("tm4", [128, 16])
    Dbc = sb("Dbc", [128, 8])
    cvec = sb("cvec", [128, 84])
    g8 = sb("g8", [8, 8])
    g8n = sb("g8n", [8, 8])
    nsp8 = sb("nsp8", [128, 4])
    nba = sb("nba", [128, 8])
    NWB = 3
    wb = [sb("wb%d" % i, [128, 16, 256], BF16) for i in range(NWB)]
    wsmb = sb("wsmb", [128, 16, 32], BF16)
    wabd = sb("wabd", [128, 8, 128], BF16)
    PS = {k: psum(k) for k in ("psA", "psB", "psS", "psE", "psV", "psN0", "psN1", "psU")}
    Caug = sb("Caug", [128, L2, 4, 129])
    Cbf = sb("Cbf", [128, 4, 129], BF16)
    mcar = sb("mcar", [8, L2])
    STs = sb("STs", [128, L2, 512])
    STb = sb("STb", [128, 512], BF16)
    cvcar = sb("cvcar", [128, L2, 16, 3])
    hcar = sb("hcar", [128, L2, 4])
    ubp = sb("ubp", [128, 3 + NP])
    ubs = sb("ubs", [128, NSQ, 7])
    gsm = sb("gsm", [8, 64])
    RM = sb("RM", [8, 512])
    RMv = sb("RMv", [8, 512])
    scal = sb("scal", [128, NTM, 2, 8])
    bcw = sb("bcw", [128, 2, 4, 32])
    ncol = sb("ncol", [128, NSQ])
    cvtmp = sb("cvtmp", [128, NSQ * 3])
    h0s = sb("h0s", [128, NSQ])
    small_o = sb("small_o", [8, 768])
    AFW = 14480
    ABW = 17192
    arenaF = sb("arenaF", [128, AFW])
    arenaB = sb("arenaB", [128, ABW], BF16)
    aoff = [0, 0]
    amax = [0, 0]
    import os as _os
    print('SBUF remaining after persistent', nc.sbuf_bytes_remaining, flush=True)

    def stage_reset():
        R.fence()
        aoff[0] = 0
        aoff[1] = 0

    def carve(arena, k, cap, shape, pat):
        n = 1
        for d_ in shape[1:]:
            n *= d_
        o = aoff[k]
        aoff[k] += n
        amax[k] = max(amax[k], aoff[k])
        if _os.environ.get("KDRY"):
            o = 0
        else:
            assert aoff[k] <= cap, ("arena overflow", k, aoff[k], cap)
        v = arena[:shape[0], o:o + n]
        if len(shape) == 3:
            v = v.rearrange("p (a b) -> p a b", a=shape[1])
        elif len(shape) == 4:
            v = v.rearrange("p (a b c) -> p a b c", a=shape[1], b=shape[2])
        return v

    def af(shape):
        return carve(arenaF, 0, AFW, shape, None)

    def ab(shape):
        return carve(arenaB, 1, ABW, shape, None)

    DMA(c128[:, 0:128], c128_d[:, 0:128], (), ["c128"])
    o_, n_ = o128["smT"]
    DMA(c128[:, 128:128 + NSQ], c128_d[:, o_:o_ + n_], (), ["c128"])
    DMA(c8[:], c8_d, (), ["c8"])
    DMA(identb[:], c128_d[:, 0:128], (), ["identb"], q="pool")
    o_, n_ = o128["maskP"]
    DMA(maskPb[:], c128_d[:, o_:o_ + n_], (), ["maskPb"], q="pool")
    o_, n_ = o128["maskS"]
    DMA(maskSb[:], c128_d[:, o_:o_ + n_], (), ["maskSb"], q="pool")
    o_, n_ = o128["smk"]
    DMA(smkb[:], c128_d[:, o_:o_ + n_], (), ["smkb"], q="pool")
    MEMSET("pool", xnT[:], 0.0, ["xnT"])
    MEMSET("pool", Caug[:], 0.0, ["Caug"])
    MEMSET("pool", mcar[:], -1e30, ["mcar"])
    MEMSET("pool", STs[:], 0.0, ["STs"])
    MEMSET("pool", cvcar[:], 0.0, ["cvcar"])
    MEMSET("pool", hcar[:], 0.0, ["hcar"])
    MEMSET("pool", ubs[:], 0.0, ["ubs"])
    MEMSET("pool", ubp[:], 0.0, ["ubp"])
    MEMSET("pool", arenaF[:], 0.0, ["arenaF"])
    MEMSET("pool", arenaB[:], 0.0, ["arenaB"])
    stage_reset()

    wctr = [0]

    def load_w(src2d, nct=16):
        i = wctr[0] % NWB
        wctr[0] += 1
        DMA(wb[i][:, 0:nct, :].rearrange("p a b -> p (a b)"), src2d, (), ["wb%d" % i], q="pool")
        return wb[i], "wb%d" % i

    pctr = [0]

    def next_ps():
        k = ("psA", "psB", "psN1", "psN0", "psS", "psE", "psV")[pctr[0] % 7]
        pctr[0] += 1
        return PS[k], k

    def next_ps2():
        k = ("psA", "psB")[pctr[0] % 2]
        pctr[0] += 1
        return PS[k], k

    import os as _os
    KSTOP = int(_os.environ.get("KSTOP", "99"))
    kst = [0]

    def checkpoint():
        kst[0] += 1
        if kst[0] > KSTOP:
            raise _Stop()

    try:
      for pi in range(NPASS):
          tiles = [dict(kind="p", L=128, col=128 * i, ti=i, tok=pi * NP + 128 * i) for i in range(NCH)]
          last_pass = pi == NPASS - 1
          if last_pass:
              tiles.append(dict(kind="s", L=LS, col=NP, ti=NCH, tok=0))
          NT = len(tiles)
          N = (NP + LS) if last_pass else NP
          groups = []
          c0 = 0
          while c0 < N:
              groups.append((c0, min(512, N - c0)))
              c0 += 512
          has_s = last_pass

          for t in tiles:
              src = xp_d[t["tok"]:t["tok"] + 128, :] if t["kind"] == "p" else xs_d
              DMA(xres[:t["L"], t["ti"], :], src, (), ["xres%d" % t["ti"]])

          for ly in range(L2):
              DMA(Dbc[:], ssdD_d[ly].partition_broadcast(128), (), ["Dbc"])
              DMA(cvec[:], cvec_d[ly], (), ["cvec"])
              DMA(g8[:], g8_d[ly], (), ["g8"])
              DMA(wsmb[:, :, :].rearrange("p a b -> p (a b)"), wsm_d[ly], (), ["wsmb"], q="pool")
              DMA(wabd[:, :, :].rearrange("p a b -> p (a b)"), wabd_d[ly], (), ["wabd"], q="pool")
              TS("dve", g8n[:, 0:2], g8[:, 0:2], -1.0, None, ALU.mult, None, ["g8"], ["g8n"])
              ACT(g8n[:, 4:6], g8[:, 4:6], AF.Exp, ["g8"], ["g8n"])
              TS("dve", g8n[:, 4:6], g8n[:, 4:6], -1.0, None, ALU.mult, None, ["g8n"], ["g8n"])
              ACT(nsp8[:], cvec[:, 80:84], AF.Exp, ["cvec"], ["nsp8"], scale=-1.0)
              ACT(nsp8[:], nsp8[:], AF.Ln, ["nsp8"], ["nsp8"], bias=1.0)
              TS("dve", nsp8[:], nsp8[:], -8.0, None, ALU.mult, None, ["nsp8"], ["nsp8"])
              TS("dve", nba[:], cvec[:, 72:80], -1.0, None, ALU.mult, None, ["cvec"], ["nba"])

              def rmsnorm_tile(t, wsrc_d, xn32s, junks, nwbc):
                  L, ti = t["L"], t["ti"]
                  pb = ti % 2
                  xn32, junk = xn32s[pb], junks[pb]
                  xid, jid, sid = "xn32_%d" % pb, "junk_%d" % pb, ("st4", "st4b")[pb]
                  so = 4 * pb
                  xr_id = "xres%d" % ti
                  ACT(junk[:L], xres[:L, ti, :], AF.Square, [xr_id], [jid, sid], accum=st4[:L, so:so + 1])
                  ACT(st4[:L, so + 1:so + 2], st4[:L, so:so + 1], AF.Ln, [sid], [sid], bias=EPS, scale=1.0 / D)
                  ACT(st4[:L, so + 2:so + 3], st4[:L, so + 1:so + 2], AF.Exp, [sid], [sid], scale=-0.5)
                  STT("dve", xn32[:L], xres[:L, ti, :], st4[:L, so + 2:so + 3], nwbc[:L], ALU.mult, ALU.mult,
                      [xr_id, sid, "nwbc"], [xid])
                  return xn32, xid

              stage_reset()
              xn32s = [af([128, D]), af([128, D])]
              nwbc = af([128, D])
              junks = [ab([128, D]), ab([128, D])]
              DMA(nwbc, normw_d[ly].partition_broadcast(128), (), ["nwbc"])
              for t in tiles:
                  L, ti, col = t["L"], t["ti"], t["col"]
                  xn32, xid = rmsnorm_tile(t, None, xn32s, junks, nwbc)
                  for q4 in range(4):
                      pbk = ("psU", "psV")[q4 % 2]
                      for j in range(4):
                          dt_ = q4 * 4 + j
                          TR(PS[pbk][:, j * 128:j * 128 + L], xn32[:L, dt_ * 128:(dt_ + 1) * 128], identf[:L, :L],
                             [xid, "c128"], [pbk])
                      CP("act" if q4 % 2 else "dve", xnT[:, q4 * 4:q4 * 4 + 4, col:col + L],
                         PS[pbk][:, :].rearrange("p (j c) -> p j c", j=4)[:, :, :L], [pbk], ["xnT"])

              checkpoint()
              def proj_fm(wt, wid, wo, M, evac):
                  for (g0, n) in groups:
                      ps, pid = next_ps()
                      for dt_ in range(16):
                          MM(ps[:M, :n], wt[:, dt_, wo:wo + M], xnT[:, dt_, g0:g0 + n], dt_ == 0, dt_ == 15,
                             [wid, "xnT"], [pid])
                      evac(ps, pid, g0, n)

              def proj_tm(wt, wid, wo, n, t, evac):
                  ps, pid = next_ps()
                  L, col = t["L"], t["col"]
                  for dt_ in range(16):
                      MM(ps[:L, :n], xnT[:, dt_, col:col + L], wt[:, dt_, wo:wo + n], dt_ == 0, dt_ == 15,
                         [wid, "xnT"], [pid])
                  evac(ps, pid)

              def wout_part(mixs, ct0):
                  for dg in range(8):
                      wt1, w1 = load_w(wopk_d[ly, ct0 // 4, dg], nct=4)
                      for t in tiles:
                          L, ti, col = t["L"], t["ti"], t["col"]
                          ps, pid = next_ps()
                          for ct in range(4):
                              MM(ps[:L, :256], mixs[:, ct, col:col + L], wt1[:, ct, :], ct == 0, ct == 3, ["mixs", w1], [pid])
                          TT("dve", xres[:L, ti, dg * 256:(dg + 1) * 256], xres[:L, ti, dg * 256:(dg + 1) * 256],
                             ps[:L, :256], ALU.add, ["xres%d" % ti, pid], ["xres%d" % ti])

              def conv_load_hist(stream):
                  CP("dve", ubp[:, 0:3], cvcar[:, ly, stream, :], ["cvcar"], ["ubp"])

              def conv_save_hist(stream):
                  CP("dve", cvcar[:, ly, stream, :], ubp[:, NP:NP + 3], ["ubp"], ["cvcar"])

              def conv_apply(out_fm, oid, W, wcol, bcol):
                  s0 = 3 - (W - 1)
                  views = [(out_fm[:, 0:NP], lambda j: ubp[:, s0 + j:s0 + j + NP])]
                  if has_s:
                      views.append((out_fm[:, NP:NP + LS].rearrange("p (b t) -> p b t", t=4),
                                    lambda j: ubs[:, :, s0 + j:s0 + j + 4]))
                  for (o, src) in views:
                      if bcol is None:
                          TS("dve", o, src(0), cvec[:, wcol:wcol + 1], None, ALU.mult, None,
                             ["ubp", "ubs", "cvec"], [oid])
                      else:
                          TS("dve", o, src(0), cvec[:, wcol:wcol + 1], cvec[:, bcol:bcol + 1], ALU.mult, ALU.add,
                             ["ubp", "ubs", "cvec"], [oid])
                      for j in range(1, W):
                          STT("dve", o, src(j), cvec[:, wcol + j:wcol + j + 1], o, ALU.mult, ALU.add,
                              ["ubp", "ubs", "cvec", oid], [oid])

              def sample_hist_tile(cvin, W, ctile, src_d):
                  rows = NSQ * (W - 1)
                  DMA(cvin[:rows, :], src_d[:, ctile * 128:(ctile + 1) * 128], (), ["cvin"])
                  TR(PS["psU"][:, :rows], cvin[:rows, :], identf[:rows, :rows],
                     ["cvin", "c128"], ["psU"])
                  CP("dve", ubs[:, :, 3 - (W - 1):3], PS["psU"][:, :rows].rearrange("p (b r) -> p b r", r=W - 1),
                     ["psU"], ["ubs"])

              def sample_hist_out_tile(cvout, W, ctile, dst_d):
                  rows = NSQ * (W - 1)
                  CP("dve", cvtmp[:, :rows].rearrange("p (b r) -> p b r", r=W - 1), ubs[:, :, 7 - (W - 1):7],
                     ["ubs"], ["cvtmp"])
                  TR(PS["psU"][:rows, :128], cvtmp[:, :rows], identf, ["cvtmp", "c128"], ["psU"])
                  CP("dve", cvout[:rows, :], PS["psU"][:rows, :128], ["psU"], ["cvout"])
                  DMA(dst_d[:, ctile * 128:(ctile + 1) * 128], cvout[:rows, :], ["cvout"], ())

              def prompt_hist_out(W, streams, dst):
                  for c0 in range(0, len(streams), 4):
                      sub = streams[c0:c0 + 4]
                      for i, s_ in enumerate(sub):
                          CP("dve", cvtmp[:, 0:W - 1], cvcar[:, ly, s_, 3 - (W - 1):3], ["cvcar"], ["cvtmp"])
                          TR(PS["psU"][:W - 1, :128], cvtmp[:, 0:W - 1], identf, ["cvtmp", "c128"], ["psU"])
                          CP("dve", small_o[:W - 1, i * 128:(i + 1) * 128], PS["psU"][:W - 1, :128], ["psU"], ["small_o"])
                      DMA(dst[:, c0 * 128:(c0 + len(sub)) * 128], small_o[:W - 1, :128 * len(sub)], ["small_o"], ())

              def fill_ub(ps, pid, g0, n, eng="act"):
                  pe_ = min(g0 + n, NP)
                  if pe_ > g0:
                      CP(eng, ubp[:, 3 + g0:3 + pe_], ps[:, 0:pe_ - g0], [pid], ["ubp"])
                  if has_s and g0 + n > NP:
                      a0 = max(g0, NP)
                      CP(eng, ubs[:, :, 3:7], ps[:, a0 - g0:a0 - g0 + LS].rearrange("p (b t) -> p b t", t=4),
                         [pid], ["ubs"])

              stage_reset()
              fa = [af([128, NMAX]) for _ in range(6)]
              cvin = af([NSQ * 3, 128])
              cvout = af([NSQ * 3, 128])
              htm = af([NSQ, 512])
              htmo = af([NSQ, 512])
              mixs = ab([128, 4, NMAX])
              xrb = ab([128, NMAX])
              for j in range(4):
                  wt1, w1 = load_w(wpk_d[ly, 2 * j])
                  wt2, w2 = load_w(wpk_d[ly, 2 * j + 1])
                  conv_load_hist(j)
                  if has_s:
                      sample_hist_tile(cvin, 3, j, sccv_d[ly])
                  proj_fm(wt1, w1, 0, 128, lambda ps, pid, g0, n: CP("act", fa[0][:, g0:g0 + n], ps[:, :n], [pid], ["fa0"]))
                  proj_fm(wt1, w1, 128, 128, lambda ps, pid, g0, n: CP("act", fa[1][:, g0:g0 + n], ps[:, :n], [pid], ["fa1"]))

                  def ev_h(ps, pid, g0, n):
                      TT("dve", fa[1][:, g0:g0 + n], fa[1][:, g0:g0 + n], ps[:, :n], ALU.mult, ["fa1", pid], ["fa1"])
                  proj_fm(wt2, w2, 0, 128, ev_h)

                  def ev_z(ps, pid, g0, n):
                      CP("act", fa[2][:, g0:g0 + n], ps[:, :n], [pid], ["fa2"])
                      ACT(fa[3][:, g0:g0 + n], fa[2][:, g0:g0 + n], AF.Exp, ["fa2"], ["fa3"], scale=-1.0)
                  proj_fm(wt2, w2, 128, 128, ev_z)
                  CP("dve", ubp[:, 3:3 + NP], fa[1][:, 0:NP], ["fa1"], ["ubp"])
                  if has_s:
                      CP("dve", ubs[:, :, 3:7], fa[1][:, NP:NP + LS].rearrange("p (b t) -> p b t", t=4), ["fa1"], ["ubs"])
                  conv_apply(fa[4], "fa4", 3, 40 + 3 * j, None)
                  conv_save_hist(j)
                  if has_s:
                      sample_hist_out_tile(cvout, 3, j, occv_o[ly])
                  sigmoid_from_exp(fa[3][:, :N], ["fa3"], ["fa3"])
                  TT("dve", fa[2][:, :N], fa[2][:, :N], fa[3][:, :N], ALU.mult, ["fa2", "fa3"], ["fa2"])
                  TT("dve", fa[2][:, :N], fa[2][:, :N], fa[0][:, :N], ALU.mult, ["fa2", "fa0"], ["fa2"])
                  TT("dve", mixs[:, j, :N], fa[2][:, :N], fa[4][:, :N], ALU.mult, ["fa2", "fa4"], ["mixs"])
              if last_pass:
                  prompt_hist_out(3, [0, 1, 2, 3], pccv_o[ly])
              checkpoint()
              wout_part(mixs, 8)
              checkpoint()

              if has_s:
                  DMA(htm[:], srh_d[ly], (), ["htm"])
              for j in range(4):
                  wt1, w1 = load_w(wpk_d[ly, 8 + j])
                  conv_load_hist(4 + j)
                  if has_s:
                      sample_hist_tile(cvin, 4, j, srcv_d[ly])
                      TR(PS["psU"][:, :NSQ], htm[:NSQ, j * 128:(j + 1) * 128], identf[:NSQ, :NSQ], ["htm", "c128"], ["psU"])
                      CP("dve", h0s[:], PS["psU"][:, :NSQ], ["psU"], ["h0s"])
                  proj_fm(wt1, w1, 0, 128, lambda ps, pid, g0, n: fill_ub(ps, pid, g0, n))

                  def ev_rz(ps, pid, g0, n):
                      CP("act", fa[2][:, g0:g0 + n], ps[:, :n], [pid], ["fa2"])
                      ACT(fa[3][:, g0:g0 + n], fa[2][:, g0:g0 + n], AF.Exp, ["fa2"], ["fa3"], scale=-1.0)
                  proj_fm(wt1, w1, 128, 128, ev_rz)
                  conv_apply(fa[0], "fa0", 4, 52 + 4 * j, 68 + j)
                  conv_save_hist(4 + j)
                  if has_s:
                      sample_hist_out_tile(cvout, 4, j, orcv_o[ly])
                  CP("act", xrb[:, :N], fa[0][:, :N], ["fa0"], ["xrb"])
                  for (g0, n) in groups:
                      ps, pid = next_ps()
                      MM(ps[:, :n], wabd[:, j, :], xrb[:, g0:g0 + n], True, True, ["wabd", "xrb"], [pid])
                      ACT(fa[1][:, g0:g0 + n], ps[:, :n], AF.Exp, [pid, "nba"], ["fa1"], bias=nba[:, j:j + 1], scale=-1.0)
                      ps, pid = next_ps()
                      MM(ps[:, :n], wabd[:, 4 + j, :], xrb[:, g0:g0 + n], True, True, ["wabd", "xrb"], [pid])
                      ACT(fa[4][:, g0:g0 + n], ps[:, :n], AF.Exp, [pid, "nba"], ["fa4"], bias=nba[:, 4 + j:5 + j], scale=-1.0)
                  sigmoid_from_exp(fa[1][:, :N], ["fa1"], ["fa1"])
                  sigmoid_from_exp(fa[4][:, :N], ["fa4"], ["fa4"])
                  ACT(fa[1][:, :N], fa[1][:, :N], AF.Exp, ["fa1", "nsp8"], ["fa1"], scale=nsp8[:, j:j + 1])
                  TT("dve", fa[5][:, :N], fa[1][:, :N], fa[1][:, :N], ALU.mult, ["fa1"], ["fa5"])
                  TS("dve", fa[5][:, :N], fa[5][:, :N], -1.0, 1.0, ALU.mult, ALU.add, ["fa5"], ["fa5"])
                  TS("dve", fa[5][:, :N], fa[5][:, :N], 1e-30, None, ALU.max, None, ["fa5"], ["fa5"])
                  ACT(fa[5][:, :N], fa[5][:, :N], AF.Ln, ["fa5"], ["fa5"])
                  ACT(fa[5][:, :N], fa[5][:, :N], AF.Exp, ["fa5"], ["fa5"], scale=0.5)
                  TT("dve", fa[4][:, :N], fa[4][:, :N], fa[0][:, :N], ALU.mult, ["fa4", "fa0"], ["fa4"])
                  TT("dve", fa[4][:, :N], fa[4][:, :N], fa[5][:, :N], ALU.mult, ["fa4", "fa5"], ["fa4"])
                  STT("dve", fa[4][:, 0:1], fa[1][:, 0:1], hcar[:, ly, j:j + 1], fa[4][:, 0:1], ALU.mult, ALU.add,
                      ["fa1", "fa4", "hcar"], ["fa4"])
                  MEMSET("dve", fa[1][:, 0:1], 0.0, ["fa1"])
                  SCAN(fa[0][:, 0:NP], fa[1][:, 0:NP], fa[4][:, 0:NP], 0.0, ALU.mult, ALU.add, ["fa1", "fa4"], ["fa0"])
                  CP("dve", hcar[:, ly, j:j + 1], fa[0][:, NP - 1:NP], ["fa0"], ["hcar"])
                  if has_s:
                      a0v = fa[1][:, NP:NP + LS].rearrange("p (b t) -> p b t", t=4)[:, :, 0]
                      b0v = fa[4][:, NP:NP + LS].rearrange("p (b t) -> p b t", t=4)[:, :, 0]
                      TT("dve", tm4[:, :NSQ], a0v, h0s[:], ALU.mult, ["fa1", "h0s"], ["tm4"])
                      TT("dve", b0v, b0v, tm4[:, :NSQ], ALU.add, ["fa4", "tm4"], ["fa4"])
                      MEMSET("dve", a0v, 0.0, ["fa1"])
                      SCAN(fa[0][:, NP:NP + LS], fa[1][:, NP:NP + LS], fa[4][:, NP:NP + LS], 0.0, ALU.mult, ALU.add,
                           ["fa1", "fa4"], ["fa0"])
                      CP("dve", ncol[:, :NSQ], fa[0][:, NP:NP + LS].rearrange("p (b t) -> p b t", t=4)[:, :, 3], ["fa0"], ["ncol"])
                      TR(PS["psU"][:NSQ, :128], ncol[:, :NSQ], identf, ["ncol", "c128"], ["psU"])
                      CP("dve", htmo[:NSQ, j * 128:(j + 1) * 128], PS["psU"][:NSQ, :128], ["psU"], ["htmo"])
                  sigmoid_from_exp(fa[3][:, :N], ["fa3"], ["fa3"])
                  TT("dve", fa[2][:, :N], fa[2][:, :N], fa[3][:, :N], ALU.mult, ["fa2", "fa3"], ["fa2"])
                  TT("dve", mixs[:, j, :N], fa[2][:, :N], fa[0][:, :N], ALU.mult, ["fa2", "fa0"], ["mixs"])
              if has_s:
                  DMA(orh_o[ly], htmo[:], ["htmo"], ())
              if last_pass:
                  prompt_hist_out(4, [4, 5, 6, 7], prcv_o[ly])
                  for j in range(4):
                      TR(PS["psU"][:1, j * 128:(j + 1) * 128], hcar[:, ly, j:j + 1], identf, ["hcar", "c128"], ["psU"])
                  CP("dve", small_o[:1, 0:512], PS["psU"][:1, 0:512], ["psU"], ["small_o"])
                  DMA(prh_o[ly].rearrange("(o t) c -> o (t c)", o=1), small_o[:1, 0:512], ["small_o"], ())
              checkpoint()
              wout_part(mixs, 12)
              checkpoint()

              def chunk_ends(row, t_kind):
                  if t_kind == "p":
                      return row[:, 0:NP].rearrange("p (c t) -> p c t", t=128)[:, :, 127]
                  return row[:, NP:NP + LS].rearrange("p (b t) -> p b t", t=4)[:, :, 3]

              def expo(a_row, aid, c_row, cid, cv_row, cvid, t, Wt, Wv):
                  L, col = t["L"], t["col"]
                  mf = k8("maskfull").rearrange("p (h l) -> p h l", h=4)[:, :, :L]
                  RMl = RM[:, :4 * L].rearrange("p (h l) -> p h l", h=4)
                  RMvl = RMv[:, :4 * L].rearrange("p (h l) -> p h l", h=4)
                  TT("dve", RMl, c_row[:, col:col + L].unsqueeze(1).broadcast_to([8, 4, L]), mf, ALU.mult,
                     [cid, "c8"], ["RM"])
                  TT("dve", RMvl, cv_row[:, col:col + L].unsqueeze(1).broadcast_to([8, 4, L]), mf, ALU.mult,
                     [cvid, "c8"], ["RMv"])
                  msk = (maskPb if t["kind"] == "p" else maskSb)
                  MM(PS["psE"][:L, :4 * L], a_row[:, col:col + L], RM[:, :4 * L], True, False, [aid, "RM"], ["psE"])
                  MM(PS["psE"][:L, :4 * L], identb[:L, :L], msk[:L, :4 * L], False, True, ["identb", "maskPb", "maskSb"], ["psE"])
                  MM(PS["psV"][:, :4 * L], k8("LV"), RMv[:, :4 * L], True, True, ["c8", "RMv"], ["psV"])
                  ACT(Wt[:L, :4 * L], PS["psE"][:L, :4 * L], AF.Exp, ["psE"], ["Wt"])
                  ACT(Wv[:, :4 * L], PS["psV"][:, :4 * L], AF.Exp, ["psV"], ["Wv"])

              def bcast_rows(src_row, sid, ncols, slot):
                  w8 = gsm[:, :4 * ncols].rearrange("p (h c) -> p h c", h=4)
                  TT("dve", w8, src_row.unsqueeze(1).broadcast_to([8, 4, ncols]),
                     k8("mask8h").unsqueeze(2).broadcast_to([8, 4, ncols]), ALU.mult, [sid, "c8"], ["gsm"])
                  MM(PS["psU"][:, :4 * ncols], k8("ones8"), gsm[:, :4 * ncols], True, True, ["c8", "gsm"], ["psU"])
                  CP("dve", bcw[:, slot, :, :ncols], PS["psU"][:, :4 * ncols].rearrange("p (h c) -> p h c", h=4),
                     ["psU"], ["bcw"])

              def to_tok_major(rowA, aid, rowB, bid, t, slot, scr, scrid):
                  L, col, ti = t["L"], t["col"], t["ti"]
                  TS("dve", scr[:, :L], rowA[:, col:col + L], pm8, None, ALU.mult, None, [aid, "c8"], [scrid])
                  STT("dve", scr[:, :L], rowB[:, col:col + L], opm8, scr[:, :L], ALU.mult, ALU.add,
                      [bid, "c8", scrid], [scrid])
                  TR(PS["psU"][:L, :8], scr[:, :L], identf[:8, :8], [scrid, "c128"], ["psU"])
                  CP("dve", scal[:L, ti, slot, :], PS["psU"][:L, :8], ["psU"], ["scal"])

              stage_reset()
              Gml = af([128, NTM, 512])
              gr = [af([8, NMAX]) for _ in range(10)]
              mlnwbc = af([128, BR])
              DMA(mlnwbc, mlnw_d[ly].partition_broadcast(128), (), ["mlnwbc"])
              Wt = af([128, 512])
              Wv = af([128, 512])
              hn = af([128, 512])
              ytm = af([128, 512])
              Cs = af([128, NSQ, 129])
              ntm = af([NSQ, 512])
              ntmo = af([NSQ, 512])
              qT = ab([128, 4, NMAX])
              kT = ab([128, 4, NMAX])
              ktm = ab([128, NTM, 512])
              vaug = ab([128, NTM, 4, 129])
              SwT = ab([128, 512])
              qv = ab([128, 512])
              kw = ab([128, 512])
              qvm = ab([128, NSQ, LS])
              Csb = ab([128, NSQ, 129])
              vm = ab([128, 3, 129])
              mixs = ab([128, 4, NMAX])
              junk = ab([128, 128])
              hsq = af([128, 512])
              MEMSET("pool", vaug, 1.0, ["vaug"])
              CP("act", Cbf[:], Caug[:, ly], ["Caug"], ["Cbf"])
              for cg in range(4):
                  wt1, w1 = load_w(wpk_d[ly, 12 + cg])
                  for t in tiles:
                      L, ti = t["L"], t["ti"]

                      def ev_g(ps, pid, L=L, ti=ti, cg=cg):
                          CP("act", ytm[:L, 0:256], ps[:L, 0:256], [pid], ["ytm"])
                          ACT(hn[:L, 0:256], ytm[:L, 0:256], AF.Exp, ["ytm"], ["hn"], scale=-1.0)
                          ACT(hn[:L, 0:256], hn[:L, 0:256], AF.Ln, ["hn"], ["hn"], bias=1.0)
                          TT("dve", hn[:L, 0:128], hn[:L, 0:128], hn[:L, 128:256], ALU.add, ["hn"], ["hn"])
                          ACT(hn[:L, 0:128], hn[:L, 0:128], AF.Exp, ["hn"], ["hn"], scale=-1.0)
                          TT("dve", hn[:L, 0:128], hn[:L, 0:128], ytm[:L, 128:256], ALU.mult, ["hn", "ytm"], ["hn"])
                          TT("dve", Gml[:L, ti, cg * 128:(cg + 1) * 128], hn[:L, 0:128], mlnwbc[:L, cg * 128:(cg + 1) * 128],
                             ALU.mult, ["hn", "mlnwbc"], ["Gml"])
                      proj_tm(wt1, w1, 0, 256, t, ev_g)
              for hp in range(2):
                  wt1, w1 = load_w(wpk_d[ly, 16 + hp])
                  for hh in range(2):
                      h = hp * 2 + hh
                      proj_fm(wt1, w1, hh * 128, 128,
                              lambda ps, pid, g0, n, h=h: CP("act", qT[:, h, g0:g0 + n], ps[:, :n], [pid], ["qT"]))
              for hp in range(2):
                  wt1, w1 = load_w(wpk_d[ly, 18 + hp])
                  for hh in range(2):
                      h = hp * 2 + hh
                      proj_fm(wt1, w1, hh * 128, 128,
                              lambda ps, pid, g0, n, h=h: ACT(kT[:, h, g0:g0 + n], ps[:, :n], AF.Copy, [pid], ["kT"],
                                                             scale=128.0 ** -0.5))
                  for t in tiles:
                      L, ti = t["L"], t["ti"]
                      proj_tm(wt1, w1, 0, 256, t,
                              lambda ps, pid, L=L, ti=ti, hp=hp: ACT(ktm[:L, ti, hp * 256:(hp + 1) * 256], ps[:L, :256],
                                                                    AF.Copy, [pid], ["ktm"], scale=128.0 ** -0.5))
              for hp in range(2):
                  wt1, w1 = load_w(wpk_d[ly, 20 + hp])
                  for t in tiles:
                      L, ti = t["L"], t["ti"]
                      proj_tm(wt1, w1, 0, 256, t,
                              lambda ps, pid, L=L, ti=ti, hp=hp: CP("dve", vaug[:L, ti, 2 * hp:2 * hp + 2, 0:128],
                                                                   ps[:L, :256].rearrange("p (h v) -> p h v", h=2),
                                                                   [pid], ["vaug"]))
              gi, gf, gb, gm, ga, gc, gcv, gwg, genm, gmp = gr
              glf = gf
              proj_fm(wsmb, "wsmb", 0, 8,
                      lambda ps, pid, g0, n: TS("dve", gi[:, g0:g0 + n], ps[:8, :n], g8[:, 0:1], None, ALU.add, None,
                                                [pid, "g8"], ["gr0"]))
              proj_fm(wsmb, "wsmb", 8, 8,
                      lambda ps, pid, g0, n: ACT(gf[:, g0:g0 + n], ps[:8, :n], AF.Exp, [pid, "g8n"], ["gr1"],
                                                 bias=g8n[:, 1:2], scale=-1.0))
              ACT(glf[:, :N], gf[:, :N], AF.Ln, ["gr1", "gr2"], ["gr1", "gr2"], bias=1.0)
              TS("dve", glf[:, :N], glf[:, :N], -1.0, None, ALU.mult, None, ["gr2"], ["gr2"])
              SCAN(gb[:, 0:NP], k8("keepP"), glf[:, 0:NP], 0.0, ALU.mult, ALU.add, ["c8", "gr2"], ["gr3"])
              SCAN(gm[:, 0:NP], glf[:, 0:NP], gi[:, 0:NP], mcar[:, ly:ly + 1], ALU.add, ALU.max, ["gr2", "gr0", "mcar"], ["gr4"])
              CP("dve", gmp[:, 1:NP], gm[:, 0:NP - 1], ["gr4"], ["gr10"])
              CP("dve", gmp[:, 0:1], mcar[:, ly:ly + 1], ["mcar"], ["gr10"])
              CP("dve", mcar[:, ly:ly + 1], gm[:, NP - 1:NP], ["gr4"], ["mcar"])
              m0 = small_o[:8, 0:NSQ]
              if has_s:
                  sl_ = slice(NP, NP + LS)
                  DMA(m0, sm_d[ly], (), ["small_o"])
                  SCAN(gb[:, sl_], k8("keepS"), glf[:, sl_], 0.0, ALU.mult, ALU.add, ["c8", "gr2"], ["gr3"])
                  TT("dve", gmp[:, sl_], glf[:, sl_], k8("keepS"), ALU.mult, ["gr2", "c8"], ["gr10"])
                  TT("dve", gmp[:, sl_], gmp[:, sl_], k8("negS"), ALU.add, ["gr10", "c8"], ["gr10"])
                  CP("dve", gwg[:, sl_], gi[:, sl_], ["gr0"], ["gr8"])
                  lf0 = glf[:, sl_].rearrange("p (b t) -> p b t", t=4)[:, :, 0]
                  i0 = gwg[:, sl_].rearrange("p (b t) -> p b t", t=4)[:, :, 0]
                  TT("dve", small_o[:8, 64:64 + NSQ], lf0, m0, ALU.add, ["gr2", "small_o"], ["small_o"])
                  TT("dve", i0, i0, small_o[:8, 64:64 + NSQ], ALU.max, ["gr8", "small_o"], ["gr8"])
                  SCAN(gm[:, sl_], gmp[:, sl_], gwg[:, sl_], 0.0, ALU.add, ALU.max, ["gr10", "gr8"], ["gr4"])
              TT("dve", gc[:, :N], gb[:, :N], gm[:, :N], ALU.subtract, ["gr3", "gr4"], ["gr6"])
              TT("dve", ga[:, :N], gi[:, :N], gb[:, :N], ALU.subtract, ["gr0", "gr3"], ["gr5"])
              mprev_p = gmp[:, 0:NP].rearrange("p (c t) -> p c t", t=128)[:, :, 0]
              TT("dve", gcv[:, 0:NP].rearrange("p (c t) -> p c t", t=128), gc[:, 0:NP].rearrange("p (c t) -> p c t", t=128),
                 mprev_p.unsqueeze(2).broadcast_to([8, NCH, 128]), ALU.add, ["gr6", "gr10"], ["gr7"])
              if has_s:
                  TT("dve", gcv[:, sl_].rearrange("p (b t) -> p b t", t=4), gc[:, sl_].rearrange("p (b t) -> p b t", t=4),
                     m0.unsqueeze(2).broadcast_to([8, NSQ, 4]), ALU.add, ["gr6", "small_o"], ["gr7"])
              for kind in (("p", "s") if has_s else ("p",)):
                  ncn = NCH if kind == "p" else NSQ
                  T_ = 128 if kind == "p" else 4
                  base = 0 if kind == "p" else NP
                  gcst = gsm[:, 0:ncn]
                  TT("dve", gcst, chunk_ends(gb, kind), chunk_ends(gm, kind), ALU.subtract, ["gr3", "gr4"], ["gsm"])
                  seg = slice(base, base + ncn * T_)
                  TT("dve", gwg[:, seg].rearrange("p (c t) -> p c t", t=T_), ga[:, seg].rearrange("p (c t) -> p c t", t=T_),
                     gcst.unsqueeze(2).broadcast_to([8, ncn, T_]), ALU.add, ["gr5", "gsm"], ["gr8"])
                  mpv = mprev_p if kind == "p" else m0
                  TT("dve", small_o[:8, 128:128 + ncn], gcst, mpv, ALU.add, ["gsm", "gr10", "small_o"], ["small_o"])
                  ACT(small_o[:8, 128:128 + ncn], small_o[:8, 128:128 + ncn], AF.Exp, ["small_o"], ["small_o"])
                  bcast_rows(small_o[:8, 128:128 + ncn], "small_o", ncn, 0 if kind == "p" else 1)
              ACT(gwg[:, :N], gwg[:, :N], AF.Exp, ["gr8"], ["gr8"])
              ACT(genm[:, :N], gm[:, :N], AF.Exp, ["gr4"], ["gr9"], scale=-1.0)
              TS("dve", ga[:, :N], ga[:, :N], pm8, opm8, ALU.mult, ALU.add, ["gr5", "c8"], ["gr5"])
              TS("dve", gc[:, :N], gc[:, :N], opm8, pm8, ALU.mult, ALU.add, ["gr6", "c8"], ["gr6"])
              TS("dve", gcv[:, :N], gcv[:, :N], opm8, pm8, ALU.mult, ALU.add, ["gr7", "c8"], ["gr7"])
              if has_s:
                  DMA(ntm[:], sn_d[ly], (), ["ntm"])
              if last_pass:
                  DMA(pm_o[ly].rearrange("(h o) -> h o", o=1), mcar[0:4, ly:ly + 1], ["mcar"], ())
              if has_s:
                  CP("dve", small_o[:8, 512:512 + NSQ], chunk_ends(gm, "s"), ["gr4"], ["small_o"])
                  DMA(om_o[ly], small_o[0:4, 512:512 + NSQ], ["small_o"], ())

              for t in tiles:
                  L, ti, col, kind = t["L"], t["ti"], t["col"], t["kind"]
                  to_tok_major(gwg, "gr8", genm, "gr9", t, 0, gmp, "gr10")
                  for h in range(4):
                      MM(PS["psS"][:L, h * L:(h + 1) * L], kT[:, h, col:col + L], qT[:, h, col:col + L], True, True,
                         ["kT", "qT"], ["psS"])
                  expo(ga, "gr5", gc, "gr6", gcv, "gr7", t, Wt, Wv)
                  TT("dve", SwT[:L, :4 * L], PS["psS"][:L, :4 * L], Wt[:L, :4 * L], ALU.mult, ["psS", "Wt"], ["SwT"])
                  TT("pool", qv[:, :4 * L].rearrange("p (h l) -> p h l", h=4), qT[:, :, col:col + L],
                     Wv[:, :4 * L].rearrange("p (h l) -> p h l", h=4), ALU.mult, ["qT", "Wv"], ["qv"])
                  TT("pool", kw[:L, :].rearrange("p (h d) -> p h d", h=4), ktm[:L, ti, :].rearrange("p (h d) -> p h d", h=4),
                     scal[:L, ti, 0, 0:4].unsqueeze(2).broadcast_to([L, 4, 128]), ALU.mult, ["ktm", "scal"], ["kw"])
                  for h in range(4):
                      pn = PS["psN0"] if h < 2 else PS["psN1"]
                      pnid = "psN0" if h < 2 else "psN1"
                      o_ = (h % 2) * 129
                      if kind == "s":
                          for b4 in range(0, NSQ, 4):
                              DMA(Cs[:, b4:b4 + 4, 0:128], sC_d[ly][b4:b4 + 4, h].rearrange("b d v -> d b v"), (), ["Cs"])
                          TR(PS["psU"][:, :NSQ], ntm[:NSQ, h * 128:(h + 1) * 128], identf[:NSQ, :NSQ], ["ntm", "c128"], ["psU"])
                          CP("dve", Cs[:, :, 128], PS["psU"][:, :NSQ], ["psU"], ["Cs"])
                          CP("act", Csb, Cs, ["Cs"], ["Csb"])
                          TT("pool", qvm, qv[:, h * L:(h + 1) * L].unsqueeze(1).broadcast_to([128, NSQ, L]),
                             smkb[:, :].rearrange("p (b l) -> p b l", b=NSQ), ALU.mult, ["qv", "smkb"], ["qvm"])
                      MM(pn[:L, o_:o_ + 129], SwT[:L, h * L:(h + 1) * L], vaug[:L, ti, h, :], True, False,
                         ["SwT", "vaug"], [pnid])
                      if kind == "p":
                          MM(pn[:L, o_:o_ + 129], qv[:, h * L:(h + 1) * L], Cbf[:, h, :], False, True, ["qv", "Cbf"], [pnid])
                      else:
                          for b in range(NSQ):
                              MM(pn[:L, o_:o_ + 129], qvm[:, b, :], Csb[:, b, :], False, b == NSQ - 1,
                                 ["qvm", "Csb"], [pnid])
                      kwh = kw[:L, h * 128:(h + 1) * 128]
                      if kind == "p":
                          psu, psuid = next_ps2()
                          MM(psu[:, 0:129], kwh, vaug[:L, ti, h, :], True, True, ["kw", "vaug"], [psuid])
                          STT("dve", Caug[:, ly, h, :], Caug[:, ly, h, :], bcw[:, 0, h, ti:ti + 1], psu[:, 0:129],
                              ALU.mult, ALU.add, ["Caug", "bcw", psuid], ["Caug"])
                      else:
                          b0 = 0
                          while b0 < NSQ:
                              nb = min(3, NSQ - b0)
                              TT("pool", vm[:L, 0:nb, :], vaug[:L, ti, h, :].unsqueeze(1).broadcast_to([L, nb, 129]),
                                 smT[:L, b0:b0 + nb].unsqueeze(2).broadcast_to([L, nb, 129]), ALU.mult, ["vaug", "c128"], ["vm"])
                              MM(PS["psU"][:, 0:nb * 129], kwh, vm[:L, 0:nb, :], True, True, ["kw", "vm"], ["psU"])
                              TT("dve", Cs[:, b0:b0 + nb, :], Cs[:, b0:b0 + nb, :],
                                 bcw[:, 1, h, b0:b0 + nb].unsqueeze(2).broadcast_to([128, nb, 129]), ALU.mult,
                                 ["Cs", "bcw"], ["Cs"])
                              TT("dve", Cs[:, b0:b0 + nb, :], Cs[:, b0:b0 + nb, :],
                                 PS["psU"][:, 0:nb * 129].rearrange("p (b v) -> p b v", b=nb), ALU.add, ["Cs", "psU"], ["Cs"])
                              b0 += nb
                          for b4 in range(0, NSQ, 4):
                              DMA(oC_o[ly][b4:b4 + 4, h].rearrange("b d v -> d b v"), Cs[:, b4:b4 + 4, 0:128], ["Cs"], ())
                          CP("dve", ncol[:, :NSQ], Cs[:, :, 128], ["Cs"], ["ncol"])
                          TR(PS["psU"][:NSQ, :128], ncol[:, :NSQ], identf, ["ncol", "c128"], ["psU"])
                          CP("dve", ntmo[:NSQ, h * 128:(h + 1) * 128], PS["psU"][:NSQ, :128], ["psU"], ["ntmo"])
                  if kind == "p":
                      CP("act", Cbf[:], Caug[:, ly], ["Caug"], ["Cbf"])
                  for half in range(2):
                      pn = PS["psN0"] if half == 0 else PS["psN1"]
                      pnid = "psN0" if half == 0 else "psN1"
                      den = pn[:L, 0:258].rearrange("p (h v) -> p h v", h=2)[:, :, 128]
                      ACT(tm4[:L, 2 * half:2 * half + 2], den, AF.Abs, [pnid], ["tm4"])
                  TT("dve", tm4[:L, 0:4], tm4[:L, 0:4], scal[:L, ti, 0, 4:8], ALU.max, ["tm4", "scal"], ["tm4"])
                  RECIP(tm4[:L, 0:4], tm4[:L, 0:4], ["tm4"], ["tm4"])
                  for half in range(2):
                      pn = PS["psN0"] if half == 0 else PS["psN1"]
                      pnid = "psN0" if half == 0 else "psN1"
                      CP("act", hn[:L, half * 256:(half + 1) * 256].rearrange("p (h v) -> p h v", h=2),
                         pn[:L, 0:258].rearrange("p (h v) -> p h v", h=2)[:, :, 0:128], [pnid], ["hn"])
                  TT("pool", hsq[:L, :], hn[:L, :], hn[:L, :], ALU.mult, ["hn"], ["hsq"])
                  RSUM(tm4[:L, 4:8], hn[:L, :].rearrange("p (h v) -> p h v", h=4), ["hn"], ["tm4"])
                  RSUM(tm4[:L, 8:12], hsq[:L, :].rearrange("p (h v) -> p h v", h=4), ["hsq"], ["tm4"])
                  TS("dve", tm4[:L, 4:8], tm4[:L, 4:8], 1.0 / 128, None, ALU.mult, None, ["tm4"], ["tm4"])
                  TS("dve", tm4[:L, 8:12], tm4[:L, 8:12], 1.0 / 128, None, ALU.mult, None, ["tm4"], ["tm4"])
                  TT("dve", tm4[:L, 12:16], tm4[:L, 4:8], tm4[:L, 4:8], ALU.mult, ["tm4"], ["tm4"])
                  TT("dve", tm4[:L, 8:12], tm4[:L, 8:12], tm4[:L, 12:16], ALU.subtract, ["tm4"], ["tm4"])
                  TT("dve", tm4[:L, 12:16], tm4[:L, 0:4], tm4[:L, 0:4], ALU.mult, ["tm4"], ["tm4"])
                  TT("dve", tm4[:L, 8:12], tm4[:L, 8:12], tm4[:L, 12:16], ALU.mult, ["tm4"], ["tm4"])
                  TS("dve", tm4[:L, 8:12], tm4[:L, 8:12], 0.0, None, ALU.max, None, ["tm4"], ["tm4"])
                  ACT(tm4[:L, 8:12], tm4[:L, 8:12], AF.Ln, ["tm4"], ["tm4"], bias=EPS)
                  ACT(tm4[:L, 8:12], tm4[:L, 8:12], AF.Exp, ["tm4"], ["tm4"], scale=-0.5)
                  TT("dve", tm4[:L, 8:12], tm4[:L, 8:12], tm4[:L, 0:4], ALU.mult, ["tm4"], ["tm4"])
                  hn3 = hn[:L, :].rearrange("p (h v) -> p h v", h=4)
                  TT("dve", hn3, hn3, tm4[:L, 4:8].unsqueeze(2).broadcast_to([L, 4, 128]), ALU.subtract, ["hn", "tm4"], ["hn"])
                  TT("dve", hn3, hn3, tm4[:L, 8:12].unsqueeze(2).broadcast_to([L, 4, 128]), ALU.mult, ["hn", "tm4"], ["hn"])
                  TT("pool", ytm[:L, :], hn[:L, :], Gml[:L, ti, :], ALU.mult, ["hn", "Gml"], ["ytm"])
                  for h in range(4):
                      TR(PS["psU"][:, h * 128:h * 128 + L], ytm[:L, h * 128:(h + 1) * 128], identf[:L, :L], ["ytm", "c128"], ["psU"])
                  CP("act", mixs[:, 0:4, col:col + L], PS["psU"][:, :].rearrange("p (j c) -> p j c", j=4)[:, :, :L],
                     ["psU"], ["mixs"])
              if has_s:
                  DMA(on_o[ly], ntmo[:], ["ntmo"], ())
              if last_pass:
                  DMA(pC_o[ly].rearrange("h d v -> d h v"), Caug[:, ly, :, 0:128], ["Caug"], ())
                  CP("dve", ncol[:, 0:4], Caug[:, ly, :, 128], ["Caug"], ["ncol"])
                  TR(PS["psU"][:4, :128], ncol[:, 0:4], identf, ["ncol", "c128"], ["psU"])
                  CP("dve", small_o[:4, 600:728], PS["psU"][:4, :128], ["psU"], ["small_o"])
                  DMA(pn_o[ly], small_o[:4, 600:728], ["small_o"], ())
              checkpoint()
              wout_part(mixs, 0)
              checkpoint()

              stage_reset()
              fa = [af([128, NMAX]) for _ in range(2)]
              xtm = af([128, NTM, 512])
              gzs = af([128, NTM, 512])
              gr = [af([8, NMAX]) for _ in range(6)]
              Wt = af([128, 512])
              Wv = af([128, 512])
              hn = af([128, 512])
              ytm = af([128, 256])
              Sin = af([64, NSQ, 128])
              cvin = af([NSQ * 3, 128])
              cvout = af([NSQ * 3, 128])
              ssdnwbc = af([128, BR])
              DMA(ssdnwbc, ssdnw_d[ly].partition_broadcast(128), (), ["ssdnwbc"])
              Btm = ab([128, NTM, 2, 128])
              BTb = ab([128, 2, NMAX])
              CTb = ab([128, 2, NMAX])
              Mh = ab([128, 512])
              Ct = ab([128, 512])
              Ctm = ab([128, NSQ, LS])
              xdt = ab([128, 256])
              xw = ab([128, 256])
              Mh2 = [Mh, ab([128, 512])]
              Ct2 = [Ct, ab([128, 512])]
              xdt2 = [xdt, ab([128, 256])]
              xw2 = [xw, ab([128, 256])]
              Ssb = ab([128, NSQ, 64])
              Bm = ab([128, NSQ, 128])
              mixs = ab([128, 4, NMAX])
              junk = ab([128, 256])
              CP("act", STb[:], STs[:, ly], ["STs"], ["STb"])
              for j in range(8):
                  if j % 2 == 0:
                      wt1, w1 = load_w(wpk_d[ly, 22 + j // 2])
                  conv_load_hist(8 + j)
                  if has_s:
                      sample_hist_tile(cvin, 4, j, sscv_d[ly])
                  proj_fm(wt1, w1, (j % 2) * 128, 128, lambda ps, pid, g0, n: fill_ub(ps, pid, g0, n))
                  conv_apply(fa[0], "fa0", 4, 4 * j, 32 + j)
                  conv_save_hist(8 + j)
                  if has_s:
                      sample_hist_out_tile(cvout, 4, j, oscv_o[ly])
                  ACT(fa[1][:, :N], fa[0][:, :N], AF.Exp, ["fa0"], ["fa1"], scale=-1.0)
                  sigmoid_from_exp(fa[1][:, :N], ["fa1"], ["fa1"])
                  TT("dve", fa[0][:, :N], fa[0][:, :N], fa[1][:, :N], ALU.mult, ["fa0", "fa1"], ["fa0"])
                  if j < 4:
                      for t in tiles:
                          L, ti, col = t["L"], t["ti"], t["col"]
                          pk = ("psU", "psV")[ti % 2]
                          TR(PS[pk][:L, 0:128], fa[0][:, col:col + L], identf, ["fa0", "c128"], [pk])
                          CP("act", xtm[:L, ti, j * 128:(j + 1) * 128], PS[pk][:L, 0:128], [pk], ["xtm"])
                  elif j < 6:
                      g = j - 4
                      CP("act", BTb[:, g, :N], fa[0][:, :N], ["fa0"], ["BTb"])
                      for t in tiles:
                          L, ti, col = t["L"], t["ti"], t["col"]
                          pk = ("psU", "psV")[ti % 2]
                          TR(PS[pk][:L, 0:128], fa[0][:, col:col + L], identf, ["fa0", "c128"], [pk])
                          CP("act", Btm[:L, ti, g, :], PS[pk][:L, 0:128], [pk], ["Btm"])
                  else:
                      g = j - 6
                      CP("act", CTb[:, g, :N], fa[0][:, :N], ["fa0"], ["CTb"])
              if last_pass:
                  prompt_hist_out(4, list(range(8, 16)), pscv_o[ly])
              for hp in range(2):
                  wt1, w1 = load_w(wpk_d[ly, 26 + hp])
                  for t in tiles:
                      L, ti = t["L"], t["ti"]

                      def ev_sz(ps, pid, L=L, ti=ti, hp=hp):
                          CP("act", ytm[:L, 0:256], ps[:L, 0:256], [pid], ["ytm"])
                          ACT(hn[:L, 0:256], ytm[:L, 0:256], AF.Exp, ["ytm"], ["hn"], scale=-1.0)
                          sigmoid_from_exp(hn[:L, 0:256], ["hn"], ["hn"])
                          TT("dve", gzs[:L, ti, hp * 256:(hp + 1) * 256], hn[:L, 0:256], ytm[:L, 0:256], ALU.mult,
                             ["hn", "ytm"], ["gzs"])
                      proj_tm(wt1, w1, 0, 256, t, ev_sz)
              for g in range(2):
                  gdt, gda, gacs, gwr, grr, gscr = gr
                  gna = gda
                  proj_fm(wsmb, "wsmb", 16 + 8 * g, 8,
                          lambda ps, pid, g0, n, g=g: ACT(gdt[:, g0:g0 + n], ps[:8, :n], AF.Exp, [pid, "g8"], ["gr0"],
                                                          bias=g8[:, 2 + g:3 + g]))
                  ACT(gdt[:, :N], gdt[:, :N], AF.Ln, ["gr0"], ["gr0"], bias=1.0)
                  TS("dve", gda[:, :N], gdt[:, :N], g8n[:, 4 + g:5 + g], None, ALU.mult, None, ["gr0", "g8n"], ["gr1", "gr3"])
                  SCAN(gacs[:, 0:NP], k8("keepP"), gda[:, 0:NP], 0.0, ALU.mult, ALU.add, ["c8", "gr1", "gr3"], ["gr2"])
                  if has_s:
                      SCAN(gacs[:, NP:NP + LS], k8("keepS"), gda[:, NP:NP + LS], 0.0, ALU.mult, ALU.add, ["c8", "gr1"], ["gr2"])
                  for kind in (("p", "s") if has_s else ("p",)):
                      ncn = NCH if kind == "p" else NSQ
                      T_ = 128 if kind == "p" else 4
                      base = 0 if kind == "p" else NP
                      seg = slice(base, base + ncn * T_)
                      al = gsm[:, 0:ncn]
                      CP("dve", al, chunk_ends(gacs, kind), ["gr2"], ["gsm"])
                      TT("dve", gwr[:, seg].rearrange("p (c t) -> p c t", t=T_), al.unsqueeze(2).broadcast_to([8, ncn, T_]),
                         gacs[:, seg].rearrange("p (c t) -> p c t", t=T_), ALU.subtract, ["gsm", "gr2"], ["gr4"])
                      ACT(small_o[:8, 128:128 + ncn], al, AF.Exp, ["gsm"], ["small_o"])
                      bcast_rows(small_o[:8, 128:128 + ncn], "small_o", ncn, 0 if kind == "p" else 1)
                  ACT(gwr[:, :N], gwr[:, :N], AF.Exp, ["gr4"], ["gr4"])
                  TT("dve", gwr[:, :N], gwr[:, :N], gdt[:, :N], ALU.mult, ["gr4", "gr0"], ["gr4"])
                  TS("dve", gna[:, :N], gacs[:, :N], -1.0, None, ALU.mult, None, ["gr2", "gr1"], ["gr3", "gr1"])
                  TS("dve", gna[:, :N], gna[:, :N], pm8, opm8, ALU.mult, ALU.add, ["gr3", "c8"], ["gr3"])
                  TS("dve", grr[:, :N], gacs[:, :N], opm8, pm8, ALU.mult, ALU.add, ["gr2", "c8"], ["gr5"])
                  def sd_front(t, g=g):
                      L, ti, col, kind = t["L"], t["ti"], t["col"], t["kind"]
                      pb = t["ti"] % 2
                      Mh, Ct, xdt, xw = Mh2[pb], Ct2[pb], xdt2[pb], xw2[pb]
                      idM, idC, idX, idW = "Mh%d" % pb, "Ct%d" % pb, "xdt%d" % pb, "xw%d" % pb
                      xg = xtm[:L, ti, g * 256:(g + 1) * 256].rearrange("p (h c) -> p h c", h=4)
                      to_tok_major(gdt, "gr0", gwr, "gr4", t, 1, gscr, "gr6")
                      MM(PS["psS"][:L, :L], BTb[:, g, col:col + L], CTb[:, g, col:col + L], True, True, ["BTb", "CTb"], ["psS"])
                      expo(gna, "gr3", grr, "gr5", grr, "gr5", t, Wt, Wv)
                      TT("dve", Mh[:L, :4 * L].rearrange("p (h l) -> p h l", h=4),
                         PS["psS"][:L, :L].unsqueeze(1).broadcast_to([L, 4, L]),
                         Wt[:L, :4 * L].rearrange("p (h l) -> p h l", h=4), ALU.mult, ["psS", "Wt"], [idM])
                      TT("pool", Ct[:, :4 * L].rearrange("p (h l) -> p h l", h=4),
                         CTb[:, g, col:col + L].unsqueeze(1).broadcast_to([128, 4, L]),
                         Wv[:, :4 * L].rearrange("p (h l) -> p h l", h=4), ALU.mult, ["CTb", "Wv"], [idC])
                      xg = xtm[:L, ti, g * 256:(g + 1) * 256].rearrange("p (h c) -> p h c", h=4)
                      TT("dve", xdt[:L, :].rearrange("p (h c) -> p h c", h=4), xg,
                         scal[:L, ti, 1, 0:4].unsqueeze(2).broadcast_to([L, 4, 64]), ALU.mult, ["xtm", "scal"], [idX])
                      TT("dve", xw[:L, :].rearrange("p (h c) -> p h c", h=4), xg,
                         scal[:L, ti, 1, 4:8].unsqueeze(2).broadcast_to([L, 4, 64]), ALU.mult, ["xtm", "scal"], [idW])
                      if kind == "s":
                          TT("pool", Bm[:L, :, :], Btm[:L, ti, g, :].unsqueeze(1).broadcast_to([L, NSQ, 128]),
                             smT[:L, :].unsqueeze(2).broadcast_to([L, NSQ, 128]), ALU.mult, ["Btm", "c128"], ["Bm"])

                  def sd_rest(t, g=g):
                      L, ti, col, kind = t["L"], t["ti"], t["col"], t["kind"]
                      pb = t["ti"] % 2
                      Mh, Ct, xdt, xw = Mh2[pb], Ct2[pb], xdt2[pb], xw2[pb]
                      idM, idC, idX, idW = "Mh%d" % pb, "Ct%d" % pb, "xdt%d" % pb, "xw%d" % pb
                      xg = xtm[:L, ti, g * 256:(g + 1) * 256].rearrange("p (h c) -> p h c", h=4)
                      for h in range(4):
                          hh = g * 4 + h
                          if kind == "s":
                              TT("pool", Ctm, Ct[:, h * L:(h + 1) * L].unsqueeze(1).broadcast_to([128, NSQ, L]),
                                 smkb[:, :].rearrange("p (b l) -> p b l", b=NSQ), ALU.mult, [idC, "smkb"], ["Ctm"])
                              for b4 in range(0, NSQ, 8):
                                  DMA(Sin[:, b4:b4 + 8, :], sS_d[ly][b4:b4 + 8, hh].rearrange("b p n -> p b n"), (), ["Sin"])
                              for bq in range(NSQ // 8):
                                  for b in range(8):
                                      TR(PS["psU"][:, b * 64:(b + 1) * 64], Sin[:, bq * 8 + b, :], identf[:64, :64],
                                         ["Sin", "c128"], ["psU"])
                                  CP("act", Ssb[:, bq * 8:(bq + 1) * 8, :],
                                     PS["psU"][:, :].rearrange("p (b c) -> p b c", b=8), ["psU"], ["Ssb"])
                          MM(PS["psN0"][:L, h * 64:(h + 1) * 64], Mh[:L, h * L:(h + 1) * L], xdt[:L, h * 64:(h + 1) * 64],
                             True, False, [idM, idX], ["psN0"])
                          if kind == "p":
                              MM(PS["psN0"][:L, h * 64:(h + 1) * 64], Ct[:, h * L:(h + 1) * L],
                                 STb[:, hh * 64:(hh + 1) * 64], False, True, [idC, "STb"], ["psN0"])
                          else:
                              for b in range(NSQ):
                                  MM(PS["psN0"][:L, h * 64:(h + 1) * 64], Ctm[:, b, :], Ssb[:, b, :],
                                     False, b == NSQ - 1, ["Ctm", "Ssb"], ["psN0"])
                              for bq in range(NSQ // 4):
                                  MM(PS["psV"][:64, :512], xw[:L, h * 64:(h + 1) * 64],
                                     Bm[:L, bq * 4:(bq + 1) * 4, :], True, True, [idW, "Bm"], ["psV"])
                                  TT("dve", Sin[:, bq * 4:(bq + 1) * 4, :], Sin[:, bq * 4:(bq + 1) * 4, :],
                                     bcw[:64, 1, h, bq * 4:(bq + 1) * 4].unsqueeze(2).broadcast_to([64, 4, 128]), ALU.mult,
                                     ["Sin", "bcw"], ["Sin"])
                                  TT("dve", Sin[:, bq * 4:(bq + 1) * 4, :], Sin[:, bq * 4:(bq + 1) * 4, :],
                                     PS["psV"][:64, :512].rearrange("p (b n) -> p b n", b=4), ALU.add, ["Sin", "psV"], ["Sin"])
                              for b4 in range(0, NSQ, 8):
                                  DMA(oS_o[ly][b4:b4 + 8, hh].rearrange("b p n -> p b n"), Sin[:, b4:b4 + 8, :], ["Sin"], ())
                      TT("dve", hn[:L, 0:256].rearrange("p (h c) -> p h c", h=4), xg,
                         Dbc[:L, g * 4:(g + 1) * 4].unsqueeze(2).broadcast_to([L, 4, 64]), ALU.mult, ["xtm", "Dbc"], ["hn"])
                      TT("dve", hn[:L, 0:256], hn[:L, 0:256], PS["psN0"][:L, 0:256], ALU.add, ["hn", "psN0"], ["hn"])
                      TT("dve", hn[:L, 0:256], hn[:L, 0:256], gzs[:L, ti, g * 256:(g + 1) * 256], ALU.mult, ["hn", "gzs"], ["hn"])
                      ACT(junk[:L, 0:256], hn[:L, 0:256], AF.Square, ["hn"], ["junk", "tm4"], accum=tm4[:L, 0:1])
                      ACT(tm4[:L, 1:2], tm4[:L, 0:1], AF.Ln, ["tm4"], ["tm4"], bias=EPS, scale=1.0 / 256)
                      ACT(tm4[:L, 2:3], tm4[:L, 1:2], AF.Exp, ["tm4"], ["tm4"], scale=-0.5)
                      STT("dve", ytm[:L, 0:256], hn[:L, 0:256], tm4[:L, 2:3], ssdnwbc[:L, g * 256:(g + 1) * 256],
                          ALU.mult, ALU.mult, ["hn", "tm4", "ssdnwbc"], ["ytm"])
                      for h2 in range(2):
                          TR(PS["psU"][:, h2 * 128:h2 * 128 + L], ytm[:L, h2 * 128:(h2 + 1) * 128], identf[:L, :L],
                             ["ytm", "c128"], ["psU"])
                      CP("act", mixs[:, 2 * g:2 * g + 2, col:col + L],
                         PS["psU"][:, 0:256].rearrange("p (j c) -> p j c", j=2)[:, :, :L], ["psU"], ["mixs"])
                      if kind == "p":
                          psu, psuid = next_ps2()
                          MM(psu[:, 0:256], Btm[:L, ti, g, :], xw[:L, :], True, True, ["Btm", idW], [psuid])
                          sg = STs[:, ly, g * 256:(g + 1) * 256]
                          TT("pool", sg.rearrange("p (h c) -> p h c", h=4), sg.rearrange("p (h c) -> p h c", h=4),
                             bcw[:, 0, :, ti:ti + 1].broadcast_to([128, 4, 64]), ALU.mult, ["STs", "bcw"], ["STs"])
                          TT("dve", sg, sg, psu[:, 0:256], ALU.add, ["STs", psuid], ["STs"])
                          CP("act", STb[:, g * 256:(g + 1) * 256], sg, ["STs"], ["STb"])

                  sd_front(tiles[0])
                  for i_, t in enumerate(tiles):
                      if i_ + 1 < len(tiles):
                          sd_front(tiles[i_ + 1])
                      sd_rest(t)
              if last_pass:
                  for q4 in range(4):
                      TR(PS["psU"][:, q4 * 128:(q4 + 1) * 128], STs[:, ly, q4 * 128:(q4 + 1) * 128], identf, ["STs", "c128"], ["psU"])
                  CP("dve", hn[:, :], PS["psU"][:, :], ["psU"], ["hn"])
                  DMA(pS_o[ly].rearrange("(q p) n -> p q n", p=128), hn[:, :].rearrange("p (q n) -> p q n", q=4), ["hn"], ())
              checkpoint()
              wout_part(mixs, 4)
              checkpoint()

              if ly == L2 - 1:
                  stage_reset()
                  xn32s = [af([128, D]), af([128, D])]
                  nwbc = af([128, D])
                  junks = [ab([128, D]), ab([128, D])]
                  DMA(nwbc, fnw_d.partition_broadcast(128), (), ["nwbc"])
                  for t in tiles:
                      L, ti = t["L"], t["ti"]
                      xn32, xid = rmsnorm_tile(t, None, xn32s, junks, nwbc)
                      dst = yp_o[t["tok"]:t["tok"] + 128, :] if t["kind"] == "p" else ys_o
                      DMA(dst, xn32[:L], [xid], ())
    except _Stop:
        pass

    print("NOPS", len(R.ops), "arena max words f32/bf16", amax, flush=True)
    block = st.enter_context(nc.Block())
    R.emit(nc, block, st)
    st.close()
    return nc


_CACHE = {}


def _host_layout(inputs, NSQ, c):
    f = np.ascontiguousarray
    b0 = c * NSQ
    L2 = DEPTH
    w_in = inputs["w_in"]
    m = {}
    m["xp"] = f(inputs["x_prompt"][c % inputs["x_prompt"].shape[0]])
    m["xs"] = f(inputs["x_sample"][b0:b0 + NSQ].reshape(NSQ * 4, D))
    m["sC"] = f(inputs["state_mlstm_C"][:, b0:b0 + NSQ])
    m["sn"] = f(inputs["state_mlstm_n"][:, b0:b0 + NSQ].reshape(L2, NSQ, 512))
    smt = np.transpose(inputs["state_mlstm_m"][:, b0:b0 + NSQ], (0, 2, 1))
    m["sm"] = f(np.concatenate([smt, smt], axis=1))
    m["sS"] = f(inputs["state_ssd"][:, b0:b0 + NSQ])
    m["sscv"] = f(inputs["state_ssd_conv"][:, b0:b0 + NSQ].reshape(L2, NSQ * 3, 1024))
    m["sccv"] = f(inputs["state_sconv_conv"][:, b0:b0 + NSQ].reshape(L2, NSQ * 2, 512))
    m["srh"] = f(inputs["state_rglru_h"][:, b0:b0 + NSQ])
    m["srcv"] = f(inputs["state_rglru_conv"][:, b0:b0 + NSQ].reshape(L2, NSQ * 3, 512))
    return m


def _shared_layout(inputs, NCH, NSQ):
    f = np.ascontiguousarray
    L2 = DEPTH
    w_in = inputs["w_in"]
    s = {}
    def cols(o, n):
        return list(range(o, o + n))
    chunks = []
    for j in range(4):
        chunks.append(cols(OSB + j * 128, 128) + cols(OSC + j * 128, 128))
        chunks.append(cols(OSH + j * 128, 128) + cols(OSCZ + j * 128, 128))
    for j in range(4):
        chunks.append(cols(ORX + j * 128, 128) + cols(ORZ + j * 128, 128))
    for cg in range(4):
        chunks.append(cols(OO + cg * 128, 128) + cols(OZ + cg * 128, 128))
    for base in (OQ, OKK, OV):
        for hp in range(2):
            chunks.append(cols(base + hp * 256, 256))
    for jp in range(4):
        chunks.append(cols(OXBC + jp * 256, 256))
    for hp in range(2):
        chunks.append(cols(OSZ + hp * 256, 256))
    assert len(chunks) == 28

    def pack(W):
        K = W.shape[0] // 128
        return np.ascontiguousarray(W.reshape(K, 128, W.shape[1]).transpose(1, 0, 2).reshape(128, K * W.shape[1]))
    wpk = np.empty((L2, 28, 128, 4096), np.float32)
    for ly in range(L2):
        for k, cc in enumerate(chunks):
            wpk[ly, k] = pack(w_in[ly][:, cc])
    s["wpk"] = wpk
    wopk = np.empty((L2, 4, 8, 128, 1024), np.float32)
    for ly in range(L2):
        for g in range(4):
            for dg in range(8):
                wopk[ly, g, dg] = pack(inputs["w_out"][ly][g * 512:(g + 1) * 512, dg * 256:(dg + 1) * 256])
    s["wopk"] = wopk
    wi = w_in[:, :, OI:OI + 4]
    wf = w_in[:, :, OF:OF + 4]
    wd0 = w_in[:, :, ODT:ODT + 4]
    wd1 = w_in[:, :, ODT + 4:ODT + 8]
    wsm = np.concatenate([wi, wi, wf, wf, wd0, wd0, wd1, wd1], axis=2)
    s["wsm"] = np.stack([pack(wsm[ly]) for ly in range(L2)], axis=0)
    wabd = np.zeros((L2, 8, 128, 128), np.float32)
    for ly in range(L2):
        for j in range(4):
            for k in range(2):
                blk = 2 * j + k
                wabd[ly, j, k * 64:(k + 1) * 64, k * 64:(k + 1) * 64] = inputs["rg_wa"][ly, blk]
                wabd[ly, 4 + j, k * 64:(k + 1) * 64, k * 64:(k + 1) * 64] = inputs["rg_wx"][ly, blk]
    s["wabd"] = f(np.transpose(wabd, (0, 2, 1, 3)).reshape(L2, 128, 1024))
    cvec = np.zeros((L2, 128, 84), np.float32)

    def fm(v, nt):
        return v.reshape(nt, 128).T
    for ly in range(L2):
        cw = inputs["ssd_conv_w"][ly]
        cvec[ly, :, 0:32] = np.transpose(cw.reshape(4, 8, 128), (2, 1, 0)).reshape(128, 32)
        cvec[ly, :, 32:40] = fm(inputs["ssd_conv_b"][ly], 8)
        sw = inputs["sc_conv_w"][ly]
        cvec[ly, :, 40:52] = np.transpose(sw.reshape(3, 4, 128), (2, 1, 0)).reshape(128, 12)
        rw = inputs["rg_conv_w"][ly]
        cvec[ly, :, 52:68] = np.transpose(rw.reshape(4, 4, 128), (2, 1, 0)).reshape(128, 16)
        cvec[ly, :, 68:72] = fm(inputs["rg_conv_b"][ly], 4)
        cvec[ly, :, 72:76] = fm(inputs["rg_ba"][ly], 4)
        cvec[ly, :, 76:80] = fm(inputs["rg_bx"][ly], 4)
        cvec[ly, :, 80:84] = fm(inputs["rg_lambda"][ly], 4)
    s["cvec"] = cvec
    g8 = np.zeros((L2, 8, 8), np.float32)
    for ly in range(L2):
        d2 = lambda v: np.concatenate([v, v])
        g8[ly, :, 0] = d2(inputs["ml_i_bias"][ly])
        g8[ly, :, 1] = d2(inputs["ml_f_bias"][ly])
        g8[ly, :, 2] = d2(inputs["ssd_dt_bias"][ly][0:4])
        g8[ly, :, 3] = d2(inputs["ssd_dt_bias"][ly][4:8])
        g8[ly, :, 4] = d2(inputs["ssd_A_log"][ly][0:4])
        g8[ly, :, 5] = d2(inputs["ssd_A_log"][ly][4:8])
    s["g8"] = g8
    for k in ("norm_w", "ml_norm_w", "ssd_norm_w", "ssd_D", "final_norm_w"):
        s[k] = f(inputs[k])
    C128h, _, C8h, _ = host_consts(NCH, NSQ)
    s["c128"] = C128h
    s["c8"] = C8h
    return s


def run(inputs, NCH, n_cores=8):
    inputs = {k: np.asarray(v, dtype=np.float32) for k, v in inputs.items()}
    BP, TP = inputs["x_prompt"].shape[0], inputs["x_prompt"].shape[1]
    BS = inputs["x_sample"].shape[0]
    NSQ = BS // n_cores
    key = (TP, NCH, NSQ)
    if key not in _CACHE:
        _CACHE[key] = build(TP, NCH, NSQ)
    nc = _CACHE[key]
    shared = _shared_layout(inputs, NCH, NSQ)
    in_maps = []
    for c in range(n_cores):
        m = _host_layout(inputs, NSQ, c)
        m.update(shared)
        in_maps.append(m)
    res = run_bass_kernel_spmd(nc, in_maps, core_ids=list(range(n_cores)))
    r = res.results
    L2 = DEPTH
    cat = lambda k, ax: np.concatenate([r[c][k] for c in range(n_cores)], axis=ax)
    stk = lambda k: np.stack([r[c][k] for c in range(BP)], axis=1)
    y_prompt = np.stack([r[c]["yp"] for c in range(BP)], axis=0)
    y_sample = cat("ys", 0).reshape(BS, 4, D)
    p_C = stk("pC")
    p_n = stk("pn")
    p_m = stk("pm")
    p_ssd = stk("pS").reshape(L2, BP, 8, 64, 128)
    p_ssd_conv = stk("pscv")
    p_sc_conv = stk("pccv")
    p_rg_h = stk("prh").reshape(L2, BP, 512)
    p_rg_conv = stk("prcv")
    s_C = cat("oC", 1)
    s_n = cat("on", 1).reshape(L2, BS, 4, 128)
    s_m = np.transpose(cat("om", 2), (0, 2, 1))
    s_ssd = cat("oS", 1)
    s_ssd_conv = cat("oscv", 1).reshape(L2, BS, 3, 1024)
    s_sc_conv = cat("occv", 1).reshape(L2, BS, 2, 512)
    s_rg_h = cat("orh", 1)
    s_rg_conv = cat("orcv", 1).reshape(L2, BS, 3, 512)
    outs = (y_prompt, y_sample, p_C, p_n, p_m, p_ssd, p_ssd_conv, p_sc_conv, p_rg_h, p_rg_conv,
            s_C, s_n, s_m, s_ssd, s_ssd_conv, s_sc_conv, s_rg_h, s_rg_conv)
    return tuple(np.ascontiguousarray(o, dtype=np.float32) for o in outs)


def kernel(**inputs):
    return run(inputs, NCH=4)
```

```python
from contextlib import ExitStack

import numpy as np

import concourse.bass as bass
import concourse.mybir as mybir
from concourse.bass_utils import run_bass_kernel_spmd

F32 = mybir.dt.float32
BF16 = mybir.dt.bfloat16
ALU = mybir.AluOpType
AF = mybir.ActivationFunctionType

D = 2048
BR = 512
PO = 7184
DEPTH = 2
EPS = 1e-6
NEG = -30000.0
OQ, OKK, OV, OO, OZ, OI, OF = 0, 512, 1024, 1536, 2048, 2560, 2564
OSZ, OXBC, ODT = 2568, 3080, 4104
OSB, OSC, OSH, OSCZ = 4112, 4624, 5136, 5648
ORX, ORZ = 6160, 6672

ENGS = ("pe", "act", "dve", "pool", "sp")
NDMASEM = 8


class Op:
    __slots__ = ("eng", "fn", "reads", "writes", "dma", "deps", "sig", "tick", "dsem", "dcount", "idx", "prev")

    def __init__(self, eng, fn, reads, writes, dma):
        self.eng, self.fn, self.reads, self.writes, self.dma = eng, fn, reads, writes, dma
        self.deps = []
        self.sig = False
        self.tick = 0
        self.dsem = -1
        self.dcount = 0
        self.prev = 0


class _Stop(Exception):
    pass


class Rec:
    def __init__(self):
        self.ops = []
        self.last_w = {}
        self.readers = {}
        import os as _os
        self.maxops = int(_os.environ.get("KOPS", "100000000"))
        self.fence_ops = []
        self.pending = set()
        self.fence_from = 0

    PERSIST = frozenset(("c128", "c8", "identb", "maskPb", "maskSb", "smkb", "xnT", "st4", "st4b", "tm4", "Dbc", "cvec", "g8",
                         "g8n", "nsp8", "nba", "wb0", "wb1", "wb2", "wsmb", "wabd", "Caug", "Cbf", "mcar", "STs", "STb", "cvcar",
                         "hcar", "ubp", "ubs", "gsm", "RM", "RMv", "scal", "bcw", "ncol", "cvtmp", "h0s", "small_o"))

    def touches_arena(self, op):
        for b in op.reads + op.writes:
            if b in self.PERSIST or b.startswith("ps") or b.startswith("xres"):
                continue
            return True
        return False

    def fence(self):
        last = {}
        dmas = []
        for op in self.ops[self.fence_from:]:
            if not self.touches_arena(op):
                continue
            if op.dma:
                dmas.append(op.idx)
            else:
                last[op.eng] = op.idx
        self.fence_ops = sorted(set(self.fence_ops) | set(last.values()) | set(dmas)) if self.pending else \
            sorted(set(last.values()) | set(dmas))
        self.pending = set(ENGS)
        self.fence_from = len(self.ops)

    def add(self, eng, fn, reads=(), writes=(), dma=False):
        if len(self.ops) >= self.maxops:
            return None
        writes = tuple(writes) + tuple(b for b in reads if isinstance(b, str) and b.startswith("ps") and b not in writes)
        op = Op(eng, fn, tuple(reads), tuple(writes), dma)
        op.idx = len(self.ops)
        deps = set()
        if eng in self.pending and self.touches_arena(op):
            deps |= set(self.fence_ops)
            self.pending.discard(eng)
        for b in op.reads:
            w = self.last_w.get(b)
            if w is not None:
                deps.add(w)
        for b in op.writes:
            w = self.last_w.get(b)
            if w is not None:
                deps.add(w)
            for r in self.readers.get(b, ()):
                deps.add(r)
        deps.discard(op.idx)
        op.deps = sorted(deps)
        for b in op.reads:
            self.readers.setdefault(b, []).append(op.idx)
        for b in op.writes:
            self.last_w[b] = op.idx
            self.readers[b] = []
        self.ops.append(op)
        return op

    def emit(self, nc, block, stack):
        ops = self.ops
        for op in ops:
            for d in op.deps:
                p = ops[d]
                if p.dma:
                    continue
                if p.eng == "pe" and op.eng == "pe" and not op.dma:
                    continue
                p.sig = True
        sem = {e: stack.enter_context(nc.semaphore("s_" + e)) for e in ("pe", "act", "dve", "pool")}
        dsem = {q: [stack.enter_context(nc.semaphore("d_%s%d" % (q, i))) for i in range(NDMASEM)]
                for q in ("sp", "pool")}
        cnt = {e: 0 for e in sem}
        dn = {q: 0 for q in dsem}
        dc = {q: [0] * NDMASEM for q in dsem}
        for op in ops:
            if op.dma:
                q = op.eng
                i = dn[q] % NDMASEM
                dn[q] += 1
                op.dsem = i
                op.prev = dc[q][i]
                dc[q][i] += 1
                op.dcount = dc[q][i]
            elif op.sig:
                cnt[op.eng] += 1
                op.tick = cnt[op.eng]
        per = {e: [o for o in ops if o.eng == e] for e in ENGS}

        def run(eng_name, e):
            waited = {}

            def need(key, s, v):
                if waited.get(key, 0) >= v:
                    return
                waited[key] = v
                e.wait_ge(s, v)

            for op in per[eng_name]:
                for d in op.deps:
                    p = ops[d]
                    if p.dma:
                        need(("d", p.eng, p.dsem), dsem[p.eng][p.dsem], 16 * p.dcount)
                    else:
                        if p.eng == "pe" and eng_name == "pe" and not op.dma:
                            continue
                        need(("c", p.eng), sem[p.eng], p.tick)
                if op.dma:
                    if op.prev > 0:
                        need(("d", op.eng, op.dsem), dsem[op.eng][op.dsem], 16 * op.prev)
                    op.fn(e).then_inc(dsem[op.eng][op.dsem], 16)
                else:
                    ins = op.fn(e)
                    if op.sig:
                        ins.then_inc(sem[op.eng], 1)
            if eng_name in dsem:
                for i in range(NDMASEM):
                    if dc[eng_name][i] > 0:
                        need(("d", eng_name, i), dsem[eng_name][i], 16 * dc[eng_name][i])

        @block.tensor
        def _(e):
            run("pe", e)

        @block.scalar
        def _(e):
            run("act", e)

        @block.vector
        def _(e):
            run("dve", e)

        @block.gpsimd
        def _(e):
            run("pool", e)

        @block.sync
        def _(e):
            run("sp", e)


def host_consts(NCH, NSQ):
    LS = 4 * NSQ
    NP = 128 * NCH
    c128 = {}
    c128["ident"] = np.eye(128, dtype=np.float32)
    s = np.arange(128)[:, None]
    l = np.arange(128)[None, :]
    mp = np.where(s <= l, 0.0, NEG).astype(np.float32)
    c128["maskP"] = np.tile(mp[:, None, :], (1, 4, 1)).reshape(128, 512)
    ms = np.full((128, 128), NEG, np.float32)
    sl = np.arange(LS)
    okm = (sl[:, None] // 4 == sl[None, :] // 4) & (sl[:, None] <= sl[None, :])
    ms[:LS, :LS] = np.where(okm, 0.0, NEG)
    c128["maskS"] = np.tile(ms[:, None, :LS], (1, 4, 1)).reshape(128, 4 * LS)
    smk = (np.arange(NSQ)[:, None] == (sl[None, :] // 4)).astype(np.float32)
    c128["smk"] = np.tile(smk.reshape(1, NSQ * LS), (128, 1))
    smT = np.zeros((128, NSQ), np.float32)
    smT[:LS] = smk.T
    c128["smT"] = smT
    C128 = np.concatenate([c128[k] for k in ("ident", "maskP", "maskS", "smk", "smT")], axis=1)
    offs128 = {}
    o = 0
    for k in ("ident", "maskP", "maskS", "smk", "smT"):
        offs128[k] = (o, c128[k].shape[1])
        o += c128[k].shape[1]
    c8 = {}
    keepP = np.ones((8, NP), np.float32)
    keepP[:, ::128] = 0
    c8["keepP"] = keepP
    keepS = np.ones((8, LS), np.float32)
    keepS[:, ::4] = 0
    c8["keepS"] = keepS
    negS = np.zeros((8, LS), np.float32)
    negS[:, ::4] = -1e30
    c8["negS"] = negS
    pm = np.array([1, 1, 1, 1, 0, 0, 0, 0], np.float32)[:, None]
    c8["pm"] = pm
    c8["opm"] = 1 - pm
    mf = np.zeros((8, 4, 128), np.float32)
    for k in range(8):
        mf[k, k % 4, :] = 1
    c8["maskfull"] = mf.reshape(8, 512)
    m8 = np.zeros((8, 4), np.float32)
    for k in range(4):
        m8[k, k] = 1
    c8["mask8h"] = m8
    lv = np.zeros((8, 128), np.float32)
    lv[4:] = 1
    c8["LV"] = lv
    c8["ones8"] = np.ones((8, 128), np.float32)
    names8 = ("keepP", "keepS", "negS", "pm", "opm", "maskfull", "mask8h", "LV", "ones8")
    C8 = np.concatenate([c8[k] for k in names8], axis=1)
    offs8 = {}
    o = 0
    for k in names8:
        offs8[k] = (o, c8[k].shape[1])
        o += c8[k].shape[1]
    return C128, offs128, C8, offs8


def build(TP, NCH, NSQ):
    LS = 4 * NSQ
    NPASS = TP // (128 * NCH)
    assert NPASS * 128 * NCH == TP
    NTM = NCH + 1
    NMAX = 128 * NCH + LS
    NP = 128 * NCH
    L2 = DEPTH
    C128h, o128, C8h, o8 = host_consts(NCH, NSQ)
    nc = bass.Bass("TRN2", target_bir_lowering=False)

    def din(name, shape):
        return nc.dram_tensor(name, list(shape), F32, kind="ExternalInput").ap()

    def dout(name, shape):
        return nc.dram_tensor(name, list(shape), F32, kind="ExternalOutput").ap()

    xp_d = din("xp", [TP, D])
    xs_d = din("xs", [LS, D])
    sC_d = din("sC", [L2, NSQ, 4, 128, 128])
    sn_d = din("sn", [L2, NSQ, 512])
    sm_d = din("sm", [L2, 8, NSQ])
    sS_d = din("sS", [L2, NSQ, 8, 64, 128])
    sscv_d = din("sscv", [L2, NSQ * 3, 1024])
    sccv_d = din("sccv", [L2, NSQ * 2, 512])
    srh_d = din("srh", [L2, NSQ, 512])
    srcv_d = din("srcv", [L2, NSQ * 3, 512])
    wpk_d = din("wpk", [L2, 28, 128, 4096])
    wsm_d = din("wsm", [L2, 128, 512])
    wopk_d = din("wopk", [L2, 4, 8, 128, 1024])
    wabd_d = din("wabd", [L2, 128, 1024])
    cvec_d = din("cvec", [L2, 128, 84])
    g8_d = din("g8", [L2, 8, 8])
    normw_d = din("norm_w", [L2, D])
    mlnw_d = din("ml_norm_w", [L2, BR])
    ssdnw_d = din("ssd_norm_w", [L2, BR])
    ssdD_d = din("ssd_D", [L2, 8])
    fnw_d = din("final_norm_w", [D])
    c128_d = din("c128", list(C128h.shape))
    c8_d = din("c8", list(C8h.shape))

    yp_o = dout("yp", [TP, D])
    ys_o = dout("ys", [LS, D])
    pC_o = dout("pC", [L2, 4, 128, 128])
    pn_o = dout("pn", [L2, 4, 128])
    pm_o = dout("pm", [L2, 4])
    pS_o = dout("pS", [L2, 512, 128])
    pscv_o = dout("pscv", [L2, 3, 1024])
    pccv_o = dout("pccv", [L2, 2, 512])
    prh_o = dout("prh", [L2, 4, 128])
    prcv_o = dout("prcv", [L2, 3, 512])
    oC_o = dout("oC", [L2, NSQ, 4, 128, 128])
    on_o = dout("on", [L2, NSQ, 512])
    om_o = dout("om", [L2, 4, NSQ])
    oS_o = dout("oS", [L2, NSQ, 8, 64, 128])
    oscv_o = dout("oscv", [L2, NSQ * 3, 1024])
    occv_o = dout("occv", [L2, NSQ * 2, 512])
    orh_o = dout("orh", [L2, NSQ, 512])
    orcv_o = dout("orcv", [L2, NSQ * 3, 512])

    R = Rec()
    st = ExitStack()
    st.enter_context(nc.allow_non_contiguous_dma(reason="small strided state/constant transfers"))

    def sb(name, shape, dt=F32):
        return st.enter_context(nc.sbuf_tensor("sb_" + name, list(shape), dt))

    def psum(name):
        return st.enter_context(nc.psum_tensor(name, [128, 512], F32))

    def TT(eng, out, in0, in1, op, r, w):
        R.add(eng, lambda e: e.tensor_tensor(out=out, in0=in0, in1=in1, op=op), r, w)

    def TS(eng, out, in0, s1, s2, op0, op1, r, w):
        if op1 is None:
            R.add(eng, lambda e: e.tensor_scalar(out=out, in0=in0, scalar1=s1, scalar2=None, op0=op0), r, w)
        else:
            R.add(eng, lambda e: e.tensor_scalar(out=out, in0=in0, scalar1=s1, scalar2=s2, op0=op0, op1=op1), r, w)

    def STT(eng, out, in0, scalar, in1, op0, op1, r, w):
        R.add(eng, lambda e: e.scalar_tensor_tensor(out=out, in0=in0, scalar=scalar, in1=in1, op0=op0, op1=op1), r, w)

    def CP(eng, out, in_, r, w):
        if eng == "act":
            R.add(eng, lambda e: e.activation(out=out, in_=in_, func=AF.Copy), r, w)
        else:
            R.add(eng, lambda e: e.tensor_copy(out=out, in_=in_), r, w)

    def ACT(out, in_, func, r, w, bias=None, scale=1.0, accum=None):
        kw = {}
        if bias is not None:
            kw["bias"] = bias
        if accum is not None:
            kw["accum_out"] = accum
        R.add("act", lambda e: e.activation(out=out, in_=in_, func=func, scale=scale, **kw), r, w)

    def MM(out, lhsT, rhs, start, stop, r, w):
        R.add("pe", lambda e: e.matmul(out, lhsT=lhsT, rhs=rhs, start=start, stop=stop), r, w)

    def TR(out, in_, ident, r, w):
        R.add("pe", lambda e: e.transpose(out=out, in_=in_, identity=ident), r, w)

    def DMA(out, in_, r, w, q="sp"):
        R.add(q, lambda e: e.dma_start(out=out, in_=in_), r, w, dma=True)

    def SCAN(out, d0, d1, init, op0, op1, r, w):
        R.add("dve", lambda e: e.tensor_tensor_scan(out=out, data0=d0, data1=d1, initial=init, op0=op0, op1=op1), r, w)

    def MEMSET(eng, ap, val, w):
        R.add(eng, lambda e: e.memset(ap, val), (), w)

    def RSUM(out, in_, r, w):
        R.add("dve", lambda e: e.reduce_sum(out=out, in_=in_, axis=mybir.AxisListType.X), r, w)

    def RECIP(out, in_, r, w):
        R.add("dve", lambda e: e.reciprocal(out=out, in_=in_), r, w)

    def sigmoid_from_exp(t, r, w):
        ACT(t, t, AF.Ln, r, w, bias=1.0)
        ACT(t, t, AF.Exp, r, w, scale=-1.0)

    c128 = sb("c128", [128, 128 + NSQ])
    c8 = sb("c8", C8h.shape)
    identf = c128[:, 0:128]
    smT = c128[:, 128:128 + NSQ]

    def k8(name):
        o, n = o8[name]
        return c8[:, o:o + n]

    identb = sb("identb", [128, 128], BF16)
    maskPb = sb("maskPb", [128, 512], BF16)
    maskSb = sb("maskSb", [128, 4 * LS], BF16)
    smkb = sb("smkb", [128, NSQ * LS], BF16)
    pm8 = k8("pm")
    opm8 = k8("opm")
    xres = sb("xres", [128, NTM, D])
    xnT = sb("xnT", [128, 16, NMAX], BF16)
    st4 = sb("st4", [128, 8])
    tm4 = sb("tm4", [128, 16])
    Dbc = sb("Dbc", [128, 8])
    cvec = sb("cvec", [128, 84])
    g8 = sb("g8", [8, 8])
    g8n = sb("g8n", [8, 8])
    nsp8 = sb("nsp8", [128, 4])
    nba = sb("nba", [128, 8])
    NWB = 3
    wb = [sb("wb%d" % i, [128, 16, 256], BF16) for i in range(NWB)]
    wsmb = sb("wsmb", [128, 16, 32], BF16)
    wabd = sb("wabd", [128, 8, 128], BF16)
    PS = {k: psum(k) for k in ("psA", "psB", "psS", "psE", "psV", "psN0", "psN1", "psU")}
    Caug = sb("Caug", [128, L2, 4, 129])
    Cbf = sb("Cbf", [128, 4, 129], BF16)
    mcar = sb("mcar", [8, L2])
    STs = sb("STs", [128, L2, 512])
    STb = sb("STb", [128, 512], BF16)
    cvcar = sb("cvcar", [128, L2, 16, 3])
    hcar = sb("hcar", [128, L2, 4])
    ubp = sb("ubp", [128, 3 + NP])
    ubs = sb("ubs", [128, NSQ, 7])
    gsm = sb("gsm", [8, 64])
    RM = sb("RM", [8, 512])
    RMv = sb("RMv", [8, 512])
    scal = sb("scal", [128, NTM, 2, 8])
    bcw = sb("bcw", [128, 2, 4, 32])
    ncol = sb("ncol", [128, NSQ])
    cvtmp = sb("cvtmp", [128, NSQ * 3])
    h0s = sb("h0s", [128, NSQ])
    small_o = sb("small_o", [8, 768])
    AFW = 14480
    ABW = 17192
    arenaF = sb("arenaF", [128, AFW])
    arenaB = sb("arenaB", [128, ABW], BF16)
    aoff = [0, 0]
    amax = [0, 0]
    import os as _os
    print('SBUF remaining after persistent', nc.sbuf_bytes_remaining, flush=True)

    def stage_reset():
        R.fence()
        aoff[0] = 0
        aoff[1] = 0

    def carve(arena, k, cap, shape, pat):
        n = 1
        for d_ in shape[1:]:
            n *= d_
        o = aoff[k]
        aoff[k] += n
        amax[k] = max(amax[k], aoff[k])
        if _os.environ.get("KDRY"):
            o = 0
        else:
            assert aoff[k] <= cap, ("arena overflow", k, aoff[k], cap)
        v = arena[:shape[0], o:o + n]
        if len(shape) == 3:
            v = v.rearrange("p (a b) -> p a b", a=shape[1])
        elif len(shape) == 4:
            v = v.rearrange("p (a b c) -> p a b c", a=shape[1], b=shape[2])
        return v

    def af(shape):
        return carve(arenaF, 0, AFW, shape, None)

    def ab(shape):
        return carve(arenaB, 1, ABW, shape, None)

    DMA(c128[:, 0:128], c128_d[:, 0:128], (), ["c128"])
    o_, n_ = o128["smT"]
    DMA(c128[:, 128:128 + NSQ], c128_d[:, o_:o_ + n_], (), ["c128"])
    DMA(c8[:], c8_d, (), ["c8"])
    DMA(identb[:], c128_d[:, 0:128], (), ["identb"], q="pool")
    o_, n_ = o128["maskP"]
    DMA(maskPb[:], c128_d[:, o_:o_ + n_], (), ["maskPb"], q="pool")
    o_, n_ = o128["maskS"]
    DMA(maskSb[:], c128_d[:, o_:o_ + n_], (), ["maskSb"], q="pool")
    o_, n_ = o128["smk"]
    DMA(smkb[:], c128_d[:, o_:o_ + n_], (), ["smkb"], q="pool")
    MEMSET("pool", xnT[:], 0.0, ["xnT"])
    MEMSET("pool", Caug[:], 0.0, ["Caug"])
    MEMSET("pool", mcar[:], -1e30, ["mcar"])
    MEMSET("pool", STs[:], 0.0, ["STs"])
    MEMSET("pool", cvcar[:], 0.0, ["cvcar"])
    MEMSET("pool", hcar[:], 0.0, ["hcar"])
    MEMSET("pool", ubs[:], 0.0, ["ubs"])
    MEMSET("pool", ubp[:], 0.0, ["ubp"])
    MEMSET("pool", arenaF[:], 0.0, ["arenaF"])
    MEMSET("pool", arenaB[:], 0.0, ["arenaB"])
    stage_reset()

    wctr = [0]

    def load_w(src2d, nct=16):
        i = wctr[0] % NWB
        wctr[0] += 1
        DMA(wb[i][:, 0:nct, :].rearrange("p a b -> p (a b)"), src2d, (), ["wb%d" % i], q="pool")
        return wb[i], "wb%d" % i

    pctr = [0]

    def next_ps():
        k = ("psA", "psB", "psN1", "psN0", "psS", "psE", "psV")[pctr[0] % 7]
        pctr[0] += 1
        return PS[k], k

    def next_ps2():
        k = ("psA", "psB")[pctr[0] % 2]
        pctr[0] += 1
        return PS[k], k

    import os as _os
    KSTOP = int(_os.environ.get("KSTOP", "99"))
    kst = [0]

    def checkpoint():
        kst[0] += 1
        if kst[0] > KSTOP:
            raise _Stop()

    try:
      for pi in range(NPASS):
          tiles = [dict(kind="p", L=128, col=128 * i, ti=i, tok=pi * NP + 128 * i) for i in range(NCH)]
          last_pass = pi == NPASS - 1
          if last_pass:
              tiles.append(dict(kind="s", L=LS, col=NP, ti=NCH, tok=0))
          NT = len(tiles)
          N = (NP + LS) if last_pass else NP
          groups = []
          c0 = 0
          while c0 < N:
              groups.append((c0, min(512, N - c0)))
              c0 += 512
          has_s = last_pass

          for t in tiles:
              src = xp_d[t["tok"]:t["tok"] + 128, :] if t["kind"] == "p" else xs_d
              DMA(xres[:t["L"], t["ti"], :], src, (), ["xres%d" % t["ti"]])

          for ly in range(L2):
              DMA(Dbc[:], ssdD_d[ly].partition_broadcast(128), (), ["Dbc"])
              DMA(cvec[:], cvec_d[ly], (), ["cvec"])
              DMA(g8[:], g8_d[ly], (), ["g8"])
              DMA(wsmb[:, :, :].rearrange("p a b -> p (a b)"), wsm_d[ly], (), ["wsmb"], q="pool")
              DMA(wabd[:, :, :].rearrange("p a b -> p (a b)"), wabd_d[ly], (), ["wabd"], q="pool")
              TS("dve", g8n[:, 0:2], g8[:, 0:2], -1.0, None, ALU.mult, None, ["g8"], ["g8n"])
              ACT(g8n[:, 4:6], g8[:, 4:6], AF.Exp, ["g8"], ["g8n"])
              TS("dve", g8n[:, 4:6], g8n[:, 4:6], -1.0, None, ALU.mult, None, ["g8n"], ["g8n"])
              ACT(nsp8[:], cvec[:, 80:84], AF.Exp, ["cvec"], ["nsp8"], scale=-1.0)
              ACT(nsp8[:], nsp8[:], AF.Ln, ["nsp8"], ["nsp8"], bias=1.0)
              TS("dve", nsp8[:], nsp8[:], -8.0, None, ALU.mult, None, ["nsp8"], ["nsp8"])
              TS("dve", nba[:], cvec[:, 72:80], -1.0, None, ALU.mult, None, ["cvec"], ["nba"])

              def rmsnorm_tile(t, wsrc_d, xn32s, junks, nwbc):
                  L, ti = t["L"], t["ti"]
                  pb = ti % 2
                  xn32, junk = xn32s[pb], junks[pb]
                  xid, jid, sid = "xn32_%d" % pb, "junk_%d" % pb, ("st4", "st4b")[pb]
                  so = 4 * pb
                  xr_id = "xres%d" % ti
                  ACT(junk[:L], xres[:L, ti, :], AF.Square, [xr_id], [jid, sid], accum=st4[:L, so:so + 1])
                  ACT(st4[:L, so + 1:so + 2], st4[:L, so:so + 1], AF.Ln, [sid], [sid], bias=EPS, scale=1.0 / D)
                  ACT(st4[:L, so + 2:so + 3], st4[:L, so + 1:so + 2], AF.Exp, [sid], [sid], scale=-0.5)
                  STT("dve", xn32[:L], xres[:L, ti, :], st4[:L, so + 2:so + 3], nwbc[:L], ALU.mult, ALU.mult,
                      [xr_id, sid, "nwbc"], [xid])
                  return xn32, xid

              stage_reset()
              xn32s = [af([128, D]), af([128, D])]
              nwbc = af([128, D])
              junks = [ab([128, D]), ab([128, D])]
              DMA(nwbc, normw_d[ly].partition_broadcast(128), (), ["nwbc"])
              for t in tiles:
                  L, ti, col = t["L"], t["ti"], t["col"]
                  xn32, xid = rmsnorm_tile(t, None, xn32s, junks, nwbc)
                  for q4 in range(4):
                      pbk = ("psU", "psV")[q4 % 2]
                      for j in range(4):
                          dt_ = q4 * 4 + j
                          TR(PS[pbk][:, j * 128:j * 128 + L], xn32[:L, dt_ * 128:(dt_ + 1) * 128], identf[:L, :L],
                             [xid, "c128"], [pbk])
                      CP("act" if q4 % 2 else "dve", xnT[:, q4 * 4:q4 * 4 + 4, col:col + L],
                         PS[pbk][:, :].rearrange("p (j c) -> p j c", j=4)[:, :, :L], [pbk], ["xnT"])

              checkpoint()
              def proj_fm(wt, wid, wo, M, evac):
                  for (g0, n) in groups:
                      ps, pid = next_ps()
                      for dt_ in range(16):
                          MM(ps[:M, :n], wt[:, dt_, wo:wo + M], xnT[:, dt_, g0:g0 + n], dt_ == 0, dt_ == 15,
                             [wid, "xnT"], [pid])
                      evac(ps, pid, g0, n)

              def proj_tm(wt, wid, wo, n, t, evac):
                  ps, pid = next_ps()
                  L, col = t["L"], t["col"]
                  for dt_ in range(16):
                      MM(ps[:L, :n], xnT[:, dt_, col:col + L], wt[:, dt_, wo:wo + n], dt_ == 0, dt_ == 15,
                         [wid, "xnT"], [pid])
                  evac(ps, pid)

              def wout_part(mixs, ct0):
                  for dg in range(8):
                      wt1, w1 = load_w(wopk_d[ly, ct0 // 4, dg], nct=4)
                      for t in tiles:
                          L, ti, col = t["L"], t["ti"], t["col"]
                          ps, pid = next_ps()
                          for ct in range(4):
                              MM(ps[:L, :256], mixs[:, ct, col:col + L], wt1[:, ct, :], ct == 0, ct == 3, ["mixs", w1], [pid])
                          TT("dve", xres[:L, ti, dg * 256:(dg + 1) * 256], xres[:L, ti, dg * 256:(dg + 1) * 256],
                             ps[:L, :256], ALU.add, ["xres%d" % ti, pid], ["xres%d" % ti])

              def conv_load_hist(stream):
                  CP("dve", ubp[:, 0:3], cvcar[:, ly, stream, :], ["cvcar"], ["ubp"])

              def conv_save_hist(stream):
                  CP("dve", cvcar[:, ly, stream, :], ubp[:, NP:NP + 3], ["ubp"], ["cvcar"])

              def conv_apply(out_fm, oid, W, wcol, bcol):
                  s0 = 3 - (W - 1)
                  views = [(out_fm[:, 0:NP], lambda j: ubp[:, s0 + j:s0 + j + NP])]
                  if has_s:
                      views.append((out_fm[:, NP:NP + LS].rearrange("p (b t) -> p b t", t=4),
                                    lambda j: ubs[:, :, s0 + j:s0 + j + 4]))
                  for (o, src) in views:
                      if bcol is None:
                          TS("dve", o, src(0), cvec[:, wcol:wcol + 1], None, ALU.mult, None,
                             ["ubp", "ubs", "cvec"], [oid])
                      else:
                          TS("dve", o, src(0), cvec[:, wcol:wcol + 1], cvec[:, bcol:bcol + 1], ALU.mult, ALU.add,
                             ["ubp", "ubs", "cvec"], [oid])
                      for j in range(1, W):
                          STT("dve", o, src(j), cvec[:, wcol + j:wcol + j + 1], o, ALU.mult, ALU.add,
                              ["ubp", "ubs", "cvec", oid], [oid])

              def sample_hist_tile(cvin, W, ctile, src_d):
                  rows = NSQ * (W - 1)
                  DMA(cvin[:rows, :], src_d[:, ctile * 128:(ctile + 1) * 128], (), ["cvin"])
                  TR(PS["psU"][:, :rows], cvin[:rows, :], identf[:rows, :rows],
                     ["cvin", "c128"], ["psU"])
                  CP("dve", ubs[:, :, 3 - (W - 1):3], PS["psU"][:, :rows].rearrange("p (b r) -> p b r", r=W - 1),
                     ["psU"], ["ubs"])

              def sample_hist_out_tile(cvout, W, ctile, dst_d):
                  rows = NSQ * (W - 1)
                  CP("dve", cvtmp[:, :rows].rearrange("p (b r) -> p b r", r=W - 1), ubs[:, :, 7 - (W - 1):7],
                     ["ubs"], ["cvtmp"])
                  TR(PS["psU"][:rows, :128], cvtmp[:, :rows], identf, ["cvtmp", "c128"], ["psU"])
                  CP("dve", cvout[:rows, :], PS["psU"][:rows, :128], ["psU"], ["cvout"])
                  DMA(dst_d[:, ctile * 128:(ctile + 1) * 128], cvout[:rows, :], ["cvout"], ())

              def prompt_hist_out(W, streams, dst):
                  for c0 in range(0, len(streams), 4):
                      sub = streams[c0:c0 + 4]
                      for i, s_ in enumerate(sub):
                          CP("dve", cvtmp[:, 0:W - 1], cvcar[:, ly, s_, 3 - (W - 1):3], ["cvcar"], ["cvtmp"])
                          TR(PS["psU"][:W - 1, :128], cvtmp[:, 0:W - 1], identf, ["cvtmp", "c128"], ["psU"])
                          CP("dve", small_o[:W - 1, i * 128:(i + 1) * 128], PS["psU"][:W - 1, :128], ["psU"], ["small_o"])
                      DMA(dst[:, c0 * 128:(c0 + len(sub)) * 128], small_o[:W - 1, :128 * len(sub)], ["small_o"], ())

              def fill_ub(ps, pid, g0, n, eng="act"):
                  pe_ = min(g0 + n, NP)
                  if pe_ > g0:
                      CP(eng, ubp[:, 3 + g0:3 + pe_], ps[:, 0:pe_ - g0], [pid], ["ubp"])
                  if has_s and g0 + n > NP:
                      a0 = max(g0, NP)
                      CP(eng, ubs[:, :, 3:7], ps[:, a0 - g0:a0 - g0 + LS].rearrange("p (b t) -> p b t", t=4),
                         [pid], ["ubs"])

              stage_reset()
              fa = [af([128, NMAX]) for _ in range(6)]
              cvin = af([NSQ * 3, 128])
              cvout = af([NSQ * 3, 128])
              htm = af([NSQ, 512])
              htmo = af([NSQ, 512])
              mixs = ab([128, 4, NMAX])
              xrb = ab([128, NMAX])
              for j in range(4):
                  wt1, w1 = load_w(wpk_d[ly, 2 * j])
                  wt2, w2 = load_w(wpk_d[ly, 2 * j + 1])
                  conv_load_hist(j)
                  if has_s:
                      sample_hist_tile(cvin, 3, j, sccv_d[ly])
                  proj_fm(wt1, w1, 0, 128, lambda ps, pid, g0, n: CP("act", fa[0][:, g0:g0 + n], ps[:, :n], [pid], ["fa0"]))
                  proj_fm(wt1, w1, 128, 128, lambda ps, pid, g0, n: CP("act", fa[1][:, g0:g0 + n], ps[:, :n], [pid], ["fa1"]))

                  def ev_h(ps, pid, g0, n):
                      TT("dve", fa[1][:, g0:g0 + n], fa[1][:, g0:g0 + n], ps[:, :n], ALU.mult, ["fa1", pid], ["fa1"])
                  proj_fm(wt2, w2, 0, 128, ev_h)

                  def ev_z(ps, pid, g0, n):
                      CP("act", fa[2][:, g0:g0 + n], ps[:, :n], [pid], ["fa2"])
                      ACT(fa[3][:, g0:g0 + n], fa[2][:, g0:g0 + n], AF.Exp, ["fa2"], ["fa3"], scale=-1.0)
                  proj_fm(wt2, w2, 128, 128, ev_z)
                  CP("dve", ubp[:, 3:3 + NP], fa[1][:, 0:NP], ["fa1"], ["ubp"])
                  if has_s:
                      CP("dve", ubs[:, :, 3:7], fa[1][:, NP:NP + LS].rearrange("p (b t) -> p b t", t=4), ["fa1"], ["ubs"])
                  conv_apply(fa[4], "fa4", 3, 40 + 3 * j, None)
                  conv_save_hist(j)
                  if has_s:
                      sample_hist_out_tile(cvout, 3, j, occv_o[ly])
                  sigmoid_from_exp(fa[3][:, :N], ["fa3"], ["fa3"])
                  TT("dve", fa[2][:, :N], fa[2][:, :N], fa[3][:, :N], ALU.mult, ["fa2", "fa3"], ["fa2"])
                  TT("dve", fa[2][:, :N], fa[2][:, :N], fa[0][:, :N], ALU.mult, ["fa2", "fa0"], ["fa2"])
                  TT("dve", mixs[:, j, :N], fa[2][:, :N], fa[4][:, :N], ALU.mult, ["fa2", "fa4"], ["mixs"])
              if last_pass:
                  prompt_hist_out(3, [0, 1, 2, 3], pccv_o[ly])
              checkpoint()
              wout_part(mixs, 8)
              checkpoint()

              if has_s:
                  DMA(htm[:], srh_d[ly], (), ["htm"])
              for j in range(4):
                  wt1, w1 = load_w(wpk_d[ly, 8 + j])
                  conv_load_hist(4 + j)
                  if has_s:
                      sample_hist_tile(cvin, 4, j, srcv_d[ly])
                      TR(PS["psU"][:, :NSQ], htm[:NSQ, j * 128:(j + 1) * 128], identf[:NSQ, :NSQ], ["htm", "c128"], ["psU"])
                      CP("dve", h0s[:], PS["psU"][:, :NSQ], ["psU"], ["h0s"])
                  proj_fm(wt1, w1, 0, 128, lambda ps, pid, g0, n: fill_ub(ps, pid, g0, n))

                  def ev_rz(ps, pid, g0, n):
                      CP("act", fa[2][:, g0:g0 + n], ps[:, :n], [pid], ["fa2"])
                      ACT(fa[3][:, g0:g0 + n], fa[2][:, g0:g0 + n], AF.Exp, ["fa2"], ["fa3"], scale=-1.0)
                  proj_fm(wt1, w1, 128, 128, ev_rz)
                  conv_apply(fa[0], "fa0", 4, 52 + 4 * j, 68 + j)
                  conv_save_hist(4 + j)
                  if has_s:
                      sample_hist_out_tile(cvout, 4, j, orcv_o[ly])
                  CP("act", xrb[:, :N], fa[0][:, :N], ["fa0"], ["xrb"])
                  for (g0, n) in groups:
                      ps, pid = next_ps()
                      MM(ps[:, :n], wabd[:, j, :], xrb[:, g0:g0 + n], True, True, ["wabd", "xrb"], [pid])
                      ACT(fa[1][:, g0:g0 + n], ps[:, :n], AF.Exp, [pid, "nba"], ["fa1"], bias=nba[:, j:j + 1], scale=-1.0)
                      ps, pid = next_ps()
                      MM(ps[:, :n], wabd[:, 4 + j, :], xrb[:, g0:g0 + n], True, True, ["wabd", "xrb"], [pid])
                      ACT(fa[4][:, g0:g0 + n], ps[:, :n], AF.Exp, [pid, "nba"], ["fa4"], bias=nba[:, 4 + j:5 + j], scale=-1.0)
                  sigmoid_from_exp(fa[1][:, :N], ["fa1"], ["fa1"])
                  sigmoid_from_exp(fa[4][:, :N], ["fa4"], ["fa4"])
                  ACT(fa[1][:, :N], fa[1][:, :N], AF.Exp, ["fa1", "nsp8"], ["fa1"], scale=nsp8[:, j:j + 1])
                  TT("dve", fa[5][:, :N], fa[1][:, :N], fa[1][:, :N], ALU.mult, ["fa1"], ["fa5"])
                  TS("dve", fa[5][:, :N], fa[5][:, :N], -1.0, 1.0, ALU.mult, ALU.add, ["fa5"], ["fa5"])
                  TS("dve", fa[5][:, :N], fa[5][:, :N], 1e-30, None, ALU.max, None, ["fa5"], ["fa5"])
                  ACT(fa[5][:, :N], fa[5][:, :N], AF.Ln, ["fa5"], ["fa5"])
                  ACT(fa[5][:, :N], fa[5][:, :N], AF.Exp, ["fa5"], ["fa5"], scale=0.5)
                  TT("dve", fa[4][:, :N], fa[4][:, :N], fa[0][:, :N], ALU.mult, ["fa4", "fa0"], ["fa4"])
                  TT("dve", fa[4][:, :N], fa[4][:, :N], fa[5][:, :N], ALU.mult, ["fa4", "fa5"], ["fa4"])
                  STT("dve", fa[4][:, 0:1], fa[1][:, 0:1], hcar[:, ly, j:j + 1], fa[4][:, 0:1], ALU.mult, ALU.add,
                      ["fa1", "fa4", "hcar"], ["fa4"])
                  MEMSET("dve", fa[1][:, 0:1], 0.0, ["fa1"])
                  SCAN(fa[0][:, 0:NP], fa[1][:, 0:NP], fa[4][:, 0:NP], 0.0, ALU.mult, ALU.add, ["fa1", "fa4"], ["fa0"])
                  CP("dve", hcar[:, ly, j:j + 1], fa[0][:, NP - 1:NP], ["fa0"], ["hcar"])
                  if has_s:
                      a0v = fa[1][:, NP:NP + LS].rearrange("p (b t) -> p b t", t=4)[:, :, 0]
                      b0v = fa[4][:, NP:NP + LS].rearrange("p (b t) -> p b t", t=4)[:, :, 0]
                      TT("dve", tm4[:, :NSQ], a0v, h0s[:], ALU.mult, ["fa1", "h0s"], ["tm4"])
                      TT("dve", b0v, b0v, tm4[:, :NSQ], ALU.add, ["fa4", "tm4"], ["fa4"])
                      MEMSET("dve", a0v, 0.0, ["fa1"])
                      SCAN(fa[0][:, NP:NP + LS], fa[1][:, NP:NP + LS], fa[4][:, NP:NP + LS], 0.0, ALU.mult, ALU.add,
                           ["fa1", "fa4"], ["fa0"])
                      CP("dve", ncol[:, :NSQ], fa[0][:, NP:NP + LS].rearrange("p (b t) -> p b t", t=4)[:, :, 3], ["fa0"], ["ncol"])
                      TR(PS["psU"][:NSQ, :128], ncol[:, :NSQ], identf, ["ncol", "c128"], ["psU"])
                      CP("dve", htmo[:NSQ, j * 128:(j + 1) * 128], PS["psU"][:NSQ, :128], ["psU"], ["htmo"])
                  sigmoid_from_exp(fa[3][:, :N], ["fa3"], ["fa3"])
                  TT("dve", fa[2][:, :N], fa[2][:, :N], fa[3][:, :N], ALU.mult, ["fa2", "fa3"], ["fa2"])
                  TT("dve", mixs[:, j, :N], fa[2][:, :N], fa[0][:, :N], ALU.mult, ["fa2", "fa0"], ["mixs"])
              if has_s:
                  DMA(orh_o[ly], htmo[:], ["htmo"], ())
              if last_pass:
                  prompt_hist_out(4, [4, 5, 6, 7], prcv_o[ly])
                  for j in range(4):
                      TR(PS["psU"][:1, j * 128:(j + 1) * 128], hcar[:, ly, j:j + 1], identf, ["hcar", "c128"], ["psU"])
                  CP("dve", small_o[:1, 0:512], PS["psU"][:1, 0:512], ["psU"], ["small_o"])
                  DMA(prh_o[ly].rearrange("(o t) c -> o (t c)", o=1), small_o[:1, 0:512], ["small_o"], ())
              checkpoint()
              wout_part(mixs, 12)
              checkpoint()

              def chunk_ends(row, t_kind):
                  if t_kind == "p":
                      return row[:, 0:NP].rearrange("p (c t) -> p c t", t=128)[:, :, 127]
                  return row[:, NP:NP + LS].rearrange("p (b t) -> p b t", t=4)[:, :, 3]

              def expo(a_row, aid, c_row, cid, cv_row, cvid, t, Wt, Wv):
                  L, col = t["L"], t["col"]
                  mf = k8("maskfull").rearrange("p (h l) -> p h l", h=4)[:, :, :L]
                  RMl = RM[:, :4 * L].rearrange("p (h l) -> p h l", h=4)
                  RMvl = RMv[:, :4 * L].rearrange("p (h l) -> p h l", h=4)
                  TT("dve", RMl, c_row[:, col:col + L].unsqueeze(1).broadcast_to([8, 4, L]), mf, ALU.mult,
                     [cid, "c8"], ["RM"])
                  TT("dve", RMvl, cv_row[:, col:col + L].unsqueeze(1).broadcast_to([8, 4, L]), mf, ALU.mult,
                     [cvid, "c8"], ["RMv"])
                  msk = (maskPb if t["kind"] == "p" else maskSb)
                  MM(PS["psE"][:L, :4 * L], a_row[:, col:col + L], RM[:, :4 * L], True, False, [aid, "RM"], ["psE"])
                  MM(PS["psE"][:L, :4 * L], identb[:L, :L], msk[:L, :4 * L], False, True, ["identb", "maskPb", "maskSb"], ["psE"])
                  MM(PS["psV"][:, :4 * L], k8("LV"), RMv[:, :4 * L], True, True, ["c8", "RMv"], ["psV"])
                  ACT(Wt[:L, :4 * L], PS["psE"][:L, :4 * L], AF.Exp, ["psE"], ["Wt"])
                  ACT(Wv[:, :4 * L], PS["psV"][:, :4 * L], AF.Exp, ["psV"], ["Wv"])

              def bcast_rows(src_row, sid, ncols, slot):
                  w8 = gsm[:, :4 * ncols].rearrange("p (h c) -> p h c", h=4)
                  TT("dve", w8, src_row.unsqueeze(1).broadcast_to([8, 4, ncols]),
                     k8("mask8h").unsqueeze(2).broadcast_to([8, 4, ncols]), ALU.mult, [sid, "c8"], ["gsm"])
                  MM(PS["psU"][:, :4 * ncols], k8("ones8"), gsm[:, :4 * ncols], True, True, ["c8", "gsm"], ["psU"])
                  CP("dve", bcw[:, slot, :, :ncols], PS["psU"][:, :4 * ncols].rearrange("p (h c) -> p h c", h=4),
                     ["psU"], ["bcw"])

              def to_tok_major(rowA, aid, rowB, bid, t, slot, scr, scrid):
                  L, col, ti = t["L"], t["col"], t["ti"]
                  TS("dve", scr[:, :L], rowA[:, col:col + L], pm8, None, ALU.mult, None, [aid, "c8"], [scrid])
                  STT("dve", scr[:, :L], rowB[:, col:col + L], opm8, scr[:, :L], ALU.mult, ALU.add,
                      [bid, "c8", scrid], [scrid])
                  psu, psuid = next_ps2()
                  TR(psu[:L, :8], scr[:, :L], identf[:8, :8], [scrid, "c128"], [psuid])
                  CP("dve", scal[:L, ti, slot, :], psu[:L, :8], [psuid], ["scal"])

              stage_reset()
              Gml = af([128, NTM, 512])
              gr = [af([8, NMAX]) for _ in range(10)]
              mlnwbc = af([128, BR])
              DMA(mlnwbc, mlnw_d[ly].partition_broadcast(128), (), ["mlnwbc"])
              Wt = af([128, 512])
              Wv = af([128, 512])
              hn = af([128, 512])
              ytm = af([128, 512])
              Cs = af([128, NSQ, 129])
              ntm = af([NSQ, 512])
              ntmo = af([NSQ, 512])
              qT = ab([128, 4, NMAX])
              kT = ab([128, 4, NMAX])
              ktm = ab([128, NTM, 512])
              vaug = ab([128, NTM, 4, 129])
              SwT = ab([128, 512])
              qv = ab([128, 512])
              kw = ab([128, 512])
              qvm = ab([128, NSQ, LS])
              Csb = ab([128, NSQ, 129])
              vm = ab([128, 3, 129])
              mixs = ab([128, 4, NMAX])
              junk = ab([128, 128])
              hsq = af([128, 512])
              MEMSET("pool", vaug, 1.0, ["vaug"])
              CP("act", Cbf[:], Caug[:, ly], ["Caug"], ["Cbf"])
              for cg in range(4):
                  wt1, w1 = load_w(wpk_d[ly, 12 + cg])
                  for t in tiles:
                      L, ti = t["L"], t["ti"]

                      def ev_g(ps, pid, L=L, ti=ti, cg=cg):
                          CP("act", ytm[:L, 0:256], ps[:L, 0:256], [pid], ["ytm"])
                          ACT(hn[:L, 0:256], ytm[:L, 0:256], AF.Exp, ["ytm"], ["hn"], scale=-1.0)
                          ACT(hn[:L, 0:256], hn[:L, 0:256], AF.Ln, ["hn"], ["hn"], bias=1.0)
                          TT("dve", hn[:L, 0:128], hn[:L, 0:128], hn[:L, 128:256], ALU.add, ["hn"], ["hn"])
                          ACT(hn[:L, 0:128], hn[:L, 0:128], AF.Exp, ["hn"], ["hn"], scale=-1.0)
                          TT("dve", hn[:L, 0:128], hn[:L, 0:128], ytm[:L, 128:256], ALU.mult, ["hn", "ytm"], ["hn"])
                          TT("dve", Gml[:L, ti, cg * 128:(cg + 1) * 128], hn[:L, 0:128], mlnwbc[:L, cg * 128:(cg + 1) * 128],
                             ALU.mult, ["hn", "mlnwbc"], ["Gml"])
                      proj_tm(wt1, w1, 0, 256, t, ev_g)
              for hp in range(2):
                  wt1, w1 = load_w(wpk_d[ly, 16 + hp])
                  for hh in range(2):
                      h = hp * 2 + hh
                      proj_fm(wt1, w1, hh * 128, 128,
                              lambda ps, pid, g0, n, h=h: CP("act", qT[:, h, g0:g0 + n], ps[:, :n], [pid], ["qT"]))
              for hp in range(2):
                  wt1, w1 = load_w(wpk_d[ly, 18 + hp])
                  for hh in range(2):
                      h = hp * 2 + hh
                      proj_fm(wt1, w1, hh * 128, 128,
                              lambda ps, pid, g0, n, h=h: ACT(kT[:, h, g0:g0 + n], ps[:, :n], AF.Copy, [pid], ["kT"],
                                                             scale=128.0 ** -0.5))
                  for t in tiles:
                      L, ti = t["L"], t["ti"]
                      proj_tm(wt1, w1, 0, 256, t,
                              lambda ps, pid, L=L, ti=ti, hp=hp: ACT(ktm[:L, ti, hp * 256:(hp + 1) * 256], ps[:L, :256],
                                                                    AF.Copy, [pid], ["ktm"], scale=128.0 ** -0.5))
              for hp in range(2):
                  wt1, w1 = load_w(wpk_d[ly, 20 + hp])
                  for t in tiles:
                      L, ti = t["L"], t["ti"]
                      proj_tm(wt1, w1, 0, 256, t,
                              lambda ps, pid, L=L, ti=ti, hp=hp: CP("dve", vaug[:L, ti, 2 * hp:2 * hp + 2, 0:128],
                                                                   ps[:L, :256].rearrange("p (h v) -> p h v", h=2),
                                                                   [pid], ["vaug"]))
              gi, gf, gb, gm, ga, gc, gcv, gwg, genm, gmp = gr
              glf = gf
              proj_fm(wsmb, "wsmb", 0, 8,
                      lambda ps, pid, g0, n: TS("dve", gi[:, g0:g0 + n], ps[:8, :n], g8[:, 0:1], None, ALU.add, None,
                                                [pid, "g8"], ["gr0"]))
              proj_fm(wsmb, "wsmb", 8, 8,
                      lambda ps, pid, g0, n: ACT(gf[:, g0:g0 + n], ps[:8, :n], AF.Exp, [pid, "g8n"], ["gr1"],
                                                 bias=g8n[:, 1:2], scale=-1.0))
              ACT(glf[:, :N], gf[:, :N], AF.Ln, ["gr1", "gr2"], ["gr1", "gr2"], bias=1.0)
              TS("dve", glf[:, :N], glf[:, :N], -1.0, None, ALU.mult, None, ["gr2"], ["gr2"])
              SCAN(gb[:, 0:NP], k8("keepP"), glf[:, 0:NP], 0.0, ALU.mult, ALU.add, ["c8", "gr2"], ["gr3"])
              SCAN(gm[:, 0:NP], glf[:, 0:NP], gi[:, 0:NP], mcar[:, ly:ly + 1], ALU.add, ALU.max, ["gr2", "gr0", "mcar"], ["gr4"])
              CP("dve", gmp[:, 1:NP], gm[:, 0:NP - 1], ["gr4"], ["gr10"])
              CP("dve", gmp[:, 0:1], mcar[:, ly:ly + 1], ["mcar"], ["gr10"])
              CP("dve", mcar[:, ly:ly + 1], gm[:, NP - 1:NP], ["gr4"], ["mcar"])
              m0 = small_o[:8, 0:NSQ]
              if has_s:
                  sl_ = slice(NP, NP + LS)
                  DMA(m0, sm_d[ly], (), ["small_o"])
                  SCAN(gb[:, sl_], k8("keepS"), glf[:, sl_], 0.0, ALU.mult, ALU.add, ["c8", "gr2"], ["gr3"])
                  TT("dve", gmp[:, sl_], glf[:, sl_], k8("keepS"), ALU.mult, ["gr2", "c8"], ["gr10"])
                  TT("dve", gmp[:, sl_], gmp[:, sl_], k8("negS"), ALU.add, ["gr10", "c8"], ["gr10"])
                  CP("dve", gwg[:, sl_], gi[:, sl_], ["gr0"], ["gr8"])
                  lf0 = glf[:, sl_].rearrange("p (b t) -> p b t", t=4)[:, :, 0]
                  i0 = gwg[:, sl_].rearrange("p (b t) -> p b t", t=4)[:, :, 0]
                  TT("dve", small_o[:8, 64:64 + NSQ], lf0, m0, ALU.add, ["gr2", "small_o"], ["small_o"])
                  TT("dve", i0, i0, small_o[:8, 64:64 + NSQ], ALU.max, ["gr8", "small_o"], ["gr8"])
                  SCAN(gm[:, sl_], gmp[:, sl_], gwg[:, sl_], 0.0, ALU.add, ALU.max, ["gr10", "gr8"], ["gr4"])
              TT("dve", gc[:, :N], gb[:, :N], gm[:, :N], ALU.subtract, ["gr3", "gr4"], ["gr6"])
              TT("dve", ga[:, :N], gi[:, :N], gb[:, :N], ALU.subtract, ["gr0", "gr3"], ["gr5"])
              mprev_p = gmp[:, 0:NP].rearrange("p (c t) -> p c t", t=128)[:, :, 0]
              TT("dve", gcv[:, 0:NP].rearrange("p (c t) -> p c t", t=128), gc[:, 0:NP].rearrange("p (c t) -> p c t", t=128),
                 mprev_p.unsqueeze(2).broadcast_to([8, NCH, 128]), ALU.add, ["gr6", "gr10"], ["gr7"])
              if has_s:
                  TT("dve", gcv[:, sl_].rearrange("p (b t) -> p b t", t=4), gc[:, sl_].rearrange("p (b t) -> p b t", t=4),
                     m0.unsqueeze(2).broadcast_to([8, NSQ, 4]), ALU.add, ["gr6", "small_o"], ["gr7"])
              for kind in (("p", "s") if has_s else ("p",)):
                  ncn = NCH if kind == "p" else NSQ
                  T_ = 128 if kind == "p" else 4
                  base = 0 if kind == "p" else NP
                  gcst = gsm[:, 0:ncn]
                  TT("dve", gcst, chunk_ends(gb, kind), chunk_ends(gm, kind), ALU.subtract, ["gr3", "gr4"], ["gsm"])
                  seg = slice(base, base + ncn * T_)
                  TT("dve", gwg[:, seg].rearrange("p (c t) -> p c t", t=T_), ga[:, seg].rearrange("p (c t) -> p c t", t=T_),
                     gcst.unsqueeze(2).broadcast_to([8, ncn, T_]), ALU.add, ["gr5", "gsm"], ["gr8"])
                  mpv = mprev_p if kind == "p" else m0
                  TT("dve", small_o[:8, 128:128 + ncn], gcst, mpv, ALU.add, ["gsm", "gr10", "small_o"], ["small_o"])
                  ACT(small_o[:8, 128:128 + ncn], small_o[:8, 128:128 + ncn], AF.Exp, ["small_o"], ["small_o"])
                  bcast_rows(small_o[:8, 128:128 + ncn], "small_o", ncn, 0 if kind == "p" else 1)
              ACT(gwg[:, :N], gwg[:, :N], AF.Exp, ["gr8"], ["gr8"])
              ACT(genm[:, :N], gm[:, :N], AF.Exp, ["gr4"], ["gr9"], scale=-1.0)
              TS("dve", ga[:, :N], ga[:, :N], pm8, opm8, ALU.mult, ALU.add, ["gr5", "c8"], ["gr5"])
              TS("dve", gc[:, :N], gc[:, :N], opm8, pm8, ALU.mult, ALU.add, ["gr6", "c8"], ["gr6"])
              TS("dve", gcv[:, :N], gcv[:, :N], opm8, pm8, ALU.mult, ALU.add, ["gr7", "c8"], ["gr7"])
              if has_s:
                  DMA(ntm[:], sn_d[ly], (), ["ntm"])
              if last_pass:
                  DMA(pm_o[ly].rearrange("(h o) -> h o", o=1), mcar[0:4, ly:ly + 1], ["mcar"], ())
              if has_s:
                  CP("dve", small_o[:8, 512:512 + NSQ], chunk_ends(gm, "s"), ["gr4"], ["small_o"])
                  DMA(om_o[ly], small_o[0:4, 512:512 + NSQ], ["small_o"], ())

              for t in tiles:
                  L, ti, col, kind = t["L"], t["ti"], t["col"], t["kind"]
                  to_tok_major(gwg, "gr8", genm, "gr9", t, 0, gmp, "gr10")
                  for h in range(4):
                      MM(PS["psS"][:L, h * L:(h + 1) * L], kT[:, h, col:col + L], qT[:, h, col:col + L], True, True,
                         ["kT", "qT"], ["psS"])
                  expo(ga, "gr5", gc, "gr6", gcv, "gr7", t, Wt, Wv)
                  TT("dve", SwT[:L, :4 * L], PS["psS"][:L, :4 * L], Wt[:L, :4 * L], ALU.mult, ["psS", "Wt"], ["SwT"])
                  TT("pool", qv[:, :4 * L].rearrange("p (h l) -> p h l", h=4), qT[:, :, col:col + L],
                     Wv[:, :4 * L].rearrange("p (h l) -> p h l", h=4), ALU.mult, ["qT", "Wv"], ["qv"])
                  TT("pool", kw[:L, :].rearrange("p (h d) -> p h d", h=4), ktm[:L, ti, :].rearrange("p (h d) -> p h d", h=4),
                     scal[:L, ti, 0, 0:4].unsqueeze(2).broadcast_to([L, 4, 128]), ALU.mult, ["ktm", "scal"], ["kw"])
                  for h in range(4):
                      pn = PS["psN0"] if h < 2 else PS["psN1"]
                      pnid = "psN0" if h < 2 else "psN1"
                      o_ = (h % 2) * 129
                      if kind == "s":
                          for b4 in range(0, NSQ, 4):
                              DMA(Cs[:, b4:b4 + 4, 0:128], sC_d[ly][b4:b4 + 4, h].rearrange("b d v -> d b v"), (), ["Cs"])
                          TR(PS["psU"][:, :NSQ], ntm[:NSQ, h * 128:(h + 1) * 128], identf[:NSQ, :NSQ], ["ntm", "c128"], ["psU"])
                          CP("dve", Cs[:, :, 128], PS["psU"][:, :NSQ], ["psU"], ["Cs"])
                          CP("act", Csb, Cs, ["Cs"], ["Csb"])
                          TT("pool", qvm, qv[:, h * L:(h + 1) * L].unsqueeze(1).broadcast_to([128, NSQ, L]),
                             smkb[:, :].rearrange("p (b l) -> p b l", b=NSQ), ALU.mult, ["qv", "smkb"], ["qvm"])
                      MM(pn[:L, o_:o_ + 129], SwT[:L, h * L:(h + 1) * L], vaug[:L, ti, h, :], True, False,
                         ["SwT", "vaug"], [pnid])
                      if kind == "p":
                          MM(pn[:L, o_:o_ + 129], qv[:, h * L:(h + 1) * L], Cbf[:, h, :], False, True, ["qv", "Cbf"], [pnid])
                      else:
                          for b in range(NSQ):
                              MM(pn[:L, o_:o_ + 129], qvm[:, b, :], Csb[:, b, :], False, b == NSQ - 1,
                                 ["qvm", "Csb"], [pnid])
                      kwh = kw[:L, h * 128:(h + 1) * 128]
                      if kind == "p":
                          psu, psuid = next_ps2()
                          MM(psu[:, 0:129], kwh, vaug[:L, ti, h, :], True, True, ["kw", "vaug"], [psuid])
                          STT("dve", Caug[:, ly, h, :], Caug[:, ly, h, :], bcw[:, 0, h, ti:ti + 1], psu[:, 0:129],
                              ALU.mult, ALU.add, ["Caug", "bcw", psuid], ["Caug"])
                      else:
                          b0 = 0
                          while b0 < NSQ:
                              nb = min(3, NSQ - b0)
                              TT("pool", vm[:L, 0:nb, :], vaug[:L, ti, h, :].unsqueeze(1).broadcast_to([L, nb, 129]),
                                 smT[:L, b0:b0 + nb].unsqueeze(2).broadcast_to([L, nb, 129]), ALU.mult, ["vaug", "c128"], ["vm"])
                              MM(PS["psU"][:, 0:nb * 129], kwh, vm[:L, 0:nb, :], True, True, ["kw", "vm"], ["psU"])
                              TT("dve", Cs[:, b0:b0 + nb, :], Cs[:, b0:b0 + nb, :],
                                 bcw[:, 1, h, b0:b0 + nb].unsqueeze(2).broadcast_to([128, nb, 129]), ALU.mult,
                                 ["Cs", "bcw"], ["Cs"])
                              TT("dve", Cs[:, b0:b0 + nb, :], Cs[:, b0:b0 + nb, :],
                                 PS["psU"][:, 0:nb * 129].rearrange("p (b v) -> p b v", b=nb), ALU.add, ["Cs", "psU"], ["Cs"])
                              b0 += nb
                          for b4 in range(0, NSQ, 4):
                              DMA(oC_o[ly][b4:b4 + 4, h].rearrange("b d v -> d b v"), Cs[:, b4:b4 + 4, 0:128], ["Cs"], ())
                          CP("dve", ncol[:, :NSQ], Cs[:, :, 128], ["Cs"], ["ncol"])
                          TR(PS["psU"][:NSQ, :128], ncol[:, :NSQ], identf, ["ncol", "c128"], ["psU"])
                          CP("dve", ntmo[:NSQ, h * 128:(h + 1) * 128], PS["psU"][:NSQ, :128], ["psU"], ["ntmo"])
                  if kind == "p":
                      CP("act", Cbf[:], Caug[:, ly], ["Caug"], ["Cbf"])
                  for half in range(2):
                      pn = PS["psN0"] if half == 0 else PS["psN1"]
                      pnid = "psN0" if half == 0 else "psN1"
                      den = pn[:L, 0:258].rearrange("p (h v) -> p h v", h=2)[:, :, 128]
                      ACT(tm4[:L, 2 * half:2 * half + 2], den, AF.Abs, [pnid], ["tm4"])
                  TT("dve", tm4[:L, 0:4], tm4[:L, 0:4], scal[:L, ti, 0, 4:8], ALU.max, ["tm4", "scal"], ["tm4"])
                  RECIP(tm4[:L, 0:4], tm4[:L, 0:4], ["tm4"], ["tm4"])
                  for half in range(2):
                      pn = PS["psN0"] if half == 0 else PS["psN1"]
                      pnid = "psN0" if half == 0 else "psN1"
                      CP("act", hn[:L, half * 256:(half + 1) * 256].rearrange("p (h v) -> p h v", h=2),
                         pn[:L, 0:258].rearrange("p (h v) -> p h v", h=2)[:, :, 0:128], [pnid], ["hn"])
                  TT("pool", hsq[:L, :], hn[:L, :], hn[:L, :], ALU.mult, ["hn"], ["hsq"])
                  RSUM(tm4[:L, 4:8], hn[:L, :].rearrange("p (h v) -> p h v", h=4), ["hn"], ["tm4"])
                  RSUM(tm4[:L, 8:12], hsq[:L, :].rearrange("p (h v) -> p h v", h=4), ["hsq"], ["tm4"])
                  TS("dve", tm4[:L, 4:8], tm4[:L, 4:8], 1.0 / 128, None, ALU.mult, None, ["tm4"], ["tm4"])
                  TS("dve", tm4[:L, 8:12], tm4[:L, 8:12], 1.0 / 128, None, ALU.mult, None, ["tm4"], ["tm4"])
                  TT("dve", tm4[:L, 12:16], tm4[:L, 4:8], tm4[:L, 4:8], ALU.mult, ["tm4"], ["tm4"])
                  TT("dve", tm4[:L, 8:12], tm4[:L, 8:12], tm4[:L, 12:16], ALU.subtract, ["tm4"], ["tm4"])
                  TT("dve", tm4[:L, 12:16], tm4[:L, 0:4], tm4[:L, 0:4], ALU.mult, ["tm4"], ["tm4"])
                  TT("dve", tm4[:L, 8:12], tm4[:L, 8:12], tm4[:L, 12:16], ALU.mult, ["tm4"], ["tm4"])
                  TS("dve", tm4[:L, 8:12], tm4[:L, 8:12], 0.0, None, ALU.max, None, ["tm4"], ["tm4"])
                  ACT(tm4[:L, 8:12], tm4[:L, 8:12], AF.Ln, ["tm4"], ["tm4"], bias=EPS)
                  ACT(tm4[:L, 8:12], tm4[:L, 8:12], AF.Exp, ["tm4"], ["tm4"], scale=-0.5)
                  TT("dve", tm4[:L, 8:12], tm4[:L, 8:12], tm4[:L, 0:4], ALU.mult, ["tm4"], ["tm4"])
                  hn3 = hn[:L, :].rearrange("p (h v) -> p h v", h=4)
                  TT("dve", hn3, hn3, tm4[:L, 4:8].unsqueeze(2).broadcast_to([L, 4, 128]), ALU.subtract, ["hn", "tm4"], ["hn"])
                  TT("dve", hn3, hn3, tm4[:L, 8:12].unsqueeze(2).broadcast_to([L, 4, 128]), ALU.mult, ["hn", "tm4"], ["hn"])
                  TT("pool", ytm[:L, :], hn[:L, :], Gml[:L, ti, :], ALU.mult, ["hn", "Gml"], ["ytm"])
                  for h in range(4):
                      TR(PS["psU"][:, h * 128:h * 128 + L], ytm[:L, h * 128:(h + 1) * 128], identf[:L, :L], ["ytm", "c128"], ["psU"])
                  CP("act", mixs[:, 0:4, col:col + L], PS["psU"][:, :].rearrange("p (j c) -> p j c", j=4)[:, :, :L],
                     ["psU"], ["mixs"])
              if has_s:
                  DMA(on_o[ly], ntmo[:], ["ntmo"], ())
              if last_pass:
                  DMA(pC_o[ly].rearrange("h d v -> d h v"), Caug[:, ly, :, 0:128], ["Caug"], ())
                  CP("dve", ncol[:, 0:4], Caug[:, ly, :, 128], ["Caug"], ["ncol"])
                  TR(PS["psU"][:4, :128], ncol[:, 0:4], identf, ["ncol", "c128"], ["psU"])
                  CP("dve", small_o[:4, 600:728], PS["psU"][:4, :128], ["psU"], ["small_o"])
                  DMA(pn_o[ly], small_o[:4, 600:728], ["small_o"], ())
              checkpoint()
              wout_part(mixs, 0)
              checkpoint()

              stage_reset()
              fa = [af([128, NMAX]) for _ in range(2)]
              xtm = af([128, NTM, 512])
              gzs = af([128, NTM, 512])
              gr = [af([8, NMAX]) for _ in range(6)]
              Wt = af([128, 512])
              Wv = af([128, 512])
              hn = af([128, 512])
              ytm = af([128, 256])
              Sin = af([64, NSQ, 128])
              cvin = af([NSQ * 3, 128])
              cvout = af([NSQ * 3, 128])
              ssdnwbc = af([128, BR])
              DMA(ssdnwbc, ssdnw_d[ly].partition_broadcast(128), (), ["ssdnwbc"])
              Btm = ab([128, NTM, 2, 128])
              BTb = ab([128, 2, NMAX])
              CTb = ab([128, 2, NMAX])
              Mh = ab([128, 512])
              Ct = ab([128, 512])
              Ctm = ab([128, NSQ, LS])
              xdt = ab([128, 256])
              xw = ab([128, 256])
              Mh2 = [Mh, ab([128, 512])]
              Ct2 = [Ct, ab([128, 512])]
              xdt2 = [xdt, ab([128, 256])]
              xw2 = [xw, ab([128, 256])]
              Ssb = ab([128, NSQ, 64])
              Bm = ab([128, NSQ, 128])
              mixs = ab([128, 4, NMAX])
              junk = ab([128, 256])
              CP("act", STb[:], STs[:, ly], ["STs"], ["STb"])
              for j in range(8):
                  if j % 2 == 0:
                      wt1, w1 = load_w(wpk_d[ly, 22 + j // 2])
                  conv_load_hist(8 + j)
                  if has_s:
                      sample_hist_tile(cvin, 4, j, sscv_d[ly])
                  proj_fm(wt1, w1, (j % 2) * 128, 128, lambda ps, pid, g0, n: fill_ub(ps, pid, g0, n))
                  conv_apply(fa[0], "fa0", 4, 4 * j, 32 + j)
                  conv_save_hist(8 + j)
                  if has_s:
                      sample_hist_out_tile(cvout, 4, j, oscv_o[ly])
                  ACT(fa[1][:, :N], fa[0][:, :N], AF.Exp, ["fa0"], ["fa1"], scale=-1.0)
                  sigmoid_from_exp(fa[1][:, :N], ["fa1"], ["fa1"])
                  TT("dve", fa[0][:, :N], fa[0][:, :N], fa[1][:, :N], ALU.mult, ["fa0", "fa1"], ["fa0"])
                  if j < 4:
                      for t in tiles:
                          L, ti, col = t["L"], t["ti"], t["col"]
                          pk = ("psU", "psV")[ti % 2]
                          TR(PS[pk][:L, 0:128], fa[0][:, col:col + L], identf, ["fa0", "c128"], [pk])
                          CP("act", xtm[:L, ti, j * 128:(j + 1) * 128], PS[pk][:L, 0:128], [pk], ["xtm"])
                  elif j < 6:
                      g = j - 4
                      CP("act", BTb[:, g, :N], fa[0][:, :N], ["fa0"], ["BTb"])
                      for t in tiles:
                          L, ti, col = t["L"], t["ti"], t["col"]
                          pk = ("psU", "psV")[ti % 2]
                          TR(PS[pk][:L, 0:128], fa[0][:, col:col + L], identf, ["fa0", "c128"], [pk])
                          CP("act", Btm[:L, ti, g, :], PS[pk][:L, 0:128], [pk], ["Btm"])
                  else:
                      g = j - 6
                      CP("act", CTb[:, g, :N], fa[0][:, :N], ["fa0"], ["CTb"])
              if last_pass:
                  prompt_hist_out(4, list(range(8, 16)), pscv_o[ly])
              for hp in range(2):
                  wt1, w1 = load_w(wpk_d[ly, 26 + hp])
                  for t in tiles:
                      L, ti = t["L"], t["ti"]

                      def ev_sz(ps, pid, L=L, ti=ti, hp=hp):
                          CP("act", ytm[:L, 0:256], ps[:L, 0:256], [pid], ["ytm"])
                          ACT(hn[:L, 0:256], ytm[:L, 0:256], AF.Exp, ["ytm"], ["hn"], scale=-1.0)
                          sigmoid_from_exp(hn[:L, 0:256], ["hn"], ["hn"])
                          TT("dve", gzs[:L, ti, hp * 256:(hp + 1) * 256], hn[:L, 0:256], ytm[:L, 0:256], ALU.mult,
                             ["hn", "ytm"], ["gzs"])
                      proj_tm(wt1, w1, 0, 256, t, ev_sz)
              for g in range(2):
                  gdt, gda, gacs, gwr, grr, gscr = gr
                  gna = gda
                  proj_fm(wsmb, "wsmb", 16 + 8 * g, 8,
                          lambda ps, pid, g0, n, g=g: ACT(gdt[:, g0:g0 + n], ps[:8, :n], AF.Exp, [pid, "g8"], ["gr0"],
                                                          bias=g8[:, 2 + g:3 + g]))
                  ACT(gdt[:, :N], gdt[:, :N], AF.Ln, ["gr0"], ["gr0"], bias=1.0)
                  TS("dve", gda[:, :N], gdt[:, :N], g8n[:, 4 + g:5 + g], None, ALU.mult, None, ["gr0", "g8n"], ["gr1", "gr3"])
                  SCAN(gacs[:, 0:NP], k8("keepP"), gda[:, 0:NP], 0.0, ALU.mult, ALU.add, ["c8", "gr1", "gr3"], ["gr2"])
                  if has_s:
                      SCAN(gacs[:, NP:NP + LS], k8("keepS"), gda[:, NP:NP + LS], 0.0, ALU.mult, ALU.add, ["c8", "gr1"], ["gr2"])
                  for kind in (("p", "s") if has_s else ("p",)):
                      ncn = NCH if kind == "p" else NSQ
                      T_ = 128 if kind == "p" else 4
                      base = 0 if kind == "p" else NP
                      seg = slice(base, base + ncn * T_)
                      al = gsm[:, 0:ncn]
                      CP("dve", al, chunk_ends(gacs, kind), ["gr2"], ["gsm"])
                      TT("dve", gwr[:, seg].rearrange("p (c t) -> p c t", t=T_), al.unsqueeze(2).broadcast_to([8, ncn, T_]),
                         gacs[:, seg].rearrange("p (c t) -> p c t", t=T_), ALU.subtract, ["gsm", "gr2"], ["gr4"])
                      ACT(small_o[:8, 128:128 + ncn], al, AF.Exp, ["gsm"], ["small_o"])
                      bcast_rows(small_o[:8, 128:128 + ncn], "small_o", ncn, 0 if kind == "p" else 1)
                  ACT(gwr[:, :N], gwr[:, :N], AF.Exp, ["gr4"], ["gr4"])
                  TT("dve", gwr[:, :N], gwr[:, :N], gdt[:, :N], ALU.mult, ["gr4", "gr0"], ["gr4"])
                  TS("dve", gna[:, :N], gacs[:, :N], -1.0, None, ALU.mult, None, ["gr2", "gr1"], ["gr3", "gr1"])
                  TS("dve", gna[:, :N], gna[:, :N], pm8, opm8, ALU.mult, ALU.add, ["gr3", "c8"], ["gr3"])
                  TS("dve", grr[:, :N], gacs[:, :N], opm8, pm8, ALU.mult, ALU.add, ["gr2", "c8"], ["gr5"])
                  def sd_front(t, g=g):
                      L, ti, col, kind = t["L"], t["ti"], t["col"], t["kind"]
                      pb = t["ti"] % 2
                      Mh, Ct, xdt, xw = Mh2[pb], Ct2[pb], xdt2[pb], xw2[pb]
                      idM, idC, idX, idW = "Mh%d" % pb, "Ct%d" % pb, "xdt%d" % pb, "xw%d" % pb
                      xg = xtm[:L, ti, g * 256:(g + 1) * 256].rearrange("p (h c) -> p h c", h=4)
                      to_tok_major(gdt, "gr0", gwr, "gr4", t, 1, gscr, "gr6")
                      MM(PS["psS"][:L, :L], BTb[:, g, col:col + L], CTb[:, g, col:col + L], True, True, ["BTb", "CTb"], ["psS"])
                      expo(gna, "gr3", grr, "gr5", grr, "gr5", t, Wt, Wv)
                      TT("dve", Mh[:L, :4 * L].rearrange("p (h l) -> p h l", h=4),
                         PS["psS"][:L, :L].unsqueeze(1).broadcast_to([L, 4, L]),
                         Wt[:L, :4 * L].rearrange("p (h l) -> p h l", h=4), ALU.mult, ["psS", "Wt"], [idM])
                      TT("pool", Ct[:, :4 * L].rearrange("p (h l) -> p h l", h=4),
                         CTb[:, g, col:col + L].unsqueeze(1).broadcast_to([128, 4, L]),
                         Wv[:, :4 * L].rearrange("p (h l) -> p h l", h=4), ALU.mult, ["CTb", "Wv"], [idC])
                      xg = xtm[:L, ti, g * 256:(g + 1) * 256].rearrange("p (h c) -> p h c", h=4)
                      TT("dve", xdt[:L, :].rearrange("p (h c) -> p h c", h=4), xg,
                         scal[:L, ti, 1, 0:4].unsqueeze(2).broadcast_to([L, 4, 64]), ALU.mult, ["xtm", "scal"], [idX])
                      TT("dve", xw[:L, :].rearrange("p (h c) -> p h c", h=4), xg,
                         scal[:L, ti, 1, 4:8].unsqueeze(2).broadcast_to([L, 4, 64]), ALU.mult, ["xtm", "scal"], [idW])
                      if kind == "s":
                          TT("pool", Bm[:L, :, :], Btm[:L, ti, g, :].unsqueeze(1).broadcast_to([L, NSQ, 128]),
                             smT[:L, :].unsqueeze(2).broadcast_to([L, NSQ, 128]), ALU.mult, ["Btm", "c128"], ["Bm"])

                  def sd_rest(t, g=g):
                      L, ti, col, kind = t["L"], t["ti"], t["col"], t["kind"]
                      pb = t["ti"] % 2
                      Mh, Ct, xdt, xw = Mh2[pb], Ct2[pb], xdt2[pb], xw2[pb]
                      idM, idC, idX, idW = "Mh%d" % pb, "Ct%d" % pb, "xdt%d" % pb, "xw%d" % pb
                      xg = xtm[:L, ti, g * 256:(g + 1) * 256].rearrange("p (h c) -> p h c", h=4)
                      for h in range(4):
                          hh = g * 4 + h
                          if kind == "s":
                              TT("pool", Ctm, Ct[:, h * L:(h + 1) * L].unsqueeze(1).broadcast_to([128, NSQ, L]),
                                 smkb[:, :].rearrange("p (b l) -> p b l", b=NSQ), ALU.mult, [idC, "smkb"], ["Ctm"])
                              for b4 in range(0, NSQ, 8):
                                  DMA(Sin[:, b4:b4 + 8, :], sS_d[ly][b4:b4 + 8, hh].rearrange("b p n -> p b n"), (), ["Sin"])
                              for bq in range(NSQ // 8):
                                  for b in range(8):
                                      TR(PS["psU"][:, b * 64:(b + 1) * 64], Sin[:, bq * 8 + b, :], identf[:64, :64],
                                         ["Sin", "c128"], ["psU"])
                                  CP("act", Ssb[:, bq * 8:(bq + 1) * 8, :],
                                     PS["psU"][:, :].rearrange("p (b c) -> p b c", b=8), ["psU"], ["Ssb"])
                          MM(PS["psN0"][:L, h * 64:(h + 1) * 64], Mh[:L, h * L:(h + 1) * L], xdt[:L, h * 64:(h + 1) * 64],
                             True, False, [idM, idX], ["psN0"])
                          if kind == "p":
                              MM(PS["psN0"][:L, h * 64:(h + 1) * 64], Ct[:, h * L:(h + 1) * L],
                                 STb[:, hh * 64:(hh + 1) * 64], False, True, [idC, "STb"], ["psN0"])
                          else:
                              for b in range(NSQ):
                                  MM(PS["psN0"][:L, h * 64:(h + 1) * 64], Ctm[:, b, :], Ssb[:, b, :],
                                     False, b == NSQ - 1, ["Ctm", "Ssb"], ["psN0"])
                              for bq in range(NSQ // 4):
                                  MM(PS["psV"][:64, :512], xw[:L, h * 64:(h + 1) * 64],
                                     Bm[:L, bq * 4:(bq + 1) * 4, :], True, True, [idW, "Bm"], ["psV"])
                                  TT("dve", Sin[:, bq * 4:(bq + 1) * 4, :], Sin[:, bq * 4:(bq + 1) * 4, :],
                                     bcw[:64, 1, h, bq * 4:(bq + 1) * 4].unsqueeze(2).broadcast_to([64, 4, 128]), ALU.mult,
                                     ["Sin", "bcw"], ["Sin"])
                                  TT("dve", Sin[:, bq * 4:(bq + 1) * 4, :], Sin[:, bq * 4:(bq + 1) * 4, :],
                                     PS["psV"][:64, :512].rearrange("p (b n) -> p b n", b=4), ALU.add, ["Sin", "psV"], ["Sin"])
                              for b4 in range(0, NSQ, 8):
                                  DMA(oS_o[ly][b4:b4 + 8, hh].rearrange("b p n -> p b n"), Sin[:, b4:b4 + 8, :], ["Sin"], ())
                      TT("dve", hn[:L, 0:256].rearrange("p (h c) -> p h c", h=4), xg,
                         Dbc[:L, g * 4:(g + 1) * 4].unsqueeze(2).broadcast_to([L, 4, 64]), ALU.mult, ["xtm", "Dbc"], ["hn"])
                      TT("dve", hn[:L, 0:256], hn[:L, 0:256], PS["psN0"][:L, 0:256], ALU.add, ["hn", "psN0"], ["hn"])
                      TT("dve", hn[:L, 0:256], hn[:L, 0:256], gzs[:L, ti, g * 256:(g + 1) * 256], ALU.mult, ["hn", "gzs"], ["hn"])
                      ACT(junk[:L, 0:256], hn[:L, 0:256], AF.Square, ["hn"], ["junk", "tm4"], accum=tm4[:L, 0:1])
                      ACT(tm4[:L, 1:2], tm4[:L, 0:1], AF.Ln, ["tm4"], ["tm4"], bias=EPS, scale=1.0 / 256)
                      ACT(tm4[:L, 2:3], tm4[:L, 1:2], AF.Exp, ["tm4"], ["tm4"], scale=-0.5)
                      STT("dve", ytm[:L, 0:256], hn[:L, 0:256], tm4[:L, 2:3], ssdnwbc[:L, g * 256:(g + 1) * 256],
                          ALU.mult, ALU.mult, ["hn", "tm4", "ssdnwbc"], ["ytm"])
                      for h2 in range(2):
                          TR(PS["psU"][:, h2 * 128:h2 * 128 + L], ytm[:L, h2 * 128:(h2 + 1) * 128], identf[:L, :L],
                             ["ytm", "c128"], ["psU"])
                      CP("act", mixs[:, 2 * g:2 * g + 2, col:col + L],
                         PS["psU"][:, 0:256].rearrange("p (j c) -> p j c", j=2)[:, :, :L], ["psU"], ["mixs"])
                      if kind == "p":
                          psu, psuid = next_ps2()
                          MM(psu[:, 0:256], Btm[:L, ti, g, :], xw[:L, :], True, True, ["Btm", idW], [psuid])
                          sg = STs[:, ly, g * 256:(g + 1) * 256]
                          TT("pool", sg.rearrange("p (h c) -> p h c", h=4), sg.rearrange("p (h c) -> p h c", h=4),
                             bcw[:, 0, :, ti:ti + 1].broadcast_to([128, 4, 64]), ALU.mult, ["STs", "bcw"], ["STs"])
                          TT("dve", sg, sg, psu[:, 0:256], ALU.add, ["STs", psuid], ["STs"])
                          CP("act", STb[:, g * 256:(g + 1) * 256], sg, ["STs"], ["STb"])

                  sd_front(tiles[0])
                  for i_, t in enumerate(tiles):
                      if i_ + 1 < len(tiles):
                          sd_front(tiles[i_ + 1])
                      sd_rest(t)
              if last_pass:
                  for q4 in range(4):
                      TR(PS["psU"][:, q4 * 128:(q4 + 1) * 128], STs[:, ly, q4 * 128:(q4 + 1) * 128], identf, ["STs", "c128"], ["psU"])
                  CP("dve", hn[:, :], PS["psU"][:, :], ["psU"], ["hn"])
                  DMA(pS_o[ly].rearrange("(q p) n -> p q n", p=128), hn[:, :].rearrange("p (q n) -> p q n", q=4), ["hn"], ())
              checkpoint()
              wout_part(mixs, 4)
              checkpoint()

              if ly == L2 - 1:
                  stage_reset()
                  xn32s = [af([128, D]), af([128, D])]
                  nwbc = af([128, D])
                  junks = [ab([128, D]), ab([128, D])]
                  DMA(nwbc, fnw_d.partition_broadcast(128), (), ["nwbc"])
                  for t in tiles:
                      L, ti = t["L"], t["ti"]
                      xn32, xid = rmsnorm_tile(t, None, xn32s, junks, nwbc)
                      dst = yp_o[t["tok"]:t["tok"] + 128, :] if t["kind"] == "p" else ys_o
                      DMA(dst, xn32[:L], [xid], ())
    except _Stop:
        pass

    print("NOPS", len(R.ops), "arena max words f32/bf16", amax, flush=True)
    block = st.enter_context(nc.Block())
    R.emit(nc, block, st)
    st.close()
    return nc


_CACHE = {}


def _host_layout(inputs, NSQ, c):
    f = np.ascontiguousarray
    b0 = c * NSQ
    L2 = DEPTH
    w_in = inputs["w_in"]
    m = {}
    m["xp"] = f(inputs["x_prompt"][c % inputs["x_prompt"].shape[0]])
    m["xs"] = f(inputs["x_sample"][b0:b0 + NSQ].reshape(NSQ * 4, D))
    m["sC"] = f(inputs["state_mlstm_C"][:, b0:b0 + NSQ])
    m["sn"] = f(inputs["state_mlstm_n"][:, b0:b0 + NSQ].reshape(L2, NSQ, 512))
    smt = np.transpose(inputs["state_mlstm_m"][:, b0:b0 + NSQ], (0, 2, 1))
    m["sm"] = f(np.concatenate([smt, smt], axis=1))
    m["sS"] = f(inputs["state_ssd"][:, b0:b0 + NSQ])
    m["sscv"] = f(inputs["state_ssd_conv"][:, b0:b0 + NSQ].reshape(L2, NSQ * 3, 1024))
    m["sccv"] = f(inputs["state_sconv_conv"][:, b0:b0 + NSQ].reshape(L2, NSQ * 2, 512))
    m["srh"] = f(inputs["state_rglru_h"][:, b0:b0 + NSQ])
    m["srcv"] = f(inputs["state_rglru_conv"][:, b0:b0 + NSQ].reshape(L2, NSQ * 3, 512))
    return m


def _shared_layout(inputs, NCH, NSQ):
    f = np.ascontiguousarray
    L2 = DEPTH
    w_in = inputs["w_in"]
    s = {}
    def cols(o, n):
        return list(range(o, o + n))
    chunks = []
    for j in range(4):
        chunks.append(cols(OSB + j * 128, 128) + cols(OSC + j * 128, 128))
        chunks.append(cols(OSH + j * 128, 128) + cols(OSCZ + j * 128, 128))
    for j in range(4):
        chunks.append(cols(ORX + j * 128, 128) + cols(ORZ + j * 128, 128))
    for cg in range(4):
        chunks.append(cols(OO + cg * 128, 128) + cols(OZ + cg * 128, 128))
    for base in (OQ, OKK, OV):
        for hp in range(2):
            chunks.append(cols(base + hp * 256, 256))
    for jp in range(4):
        chunks.append(cols(OXBC + jp * 256, 256))
    for hp in range(2):
        chunks.append(cols(OSZ + hp * 256, 256))
    assert len(chunks) == 28

    def pack(W):
        K = W.shape[0] // 128
        return np.ascontiguousarray(W.reshape(K, 128, W.shape[1]).transpose(1, 0, 2).reshape(128, K * W.shape[1]))
    wpk = np.empty((L2, 28, 128, 4096), np.float32)
    for ly in range(L2):
        for k, cc in enumerate(chunks):
            wpk[ly, k] = pack(w_in[ly][:, cc])
    s["wpk"] = wpk
    wopk = np.empty((L2, 4, 8, 128, 1024), np.float32)
    for ly in range(L2):
        for g in range(4):
            for dg in range(8):
                wopk[ly, g, dg] = pack(inputs["w_out"][ly][g * 512:(g + 1) * 512, dg * 256:(dg + 1) * 256])
    s["wopk"] = wopk
    wi = w_in[:, :, OI:OI + 4]
    wf = w_in[:, :, OF:OF + 4]
    wd0 = w_in[:, :, ODT:ODT + 4]
    wd1 = w_in[:, :, ODT + 4:ODT + 8]
    wsm = np.concatenate([wi, wi, wf, wf, wd0, wd0, wd1, wd1], axis=2)
    s["wsm"] = np.stack([pack(wsm[ly]) for ly in range(L2)], axis=0)
    wabd = np.zeros((L2, 8, 128, 128), np.float32)
    for ly in range(L2):
        for j in range(4):
            for k in range(2):
                blk = 2 * j + k
                wabd[ly, j, k * 64:(k + 1) * 64, k * 64:(k + 1) * 64] = inputs["rg_wa"][ly, blk]
                wabd[ly, 4 + j, k * 64:(k + 1) * 64, k * 64:(k + 1) * 64] = inputs["rg_wx"][ly, blk]
    s["wabd"] = f(np.transpose(wabd, (0, 2, 1, 3)).reshape(L2, 128, 1024))
    cvec = np.zeros((L2, 128, 84), np.float32)

    def fm(v, nt):
        return v.reshape(nt, 128).T
    for ly in range(L2):
        cw = inputs["ssd_conv_w"][ly]
        cvec[ly, :, 0:32] = np.transpose(cw.reshape(4, 8, 128), (2, 1, 0)).reshape(128, 32)
        cvec[ly, :, 32:40] = fm(inputs["ssd_conv_b"][ly], 8)
        sw = inputs["sc_conv_w"][ly]
        cvec[ly, :, 40:52] = np.transpose(sw.reshape(3, 4, 128), (2, 1, 0)).reshape(128, 12)
        rw = inputs["rg_conv_w"][ly]
        cvec[ly, :, 52:68] = np.transpose(rw.reshape(4, 4, 128), (2, 1, 0)).reshape(128, 16)
        cvec[ly, :, 68:72] = fm(inputs["rg_conv_b"][ly], 4)
        cvec[ly, :, 72:76] = fm(inputs["rg_ba"][ly], 4)
        cvec[ly, :, 76:80] = fm(inputs["rg_bx"][ly], 4)
        cvec[ly, :, 80:84] = fm(inputs["rg_lambda"][ly], 4)
    s["cvec"] = cvec
    g8 = np.zeros((L2, 8, 8), np.float32)
    for ly in range(L2):
        d2 = lambda v: np.concatenate([v, v])
        g8[ly, :, 0] = d2(inputs["ml_i_bias"][ly])
        g8[ly, :, 1] = d2(inputs["ml_f_bias"][ly])
        g8[ly, :, 2] = d2(inputs["ssd_dt_bias"][ly][0:4])
        g8[ly, :, 3] = d2(inputs["ssd_dt_bias"][ly][4:8])
        g8[ly, :, 4] = d2(inputs["ssd_A_log"][ly][0:4])
        g8[ly, :, 5] = d2(inputs["ssd_A_log"][ly][4:8])
    s["g8"] = g8
    for k in ("norm_w", "ml_norm_w", "ssd_norm_w", "ssd_D", "final_norm_w"):
        s[k] = f(inputs[k])
    C128h, _, C8h, _ = host_consts(NCH, NSQ)
    s["c128"] = C128h
    s["c8"] = C8h
    return s


def run(inputs, NCH, n_cores=8):
    inputs = {k: np.asarray(v, dtype=np.float32) for k, v in inputs.items()}
    BP, TP = inputs["x_prompt"].shape[0], inputs["x_prompt"].shape[1]
    BS = inputs["x_sample"].shape[0]
    NSQ = BS // n_cores
    key = (TP, NCH, NSQ)
    if key not in _CACHE:
        _CACHE[key] = build(TP, NCH, NSQ)
    nc = _CACHE[key]
    shared = _shared_layout(inputs, NCH, NSQ)
    in_maps = []
    for c in range(n_cores):
        m = _host_layout(inputs, NSQ, c)
        m.update(shared)
        in_maps.append(m)
    res = run_bass_kernel_spmd(nc, in_maps, core_ids=list(range(n_cores)))
    r = res.results
    L2 = DEPTH
    cat = lambda k, ax: np.concatenate([r[c][k] for c in range(n_cores)], axis=ax)
    stk = lambda k: np.stack([r[c][k] for c in range(BP)], axis=1)
    y_prompt = np.stack([r[c]["yp"] for c in range(BP)], axis=0)
    y_sample = cat("ys", 0).reshape(BS, 4, D)
    p_C = stk("pC")
    p_n = stk("pn")
    p_m = stk("pm")
    p_ssd = stk("pS").reshape(L2, BP, 8, 64, 128)
    p_ssd_conv = stk("pscv")
    p_sc_conv = stk("pccv")
    p_rg_h = stk("prh").reshape(L2, BP, 512)
    p_rg_conv = stk("prcv")
    s_C = cat("oC", 1)
    s_n = cat("on", 1).reshape(L2, BS, 4, 128)
    s_m = np.transpose(cat("om", 2), (0, 2, 1))
    s_ssd = cat("oS", 1)
    s_ssd_conv = cat("oscv", 1).reshape(L2, BS, 3, 1024)
    s_sc_conv = cat("occv", 1).reshape(L2, BS, 2, 512)
    s_rg_h = cat("orh", 1)
    s_rg_conv = cat("orcv", 1).reshape(L2, BS, 3, 512)
    outs = (y_prompt, y_sample, p_C, p_n, p_m, p_ssd, p_ssd_conv, p_sc_conv, p_rg_h, p_rg_conv,
            s_C, s_n, s_m, s_ssd, s_ssd_conv, s_sc_conv, s_rg_h, s_rg_conv)
    return tuple(np.ascontiguousarray(o, dtype=np.float32) for o in outs)


def kernel(**inputs):
    return run(inputs, NCH=4)
```
